# Optimizing a Trainium2 kernel written in Bass

```python
import math
import jax
import jax.numpy as jnp
from jax import lax
import numpy as np

D_MODEL = 2048
BATCH = 16
SEQ = 2048
DEPTH = 4

GRID_W = 64
CTX_LEN = 256
HEAD_DIM = 128
MIX_HEADS = D_MODEL // 256
BR_W = MIX_HEADS * HEAD_DIM
A_HEADS = MIX_HEADS
A_KV_HEADS = MIX_HEADS // 4
WINDOW = 128
WBLK = 128
B_HEADS = MIX_HEADS
B_DK = HEAD_DIM
B_DV = HEAD_DIM
CONV_W = 5
CHUNK = 64
C_HEADS = MIX_HEADS
NA_ROWS = 8
NA_COLS = 16
N_BRANCH = 3
D_FF = 4 * D_MODEL
ROPE_BASE = 10000.0
EPS = 1e-6
A_Q_W = A_HEADS * HEAD_DIM
A_KV_W = A_KV_HEADS * HEAD_DIM
B_QK_W = B_HEADS * B_DK
B_V_W = B_HEADS * B_DV
C_W = C_HEADS * HEAD_DIM
SPLIT_SIZES = (A_Q_W, A_KV_W, A_KV_W, B_QK_W, B_QK_W, B_V_W, B_V_W, 4 * B_HEADS, C_W, C_W, C_W, N_BRANCH * D_MODEL)
P_TOTAL = A_Q_W + 2 * A_KV_W + 2 * B_QK_W + 2 * B_V_W + 4 * B_HEADS + 3 * C_W + N_BRANCH * D_MODEL

kernel_name = "hybrid_gated_branch_diffusion_block"


def rmsnorm(x, g):
    xf = x.astype(jnp.float32)
    y = xf * lax.rsqrt(jnp.mean(xf * xf, axis=-1, keepdims=True) + EPS)
    return (y * g.astype(jnp.float32)).astype(x.dtype)


def l2norm(x):
    xf = x.astype(jnp.float32)
    return xf * lax.rsqrt(jnp.sum(xf * xf, axis=-1, keepdims=True) + EPS)


def modulate(x, g, shift, scale):
    return rmsnorm(x, g) * (1 + scale) + shift


def heads(t, n_heads):
    return t.reshape(t.shape[:-1] + (n_heads, t.shape[-1] // n_heads))


def split_proj(p):
    return jnp.split(p, np.cumsum(SPLIT_SIZES)[:-1], axis=-1)


def axial_rope_tables(n):
    t = jnp.arange(n)
    nf = HEAD_DIM // 4
    inv = ROPE_BASE ** (-jnp.arange(nf, dtype=jnp.float32) / nf)
    ang_r = (t // GRID_W).astype(jnp.float32)[:, None] * inv
    ang_c = (t % GRID_W).astype(jnp.float32)[:, None] * inv
    ang = jnp.concatenate([ang_r, ang_c], axis=-1)
    return jnp.cos(ang), jnp.sin(ang)


def apply_axial_rope(x, cos, sin):
    b, n, h, d = x.shape
    xr = x.reshape(b, n, h, 2, 2, d // 4)
    x1, x2 = xr[..., 0, :], xr[..., 1, :]
    cs = cos.reshape(n, 1, 2, d // 4).astype(x.dtype)
    sn = sin.reshape(n, 1, 2, d // 4).astype(x.dtype)
    out = jnp.stack([x1 * cs - x2 * sn, x1 * sn + x2 * cs], axis=-2)
    return out.reshape(b, n, h, d)


def sink_softmax(s, sink_col):
    s = jnp.concatenate([s, jnp.broadcast_to(sink_col, s.shape[:-1] + (1,))], axis=-1)
    return jax.nn.softmax(s, axis=-1)[..., :-1]


def context_attention(q, kc, vc, sink=None):
    b, lq, hq, d = q.shape
    hkv = kc.shape[2]
    grp = hq // hkv
    qg = (q * d ** -0.5).reshape(b, lq, hkv, grp, d)
    s = jnp.einsum('bqhgd,bkhd->bhgqk', qg, kc).astype(jnp.float32)
    if sink is None:
        p = jax.nn.softmax(s, axis=-1)
    else:
        p = sink_softmax(s, sink.astype(jnp.float32).reshape(hkv, grp, 1, 1))
    o = jnp.einsum('bhgqk,bkhd->bqhgd', p.astype(vc.dtype), vc)
    return o.reshape(b, lq, hq * d)


def window_gqa(q, k, v, kc, vc, sink):
    b, n, hq, d = q.shape
    hkv = k.shape[2]
    grp = hq // hkv
    nb = n // WBLK
    qs = (q * d ** -0.5).reshape(b, n, hkv, grp, d)
    pad = ((0, 0), (WBLK, WBLK), (0, 0), (0, 0))
    kp = jnp.pad(k, pad)
    vp = jnp.pad(v, pad)
    rel = (np.arange(WBLK)[:, None] + WBLK) - np.arange(3 * WBLK)[None, :]
    in_band = np.abs(rel) <= WINDOW
    sink_col = sink.astype(jnp.float32).reshape(hkv, grp, 1, 1)

    def block(i):
        start = i * WBLK
        qb = lax.dynamic_slice_in_dim(qs, start, WBLK, axis=1)
        kb = lax.dynamic_slice_in_dim(kp, start, 3 * WBLK, axis=1)
        vb = lax.dynamic_slice_in_dim(vp, start, 3 * WBLK, axis=1)
        kpos = start - WBLK + jnp.arange(3 * WBLK)
        valid = in_band & ((kpos >= 0) & (kpos < n))[None, :]
        s_loc = jnp.einsum('bqhgd,bkhd->bhgqk', qb, kb).astype(jnp.float32)
        s_loc = jnp.where(valid, s_loc, -jnp.inf)
        s_ctx = jnp.einsum('bqhgd,blhd->bhgql', qb, kc).astype(jnp.float32)
        p = sink_softmax(jnp.concatenate([s_loc, s_ctx], axis=-1), sink_col).astype(v.dtype)
        o = (jnp.einsum('bhgqk,bkhd->bqhgd', p[..., :3 * WBLK], vb)
             + jnp.einsum('bhgql,blhd->bqhgd', p[..., 3 * WBLK:], vc))
        return o.reshape(b, WBLK, hq * d)

    out = lax.map(block, jnp.arange(nb))
    return jnp.moveaxis(out, 0, 1).reshape(b, n, hq * d)


def neighbourhood_attn(q, k, v, kc, vc, rpb):
    b, n, h, d = q.shape
    rows = n // GRID_W
    kr = min(NA_ROWS, rows)
    ncb = GRID_W // NA_COLS
    band = 2 * NA_COLS
    qg = (q * d ** -0.5).reshape(b, rows, GRID_W, h, d)
    kg = k.reshape(b, rows, GRID_W, h, d)
    vg = v.reshape(b, rows, GRID_W, h, d)
    qcol = np.arange(GRID_W).reshape(ncb, NA_COLS)
    bstart = np.clip(np.arange(ncb) * NA_COLS - NA_COLS // 2, 0, GRID_W - band)
    kcol = bstart[:, None] + np.arange(band)[None, :]
    cstart = np.clip(qcol - NA_COLS // 2, 0, GRID_W - NA_COLS)
    col_ok = (kcol[:, None, :] >= cstart[:, :, None]) & (kcol[:, None, :] < cstart[:, :, None] + NA_COLS)
    col_off = np.clip(kcol[:, None, :] - qcol[:, :, None], -(NA_COLS - 1), NA_COLS - 1) + NA_COLS - 1
    rpb_c = rpb.astype(jnp.float32)[:, :, col_off]

    def row_block(r):
        rs = jnp.clip(r - kr // 2, 0, rows - kr)
        kb = lax.dynamic_slice_in_dim(kg, rs, kr, axis=1)[:, :, kcol]
        vb = lax.dynamic_slice_in_dim(vg, rs, kr, axis=1)[:, :, kcol]
        qr = lax.dynamic_index_in_dim(qg, r, axis=1, keepdims=False).reshape(b, ncb, NA_COLS, h, d)
        roff = rs + jnp.arange(kr) - r + NA_ROWS - 1
        bias = jnp.transpose(rpb_c[:, roff], (0, 2, 3, 1, 4))
        s_loc = jnp.einsum('bcqhd,bicjhd->bhcqij', qr, kb).astype(jnp.float32) + bias
        s_loc = jnp.where(col_ok[:, :, None, :], s_loc, -jnp.inf)
        s_loc = s_loc.reshape(b, h, ncb, NA_COLS, kr * band)
        s_ctx = jnp.einsum('bcqhd,blhd->bhcql', qr, kc).astype(jnp.float32)
        p = jax.nn.softmax(jnp.concatenate([s_loc, s_ctx], axis=-1), axis=-1).astype(v.dtype)
        p_loc = p[..., :kr * band].reshape(b, h, ncb, NA_COLS, kr, band)
        o = (jnp.einsum('bhcqij,bicjhd->bcqhd', p_loc, vb)
             + jnp.einsum('bhcql,blhd->bcqhd', p[..., kr * band:], vc))
        return o.reshape(b, GRID_W, h, d)

    out = lax.map(row_block, jnp.arange(rows))
    return jnp.moveaxis(out, 0, 1).reshape(b, n, h * d)


def short_conv(x, w):
    ch = x.shape[-1]
    return lax.conv_general_dilated(
        x, w[:, None, :].astype(x.dtype), window_strides=(1,),
        padding=[(CONV_W // 2, CONV_W // 2)],
        dimension_numbers=('NWC', 'WIO', 'NWC'), feature_group_count=ch)


def chunk_gated_delta(q, k, v, g, beta, state):
    b, n, h, _ = q.shape
    dv = v.shape[-1]
    nc = n // CHUNK

    def chunks(t):
        t = t.astype(jnp.float32).reshape((b, nc, CHUNK, h) + t.shape[3:])
        return jnp.moveaxis(t, (1, 3), (0, 2))

    qc, kc, vc = chunks(q), chunks(k), chunks(v)
    gc = jnp.cumsum(chunks(g), axis=-1)
    bc = chunks(beta)
    kb = kc * bc[..., None]
    vb = vc * bc[..., None]
    incl = np.tril(np.ones((CHUNK, CHUNK), dtype=bool))
    strict = np.tril(np.ones((CHUNK, CHUNK), dtype=bool), -1)
    decay = jnp.exp(jnp.where(incl, gc[..., :, None] - gc[..., None, :], -jnp.inf))
    lmat = jnp.where(strict, jnp.einsum('...id,...jd->...ij', kb, kc) * decay, 0.0)
    eye = jnp.eye(CHUNK, dtype=jnp.float32)
    a = eye + lmat
    tinv = lax.linalg.triangular_solve(a, jnp.broadcast_to(eye, a.shape), left_side=True, lower=True)
    u = tinv @ vb
    w = tinv @ (kb * jnp.exp(gc)[..., None])
    attn = jnp.einsum('...id,...jd->...ij', qc, kc) * decay

    def step(s, xs):
        qi, ki, ui, wi, gi, ai = xs
        v_new = ui - wi @ s
        o = (qi * jnp.exp(gi)[..., None]) @ s + ai @ v_new
        gl = gi[..., -1:]
        s = s * jnp.exp(gl)[..., None] + jnp.einsum('bhck,bhcv->bhkv', ki * jnp.exp(gl - gi)[..., None], v_new)
        return s, o

    state, o = lax.scan(step, state, (qc, kc, u, w, gc, attn))
    o = jnp.moveaxis(o, (0, 2), (1, 3)).reshape(b, n, h, dv)
    return o, state


def gdn_branch(q, k, v, z, ab, conv_w, a_log, dt_bias, g_out, state_f, state_b):
    qkv = jax.nn.silu(short_conv(jnp.concatenate([q, k, v], axis=-1), conv_w))
    q, k, v = jnp.split(qkv, [B_QK_W, 2 * B_QK_W], axis=-1)
    q = l2norm(heads(q, B_HEADS)) * B_DK ** -0.5
    k = l2norm(heads(k, B_HEADS))
    v = heads(v, B_HEADS)
    a_f, a_b, b_f, b_b = jnp.split(ab.astype(jnp.float32), 4, axis=-1)
    a_log = a_log.astype(jnp.float32)
    dt_bias = dt_bias.astype(jnp.float32)
    g_f = -jnp.exp(a_log[0]) * jax.nn.softplus(a_f + dt_bias[0])
    g_b = -jnp.exp(a_log[1]) * jax.nn.softplus(a_b + dt_bias[1])
    o_f, s_f = chunk_gated_delta(q, k, v, g_f, jax.nn.sigmoid(b_f), state_f)
    rev = lambda t: t[:, ::-1]
    o_b, s_b = chunk_gated_delta(rev(q), rev(k), rev(v), rev(g_b), rev(jax.nn.sigmoid(b_b)), state_b)
    o = (o_f + rev(o_b)).astype(z.dtype)
    o = rmsnorm(o, g_out) * jax.nn.silu(heads(z, B_HEADS))
    return o.reshape(o.shape[:2] + (B_V_W,)), s_f, s_b


def merge_branches(branches, gate_logits, w_branch, w_out):
    gates = jax.nn.sigmoid(gate_logits.reshape(gate_logits.shape[:-1] + (N_BRANCH, D_MODEL)))
    y = gates[..., 0, :] * (branches[0] @ w_branch[0])
    for i in range(1, N_BRANCH):
        y = y + gates[..., i, :] * (branches[i] @ w_branch[i])
    return y @ w_out


def sqrelu_mlp(h, w_up, w_down):
    return jnp.square(jax.nn.relu(h @ w_up)) @ w_down


def setup_inputs(seed: int = 0) -> dict:
    key = jax.random.key(seed)
    ks = jax.random.split(key, 24)
    f32 = jnp.float32
    nrm = lambda k, shape, scale: jax.random.normal(k, shape, f32) * scale
    dt = jnp.exp(jax.random.uniform(ks[10], (DEPTH, 2, B_HEADS), f32, math.log(1e-3), math.log(1e-1)))
    return {
        "x": nrm(ks[0], (BATCH, SEQ, D_MODEL), 1.0),
        "c": nrm(ks[1], (BATCH, D_MODEL), 1.0),
        "ctx": nrm(ks[2], (BATCH, CTX_LEN, D_MODEL), 1.0),
        "c_ctx": nrm(ks[3], (D_MODEL,), 1.0),
        "w_mod": nrm(ks[4], (DEPTH, D_MODEL, 6 * D_MODEL), 0.5 * D_MODEL ** -0.5),
        "b_mod": nrm(ks[5], (DEPTH, 6 * D_MODEL), 0.02),
        "g_pre_mix": 1.0 + nrm(ks[6], (DEPTH, D_MODEL), 0.02),
        "w_in": nrm(ks[7], (DEPTH, D_MODEL, P_TOTAL), D_MODEL ** -0.5),
        "conv_w": nrm(ks[8], (DEPTH, CONV_W, 2 * B_QK_W + B_V_W), CONV_W ** -0.5),
        "a_log": jnp.log(jax.random.uniform(ks[9], (DEPTH, 2, B_HEADS), f32, 1.0, 16.0)),
        "dt_bias": dt + jnp.log(-jnp.expm1(-dt)),
        "g_gdn_out": 1.0 + nrm(ks[11], (DEPTH, B_DV), 0.02),
        "sink": nrm(ks[12], (DEPTH, A_HEADS), 0.5),
        "rpb": nrm(ks[13], (DEPTH, C_HEADS, 2 * NA_ROWS - 1, 2 * NA_COLS - 1), 0.5),
        "w_branch": nrm(ks[14], (DEPTH, N_BRANCH, BR_W, D_MODEL), BR_W ** -0.5),
        "w_out": nrm(ks[15], (DEPTH, D_MODEL, D_MODEL), D_MODEL ** -0.5),
        "g_post_mix": 1.0 + nrm(ks[16], (DEPTH, D_MODEL), 0.02),
        "g_pre_mlp": 1.0 + nrm(ks[17], (DEPTH, D_MODEL), 0.02),
        "w_up": nrm(ks[18], (DEPTH, D_MODEL, D_FF), D_MODEL ** -0.5),
        "w_down": nrm(ks[19], (DEPTH, D_FF, D_MODEL), D_FF ** -0.5),
        "g_post_mlp": 1.0 + nrm(ks[20], (DEPTH, D_MODEL), 0.02),
    }


def reference(x, c, ctx, c_ctx, w_mod, b_mod, g_pre_mix, w_in, conv_w, a_log, dt_bias,
              g_gdn_out, sink, rpb, w_branch, w_out, g_post_mix, g_pre_mlp, w_up, w_down, g_post_mlp):
    b, n, _ = x.shape
    cos, sin = axial_rope_tables(n)
    zero_state = jnp.zeros((b, B_HEADS, B_DK, B_DV), jnp.float32)
    cx = ctx
    for l in range(DEPTH):
        last = l == DEPTH - 1
        mod_x = (jax.nn.silu(c) @ w_mod[l] + b_mod[l])[:, None, :]
        mod_c = (jax.nn.silu(c_ctx) @ w_mod[l] + b_mod[l])[None, None, :]
        x_sh1, x_sc1, x_gt1, x_sh2, x_sc2, x_gt2 = jnp.split(mod_x, 6, axis=-1)
        c_sh1, c_sc1, c_gt1, c_sh2, c_sc2, c_gt2 = jnp.split(mod_c, 6, axis=-1)

        (x_qa, x_ka, x_va, x_qb, x_kb, x_vb, x_zb, x_ab, x_qn, x_kn, x_vn, x_gate) = split_proj(
            modulate(x, g_pre_mix[l], x_sh1, x_sc1) @ w_in[l])
        (c_qa, c_ka, c_va, c_qb, c_kb, c_vb, c_zb, c_ab, c_qn, c_kn, c_vn, c_gate) = split_proj(
            modulate(cx, g_pre_mix[l], c_sh1, c_sc1) @ w_in[l])

        ka_ctx, va_ctx = heads(c_ka, A_KV_HEADS), heads(c_va, A_KV_HEADS)
        o_a = window_gqa(apply_axial_rope(heads(x_qa, A_HEADS), cos, sin),
                         apply_axial_rope(heads(x_ka, A_KV_HEADS), cos, sin),
                         heads(x_va, A_KV_HEADS), ka_ctx, va_ctx, sink[l])

        o_b_ctx, st_f, st_b = gdn_branch(c_qb, c_kb, c_vb, c_zb, c_ab, conv_w[l], a_log[l], dt_bias[l],
                                         g_gdn_out[l], zero_state, zero_state)
        o_b, _, _ = gdn_branch(x_qb, x_kb, x_vb, x_zb, x_ab, conv_w[l], a_log[l], dt_bias[l],
                               g_gdn_out[l], st_f, st_b)

        kn_ctx, vn_ctx = heads(c_kn, C_HEADS), heads(c_vn, C_HEADS)
        o_n = neighbourhood_attn(heads(x_qn, C_HEADS), heads(x_kn, C_HEADS), heads(x_vn, C_HEADS),
                                 kn_ctx, vn_ctx, rpb[l])

        x = x + x_gt1 * rmsnorm(merge_branches((o_a, o_b, o_n), x_gate, w_branch[l], w_out[l]), g_post_mix[l])
        x = x + x_gt2 * rmsnorm(sqrelu_mlp(modulate(x, g_pre_mlp[l], x_sh2, x_sc2), w_up[l], w_down[l]),
                                g_post_mlp[l])

        if not last:
            o_a_ctx = context_attention(heads(c_qa, A_HEADS), ka_ctx, va_ctx, sink[l])
            o_n_ctx = context_attention(heads(c_qn, C_HEADS), kn_ctx, vn_ctx)
            cx = cx + c_gt1 * rmsnorm(merge_branches((o_a_ctx, o_b_ctx, o_n_ctx), c_gate, w_branch[l], w_out[l]),
                                      g_post_mix[l])
            cx = cx + c_gt2 * rmsnorm(sqrelu_mlp(modulate(cx, g_pre_mlp[l], c_sh2, c_sc2), w_up[l], w_down[l]),
                                      g_post_mlp[l])
    return x
```

```python
import math
from contextlib import ExitStack

import numpy as np
import concourse.bass as bass
import concourse.mybir as mybir
from concourse.bass_utils import run_bass_kernel_spmd

F32 = mybir.dt.float32
BF16 = mybir.dt.bfloat16
AF = mybir.ActivationFunctionType
ALU = mybir.AluOpType
AX = mybir.AxisListType

D = 2048
NCTX = 256
NX = 2048
T = NCTX + NX
DEPTH = 4
NSEQ = 2
KC = D // 128
DFF = 4 * D
HD = 128
EPS = 1e-6
TGS = [(0, 256)] + [(256 + 512 * i, 512) for i in range(4)]
NFM = 116
W_FM = NFM * 128
W_TM = 256 + 1024 + 32
W_ALL = W_FM + W_TM


class Buf:
    __slots__ = ("name", "lw", "rd", "excl")

    def __init__(self, name=""):
        self.name = name
        self.lw = None
        self.rd = {}
        self.excl = False


class TB:
    def __init__(self, t, name=""):
        self.t = t
        self.b = Buf(name)


COMPUTE = ("pe", "dve", "act", "pool")
NDS = 24


class Prog:
    def __init__(self, nc):
        self.nc = nc
        self.E = {"pe": nc.tensor, "dve": nc.vector, "act": nc.scalar, "pool": nc.gpsimd, "sp": nc.sync}
        self.sems = []
        self.semidx = {}
        for e in COMPUTE:
            self.semidx[e] = len(self.sems)
            self.sems.append(nc.alloc_semaphore("s_" + e))
        self.cnt = {e: 0 for e in COMPUTE}
        self.pend = {e: None for e in COMPUTE}
        self.seen = {e: {} for e in self.E}
        self.dslots = []
        for i in range(NDS):
            self.dslots.append([len(self.sems), 0])
            self.sems.append(nc.alloc_semaphore("d%d" % i))
        self.dnext = 0
        self.nwaits = 0
        self.nops = 0

    def _wait(self, e, toks):
        need = {}
        for t in toks:
            if t is None:
                continue
            te, si, val = t
            if e == "pe" and te == "pe":
                continue
            assert val is not None, "wait on pending token"
            if self.seen[e].get(si, 0) >= val:
                continue
            if need.get(si, 0) < val:
                need[si] = val
        for si, val in need.items():
            self.E[e].wait_ge(self.sems[si], val)
            self.seen[e][si] = val
            self.nwaits += 1

    def _deps(self, e, r, w):
        deps = []
        for b in r:
            if b.lw is not None:
                deps.append(b.lw)
            if b.excl:
                for k, t in b.rd.items():
                    if k != e:
                        deps.append(t)
        for b in w:
            if b.lw is not None:
                deps.append(b.lw)
            for k, t in b.rd.items():
                if k == e and e in COMPUTE:
                    continue
                deps.append(t)
        return deps

    def op(self, e, fn, r=(), w=(), inc=True):
        self._wait(e, self._deps(e, r, w))
        ins = fn(self.E[e])
        self.nops += 1
        if inc:
            self.cnt[e] += 1
            ins.then_inc(self.sems[self.semidx[e]], 1)
            tok = self.pend[e]
            if tok is None:
                tok = [e, self.semidx[e], None]
            tok[2] = self.cnt[e]
            self.pend[e] = None
        else:
            tok = self.pend[e]
            if tok is None:
                tok = self.pend[e] = [e, self.semidx[e], None]
        for b in r:
            b.rd[e] = tok
        for b in w:
            b.lw = tok
            b.rd = {}
        return ins

    def dma(self, q, out, in_, r=(), w=(), **kw):
        deps = self._deps(q, r, w)
        slot = self.dslots[self.dnext]
        self.dnext = (self.dnext + 1) % NDS
        if slot[1] > 0:
            deps.append(["dma", slot[0], slot[1]])
        self._wait(q, deps)
        ins = self.E[q].dma_start(out=out, in_=in_, **kw)
        slot[1] += 16
        ins.then_inc(self.sems[slot[0]], 16)
        self.nops += 1
        tok = ["dma", slot[0], slot[1]]
        for b in r:
            b.rd[("dma", slot[0])] = tok
        for b in w:
            b.lw = tok
            b.rd = {}
        return ins

    def all_toks(self):
        toks = []
        for e in COMPUTE:
            assert self.pend[e] is None, "pending at barrier on " + e
            if self.cnt[e] > 0:
                toks.append(["x", self.semidx[e], self.cnt[e]])
        for slot in self.dslots:
            if slot[1] > 0:
                toks.append(["dma", slot[0], slot[1]])
        return toks

    def barrier(self, engines=None):
        toks = self.all_toks()
        for e in (engines or self.E):
            self._wait(e, toks)


class Cfg:
    def __init__(self, **kw):
        self.layers = DEPTH
        self.nseq = NSEQ
        self.debug = False
        self.stages = "MABCDE"
        self.inject = ()
        self.wdepth = DEPTH
        self.ab_parts = "mft"
        self.mixers = "ANB"
        self.force_ctx = False
        self.nfm = NFM
        self.__dict__.update(kw)


def build_program(cfg):
    nc = bass.Bass("TRN2", target_bir_lowering=False)
    P = Prog(nc)
    L = cfg.layers
    NS = cfg.nseq
    WD = cfg.wdepth

    def din(name, shape, dt=F32):
        return nc.dram_tensor(name, list(shape), dt, kind="ExternalInput").ap()

    def dscratch(name, shape, dt):
        if name in cfg.inject:
            kind = "ExternalInput"
        elif cfg.debug:
            kind = "ExternalOutput"
        else:
            kind = "Internal"
        return TB(nc.dram_tensor(name, list(shape), dt, kind=kind).ap(), name)

    xin = din("xin", [NSEQ, D, T])
    c3 = din("c3", [3, D])
    w_mod = din("w_mod", [WD, D, 6 * D])
    b_mod = din("b_mod", [WD, 6 * D])
    gvec = din("gvec", [WD, 4, D])
    w_in = din("w_in", [WD, D, W_ALL])
    conv_w = din("conv_w", [WD, 5, 3072])
    a_log = din("a_log", [WD, 16])
    dt_bias = din("dt_bias", [WD, 16])
    g_gdn = din("g_gdn", [WD, 128])
    sink = din("sink", [WD, 8])
    rpb = din("rpb", [WD, 8 * 15 * 31])
    w_branch = din("w_branch", [WD, 3, 1024, D])
    w_out = din("w_out", [WD, D, D])
    w_up = din("w_up", [WD, D, DFF])
    w_down = din("w_down", [WD, DFF, D])
    ropec = din("ropec", [128, T])
    ropes = din("ropes", [128, T])
    cmasks = din("cmasks", [128, 13, 128])
    yout = nc.dram_tensor("yout", [NSEQ, D, NX], F32, kind="ExternalOutput").ap()

    xT = dscratch("xT", [NSEQ, D, T], F32)
    s_fm = dscratch("s_fm", [W_FM, T], BF16)
    s_va = dscratch("s_va", [T, 256], BF16)
    s_vn = dscratch("s_vn", [T, 1024], BF16)
    s_ab = dscratch("s_ab", [T, 32], F32)
    s_o = dscratch("s_o", [3, 1024, T], BF16)
    OFF_QA, OFF_KA, OFF_QKVB, OFF_ZB, OFF_QN, OFF_KN, OFF_GATE = 0, 8, 10, 34, 42, 50, 58
    NFM_OUT = 106

    gs = ExitStack()

    uniq = {"n": 0}

    def sb(name, shape, dt, stack=None):
        uniq["n"] += 1
        nm = "%s_%d" % (name, uniq["n"])
        return TB((stack or gs).enter_context(nc.sbuf_tensor(nm, list(shape), dt)), nm)

    psum = [TB(gs.enter_context(nc.psum_tensor("ps%d" % i, [128, 512], F32)), "ps%d" % i) for i in range(8)]
    for p_ in psum:
        p_.b.excl = True
    pstate = {"i": 0}

    def ps_next():
        p = psum[pstate["i"]]
        pstate["i"] = (pstate["i"] + 1) % 8
        return p

    class PSPool:
        def __init__(self, idx):
            self.idx = list(idx)
            self.i = 0

        def next(self):
            p = psum[self.idx[self.i]]
            self.i = (self.i + 1) % len(self.idx)
            return p

    ones_f = sb("ones_f", [128, 128], F32)
    ones_b = sb("ones_b", [128, 128], BF16)
    modT = sb("modT", [128, DEPTH, 96, 3], F32)
    gT = sb("gT", [128, DEPTH, 4, 16], F32)
    coef = sb("coef", [128, 6, 16, 3], F32)
    negones = sb("negones", [128, 128], F32)
    P.op("dve", lambda e: e.memset(negones.t[:], -1.0), w=[negones.b])
    epsc = sb("epsc", [128, 4], F32)
    P.op("dve", lambda e: e.memset(epsc.t[:], EPS), w=[epsc.b])
    P.op("dve", lambda e: e.memset(ones_f.t[:], 1.0), w=[ones_f.b])
    P.op("dve", lambda e: e.memset(ones_b.t[:], 1.0), w=[ones_b.b])
    cm = sb("cm", [128, 13, 128], F32)
    for j in range(13):
        P.dma("sp", cm.t[:, j, :], cmasks[:, j, :], w=[cm.b])
    ident = cm.t[:, 0, :]

    def load_T(dst_ap, src_rows, nrows, tag):
        with ExitStack() as st:
            tmp = sb("ldT" + tag, [128, 128], F32, st)
            P.dma("sp", tmp.t[0:nrows, :], src_rows, w=[tmp.b])
            ps = ps_next()
            P.op("pe", lambda e: e.transpose(out=ps.t[:, 0:nrows], in_=tmp.t[0:nrows, :], identity=cm.t[0:nrows, 0, 0:nrows]),
                 r=[tmp.b, cm.b], w=[ps.b])
            P.op("dve", lambda e: e.tensor_copy(out=dst_ap, in_=ps.t[:, 0:nrows]), r=[ps.b], w=[gT.b, modT.b])
            P.barrier()

    gv = gvec.rearrange("l k (c p) -> (l k c) p", p=128)
    gflat = gT.t[:].rearrange("p l k c -> p (l k c)")
    for j in range(0, L * 64, 128):
        nr = min(128, L * 64 - j)
        load_T(gflat[:, j:j + nr], gv[j:j + nr, :], nr, "g%d" % j)

    def stage_M():
        with ExitStack() as st:
            scT = sb("scT", [128, 3, 16], F32, st)
            wblk = [sb("wmblk%d" % i, [128, 16, 512], F32, st) for i in range(2)]
            brow = [sb("brow%d" % i, [1, 512], F32, st) for i in range(2)]
            load_T(scT.t[:].rearrange("p r c -> p (r c)"), c3.rearrange("r (c p) -> (r c) p", p=128), 48, "c3")
            P.op("act", lambda e: e.activation(out=scT.t[:], in_=scT.t[:], func=AF.Silu), r=[scT.b], w=[scT.b])
            it = 0
            for l in range(L):
                wv = w_mod[l].rearrange("(kc p) n -> p kc n", p=128)
                for blk in range(24):
                    wb = wblk[it % 2]
                    br = brow[it % 2]
                    it += 1
                    for j in range(4):
                        P.dma("sp", wb.t[:, 4 * j:4 * j + 4, :], wv[:, 4 * j:4 * j + 4, blk * 512:(blk + 1) * 512],
                              w=[wb.b])
                    P.dma("sp", br.t[:], b_mod[l:l + 1, blk * 512:(blk + 1) * 512], w=[br.b])
                    ps = ps_next()
                    for j in range(4):
                        def mm(e, j=j, wb=wb, br=br, ps=ps):
                            for kc in range(16):
                                e.matmul(ps.t[:, 3 * j:3 * j + 3], lhsT=wb.t[:, kc, j * 128:(j + 1) * 128],
                                         rhs=scT.t[:, :, kc], start=(kc == 0), stop=False)
                            return e.matmul(ps.t[:, 3 * j:3 * j + 3], lhsT=br.t[0:1, j * 128:(j + 1) * 128],
                                            rhs=ones_f.t[0:1, 0:3], start=False, stop=True)
                        P.op("pe", mm, r=[wb.b, br.b, scT.b, ones_f.b], w=[ps.b], inc=(j == 3))
                    P.op("dve", lambda e, ps=ps, l=l, blk=blk: e.tensor_copy(
                        out=modT.t[:, l, blk * 4:(blk + 1) * 4, :],
                        in_=ps.t[:, 0:12].rearrange("p (a b) -> p a b", b=3)), r=[ps.b], w=[modT.b])
            P.barrier()

    def layer_coefs(l):
        def m(idx):
            return modT.t[:, l, idx * 16:(idx + 1) * 16, :]

        def g(k):
            return gT.t[:, l, k, :].unsqueeze(2).broadcast_to([128, 16, 3])
        for (dst, gi, mi, plus1) in ((0, 0, 1, True), (2, 1, 2, False), (3, 2, 4, True), (5, 3, 5, False)):
            if plus1:
                P.op("dve", lambda e, dst=dst, gi=gi, mi=mi: e.scalar_tensor_tensor(
                    out=coef.t[:, dst], in0=m(mi), scalar=1.0, in1=g(gi), op0=ALU.add, op1=ALU.mult),
                    r=[modT.b, gT.b], w=[coef.b])
            else:
                P.op("dve", lambda e, dst=dst, gi=gi, mi=mi: e.tensor_tensor(
                    out=coef.t[:, dst], in0=m(mi), in1=g(gi), op=ALU.mult), r=[modT.b, gT.b], w=[coef.b])
        P.op("dve", lambda e: e.tensor_copy(out=coef.t[:, 1], in_=m(0)), r=[modT.b], w=[coef.b])
        P.op("dve", lambda e: e.tensor_copy(out=coef.t[:, 4], in_=m(3)), r=[modT.b], w=[coef.b])

    def modulate(st, src, s, ia, ib, hT, tgs, xt=None):
        if xt is None:
            xt = [sb("mx%d" % i, [128, 16, 512], F32, st) for i in range(2)]
        sq = [sb("msq%d" % i, [128, 512], F32, st) for i in range(2)]
        rstd = sb("mrstd", [128, 512], F32, st)
        tmp = [sb("mtmp%d" % i, [128, 512], F32, st) for i in range(2)]
        srcv = src.rearrange("(c p) t -> p c t", p=128)
        for gi, (t0, n) in enumerate(tgs):
            row = 2 if t0 < NCTX else s
            x = xt[gi % len(xt)]
            for j in range(4):
                P.dma("sp", x.t[:, 4 * j:4 * j + 4, 0:n], srcv[:, 4 * j:4 * j + 4, t0:t0 + n], r=[xT.b], w=[x.b])
            ps = ps_next()
            for kc in range(16):
                q = sq[kc % 2]
                P.op("act", lambda e, q=q, x=x, kc=kc: e.activation(out=q.t[:, 0:n], in_=x.t[:, kc, 0:n], func=AF.Square),
                     r=[x.b], w=[q.b])
                P.op("pe", lambda e, q=q, kc=kc, ps=ps: e.matmul(ps.t[:, 0:n], lhsT=ones_f.t[:], rhs=q.t[:, 0:n],
                                                                 start=(kc == 0), stop=(kc == 15)),
                     r=[q.b, ones_f.b], w=[ps.b], inc=True)
            P.op("act", lambda e, ps=ps: e.activation(out=rstd.t[:, 0:n], in_=ps.t[:, 0:n], func=AF.Sqrt, bias=epsc.t[:, 0:1],
                                                      scale=1.0 / D), r=[ps.b, epsc.b], w=[rstd.b])
            P.op("dve", lambda e: e.reciprocal(out=rstd.t[:, 0:n], in_=rstd.t[:, 0:n]), r=[rstd.b], w=[rstd.b])
            for kc in range(16):
                tm = tmp[kc % 2]
                P.op("dve", lambda e, tm=tm, x=x, kc=kc: e.scalar_tensor_tensor(
                    out=tm.t[:, 0:n], in0=x.t[:, kc, 0:n], scalar=coef.t[:, ia, kc, row:row + 1], in1=rstd.t[:, 0:n],
                    op0=ALU.mult, op1=ALU.mult), r=[x.b, coef.b, rstd.b], w=[tm.b])
                P.op("act", lambda e, tm=tm, kc=kc: e.activation(
                    out=hT.t[:, kc, t0:t0 + n], in_=tm.t[:, 0:n], func=AF.Identity,
                    bias=coef.t[:, ib, kc, row:row + 1], scale=1.0), r=[tm.b, coef.b], w=[hT.b])

    def stage_AB(l, s):
        with ExitStack() as st:
            hT = sb("hT", [128, 16, T], BF16, st)
            with ExitStack() as st2:
                src = xin[s] if l == 0 else xT.t[s]
                modulate(st2, src, s, 0, 1, hT, TGS)
                P.barrier()
            if "f" not in cfg.ab_parts:
                return
            wblk = [sb("wblk%d" % i, [128, 16, 512], BF16, st) for i in range(3)]
            stg = [sb("stg%d" % i, [128, T], BF16, st) for i in range(4)]
            cosT = sb("cosT", [128, T], F32, st)
            sinT = sb("sinT", [128, T], F32, st)
            r1 = [sb("r1_%d" % i, [128, 512], F32, st) for i in range(2)]
            r2 = [sb("r2_%d" % i, [128, 512], F32, st) for i in range(2)]
            for j in range(3):
                a, b_ = j * 768, (j + 1) * 768
                P.dma("sp", cosT.t[:, a:b_], ropec[:, a:b_], w=[cosT.b])
                P.dma("sp", sinT.t[:, a:b_], ropes[:, a:b_], w=[sinT.b])
            wv = w_in[l].rearrange("(kc p) n -> p kc n", p=128)
            nblk = (W_ALL + 511) // 512
            blkbuf = {}

            def load_blk(bi):
                wb = wblk[bi % 3]
                c0 = bi * 512
                n = min(512, W_ALL - c0)
                for j in range(2):
                    P.dma("pool", wb.t[:, 8 * j:8 * j + 8, 0:n], wv[:, 8 * j:8 * j + 8, c0:c0 + n], w=[wb.b])
                blkbuf[bi] = wb

            load_blk(0)
            load_blk(1)
            si = 0
            evq = 0
            oc_out = 0
            ci = 0
            while ci < cfg.nfm:
                bi = ci // 4
                if ci % 4 == 0 and bi + 2 < nblk:
                    load_blk(bi + 2)
                wb = blkbuf[bi]
                rope = ci < 20
                kind = "plain"
                if ci >= 20 + 24 and ci < 20 + 32:
                    kind = "silu"
                if ci >= 20 + 48:
                    kind = "sigmoid"
                sg = stg[si % 4]
                si += 1
                for gi, (t0, n) in enumerate(TGS):
                    pa = ps_next()

                    def mm(e, pa=pa, wb=wb, cc=ci % 4, t0=t0, n=n):
                        for kc in range(16):
                            ins = e.matmul(pa.t[:, 0:n], lhsT=wb.t[:, kc, cc * 128:(cc + 1) * 128],
                                           rhs=hT.t[:, kc, t0:t0 + n], start=(kc == 0), stop=(kc == 15))
                        return ins
                    P.op("pe", mm, r=[wb.b, hT.b], w=[pa.b])
                    if rope:
                        pb = ps_next()
                        P.op("pe", lambda e, pb=pb, wb=wb, cc=ci % 4 + 1, t0=t0, n=n: mm(e, pb, wb, cc, t0, n),
                             r=[wb.b, hT.b], w=[pb.b])
                        a1 = r1[gi % 2]
                        a2 = r2[gi % 2]
                        P.op("dve", lambda e, a1=a1, pa=pa, t0=t0, n=n: e.tensor_tensor(
                            out=a1.t[:, 0:n], in0=pa.t[:, 0:n], in1=cosT.t[:, t0:t0 + n], op=ALU.mult),
                            r=[pa.b, cosT.b], w=[a1.b])
                        P.op("dve", lambda e, a2=a2, pb=pb, t0=t0, n=n: e.tensor_tensor(
                            out=a2.t[:, 0:n], in0=pb.t[:, 0:n], in1=sinT.t[:, t0:t0 + n], op=ALU.mult),
                            r=[pb.b, sinT.b], w=[a2.b])
                        P.op("pool", lambda e, a1=a1, a2=a2, sg=sg, t0=t0, n=n: e.tensor_tensor(
                            out=sg.t[:, t0:t0 + n], in0=a1.t[:, 0:n], in1=a2.t[:, 0:n], op=ALU.add),
                            r=[a1.b, a2.b], w=[sg.b])
                    elif kind == "plain":
                        evq += 1
                        if evq % 2:
                            P.op("dve", lambda e, pa=pa, sg=sg, t0=t0, n=n: e.tensor_copy(
                                out=sg.t[:, t0:t0 + n], in_=pa.t[:, 0:n]), r=[pa.b], w=[sg.b])
                        else:
                            P.op("act", lambda e, pa=pa, sg=sg, t0=t0, n=n: e.activation(
                                out=sg.t[:, t0:t0 + n], in_=pa.t[:, 0:n], func=AF.Copy), r=[pa.b], w=[sg.b])
                    else:
                        fn = AF.Silu if kind == "silu" else AF.Sigmoid
                        P.op("act", lambda e, pa=pa, sg=sg, t0=t0, n=n, fn=fn: e.activation(
                            out=sg.t[:, t0:t0 + n], in_=pa.t[:, 0:n], func=fn), r=[pa.b], w=[sg.b])
                P.dma("sp", s_fm.t[oc_out * 128:(oc_out + 1) * 128, :], sg.t[:], r=[sg.b], w=[s_fm.b])
                oc_out += 1
                ci += 2 if rope else 1
            if 't' not in cfg.ab_parts:
                P.barrier()
                return
            tstg = [sb("tstg%d" % i, [128, 512], BF16, st) for i in range(3)]
            tstf = [sb("tstf%d" % i, [128, 32], F32, st) for i in range(2)]
            ti = 0
            for bi in range(NFM // 4, nblk):
                if bi + 2 < nblk and bi + 2 not in blkbuf:
                    load_blk(bi + 2)
                wb = blkbuf[bi]
                c0 = bi * 512 - W_FM
                n = min(512, W_TM - c0)
                for tt in range(T // 128):
                    pa = ps_next()

                    def mmt(e, pa=pa, wb=wb, tt=tt, n=n):
                        for kc in range(16):
                            ins = e.matmul(pa.t[:, 0:n], lhsT=hT.t[:, kc, tt * 128:(tt + 1) * 128],
                                           rhs=wb.t[:, kc, 0:n], start=(kc == 0), stop=(kc == 15))
                        return ins
                    P.op("pe", mmt, r=[wb.b, hT.b], w=[pa.b])
                    segs = []
                    for (nm, a, b_) in (("va", 0, 256), ("vn", 256, 1280), ("ab", 1280, 1312)):
                        lo, hi = max(a, c0), min(b_, c0 + n)
                        if lo < hi:
                            segs.append((nm, lo - a, hi - a, lo - c0, hi - c0))
                    for (nm, d0, d1, p0, p1) in segs:
                        if nm == "ab":
                            tf = tstf[ti % 2]
                            if tt % 2:
                                P.op("act", lambda e, tf=tf, pa=pa, p0=p0, p1=p1: e.activation(
                                    out=tf.t[:, 0:32], in_=pa.t[:, p0:p1], func=AF.Copy), r=[pa.b], w=[tf.b])
                            else:
                                P.op("dve", lambda e, tf=tf, pa=pa, p0=p0, p1=p1: e.tensor_copy(
                                    out=tf.t[:, 0:32], in_=pa.t[:, p0:p1]), r=[pa.b], w=[tf.b])
                            P.dma("sp", s_ab.t[tt * 128:(tt + 1) * 128, :], tf.t[:], r=[tf.b], w=[s_ab.b])
                        else:
                            ts_ = tstg[ti % 3]
                            ti += 1
                            eng = "act" if tt % 2 else "dve"
                            if eng == "act":
                                P.op("act", lambda e, ts_=ts_, pa=pa, p0=p0, p1=p1: e.activation(
                                    out=ts_.t[:, 0:p1 - p0], in_=pa.t[:, p0:p1], func=AF.Copy), r=[pa.b], w=[ts_.b])
                            else:
                                P.op("dve", lambda e, ts_=ts_, pa=pa, p0=p0, p1=p1: e.tensor_copy(
                                    out=ts_.t[:, 0:p1 - p0], in_=pa.t[:, p0:p1]), r=[pa.b], w=[ts_.b])
                            dst = s_va if nm == "va" else s_vn
                            P.dma("sp", dst.t[tt * 128:(tt + 1) * 128, d0:d1], ts_.t[:, 0:p1 - p0], r=[ts_.b], w=[dst.b])
            P.barrier()

    def post_norm_residual(st, l, s, ig, mT, t0, n, last_out, tag, xt=None):
        row = 2 if t0 < NCTX else s
        sq = [sb(tag + "sq%d" % i, [128, 512], F32, st) for i in range(2)]
        rstd = sb(tag + "rstd", [128, 512], F32, st)
        dstv = xT.t[s].rearrange("(c p) t -> p c t", p=128)
        if xt is None:
            xt = sb(tag + "xt", [128, 16, 512], F32, st)
            src = (xin[s] if (l == 0 and tag == "D") else xT.t[s]).rearrange("(c p) t -> p c t", p=128)
            for j in range(4):
                P.dma("sp", xt.t[:, 4 * j:4 * j + 4, 0:n], src[:, 4 * j:4 * j + 4, t0:t0 + n], r=[xT.b], w=[xt.b])
        ps = ps_next()
        for kc in range(16):
            q = sq[kc % 2]
            P.op("act", lambda e, q=q, kc=kc: e.activation(out=q.t[:, 0:n], in_=mT.t[:, kc, 0:n], func=AF.Square),
                 r=[mT.b], w=[q.b])
            P.op("pe", lambda e, q=q, kc=kc: e.matmul(ps.t[:, 0:n], lhsT=ones_f.t[:], rhs=q.t[:, 0:n],
                                                      start=(kc == 0), stop=(kc == 15)), r=[q.b], w=[ps.b])
        P.op("act", lambda e: e.activation(out=rstd.t[:, 0:n], in_=ps.t[:, 0:n], func=AF.Sqrt, bias=epsc.t[:, 0:1],
                                           scale=1.0 / D), r=[ps.b, epsc.b], w=[rstd.b])
        P.op("dve", lambda e: e.reciprocal(out=rstd.t[:, 0:n], in_=rstd.t[:, 0:n]), r=[rstd.b], w=[rstd.b])
        for kc in range(16):
            P.op("pool", lambda e, kc=kc: e.tensor_tensor(out=mT.t[:, kc, 0:n], in0=mT.t[:, kc, 0:n], in1=rstd.t[:, 0:n],
                                                          op=ALU.mult), r=[mT.b, rstd.b], w=[mT.b])
            P.op("dve", lambda e, kc=kc: e.scalar_tensor_tensor(
                out=xt.t[:, kc, 0:n], in0=mT.t[:, kc, 0:n], scalar=coef.t[:, ig, kc, row:row + 1], in1=xt.t[:, kc, 0:n],
                op0=ALU.mult, op1=ALU.add), r=[mT.b, coef.b, xt.b], w=[xt.b])
        for j in range(4):
            P.dma("sp", dstv[:, 4 * j:4 * j + 4, t0:t0 + n], xt.t[:, 4 * j:4 * j + 4, 0:n], r=[xt.b], w=[xT.b])
        if last_out and t0 >= NCTX:
            yv = yout[s].rearrange("(c p) t -> p c t", p=128)
            for j in range(4):
                P.dma("sp", yv[:, 4 * j:4 * j + 4, t0 - NCTX:t0 - NCTX + n], xt.t[:, 4 * j:4 * j + 4, 0:n], r=[xt.b])

    def stage_D(l, s, tgs):
        gate_v = s_fm.t[OFF_GATE * 128:(OFF_GATE + 48) * 128, :].rearrange("(i c p) t -> p i c t", p=128, i=3)
        wbv = w_branch[l].rearrange("i (kc p) n -> p i kc n", p=128)
        wov = w_out[l].rearrange("(kc p) n -> p kc n", p=128)
        for (t0, n) in tgs:
            with ExitStack() as st:
                oT = sb("oT", [128, 3, 8, 512], BF16, st)
                yT = sb("yT", [128, 16, 512], BF16, st)
                mT = sb("mT", [128, 16, 512], F32, st)
                gt = [sb("gt%d" % i, [128, 3, 512], BF16, st) for i in range(2)]
                wbb = [sb("wbb%d" % i, [128, 3, 8, 128], BF16, st) for i in range(3)]
                wob = [sb("wob%d" % i, [128, 16, 128], BF16, st) for i in range(3)]
                acc = [sb("acc%d" % i, [128, 512], F32, st) for i in range(2)]
                tm2 = [sb("tm2%d" % i, [128, 512], F32, st) for i in range(2)]
                ov = s_o.t.rearrange("i (c p) t -> p i c t", p=128)
                for i in range(3):
                    for j in range(2):
                        P.dma("sp", oT.t[:, i, 4 * j:4 * j + 4, 0:n], ov[:, i, 4 * j:4 * j + 4, t0:t0 + n], r=[s_o.b], w=[oT.b])
                for oc in range(16):
                    wb = wbb[oc % 3]
                    P.dma("pool", wb.t[:], wbv[:, :, :, oc * 128:(oc + 1) * 128], w=[wb.b])
                    g = gt[oc % 2]
                    P.dma("sp", g.t[:, :, 0:n], gate_v[:, :, oc, t0:t0 + n], r=[s_fm.b], w=[g.b])
                    a = acc[oc % 2]
                    for i in range(3):
                        pa = ps_next()

                        def mm(e, pa=pa, wb=wb, i=i):
                            for kc in range(8):
                                ins = e.matmul(pa.t[:, 0:n], lhsT=wb.t[:, i, kc, :], rhs=oT.t[:, i, kc, 0:n],
                                               start=(kc == 0), stop=(kc == 7))
                            return ins
                        P.op("pe", mm, r=[wb.b, oT.b], w=[pa.b])
                        if i == 0:
                            P.op("dve", lambda e, pa=pa, a=a, g=g: e.tensor_tensor(
                                out=a.t[:, 0:n], in0=pa.t[:, 0:n], in1=g.t[:, 0, 0:n], op=ALU.mult),
                                r=[pa.b, g.b], w=[a.b])
                        else:
                            t2 = tm2[i % 2]
                            P.op("dve", lambda e, pa=pa, t2=t2, g=g, i=i: e.tensor_tensor(
                                out=t2.t[:, 0:n], in0=pa.t[:, 0:n], in1=g.t[:, i, 0:n], op=ALU.mult),
                                r=[pa.b, g.b], w=[t2.b])
                            if i == 1:
                                P.op("pool", lambda e, a=a, t2=t2: e.tensor_tensor(
                                    out=a.t[:, 0:n], in0=a.t[:, 0:n], in1=t2.t[:, 0:n], op=ALU.add),
                                    r=[a.b, t2.b], w=[a.b])
                            else:
                                P.op("pool", lambda e, a=a, t2=t2, oc=oc: e.tensor_tensor(
                                    out=yT.t[:, oc, 0:n], in0=a.t[:, 0:n], in1=t2.t[:, 0:n], op=ALU.add),
                                    r=[a.b, t2.b], w=[yT.b])
                for oc in range(16):
                    wb = wob[oc % 3]
                    for j in range(2):
                        P.dma("pool", wb.t[:, 8 * j:8 * j + 8, :], wov[:, 8 * j:8 * j + 8, oc * 128:(oc + 1) * 128], w=[wb.b])
                    pa = ps_next()

                    def mm2(e, pa=pa, wb=wb):
                        for kc in range(16):
                            ins = e.matmul(pa.t[:, 0:n], lhsT=wb.t[:, kc, :], rhs=yT.t[:, kc, 0:n],
                                           start=(kc == 0), stop=(kc == 15))
                        return ins
                    P.op("pe", mm2, r=[wb.b, yT.b], w=[pa.b])
                    P.op("act", lambda e, pa=pa, oc=oc: e.activation(out=mT.t[:, oc, 0:n], in_=pa.t[:, 0:n], func=AF.Copy),
                         r=[pa.b], w=[mT.b])
                post_norm_residual(st, l, s, 2, mT, t0, n, False, "D")
                P.barrier()

    def stage_E(l, s, tgs):
        wuv = w_up[l].rearrange("(kc p) n -> p kc n", p=128)
        wdv = w_down[l].rearrange("(kc p) n -> p kc n", p=128)
        last = (l == L - 1)
        for (t0, n) in tgs:
            with ExitStack() as st:
                aT = sb("aT", [128, 64, 512], BF16, st)
                xt = sb("xtE", [128, 16, 512], F32, st)
                with ExitStack() as st2:
                    hT = sb("hE", [128, 16, 512], BF16, st2)
                    modulate_tg(st2, xT.t[s], s, 3, 4, hT, t0, n, [xt])
                    wub = [sb("wub%d" % i, [128, 16, 512], BF16, st2) for i in range(2)]
                    rl = [sb("rl%d" % i, [128, 512], F32, st2) for i in range(2)]
                    for blk in range(16):
                        wb = wub[blk % 2]
                        for j in range(2):
                            P.dma("pool", wb.t[:, 8 * j:8 * j + 8, :], wuv[:, 8 * j:8 * j + 8, blk * 512:(blk + 1) * 512], w=[wb.b])
                        for cc in range(4):
                            fc = blk * 4 + cc
                            pa = ps_next()

                            def mm(e, pa=pa, wb=wb, cc=cc):
                                for kc in range(16):
                                    ins = e.matmul(pa.t[:, 0:n], lhsT=wb.t[:, kc, cc * 128:(cc + 1) * 128],
                                                   rhs=hT.t[:, kc, 0:n], start=(kc == 0), stop=(kc == 15))
                                return ins
                            P.op("pe", mm, r=[wb.b, hT.b], w=[pa.b])
                            r_ = rl[fc % 2]
                            P.op("act", lambda e, pa=pa, r_=r_: e.activation(out=r_.t[:, 0:n], in_=pa.t[:, 0:n], func=AF.Relu),
                                 r=[pa.b], w=[r_.b])
                            eng = "dve" if fc % 2 else "pool"
                            P.op(eng, lambda e, r_=r_, fc=fc: e.tensor_tensor(out=aT.t[:, fc, 0:n], in0=r_.t[:, 0:n],
                                                                            in1=r_.t[:, 0:n], op=ALU.mult),
                                 r=[r_.b], w=[aT.b])
                    P.barrier()
                with ExitStack() as st2:
                    mT = sb("mE", [128, 16, 512], F32, st2)
                    wdb = [sb("wdb%d" % i, [128, 64, 128], BF16, st2) for i in range(2)]
                    for oc in range(16):
                        wb = wdb[oc % 2]
                        for j in range(4):
                            P.dma("pool", wb.t[:, 16 * j:16 * j + 16, :], wdv[:, 16 * j:16 * j + 16, oc * 128:(oc + 1) * 128], w=[wb.b])
                        pa = ps_next()

                        def mm2(e, pa=pa, wb=wb):
                            for kc in range(64):
                                ins = e.matmul(pa.t[:, 0:n], lhsT=wb.t[:, kc, :], rhs=aT.t[:, kc, 0:n],
                                               start=(kc == 0), stop=(kc == 63))
                            return ins
                        P.op("pe", mm2, r=[wb.b, aT.b], w=[pa.b])
                        P.op("act", lambda e, pa=pa, oc=oc: e.activation(out=mT.t[:, oc, 0:n], in_=pa.t[:, 0:n], func=AF.Copy),
                             r=[pa.b], w=[mT.b])
                    post_norm_residual(st2, l, s, 5, mT, t0, n, last, "E", xt)
                    P.barrier()

    def modulate_tg(st, src, s, ia, ib, hT, t0, n, xt=None):
        class V:
            pass
        hv = TB(None)
        hv.b = hT.b

        class _T:
            def __getitem__(self, key):
                p, kc, sl = key
                return hT.t[p, kc, sl.start - t0:sl.stop - t0]
        hv.t = _T()
        modulate(st, src, s, ia, ib, hv, [(t0, n)], xt)


    SCALE = 1.0 / math.sqrt(HD)

    def stage_C1(l, s, with_ctx):
        with ExitStack() as st:
            qT = sb("qaT", [128, 8, T], BF16, st)
            kT = sb("kaT", [128, 2, T], BF16, st)
            va = sb("vaS", [128, 18, 256], BF16, st)
            oT = sb("oaT", [128, 8, T], BF16, st)
            esk = sb("esk", [128, 8], F32, st)
            eskB = sb("eskB", [128, 8, 128], F32, st)
            mk = sb("mkA", [128, 2, 4, 128], BF16, st)
            pTs = [sb("pTa%d" % i, [128, 512], BF16, st) for i in range(3)]
            den = sb("denA", [128, 512], F32, st)
            fmv = s_fm.t.rearrange("(c p) t -> p c t", p=128)
            for h in range(8):
                P.dma("sp", qT.t[:, h, :], fmv[:, OFF_QA + h, :], r=[s_fm.b], w=[qT.b])
            for h in range(2):
                P.dma("sp", kT.t[:, h, :], fmv[:, OFF_KA + h, :], r=[s_fm.b], w=[kT.b])
            vav = s_va.t.rearrange("(tt p) c -> p tt c", p=128)
            for j in range(3):
                P.dma("sp", va.t[:, 6 * j:6 * j + 6, :], vav[:, 6 * j:6 * j + 6, :], r=[s_va.b], w=[va.b])
            P.dma("sp", esk.t[:], sink[l:l + 1, :].partition_broadcast(128), w=[esk.b])
            P.op("act", lambda e: e.activation(out=esk.t[:], in_=esk.t[:], func=AF.Exp), r=[esk.b], w=[esk.b])
            P.op("dve", lambda e: e.tensor_copy(out=eskB.t[:], in_=esk.t[:].unsqueeze(2).broadcast_to([128, 8, 128])),
                 r=[esk.b], w=[eskB.b])
            for j in range(2):
                P.op("dve", lambda e, j=j: e.tensor_copy(out=mk.t[:, j], in_=cm.t[:, 1 + j, :].unsqueeze(1).broadcast_to([128, 4, 128])),
                     r=[cm.b], w=[mk.b])
            qblocks = ([(128 * j, None) for j in range(2)] if with_ctx else []) + [(256 + 128 * i, i) for i in range(16)]
            pacc = PSPool([0, 1, 2, 3])
            pss = PSPool([4, 5, 6, 7])
            pi = 0
            for g in range(2):
                for (t0, xi) in qblocks:
                    keys = [(0, None), (128, None)]
                    if xi is not None:
                        if xi > 0:
                            keys.append((256 + 128 * (xi - 1), 0))
                        keys.append((256 + 128 * xi, None))
                        if xi < 15:
                            keys.append((256 + 128 * (xi + 1), 1))
                    ps_o = pacc.next()
                    ps_d = pacc.next()
                    for idx, (k0, mki) in enumerate(keys):
                        ps_s = pss.next()
                        pT = pTs[pi % 3]
                        pi += 1
                        P.op("pe", lambda e, ps_s=ps_s, k0=k0: e.matmul(
                            ps_s.t[:, 0:512].rearrange("p (h q) -> p h q", h=4), lhsT=kT.t[:, g, k0:k0 + 128],
                            rhs=qT.t[:, 4 * g:4 * g + 4, t0:t0 + 128], start=True, stop=True), r=[kT.b, qT.b], w=[ps_s.b])
                        P.op("act", lambda e, ps_s=ps_s, pT=pT: e.activation(out=pT.t[:], in_=ps_s.t[:], func=AF.Exp, scale=SCALE),
                             r=[ps_s.b], w=[pT.b])
                        if mki is not None:
                            P.op("pool", lambda e, pT=pT, mki=mki: e.tensor_tensor(
                                out=pT.t[:], in0=pT.t[:], in1=mk.t[:, mki].rearrange("p h q -> p (h q)"), op=ALU.mult),
                                r=[pT.b, mk.b], w=[pT.b])
                        first, lastk = idx == 0, idx == len(keys) - 1
                        P.op("pe", lambda e, pT=pT: e.matmul(ps_d.t[:], lhsT=ones_b.t[:], rhs=pT.t[:], start=first, stop=lastk),
                             r=[pT.b, ones_b.b], w=[ps_d.b])
                        P.op("pe", lambda e, pT=pT, k0=k0: e.matmul(ps_o.t[:], lhsT=va.t[:, k0 // 128, g * 128:(g + 1) * 128],
                                                                    rhs=pT.t[:], start=first, stop=lastk),
                             r=[pT.b, va.b], w=[ps_o.b])
                    P.op("dve", lambda e: e.tensor_tensor(out=den.t[:], in0=ps_d.t[:],
                                                          in1=eskB.t[:, 4 * g:4 * g + 4, :].rearrange("p h q -> p (h q)"), op=ALU.add),
                         r=[ps_d.b, eskB.b], w=[den.b])
                    P.op("dve", lambda e: e.reciprocal(out=den.t[:], in_=den.t[:]), r=[den.b], w=[den.b])
                    P.op("dve", lambda e: e.tensor_tensor(out=oT.t[:, 4 * g:4 * g + 4, t0:t0 + 128],
                                                          in0=ps_o.t[:].rearrange("p (h q) -> p h q", h=4),
                                                          in1=den.t[:].rearrange("p (h q) -> p h q", h=4), op=ALU.mult),
                         r=[ps_o.b, den.b], w=[oT.b])
            ov = s_o.t[0].rearrange("(c p) t -> p c t", p=128)
            for h in range(8):
                P.dma("sp", ov[:, h, :], oT.t[:, h, :], r=[oT.b], w=[s_o.b])
            P.barrier()

    rpbpad = dscratch("rpbpad", [120, 127], F32)
    dbgC = dscratch("dbgC", [64, 7680], F32)

    def na_valid(r, kr):
        rs = min(max(r - 4, 0), 24)
        return rs <= kr <= rs + 7

    def stage_C2(l, s, with_ctx):
        with ExitStack() as st:
            Ctab = sb("Ctab", [64, 8, 15, 64], F32, st)
            zt = sb("zt", [120, 127], F32, st)
            P.op("dve", lambda e: e.memset(zt.t[:], 0.0), w=[zt.b])
            P.dma("sp", rpbpad.t[:, :], zt.t[:], r=[zt.b], w=[rpbpad.b])
            P.dma("sp", rpbpad.t[:, 48:79], rpb[l].rearrange("(r c) -> r c", c=31), w=[rpbpad.b])
            for h in range(8):
                src = bass.AP(tensor=rpbpad.t.tensor, offset=rpbpad.t.offset + h * 15 * 127, ap=[[1, 64], [127, 15], [1, 64]])
                P.dma("sp", Ctab.t[:, h], src, r=[rpbpad.b], w=[Ctab.b])
            P.op("dve", lambda e: e.tensor_tensor(
                out=Ctab.t[:].rearrange("p h a q -> p (h a) q"), in0=Ctab.t[:].rearrange("p h a q -> p (h a) q"),
                in1=cm.t[0:64, 3, 0:64].unsqueeze(1).broadcast_to([64, 120, 64]), op=ALU.add), r=[Ctab.b, cm.b], w=[Ctab.b])
            if cfg.debug:
                for h in range(8):
                    P.dma("sp", dbgC.t[:, h * 960:(h + 1) * 960], Ctab.t[:, h].rearrange("p a q -> p (a q)"), r=[Ctab.b], w=[dbgC.b])
            ctab_ap = Ctab.t[:]
            pstep = ctab_ap.ap[0][0]
            qh = [sb("qnh%d" % i, [128, T], BF16, st) for i in range(2)]
            kh = [sb("knh%d" % i, [128, T], BF16, st) for i in range(2)]
            v64 = [sb("v64_%d" % i, [128, 36, 128], BF16, st) for i in range(2)]
            pTr = [sb("pTr%d" % i, [128, 512], BF16, st) for i in range(3)]
            for t_ in v64 + pTr:
                P.op("pool", lambda e, t_=t_: e.memset(t_.t[:], 0.0), w=[t_.b])
            vcx = [sb("vcx%d" % i, [128, 2, 128], BF16, st) for i in range(2)]
            oh = [sb("onh%d" % i, [128, T], BF16, st) for i in range(2)]
            pTs = [sb("pTn%d" % i, [128, 512], BF16, st) for i in range(3)]
            sbs = [sb("sbn%d" % i, [64, 512], F32, st) for i in range(2)]
            den = sb("denN", [128, 512], F32, st)
            fmv = s_fm.t.rearrange("(c p) t -> p c t", p=128)
            v64v = s_vn.t.rearrange("(c p) d -> p c d", p=64)
            vcv = s_vn.t[0:256, :].rearrange("(c p) d -> p c d", p=128)
            ov = s_o.t[2].rearrange("(c p) t -> p c t", p=128)
            pi = 0
            bi_ = 0
            pacc = PSPool([0, 1, 2, 3])
            pss = PSPool([4, 5, 6, 7])
            for h in range(8):
                q_, k_, v_, vc_, o_ = qh[h % 2], kh[h % 2], v64[h % 2], vcx[h % 2], oh[h % 2]
                P.dma("sp", q_.t[:], fmv[:, OFF_QN + h, :], r=[s_fm.b], w=[q_.b])
                P.dma("sp", k_.t[:], fmv[:, OFF_KN + h, :], r=[s_fm.b], w=[k_.b])
                for j in range(3):
                    P.dma("sp", v_.t[0:64, 12 * j:12 * j + 12, :], v64v[:, 12 * j:12 * j + 12, h * 128:(h + 1) * 128],
                          r=[s_vn.b], w=[v_.b])
                P.dma("sp", vc_.t[:], vcv[:, :, h * 128:(h + 1) * 128], r=[s_vn.b], w=[vc_.b])
                groups = ([("c", 0, 256)] if with_ctx else []) + [("x", 256 + 512 * G, 512) for G in range(4)]
                for (gk, t0, n) in groups:
                    ps_o = pacc.next()
                    ps_d = pacc.next()
                    items = [("ctx", 0)]
                    if gk == "x":
                        G = (t0 - 256) // 512
                        for kr in range(32):
                            rr = [r for r in range(8 * G, 8 * G + 8) if na_valid(r, kr)]
                            if rr:
                                items.append(("row", kr, rr[0], rr[-1]))
                    items.append(("ctx", 1))
                    for it in items:
                        first, lastk = it is items[0], it is items[-1]
                        pT = pTs[pi % 3]
                        pi += 1
                        ps_s = pss.next()
                        if it[0] == "ctx":
                            cb = it[1]
                            P.op("pe", lambda e: e.matmul(ps_s.t[:, 0:n], lhsT=k_.t[:, cb * 128:(cb + 1) * 128], rhs=q_.t[:, t0:t0 + n],
                                                          start=True, stop=True), r=[k_.b, q_.b], w=[ps_s.b])
                            P.op("act", lambda e: e.activation(out=pT.t[:, 0:n], in_=ps_s.t[:, 0:n], func=AF.Exp, scale=SCALE),
                                 r=[ps_s.b], w=[pT.b])
                            P.op("pe", lambda e: e.matmul(ps_d.t[:, 0:n], lhsT=ones_b.t[:], rhs=pT.t[:, 0:n], start=first, stop=lastk),
                                 r=[pT.b, ones_b.b], w=[ps_d.b])
                            P.op("pe", lambda e: e.matmul(ps_o.t[:, 0:n], lhsT=vc_.t[:, cb, :], rhs=pT.t[:, 0:n], start=first, stop=lastk),
                                 r=[pT.b, vc_.b], w=[ps_o.b])
                        else:
                            _, kr, rlo, rhi = it
                            pT = pTr[pi % 3]
                            nr = rhi - rlo + 1
                            c0 = (rlo - 8 * G) * 64
                            nc_ = nr * 64
                            kt0 = 256 + 64 * kr
                            sbt = sbs[bi_ % 2]
                            bi_ += 1
                            P.op("pe", lambda e: e.matmul(ps_s.t[0:64, 0:nc_], lhsT=k_.t[:, kt0:kt0 + 64],
                                                          rhs=q_.t[:, t0 + c0:t0 + c0 + nc_], start=True, stop=True),
                                 r=[k_.b, q_.b], w=[ps_s.b])
                            a_start = 7 + kr - rlo
                            bias = bass.AP(tensor=ctab_ap.tensor, offset=ctab_ap.offset + (h * 15 + a_start) * 64 + 63,
                                           ap=[[pstep, 64], [-64, nr], [-1, 64]])
                            P.op("dve", lambda e: e.scalar_tensor_tensor(
                                out=sbt.t[:, 0:nc_].rearrange("p (r q) -> p r q", q=64),
                                in0=ps_s.t[0:64, 0:nc_].rearrange("p (r q) -> p r q", q=64), scalar=SCALE, in1=bias,
                                op0=ALU.mult, op1=ALU.add), r=[ps_s.b, Ctab.b], w=[sbt.b])
                            P.op("act", lambda e: e.activation(out=pT.t[0:64, 0:nc_], in_=sbt.t[:, 0:nc_], func=AF.Exp),
                                 r=[sbt.b], w=[pT.b])
                            P.op("pe", lambda e: e.matmul(ps_d.t[:, c0:c0 + nc_], lhsT=ones_b.t[:, :], rhs=pT.t[:, 0:nc_],
                                                          start=False, stop=False), r=[pT.b, ones_b.b], w=[ps_d.b])
                            P.op("pe", lambda e: e.matmul(ps_o.t[:, c0:c0 + nc_], lhsT=v_.t[:, 4 + kr, :], rhs=pT.t[:, 0:nc_],
                                                          start=False, stop=False), r=[pT.b, v_.b], w=[ps_o.b])
                    P.op("dve", lambda e: e.reciprocal(out=den.t[:, 0:n], in_=ps_d.t[:, 0:n]), r=[ps_d.b], w=[den.b])
                    P.op("dve", lambda e: e.tensor_tensor(out=o_.t[:, t0:t0 + n], in0=ps_o.t[:, 0:n], in1=den.t[:, 0:n], op=ALU.mult),
                         r=[ps_o.b, den.b], w=[o_.b])
                P.dma("sp", ov[:, h, :], o_.t[:], r=[o_.b], w=[s_o.b])
            P.barrier()


    NT_ = T // 128
    TGRP = [(0, 4), (4, 4), (8, 4), (12, 4), (16, 2)]

    def stage_C3(l, s):
        with ExitStack() as st:
            abT = sb("abT", [128, NT_, 32], F32, st)
            gG = sb("gG", [128, NT_, 16], F32, st)
            bG = sb("bG", [128, NT_, 16], F32, st)
            nA = sb("nA", [128, 16], F32, st)
            dtb = sb("dtb", [128, 16], F32, st)
            cwT = sb("cwT", [128, 5, 24], F32, st)
            gg = sb("ggdn", [128, 1], F32, st)
            abv = s_ab.t.rearrange("(tt p) j -> p tt j", p=128)
            for j in range(3):
                P.dma("sp", abT.t[:, 6 * j:6 * j + 6, :], abv[:, 6 * j:6 * j + 6, :], r=[s_ab.b], w=[abT.b])
            P.dma("sp", nA.t[:], a_log[l:l + 1, :].partition_broadcast(128), w=[nA.b])
            P.dma("sp", dtb.t[:], dt_bias[l:l + 1, :].partition_broadcast(128), w=[dtb.b])
            P.dma("sp", gg.t[:], g_gdn[l].rearrange("(p o) -> p o", o=1), w=[gg.b])
            P.op("act", lambda e: e.activation(out=nA.t[:], in_=nA.t[:], func=AF.Exp), r=[nA.b], w=[nA.b])
            P.op("dve", lambda e: e.tensor_scalar(out=nA.t[:], in0=nA.t[:], scalar1=-1.0, scalar2=None, op0=ALU.mult),
                 r=[nA.b], w=[nA.b])
            P.op("dve", lambda e: e.tensor_tensor(out=gG.t[:], in0=abT.t[:, :, 0:16],
                                                  in1=dtb.t[:].unsqueeze(1).broadcast_to([128, NT_, 16]), op=ALU.add),
                 r=[abT.b, dtb.b], w=[gG.b])
            P.op("act", lambda e: e.activation(out=gG.t[:], in_=gG.t[:], func=AF.Exp), r=[gG.b], w=[gG.b])
            P.op("act", lambda e: e.activation(out=gG.t[:], in_=gG.t[:], func=AF.Ln, bias=ones_f.t[:, 0:1], scale=1.0),
                 r=[gG.b, ones_f.b], w=[gG.b])
            P.op("dve", lambda e: e.tensor_tensor(out=gG.t[:], in0=gG.t[:],
                                                  in1=nA.t[:].unsqueeze(1).broadcast_to([128, NT_, 16]), op=ALU.mult),
                 r=[gG.b, nA.b], w=[gG.b])
            P.op("act", lambda e: e.activation(out=bG.t[:], in_=abT.t[:, :, 16:32], func=AF.Sigmoid), r=[abT.b], w=[bG.b])
            with ExitStack() as st2:
                tmpc = sb("cwtmp", [128, 128], F32, st2)
                P.dma("sp", tmpc.t[0:120, :], conv_w[l].rearrange("j (c p) -> (j c) p", p=128), w=[tmpc.b])
                pst = ps_next()
                P.op("pe", lambda e: e.transpose(out=pst.t[:, 0:120], in_=tmpc.t[0:120, :], identity=cm.t[0:120, 0, 0:120]),
                     r=[tmpc.b, cm.b], w=[pst.b])
                P.op("dve", lambda e: e.tensor_copy(out=cwT.t[:].rearrange("p j c -> p (j c)"), in_=pst.t[:, 0:120]),
                     r=[pst.b], w=[cwT.b])
                P.barrier()
            fmv = s_fm.t.rearrange("(c p) t -> p c t", p=128)
            ov = s_o.t[1].rearrange("(c p) t -> p c t", p=128)
            SEGS = [(0, NCTX), (NCTX, T)]
            idb = cm.t[:, 0, :]
            for h in range(8):
                with ExitStack() as sh:
                    qT = sb("gq", [128, T], F32, sh)
                    kT = sb("gk", [128, T], F32, sh)
                    k_tm = sb("gktm", [128, NT_, 128], F32, sh)
                    v_tm = sb("gvtm", [128, NT_, 128], F32, sh)
                    KK = sb("gKK", [128, NT_, 128], F32, sh)
                    QKT = sb("gQKT", [128, NT_, 128], F32, sh)
                    oacc = sb("goacc", [128, T], F32, sh)
                    with ExitStack() as s1:
                        raw = [sb("graw%d" % i, [128, T], BF16, s1) for i in range(3)]
                        acc = [sb("gacc%d" % i, [128, T], F32, s1) for i in range(2)]
                        vT = sb("gv", [128, T], F32, s1)
                        sqb = sb("gsq", [128, 512], F32, s1)
                        rnb = sb("grn", [128, 512], F32, s1)
                        for ci_, which in enumerate(("q", "k", "v")):
                            cidx = ci_ * 8 + h
                            rw = raw[ci_]
                            a_ = acc[ci_ % 2]
                            eng = "dve"
                            P.dma("sp", rw.t[:], fmv[:, OFF_QKVB + cidx, :], r=[s_fm.b], w=[rw.b])
                            P.op(eng, lambda e: e.tensor_scalar(out=a_.t[:], in0=rw.t[:], scalar1=cwT.t[:, 2, cidx:cidx + 1], scalar2=None,
                                                                op0=ALU.mult), r=[rw.b, cwT.b], w=[a_.b])
                            for j in (0, 1, 3, 4):
                                d_ = j - 2
                                for (s0, s1_) in SEGS:
                                    lo, hi = max(s0, s0 - d_), min(s1_, s1_ - d_)
                                    P.op(eng, lambda e: e.scalar_tensor_tensor(
                                        out=a_.t[:, lo:hi], in0=rw.t[:, lo + d_:hi + d_], scalar=cwT.t[:, j, cidx:cidx + 1],
                                        in1=a_.t[:, lo:hi], op0=ALU.mult, op1=ALU.add), r=[rw.b, cwT.b, a_.b], w=[a_.b])
                            dst = {"q": qT, "k": kT, "v": vT}[which]
                            if which == "v":
                                P.op("act", lambda e: e.activation(out=dst.t[:], in_=a_.t[:], func=AF.Silu), r=[a_.b], w=[dst.b])
                            else:
                                P.op("act", lambda e: e.activation(out=a_.t[:], in_=a_.t[:], func=AF.Silu), r=[a_.b], w=[a_.b])
                                for (t0, n) in TGS:
                                    P.op("pool", lambda e: e.tensor_tensor(out=sqb.t[:, 0:n], in0=a_.t[:, t0:t0 + n], in1=a_.t[:, t0:t0 + n],
                                                                          op=ALU.mult), r=[a_.b], w=[sqb.b])
                                    pq = ps_next()
                                    P.op("pe", lambda e: e.matmul(pq.t[:, 0:n], lhsT=ones_f.t[:], rhs=sqb.t[:, 0:n], start=True, stop=True),
                                         r=[sqb.b, ones_f.b], w=[pq.b])
                                    P.op("act", lambda e: e.activation(out=rnb.t[:, 0:n], in_=pq.t[:, 0:n], func=AF.Sqrt,
                                                                       bias=epsc.t[:, 0:1], scale=1.0), r=[pq.b, epsc.b], w=[rnb.b])
                                    P.op("dve", lambda e: e.reciprocal(out=rnb.t[:, 0:n], in_=rnb.t[:, 0:n]), r=[rnb.b], w=[rnb.b])
                                    if which == "q":
                                        P.op("dve", lambda e: e.scalar_tensor_tensor(
                                            out=dst.t[:, t0:t0 + n], in0=a_.t[:, t0:t0 + n], scalar=SCALE, in1=rnb.t[:, 0:n],
                                            op0=ALU.mult, op1=ALU.mult), r=[a_.b, rnb.b], w=[dst.b])
                                    else:
                                        P.op("dve", lambda e: e.tensor_tensor(out=dst.t[:, t0:t0 + n], in0=a_.t[:, t0:t0 + n],
                                                                              in1=rnb.t[:, 0:n], op=ALU.mult), r=[a_.b, rnb.b], w=[dst.b])
                        ei = 0
                        for (g0, gn) in TGRP:
                            for (src_, dst_) in ((kT, k_tm), (vT, v_tm)):
                                pt = ps_next()
                                for ti in range(gn):
                                    tt = g0 + ti
                                    P.op("pe", lambda e: e.transpose(out=pt.t[:, ti * 128:(ti + 1) * 128], in_=src_.t[:, tt * 128:(tt + 1) * 128],
                                                                     identity=idb), r=[src_.b, cm.b], w=[pt.b], inc=(ti == gn - 1))
                                ei += 1
                                if ei % 2:
                                    P.op("act", lambda e: e.activation(out=dst_.t[:, g0:g0 + gn, :].rearrange("p a b -> p (a b)"),
                                                                       in_=pt.t[:, 0:gn * 128], func=AF.Copy), r=[pt.b], w=[dst_.b])
                                else:
                                    P.op("dve", lambda e: e.tensor_copy(out=dst_.t[:, g0:g0 + gn, :].rearrange("p a b -> p (a b)"),
                                                                        in_=pt.t[:, 0:gn * 128]), r=[pt.b], w=[dst_.b])
                            for (rhs_, dst_) in ((kT, KK), (qT, QKT)):
                                pt = ps_next()
                                for ti in range(gn):
                                    tt = g0 + ti
                                    P.op("pe", lambda e: e.matmul(pt.t[:, ti * 128:(ti + 1) * 128], lhsT=kT.t[:, tt * 128:(tt + 1) * 128],
                                                                  rhs=rhs_.t[:, tt * 128:(tt + 1) * 128], start=True, stop=True),
                                         r=[kT.b, rhs_.b], w=[pt.b], inc=(ti == gn - 1))
                                ei += 1
                                if ei % 2:
                                    P.op("act", lambda e: e.activation(out=dst_.t[:, g0:g0 + gn, :].rearrange("p a b -> p (a b)"),
                                                                       in_=pt.t[:, 0:gn * 128], func=AF.Copy), r=[pt.b], w=[dst_.b])
                                else:
                                    P.op("dve", lambda e: e.tensor_copy(out=dst_.t[:, g0:g0 + gn, :].rearrange("p a b -> p (a b)"),
                                                                        in_=pt.t[:, 0:gn * 128]), r=[pt.b], w=[dst_.b])
                        P.barrier()
                    dirs = []
                    for di in range(2):
                        iU, iML, iMU, iSL = (4, 6, 7, 8) if di == 0 else (5, 7, 6, 9)
                        gcol = di * 8 + h
                        wT = sb("gwT%d" % di, [128, T], F32, sh)
                        qgT = sb("gqg%d" % di, [128, T], BF16, sh)
                        kd = sb("gkd%d" % di, [128, NT_, 128], BF16, sh)
                        attnT = sb("gat%d" % di, [128, NT_, 128], BF16, sh)
                        u_ = sb("gu%d" % di, [128, NT_, 128], F32, sh)
                        egl = sb("gegl%d" % di, [128, NT_, 2], F32, sh)
                        dirs.append((wT, qgT, kd, attnT, u_, egl))
                        with ExitStack() as s2:
                            gc = sb("ggc", [128, NT_], F32, s2)
                            egc = sb("gegc", [128, NT_], F32, s2)
                            ekd = sb("gekd", [128, NT_], F32, s2)
                            bsc = sb("gbsc", [128, NT_], F32, s2)
                            ghd = sb("gghd", [128, NT_], F32, s2)
                            bhd = sb("gbhd", [128, NT_], F32, s2)
                            kbg = sb("gkbg", [128, NT_, 128], F32, s2)
                            vb = sb("gvb", [128, NT_, 128], F32, s2)
                            Xs = sb("gX", [128, NT_, 128], F32, s2)
                            GU = sb("gGU", [128, 512], F32, s2)
                            Gb = sb("gGb", [128, 512], F32, s2)
                            EL = sb("gEL", [128, 512], F32, s2)
                            EU = sb("gEU", [128, 512], F32, s2)
                            Lf = sb("gLf", [128, 512], F32, s2)
                            Dg = sb("gDg", [128, 512], F32, s2)
                            nb = [[sb("gN%d_%d" % (a, b_), [128, 512], F32, s2) for b_ in range(2)] for a in range(2)]
                            Xb = [sb("gXb%d" % i, [128, 512], F32, s2) for i in range(2)]
                            P.op("dve", lambda e: e.tensor_copy(out=ghd.t[:], in_=gG.t[:, :, gcol]), r=[gG.b], w=[ghd.b])
                            P.op("dve", lambda e: e.tensor_copy(out=bhd.t[:], in_=bG.t[:, :, gcol]), r=[bG.b], w=[bhd.b])
                            pg = ps_next()
                            P.op("pe", lambda e: e.matmul(pg.t[:, 0:NT_], lhsT=cm.t[:, iU, :], rhs=ghd.t[:], start=True, stop=True),
                                 r=[cm.b, ghd.b], w=[pg.b], inc=False)
                            P.op("pe", lambda e: e.matmul(pg.t[:, 32:32 + NT_], lhsT=cm.t[:, 10, :], rhs=ghd.t[:], start=True, stop=True),
                                 r=[cm.b, ghd.b], w=[pg.b], inc=False)
                            P.op("pe", lambda e: e.matmul(pg.t[:, 64:64 + NT_], lhsT=cm.t[:, 11, :], rhs=ghd.t[:], start=True, stop=True),
                                 r=[cm.b, ghd.b], w=[pg.b], inc=False)
                            P.op("pe", lambda e: e.matmul(pg.t[:, 96:96 + NT_], lhsT=cm.t[:, 12, :], rhs=ghd.t[:], start=True, stop=True),
                                 r=[cm.b, ghd.b], w=[pg.b])
                            P.op("dve", lambda e: e.tensor_copy(out=gc.t[:], in_=pg.t[:, 0:NT_]), r=[pg.b], w=[gc.b])
                            P.op("dve", lambda e: e.tensor_tensor(out=ekd.t[:], in0=pg.t[:, 32:32 + NT_], in1=gc.t[:], op=ALU.subtract),
                                 r=[pg.b, gc.b], w=[ekd.b])
                            P.op("dve", lambda e: e.tensor_copy(out=egl.t[:, :, 0], in_=pg.t[:, 64:64 + NT_]), r=[pg.b], w=[egl.b])
                            P.op("dve", lambda e: e.tensor_copy(out=egl.t[:, :, 1], in_=pg.t[:, 96:96 + NT_]), r=[pg.b], w=[egl.b])
                            P.op("act", lambda e: e.activation(out=egc.t[:], in_=gc.t[:], func=AF.Exp), r=[gc.b], w=[egc.b])
                            P.op("act", lambda e: e.activation(out=ekd.t[:], in_=ekd.t[:], func=AF.Exp), r=[ekd.b], w=[ekd.b])
                            P.op("act", lambda e: e.activation(out=egl.t[:], in_=egl.t[:], func=AF.Exp), r=[egl.b], w=[egl.b])
                            P.op("dve", lambda e: e.tensor_tensor(out=bsc.t[:], in0=bhd.t[:], in1=egc.t[:], op=ALU.mult),
                                 r=[bhd.b, egc.b], w=[bsc.b])
                            bc3 = lambda t_: t_.t[:].unsqueeze(2).broadcast_to([128, NT_, 128])
                            P.op("pool", lambda e: e.tensor_tensor(out=kbg.t[:], in0=k_tm.t[:], in1=bc3(bsc), op=ALU.mult),
                                 r=[k_tm.b, bsc.b], w=[kbg.b])
                            P.op("pool", lambda e: e.tensor_tensor(out=vb.t[:], in0=v_tm.t[:], in1=bc3(bhd), op=ALU.mult),
                                 r=[v_tm.b, bhd.b], w=[vb.b])
                            P.op("pool", lambda e: e.tensor_tensor(out=kd.t[:], in0=k_tm.t[:], in1=bc3(ekd), op=ALU.mult),
                                 r=[k_tm.b, ekd.b], w=[kd.b])
                            pacc = PSPool([0, 1, 2, 3, 4, 5, 6, 7])
                            for (g0, gn) in TGRP:
                                W_ = gn * 128
                                g3 = lambda t_: t_.t[:, 0:W_].rearrange("p (a b) -> p a b", b=128)
                                gsl = ghd.t[:, g0:g0 + gn].unsqueeze(2).broadcast_to([128, gn, 128])
                                cmb = lambda i_: cm.t[:, i_, :].unsqueeze(1).broadcast_to([128, gn, 128])
                                P.op("dve", lambda e: e.tensor_tensor(out=g3(GU), in0=cmb(iU), in1=gsl, op=ALU.mult),
                                     r=[cm.b, ghd.b], w=[GU.b])
                                P.op("pool", lambda e: e.tensor_copy(out=g3(Gb), in_=gsl), r=[ghd.b], w=[Gb.b])
                                pD = pacc.next()
                                P.op("pe", lambda e: e.matmul(pD.t[:, 0:W_], lhsT=cm.t[:, iU, :], rhs=Gb.t[:, 0:W_], start=True, stop=False),
                                     r=[cm.b, Gb.b], w=[pD.b], inc=False)
                                P.op("pe", lambda e: e.matmul(pD.t[:, 0:W_], lhsT=negones.t[:], rhs=GU.t[:, 0:W_], start=False, stop=True),
                                     r=[negones.b, GU.b], w=[pD.b])
                                P.op("dve", lambda e: e.tensor_tensor(out=g3(EL), in0=pD.t[:, 0:W_].rearrange("p (a b) -> p a b", b=128),
                                                                      in1=cmb(iML), op=ALU.add), r=[pD.b, cm.b], w=[EL.b])
                                P.op("dve", lambda e: e.scalar_tensor_tensor(
                                    out=g3(EU), in0=pD.t[:, 0:W_].rearrange("p (a b) -> p a b", b=128), scalar=-1.0, in1=cmb(iMU),
                                    op0=ALU.mult, op1=ALU.add), r=[pD.b, cm.b], w=[EU.b])
                                P.op("act", lambda e: e.activation(out=EL.t[:, 0:W_], in_=EL.t[:, 0:W_], func=AF.Exp), r=[EL.b], w=[EL.b])
                                P.op("act", lambda e: e.activation(out=EU.t[:, 0:W_], in_=EU.t[:, 0:W_], func=AF.Exp), r=[EU.b], w=[EU.b])
                                P.op("pool", lambda e: e.tensor_tensor(out=g3(Lf), in0=g3(EL), in1=KK.t[:, g0:g0 + gn, :], op=ALU.mult),
                                     r=[EL.b, KK.b], w=[Lf.b])
                                P.op("pool", lambda e: e.tensor_tensor(out=g3(Lf), in0=g3(Lf), in1=cmb(iSL), op=ALU.mult),
                                     r=[Lf.b, cm.b], w=[Lf.b])
                                P.op("pool", lambda e: e.tensor_tensor(
                                    out=g3(Lf), in0=g3(Lf), in1=bhd.t[:, g0:g0 + gn].unsqueeze(2).broadcast_to([128, gn, 128]), op=ALU.mult),
                                    r=[Lf.b, bhd.b], w=[Lf.b])
                                P.op("dve", lambda e: e.tensor_tensor(out=attnT.t[:, g0:g0 + gn, :], in0=g3(EU), in1=QKT.t[:, g0:g0 + gn, :],
                                                                      op=ALU.mult), r=[EU.b, QKT.b], w=[attnT.b])
                                NT0, N0 = nb[1][0], nb[0][0]
                                P.op("dve", lambda e: e.tensor_scalar(out=NT0.t[:, 0:W_], in0=Lf.t[:, 0:W_], scalar1=-1.0, scalar2=None,
                                                                      op0=ALU.mult), r=[Lf.b], w=[NT0.b])
                                pT_ = pacc.next()
                                for ti in range(gn):
                                    P.op("pe", lambda e: e.transpose(out=pT_.t[:, ti * 128:(ti + 1) * 128], in_=Lf.t[:, ti * 128:(ti + 1) * 128],
                                                                     identity=idb), r=[Lf.b, cm.b], w=[pT_.b], inc=(ti == gn - 1))
                                P.op("act", lambda e: e.activation(out=N0.t[:, 0:W_], in_=pT_.t[:, 0:W_], func=AF.Copy, scale=-1.0),
                                     r=[pT_.b], w=[N0.b])
                                Xc = Xb[0]
                                P.op("dve", lambda e: e.scalar_tensor_tensor(
                                    out=g3(Xc), in0=pT_.t[:, 0:W_].rearrange("p (a b) -> p a b", b=128), scalar=-1.0, in1=cmb(0),
                                    op0=ALU.mult, op1=ALU.add), r=[pT_.b, cm.b], w=[Xc.b])
                                Nc, NTc = N0, NT0
                                for k_ in range(5):
                                    par = (k_ + 1) % 2
                                    Nn, NTn = nb[0][par], nb[1][par]
                                    Xn = Xb[(k_ + 1) % 2]
                                    pN = pacc.next()
                                    pNT = pacc.next()
                                    if k_ < 4:
                                        for ti in range(gn):
                                            sl = slice(ti * 128, (ti + 1) * 128)
                                            P.op("pe", lambda e: e.matmul(pN.t[:, sl], lhsT=NTc.t[:, sl], rhs=Nc.t[:, sl], start=True, stop=True),
                                                 r=[NTc.b, Nc.b], w=[pN.b], inc=(ti == gn - 1))
                                    for ti in range(gn):
                                        sl = slice(ti * 128, (ti + 1) * 128)
                                        P.op("pe", lambda e: e.matmul(pNT.t[:, sl], lhsT=Nc.t[:, sl], rhs=NTc.t[:, sl], start=True, stop=True),
                                             r=[NTc.b, Nc.b], w=[pNT.b], inc=(ti == gn - 1))
                                    if k_ < 4:
                                        P.op("act", lambda e: e.activation(out=Nn.t[:, 0:W_], in_=pN.t[:, 0:W_], func=AF.Copy),
                                             r=[pN.b], w=[Nn.b])
                                    P.op("dve", lambda e: e.tensor_copy(out=NTn.t[:, 0:W_], in_=pNT.t[:, 0:W_]), r=[pNT.b], w=[NTn.b])
                                    pX = pacc.next()
                                    for ti in range(gn):
                                        sl = slice(ti * 128, (ti + 1) * 128)
                                        P.op("pe", lambda e: e.matmul(pX.t[:, sl], lhsT=NTn.t[:, sl], rhs=Xc.t[:, sl], start=True, stop=True),
                                             r=[NTn.b, Xc.b], w=[pX.b], inc=(ti == gn - 1))
                                    if k_ < 4:
                                        P.op("dve", lambda e: e.tensor_tensor(out=Xn.t[:, 0:W_], in0=pX.t[:, 0:W_], in1=Xc.t[:, 0:W_], op=ALU.add),
                                             r=[pX.b, Xc.b], w=[Xn.b])
                                    else:
                                        P.op("dve", lambda e: e.tensor_tensor(
                                            out=Xs.t[:, g0:g0 + gn, :].rearrange("p a b -> p (a b)"), in0=pX.t[:, 0:W_], in1=Xc.t[:, 0:W_],
                                            op=ALU.add), r=[pX.b, Xc.b], w=[Xs.b])
                                    Nc, NTc, Xc = Nn, NTn, Xn
                                pu = pacc.next()
                                pw = pacc.next()
                                for ti in range(gn):
                                    tt = g0 + ti
                                    sl = slice(ti * 128, (ti + 1) * 128)
                                    P.op("pe", lambda e: e.matmul(pu.t[:, sl], lhsT=Xs.t[:, tt, :], rhs=vb.t[:, tt, :], start=True, stop=True),
                                         r=[Xs.b, vb.b], w=[pu.b], inc=(ti == gn - 1))
                                for ti in range(gn):
                                    tt = g0 + ti
                                    sl = slice(ti * 128, (ti + 1) * 128)
                                    P.op("pe", lambda e: e.matmul(pw.t[:, sl], lhsT=kbg.t[:, tt, :], rhs=Xs.t[:, tt, :], start=True, stop=True),
                                         r=[Xs.b, kbg.b], w=[pw.b], inc=(ti == gn - 1))
                                P.op("act", lambda e: e.activation(out=u_.t[:, g0:g0 + gn, :].rearrange("p a b -> p (a b)"), in_=pu.t[:, 0:W_],
                                                                   func=AF.Copy), r=[pu.b], w=[u_.b])
                                P.op("act", lambda e: e.activation(out=wT.t[:, g0 * 128:g0 * 128 + W_], in_=pw.t[:, 0:W_], func=AF.Copy),
                                     r=[pw.b], w=[wT.b])
                                P.op("pool", lambda e: e.tensor_tensor(
                                    out=g3(Dg), in0=cmb(0), in1=egc.t[:, g0:g0 + gn].unsqueeze(2).broadcast_to([128, gn, 128]), op=ALU.mult),
                                    r=[cm.b, egc.b], w=[Dg.b])
                                pq = pacc.next()
                                P.op("pe", lambda e: e.matmul(pq.t[:, 0:W_], lhsT=ones_f.t[:], rhs=Dg.t[:, 0:W_], start=True, stop=True),
                                     r=[ones_f.b, Dg.b], w=[pq.b])
                                P.op("dve", lambda e: e.tensor_tensor(out=qgT.t[:, g0 * 128:g0 * 128 + W_], in0=pq.t[:, 0:W_],
                                                                      in1=qT.t[:, g0 * 128:g0 * 128 + W_], op=ALU.mult),
                                     r=[pq.b, qT.b], w=[qgT.b])
                            P.barrier()
                    with ExitStack() as s3:
                        Sst = [sb("gS%d" % i, [128, 128], F32, s3) for i in range(2)]
                        Sbf = [sb("gSb%d" % i, [128, 128], BF16, s3) for i in range(2)]
                        vnw = [[sb("gvn%d_%d" % (i, j), [128, 128], BF16, s3) for j in range(2)] for i in range(2)]
                        for i in range(2):
                            P.op("dve", lambda e: e.memset(Sst[i].t[:], 0.0), w=[Sst[i].b])
                            P.op("dve", lambda e: e.memset(Sbf[i].t[:], 0.0), w=[Sbf[i].b])
                        order_f = list(range(36))
                        order_b = [3, 2, 1, 0] + list(range(35, 3, -1))
                        written = set()
                        for step in range(36):
                            for di in range(2):
                                c = (order_f, order_b)[di][step]
                                tt, half = c // 2, c % 2
                                r0 = half * 64
                                wT, qgT, kd, attnT, u_, egl = dirs[di]
                                S_, Sb_ = Sst[di], Sbf[di]
                                vn = vnw[di][step % 2]
                                pA, pB, pC = psum[3 * di], psum[3 * di + 1], psum[3 * di + 2]
                                tsl = slice(tt * 128, (tt + 1) * 128)
                                P.op("pe", lambda e: e.matmul(pA.t[:, 0:128], lhsT=wT.t[:, tsl], rhs=S_.t[:], start=True, stop=True),
                                     r=[wT.b, S_.b], w=[pA.b])
                                P.op("dve", lambda e: e.tensor_tensor(out=vn.t[r0:r0 + 64, :], in0=u_.t[r0:r0 + 64, tt, :],
                                                                      in1=pA.t[r0:r0 + 64, 0:128], op=ALU.subtract),
                                     r=[u_.b, pA.b], w=[vn.b])
                                P.op("pe", lambda e: e.matmul(pB.t[:, 0:128], lhsT=Sb_.t[:], rhs=qgT.t[:, tsl], start=True, stop=False),
                                     r=[Sb_.b, qgT.b], w=[pB.b], inc=False)
                                P.op("pe", lambda e: e.matmul(pB.t[:, 0:128], lhsT=vn.t[r0:r0 + 64, :], rhs=attnT.t[r0:r0 + 64, tt, :],
                                                              start=False, stop=True), r=[vn.b, attnT.b], w=[pB.b])
                                P.op("pe", lambda e: e.matmul(pC.t[:, 0:128], lhsT=kd.t[r0:r0 + 64, tt, :], rhs=vn.t[r0:r0 + 64, :],
                                                              start=True, stop=True), r=[kd.b, vn.b], w=[pC.b])
                                P.op("dve", lambda e: e.scalar_tensor_tensor(out=S_.t[:], in0=S_.t[:], scalar=egl.t[:, tt, half:half + 1],
                                                                             in1=pC.t[:, 0:128], op0=ALU.mult, op1=ALU.add),
                                     r=[S_.b, egl.b, pC.b], w=[S_.b])
                                P.op("act", lambda e: e.activation(out=Sb_.t[:], in_=S_.t[:], func=AF.Copy), r=[S_.b], w=[Sb_.b])
                                osl = slice(c * 64, c * 64 + 64)
                                if c not in written:
                                    written.add(c)
                                    P.op("act", lambda e: e.activation(out=oacc.t[:, osl], in_=pB.t[:, r0:r0 + 64], func=AF.Copy),
                                         r=[pB.b], w=[oacc.b])
                                else:
                                    P.op("dve", lambda e: e.tensor_tensor(out=oacc.t[:, osl], in0=oacc.t[:, osl], in1=pB.t[:, r0:r0 + 64],
                                                                          op=ALU.add), r=[pB.b, oacc.b], w=[oacc.b])
                        zs = sb("gzs", [128, T], BF16, s3)
                        ob = sb("gob", [128, T], BF16, s3)
                        sq2 = sb("gsq2", [128, 512], F32, s3)
                        rn2 = sb("grn2", [128, 512], F32, s3)
                        P.dma("sp", zs.t[:], fmv[:, OFF_ZB + h, :], r=[s_fm.b], w=[zs.b])
                        for (t0, n) in TGS:
                            P.op("pool", lambda e: e.tensor_tensor(out=sq2.t[:, 0:n], in0=oacc.t[:, t0:t0 + n], in1=oacc.t[:, t0:t0 + n],
                                                                  op=ALU.mult), r=[oacc.b], w=[sq2.b])
                            pq = ps_next()
                            P.op("pe", lambda e: e.matmul(pq.t[:, 0:n], lhsT=ones_f.t[:], rhs=sq2.t[:, 0:n], start=True, stop=True),
                                 r=[sq2.b, ones_f.b], w=[pq.b])
                            P.op("act", lambda e: e.activation(out=rn2.t[:, 0:n], in_=pq.t[:, 0:n], func=AF.Sqrt, bias=epsc.t[:, 0:1],
                                                               scale=1.0 / 128), r=[pq.b, epsc.b], w=[rn2.b])
                            P.op("dve", lambda e: e.reciprocal(out=rn2.t[:, 0:n], in_=rn2.t[:, 0:n]), r=[rn2.b], w=[rn2.b])
                            P.op("dve", lambda e: e.scalar_tensor_tensor(out=rn2.t[:, 0:n], in0=oacc.t[:, t0:t0 + n], scalar=gg.t[:, 0:1],
                                                                         in1=rn2.t[:, 0:n], op0=ALU.mult, op1=ALU.mult),
                                 r=[oacc.b, gg.b, rn2.b], w=[rn2.b])
                            P.op("pool", lambda e: e.tensor_tensor(out=ob.t[:, t0:t0 + n], in0=rn2.t[:, 0:n], in1=zs.t[:, t0:t0 + n],
                                                                  op=ALU.mult), r=[rn2.b, zs.b], w=[ob.b])
                        P.dma("sp", ov[:, h, :], ob.t[:], r=[ob.b], w=[s_o.b])
                        P.barrier()

    if "M" in cfg.stages:
        stage_M()
    for l in range(L):
        layer_coefs(l)
        last = (l == L - 1) and not cfg.force_ctx
        for s in range(NS):
            tgs = TGS[1:] if last else TGS
            if "B" in cfg.stages:
                stage_AB(l, s)
            if "C" in cfg.stages:
                if "A" in cfg.mixers:
                    stage_C1(l, s, not last)
                if "N" in cfg.mixers:
                    stage_C2(l, s, not last)
                if "B" in cfg.mixers:
                    stage_C3(l, s)
            if "D" in cfg.stages:
                stage_D(l, s, tgs)
            if "E" in cfg.stages:
                stage_E(l, s, tgs)
    P.barrier()
    gs.close()
    return nc, P


def rope_tables():
    t = np.arange(NX)
    nf = HD // 4
    inv = (10000.0 ** (-np.arange(nf, dtype=np.float32) / nf)).astype(np.float32)
    ang_r = (t // 64).astype(np.float32)[:, None] * inv
    ang_c = (t % 64).astype(np.float32)[:, None] * inv
    cosT = np.ones((128, T), np.float32)
    sinT = np.zeros((128, T), np.float32)
    for a, ang in enumerate((ang_r, ang_c)):
        c = np.cos(ang).T.astype(np.float32)
        s_ = np.sin(ang).T.astype(np.float32)
        cosT[a * 64:a * 64 + 32, NCTX:] = c
        cosT[a * 64 + 32:a * 64 + 64, NCTX:] = c
        sinT[a * 64:a * 64 + 32, NCTX:] = -s_
        sinT[a * 64 + 32:a * 64 + 64, NCTX:] = s_
    return cosT, sinT


def w_in_cols():
    qa0, ka0, va0 = 0, 1024, 1280
    qb0 = 1536
    zb0 = qb0 + 3072
    ab0 = zb0 + 1024
    qn0 = ab0 + 32
    kn0, vn0 = qn0 + 1024, qn0 + 2048
    g0 = qn0 + 3072
    perm = np.concatenate([np.arange(32, 64), np.arange(0, 32), np.arange(96, 128), np.arange(64, 96)])
    cols = []
    for h in range(8):
        base = qa0 + h * 128
        cols.append(base + np.arange(128))
        cols.append(base + perm)
    for h in range(2):
        base = ka0 + h * 128
        cols.append(base + np.arange(128))
        cols.append(base + perm)
    cols.append(np.arange(qb0, qb0 + 3072))
    cols.append(np.arange(zb0, zb0 + 1024))
    cols.append(np.arange(qn0, qn0 + 1024))
    cols.append(np.arange(kn0, kn0 + 1024))
    cols.append(np.arange(g0, g0 + 6144))
    cols.append(np.arange(va0, va0 + 256))
    cols.append(np.arange(vn0, vn0 + 1024))
    cols.append(np.arange(ab0, ab0 + 32))
    cols = np.concatenate(cols)
    assert cols.shape[0] == W_ALL
    return cols


def const_masks():
    m = np.zeros((128, 13, 128), np.float32)
    m[:, 0, :] = np.eye(128, dtype=np.float32)
    p = np.arange(128)[:, None]
    f = np.arange(128)[None, :]
    m[:, 1, :] = (p >= f)
    m[:, 2, :] = (p <= f)
    kc = np.arange(64)[:, None]
    qc = 63 - np.arange(64)[None, :]
    cs = np.clip(qc - 8, 0, 48)
    ok = (kc >= cs) & (kc < cs + 16)
    m[:64, 3, :64] = np.where(ok, 0.0, -30000.0)
    same = (p // 64) == (f // 64)
    m[:, 4, :] = same & (p <= f)
    m[:, 5, :] = same & (p >= f)
    m[:, 6, :] = np.where(same & (p >= f), 0.0, -30000.0)
    m[:, 7, :] = np.where(same & (p <= f), 0.0, -30000.0)
    m[:, 8, :] = same & (p > f)
    m[:, 9, :] = same & (p < f)
    m[:, 10, :] = same
    m[:, 11, :] = (p < 64) & (f >= 0)
    m[:, 12, :] = (p >= 64) & (f >= 0)
    return m


def host_inputs(inputs, n_cores=8):
    x = np.asarray(inputs["x"], np.float32)
    ctx = np.asarray(inputs["ctx"], np.float32)
    c = np.asarray(inputs["c"], np.float32)
    cols = w_in_cols()
    shared = {
        "w_mod": np.ascontiguousarray(inputs["w_mod"], np.float32),
        "b_mod": np.ascontiguousarray(inputs["b_mod"], np.float32),
        "gvec": np.ascontiguousarray(np.stack([inputs["g_pre_mix"], inputs["g_post_mix"], inputs["g_pre_mlp"],
                                               inputs["g_post_mlp"]], axis=1), np.float32),
        "w_in": np.ascontiguousarray(np.asarray(inputs["w_in"], np.float32)[:, :, cols]),
        "conv_w": np.ascontiguousarray(inputs["conv_w"], np.float32),
        "a_log": np.ascontiguousarray(np.asarray(inputs["a_log"], np.float32).reshape(-1, 16)),
        "dt_bias": np.ascontiguousarray(np.asarray(inputs["dt_bias"], np.float32).reshape(-1, 16)),
        "g_gdn": np.ascontiguousarray(inputs["g_gdn_out"], np.float32),
        "sink": np.ascontiguousarray(inputs["sink"], np.float32),
        "rpb": np.ascontiguousarray(np.asarray(inputs["rpb"], np.float32).reshape(np.asarray(inputs["rpb"]).shape[0], -1)),
        "w_branch": np.ascontiguousarray(inputs["w_branch"], np.float32),
        "w_out": np.ascontiguousarray(inputs["w_out"], np.float32),
        "w_up": np.ascontiguousarray(inputs["w_up"], np.float32),
        "w_down": np.ascontiguousarray(inputs["w_down"], np.float32),
    }
    cosT, sinT = rope_tables()
    shared["ropec"] = cosT
    shared["ropes"] = sinT
    shared["cmasks"] = const_masks()
    maps = []
    for core in range(n_cores):
        b0 = core * NSEQ
        xin = np.empty((NSEQ, D, T), np.float32)
        for s in range(NSEQ):
            xin[s, :, :NCTX] = ctx[b0 + s].T
            xin[s, :, NCTX:] = x[b0 + s].T
        c3 = np.stack([c[b0], c[b0 + 1], np.asarray(inputs["c_ctx"], np.float32)], axis=0)
        m = dict(shared)
        m["xin"] = xin
        m["c3"] = np.ascontiguousarray(c3)
        maps.append(m)
    return maps


_CACHE = {}


def kernel(**inputs):
    n_cores = 8
    if "nc" not in _CACHE:
        _CACHE["nc"] = build_program(Cfg())[0]
    nc = _CACHE["nc"]
    maps = host_inputs(inputs, n_cores)
    res = run_bass_kernel_spmd(nc, maps, core_ids=list(range(n_cores)))
    out = np.empty((16, NX, D), np.float32)
    for core in range(n_cores):
        y = res.results[core]["yout"]
        for s in range(NSEQ):
            out[core * NSEQ + s] = y[s].T
    return out
```

```python
import math
from contextlib import ExitStack

import numpy as np
import concourse.bass as bass
import concourse.mybir as mybir
from concourse.bass_utils import run_bass_kernel_spmd

F32 = mybir.dt.float32
BF16 = mybir.dt.bfloat16
AF = mybir.ActivationFunctionType
ALU = mybir.AluOpType
AX = mybir.AxisListType

D = 2048
NCTX = 256
NX = 2048
T = NCTX + NX
DEPTH = 4
NSEQ = 2
KC = D // 128
DFF = 4 * D
HD = 128
EPS = 1e-6
TGS = [(0, 256)] + [(256 + 512 * i, 512) for i in range(4)]
NFM = 116
W_FM = NFM * 128
W_TM = 256 + 1024 + 32
W_ALL = W_FM + W_TM


class Buf:
    __slots__ = ("name", "lw", "rd", "excl")

    def __init__(self, name=""):
        self.name = name
        self.lw = None
        self.rd = {}
        self.excl = False


class TB:
    def __init__(self, t, name=""):
        self.t = t
        self.b = Buf(name)


COMPUTE = ("pe", "dve", "act", "pool")
NDS = 24


class Prog:
    def __init__(self, nc):
        self.nc = nc
        self.E = {"pe": nc.tensor, "dve": nc.vector, "act": nc.scalar, "pool": nc.gpsimd, "sp": nc.sync}
        self.sems = []
        self.semidx = {}
        for e in COMPUTE:
            self.semidx[e] = len(self.sems)
            self.sems.append(nc.alloc_semaphore("s_" + e))
        self.cnt = {e: 0 for e in COMPUTE}
        self.pend = {e: None for e in COMPUTE}
        self.seen = {e: {} for e in self.E}
        self.dslots = []
        for i in range(NDS):
            self.dslots.append([len(self.sems), 0])
            self.sems.append(nc.alloc_semaphore("d%d" % i))
        self.dnext = 0
        self.nwaits = 0
        self.nops = 0

    def _wait(self, e, toks):
        need = {}
        for t in toks:
            if t is None:
                continue
            te, si, val = t
            if e == "pe" and te == "pe":
                continue
            assert val is not None, "wait on pending token"
            if self.seen[e].get(si, 0) >= val:
                continue
            if need.get(si, 0) < val:
                need[si] = val
        for si, val in need.items():
            self.E[e].wait_ge(self.sems[si], val)
            self.seen[e][si] = val
            self.nwaits += 1

    def _deps(self, e, r, w):
        deps = []
        for b in r:
            if b.lw is not None:
                deps.append(b.lw)
            if b.excl:
                for k, t in b.rd.items():
                    if k != e:
                        deps.append(t)
        for b in w:
            if b.lw is not None:
                deps.append(b.lw)
            for k, t in b.rd.items():
                if k == e and e in COMPUTE:
                    continue
                deps.append(t)
        return deps

    def op(self, e, fn, r=(), w=(), inc=True):
        self._wait(e, self._deps(e, r, w))
        ins = fn(self.E[e])
        self.nops += 1
        if inc:
            self.cnt[e] += 1
            ins.then_inc(self.sems[self.semidx[e]], 1)
            tok = self.pend[e]
            if tok is None:
                tok = [e, self.semidx[e], None]
            tok[2] = self.cnt[e]
            self.pend[e] = None
        else:
            tok = self.pend[e]
            if tok is None:
                tok = self.pend[e] = [e, self.semidx[e], None]
        for b in r:
            b.rd[e] = tok
        for b in w:
            b.lw = tok
            b.rd = {}
        return ins

    def dma(self, q, out, in_, r=(), w=(), **kw):
        deps = self._deps(q, r, w)
        slot = self.dslots[self.dnext]
        self.dnext = (self.dnext + 1) % NDS
        if slot[1] > 0:
            deps.append(["dma", slot[0], slot[1]])
        self._wait(q, deps)
        ins = self.E[q].dma_start(out=out, in_=in_, **kw)
        slot[1] += 16
        ins.then_inc(self.sems[slot[0]], 16)
        self.nops += 1
        tok = ["dma", slot[0], slot[1]]
        for b in r:
            b.rd[("dma", slot[0])] = tok
        for b in w:
            b.lw = tok
            b.rd = {}
        return ins

    def all_toks(self):
        toks = []
        for e in COMPUTE:
            assert self.pend[e] is None, "pending at barrier on " + e
            if self.cnt[e] > 0:
                toks.append(["x", self.semidx[e], self.cnt[e]])
        for slot in self.dslots:
            if slot[1] > 0:
                toks.append(["dma", slot[0], slot[1]])
        return toks

    def barrier(self, engines=None):
        toks = self.all_toks()
        for e in (engines or self.E):
            self._wait(e, toks)


class Cfg:
    def __init__(self, **kw):
        self.layers = DEPTH
        self.nseq = NSEQ
        self.debug = False
        self.stages = "MABCDE"
        self.inject = ()
        self.wdepth = DEPTH
        self.ab_parts = "mft"
        self.mixers = "ANB"
        self.force_ctx = False
        self.nfm = NFM
        self.__dict__.update(kw)


def build_program(cfg):
    nc = bass.Bass("TRN2", target_bir_lowering=False)
    P = Prog(nc)
    print("sbuf bytes remaining at start:", nc.sbuf_bytes_remaining)
    L = cfg.layers
    NS = cfg.nseq
    WD = cfg.wdepth

    def din(name, shape, dt=F32):
        return nc.dram_tensor(name, list(shape), dt, kind="ExternalInput").ap()

    def dscratch(name, shape, dt):
        if name in cfg.inject:
            kind = "ExternalInput"
        elif cfg.debug:
            kind = "ExternalOutput"
        else:
            kind = "Internal"
        return TB(nc.dram_tensor(name, list(shape), dt, kind=kind).ap(), name)

    xin = din("xin", [NSEQ, D, T])
    c3 = din("c3", [3, D])
    w_mod = din("w_mod", [WD, D, 6 * D])
    b_mod = din("b_mod", [WD, 6 * D])
    gvec = din("gvec", [WD, 4, D])
    w_in = din("w_in", [WD, D, W_ALL])
    conv_w = din("conv_w", [WD, 5, 3072])
    a_log = din("a_log", [WD, 16])
    dt_bias = din("dt_bias", [WD, 16])
    g_gdn = din("g_gdn", [WD, 128])
    sink = din("sink", [WD, 8])
    rpb = din("rpb", [WD, 8 * 15 * 31])
    w_branch = din("w_branch", [WD, 3, 4, 128, 8 * 512])
    w_out = din("w_out", [WD, 4, 2, 128, 8 * 512])
    w_up = din("w_up", [WD, 16, 128, 16 * 512])
    w_down = din("w_down", [WD, 4, 4, 128, 16 * 512])
    ropec = din("ropec", [128, T])
    ropes = din("ropes", [128, T])
    cmasks = din("cmasks", [128, 13, 128])
    yout = nc.dram_tensor("yout", [NSEQ, D, NX], F32, kind="ExternalOutput").ap()

    xT = dscratch("xT", [NSEQ, D, T], F32)
    s_fm = dscratch("s_fm", [W_FM, T], BF16)
    s_va = dscratch("s_va", [T, 256], BF16)
    s_vn = dscratch("s_vn", [T, 1024], BF16)
    s_ab = dscratch("s_ab", [T, 32], F32)
    s_o = dscratch("s_o", [3, 1024, T], BF16)
    w_up16 = dscratch("w_up16", [16, 128, 16 * 512], BF16)
    w_down16 = dscratch("w_down16", [4, 4, 128, 16 * 512], BF16)
    w_branch16 = dscratch("w_branch16", [3, 4, 128, 8 * 512], BF16)
    w_out16 = dscratch("w_out16", [4, 2, 128, 8 * 512], BF16)
    OFF_QA, OFF_KA, OFF_QKVB, OFF_ZB, OFF_QN, OFF_KN, OFF_GATE = 0, 8, 10, 34, 42, 50, 58
    NFM_OUT = 106

    gs = ExitStack()

    uniq = {"n": 0}

    def sb(name, shape, dt, stack=None):
        uniq["n"] += 1
        nm = "%s_%d" % (name, uniq["n"])
        return TB((stack or gs).enter_context(nc.sbuf_tensor(nm, list(shape), dt)), nm)

    psum = [TB(gs.enter_context(nc.psum_tensor("ps%d" % i, [128, 512], F32)), "ps%d" % i) for i in range(8)]
    for p_ in psum:
        p_.b.excl = True
    pstate = {"i": 0}

    def ps_next():
        p = psum[pstate["i"]]
        pstate["i"] = (pstate["i"] + 1) % 8
        return p

    class PSPool:
        def __init__(self, idx):
            self.idx = list(idx)
            self.i = 0

        def next(self):
            p = psum[self.idx[self.i]]
            self.i = (self.i + 1) % len(self.idx)
            return p

    ones_f = sb("ones_f", [128, 128], F32)
    ones_b = sb("ones_b", [128, 128], BF16)
    modT = sb("modT", [128, DEPTH, 96, 3], F32)
    gT = sb("gT", [128, DEPTH, 4, 16], F32)
    coef = sb("coef", [128, 6, 16, 3], F32)
    negones = sb("negones", [128, 128], F32)
    P.op("dve", lambda e: e.memset(negones.t[:], -1.0), w=[negones.b])
    epsc = sb("epsc", [128, 4], F32)
    P.op("dve", lambda e: e.memset(epsc.t[:], EPS), w=[epsc.b])
    P.op("dve", lambda e: e.memset(ones_f.t[:], 1.0), w=[ones_f.b])
    P.op("dve", lambda e: e.memset(ones_b.t[:], 1.0), w=[ones_b.b])
    cm = sb("cm", [128, 13, 128], F32)
    for j in range(13):
        P.dma("sp", cm.t[:, j, :], cmasks[:, j, :], w=[cm.b])
    ident = cm.t[:, 0, :]

    def load_T(dst_ap, src_rows, nrows, tag):
        with ExitStack() as st:
            tmp = sb("ldT" + tag, [128, 128], F32, st)
            P.dma("sp", tmp.t[0:nrows, :], src_rows, w=[tmp.b])
            ps = ps_next()
            P.op("pe", lambda e: e.transpose(out=ps.t[:, 0:nrows], in_=tmp.t[0:nrows, :], identity=cm.t[0:nrows, 0, 0:nrows]),
                 r=[tmp.b, cm.b], w=[ps.b])
            P.op("dve", lambda e: e.tensor_copy(out=dst_ap, in_=ps.t[:, 0:nrows]), r=[ps.b], w=[gT.b, modT.b])
            P.barrier()

    gv = gvec.rearrange("l k (c p) -> (l k c) p", p=128)
    gflat = gT.t[:].rearrange("p l k c -> p (l k c)")
    for j in range(0, L * 64, 128):
        nr = min(128, L * 64 - j)
        load_T(gflat[:, j:j + nr], gv[j:j + nr, :], nr, "g%d" % j)

    def stage_M():
        with ExitStack() as st:
            scT = sb("scT", [128, 3, 16], F32, st)
            wblk = [sb("wmblk%d" % i, [128, 16, 512], F32, st) for i in range(2)]
            brow = [sb("brow%d" % i, [1, 512], F32, st) for i in range(2)]
            load_T(scT.t[:].rearrange("p r c -> p (r c)"), c3.rearrange("r (c p) -> (r c) p", p=128), 48, "c3")
            P.op("act", lambda e: e.activation(out=scT.t[:], in_=scT.t[:], func=AF.Silu), r=[scT.b], w=[scT.b])
            it = 0
            for l in range(L):
                wv = w_mod[l].rearrange("(kc p) n -> p kc n", p=128)
                for blk in range(24):
                    wb = wblk[it % 2]
                    br = brow[it % 2]
                    it += 1
                    for j in range(4):
                        P.dma("sp", wb.t[:, 4 * j:4 * j + 4, :], wv[:, 4 * j:4 * j + 4, blk * 512:(blk + 1) * 512],
                              w=[wb.b])
                    P.dma("sp", br.t[:], b_mod[l:l + 1, blk * 512:(blk + 1) * 512], w=[br.b])
                    ps = ps_next()
                    for j in range(4):
                        def mm(e, j=j, wb=wb, br=br, ps=ps):
                            for kc in range(16):
                                e.matmul(ps.t[:, 3 * j:3 * j + 3], lhsT=wb.t[:, kc, j * 128:(j + 1) * 128],
                                         rhs=scT.t[:, :, kc], start=(kc == 0), stop=False)
                            return e.matmul(ps.t[:, 3 * j:3 * j + 3], lhsT=br.t[0:1, j * 128:(j + 1) * 128],
                                            rhs=ones_f.t[0:1, 0:3], start=False, stop=True)
                        P.op("pe", mm, r=[wb.b, br.b, scT.b, ones_f.b], w=[ps.b], inc=(j == 3))
                    P.op("dve", lambda e, ps=ps, l=l, blk=blk: e.tensor_copy(
                        out=modT.t[:, l, blk * 4:(blk + 1) * 4, :],
                        in_=ps.t[:, 0:12].rearrange("p (a b) -> p a b", b=3)), r=[ps.b], w=[modT.b])
            P.barrier()

    def stage_W(l):
        for blk in range(16):
            P.dma("pool", w_up16.t[blk], w_up[l, blk], w=[w_up16.b])
        for ob in range(4):
            for kq in range(4):
                P.dma("pool", w_down16.t[ob, kq], w_down[l, ob, kq], w=[w_down16.b])
        for i in range(3):
            for ob in range(4):
                P.dma("pool", w_branch16.t[i, ob], w_branch[l, i, ob], w=[w_branch16.b])
        for ob in range(4):
            for kh in range(2):
                P.dma("pool", w_out16.t[ob, kh], w_out[l, ob, kh], w=[w_out16.b])

    class Prefetch:
        def __init__(self, bufs, loaders, dist=2):
            self.bufs, self.loaders, self.dist, self.issued = bufs, loaders, dist, 0

        def get(self, k):
            while self.issued < len(self.loaders) and self.issued <= k + self.dist:
                self.loaders[self.issued](self.bufs[self.issued % len(self.bufs)])
                self.issued += 1
            return self.bufs[k % len(self.bufs)]

    def layer_coefs(l):
        def m(idx):
            return modT.t[:, l, idx * 16:(idx + 1) * 16, :]

        def g(k):
            return gT.t[:, l, k, :].unsqueeze(2).broadcast_to([128, 16, 3])
        for (dst, gi, mi, plus1) in ((0, 0, 1, True), (2, 1, 2, False), (3, 2, 4, True), (5, 3, 5, False)):
            if plus1:
                P.op("dve", lambda e, dst=dst, gi=gi, mi=mi: e.scalar_tensor_tensor(
                    out=coef.t[:, dst], in0=m(mi), scalar=1.0, in1=g(gi), op0=ALU.add, op1=ALU.mult),
                    r=[modT.b, gT.b], w=[coef.b])
            else:
                P.op("dve", lambda e, dst=dst, gi=gi, mi=mi: e.tensor_tensor(
                    out=coef.t[:, dst], in0=m(mi), in1=g(gi), op=ALU.mult), r=[modT.b, gT.b], w=[coef.b])
        P.op("dve", lambda e: e.tensor_copy(out=coef.t[:, 1], in_=m(0)), r=[modT.b], w=[coef.b])
        P.op("dve", lambda e: e.tensor_copy(out=coef.t[:, 4], in_=m(3)), r=[modT.b], w=[coef.b])

    def modulate(st, src, s, ia, ib, hT, tgs, xt=None):
        if xt is None:
            xt = [sb("mx%d" % i, [128, 16, 512], F32, st) for i in range(2)]
        sq = [sb("msq%d" % i, [128, 512], F32, st) for i in range(2)]
        rstd = sb("mrstd", [128, 512], F32, st)
        tmp = [sb("mtmp%d" % i, [128, 512], F32, st) for i in range(2)]
        srcv = src.rearrange("(c p) t -> p c t", p=128)
        for gi, (t0, n) in enumerate(tgs):
            row = 2 if t0 < NCTX else s
            x = xt[gi % len(xt)]
            for j in range(4):
                P.dma("sp", x.t[:, 4 * j:4 * j + 4, 0:n], srcv[:, 4 * j:4 * j + 4, t0:t0 + n], r=[xT.b], w=[x.b])
            ps = ps_next()
            for kc in range(16):
                q = sq[kc % 2]
                P.op("act", lambda e, q=q, x=x, kc=kc: e.activation(out=q.t[:, 0:n], in_=x.t[:, kc, 0:n], func=AF.Square),
                     r=[x.b], w=[q.b])
                P.op("pe", lambda e, q=q, kc=kc, ps=ps: e.matmul(ps.t[:, 0:n], lhsT=ones_f.t[:], rhs=q.t[:, 0:n],
                                                                 start=(kc == 0), stop=(kc == 15)),
                     r=[q.b, ones_f.b], w=[ps.b], inc=True)
            P.op("act", lambda e, ps=ps: e.activation(out=rstd.t[:, 0:n], in_=ps.t[:, 0:n], func=AF.Sqrt, bias=epsc.t[:, 0:1],
                                                      scale=1.0 / D), r=[ps.b, epsc.b], w=[rstd.b])
            P.op("dve", lambda e: e.reciprocal(out=rstd.t[:, 0:n], in_=rstd.t[:, 0:n]), r=[rstd.b], w=[rstd.b])
            for kc in range(16):
                tm = tmp[kc % 2]
                P.op("dve", lambda e, tm=tm, x=x, kc=kc: e.scalar_tensor_tensor(
                    out=tm.t[:, 0:n], in0=x.t[:, kc, 0:n], scalar=coef.t[:, ia, kc, row:row + 1], in1=rstd.t[:, 0:n],
                    op0=ALU.mult, op1=ALU.mult), r=[x.b, coef.b, rstd.b], w=[tm.b])
                P.op("act", lambda e, tm=tm, kc=kc: e.activation(
                    out=hT.t[:, kc, t0:t0 + n], in_=tm.t[:, 0:n], func=AF.Identity,
                    bias=coef.t[:, ib, kc, row:row + 1], scale=1.0), r=[tm.b, coef.b], w=[hT.b])

    def stage_AB(l, s):
        with ExitStack() as st:
            hT = sb("hT", [128, 16, T], BF16, st)
            with ExitStack() as st2:
                src = xin[s] if l == 0 else xT.t[s]
                modulate(st2, src, s, 0, 1, hT, TGS)
                P.barrier()
            if "f" not in cfg.ab_parts:
                return
            wblk = [sb("wblk%d" % i, [128, 16, 512], BF16, st) for i in range(3)]
            stg = [sb("stg%d" % i, [128, T], BF16, st) for i in range(4)]
            cosT = sb("cosT", [128, T], F32, st)
            sinT = sb("sinT", [128, T], F32, st)
            r1 = [sb("r1_%d" % i, [128, 512], F32, st) for i in range(2)]
            r2 = [sb("r2_%d" % i, [128, 512], F32, st) for i in range(2)]
            for j in range(3):
                a, b_ = j * 768, (j + 1) * 768
                P.dma("sp", cosT.t[:, a:b_], ropec[:, a:b_], w=[cosT.b])
                P.dma("sp", sinT.t[:, a:b_], ropes[:, a:b_], w=[sinT.b])
            wv = w_in[l].rearrange("(kc p) n -> p kc n", p=128)
            nblk = (W_ALL + 511) // 512
            blkbuf = {}

            def load_blk(bi):
                wb = wblk[bi % 3]
                c0 = bi * 512
                n = min(512, W_ALL - c0)
                for j in range(2):
                    P.dma("pool", wb.t[:, 8 * j:8 * j + 8, 0:n], wv[:, 8 * j:8 * j + 8, c0:c0 + n], w=[wb.b])
                blkbuf[bi] = wb

            load_blk(0)
            load_blk(1)
            si = 0
            evq = 0
            oc_out = 0
            ci = 0
            while ci < cfg.nfm:
                bi = ci // 4
                if ci % 4 == 0 and bi + 2 < nblk:
                    load_blk(bi + 2)
                wb = blkbuf[bi]
                rope = ci < 20
                kind = "plain"
                if ci >= 20 + 24 and ci < 20 + 32:
                    kind = "silu"
                if ci >= 20 + 48:
                    kind = "sigmoid"
                sg = stg[si % 4]
                si += 1
                for gi, (t0, n) in enumerate(TGS):
                    pa = ps_next()

                    def mm(e, pa=pa, wb=wb, cc=ci % 4, t0=t0, n=n):
                        for kc in range(16):
                            ins = e.matmul(pa.t[:, 0:n], lhsT=wb.t[:, kc, cc * 128:(cc + 1) * 128],
                                           rhs=hT.t[:, kc, t0:t0 + n], start=(kc == 0), stop=(kc == 15))
                        return ins
                    P.op("pe", mm, r=[wb.b, hT.b], w=[pa.b])
                    if rope:
                        pb = ps_next()
                        P.op("pe", lambda e, pb=pb, wb=wb, cc=ci % 4 + 1, t0=t0, n=n: mm(e, pb, wb, cc, t0, n),
                             r=[wb.b, hT.b], w=[pb.b])
                        a1 = r1[gi % 2]
                        a2 = r2[gi % 2]
                        P.op("dve", lambda e, a1=a1, pa=pa, t0=t0, n=n: e.tensor_tensor(
                            out=a1.t[:, 0:n], in0=pa.t[:, 0:n], in1=cosT.t[:, t0:t0 + n], op=ALU.mult),
                            r=[pa.b, cosT.b], w=[a1.b])
                        P.op("dve", lambda e, a2=a2, pb=pb, t0=t0, n=n: e.tensor_tensor(
                            out=a2.t[:, 0:n], in0=pb.t[:, 0:n], in1=sinT.t[:, t0:t0 + n], op=ALU.mult),
                            r=[pb.b, sinT.b], w=[a2.b])
                        P.op("pool", lambda e, a1=a1, a2=a2, sg=sg, t0=t0, n=n: e.tensor_tensor(
                            out=sg.t[:, t0:t0 + n], in0=a1.t[:, 0:n], in1=a2.t[:, 0:n], op=ALU.add),
                            r=[a1.b, a2.b], w=[sg.b])
                    elif kind == "plain":
                        evq += 1
                        if evq % 2:
                            P.op("dve", lambda e, pa=pa, sg=sg, t0=t0, n=n: e.tensor_copy(
                                out=sg.t[:, t0:t0 + n], in_=pa.t[:, 0:n]), r=[pa.b], w=[sg.b])
                        else:
                            P.op("act", lambda e, pa=pa, sg=sg, t0=t0, n=n: e.activation(
                                out=sg.t[:, t0:t0 + n], in_=pa.t[:, 0:n], func=AF.Copy), r=[pa.b], w=[sg.b])
                    else:
                        fn = AF.Silu if kind == "silu" else AF.Sigmoid
                        P.op("act", lambda e, pa=pa, sg=sg, t0=t0, n=n, fn=fn: e.activation(
                            out=sg.t[:, t0:t0 + n], in_=pa.t[:, 0:n], func=fn), r=[pa.b], w=[sg.b])
                P.dma("sp", s_fm.t[oc_out * 128:(oc_out + 1) * 128, :], sg.t[:], r=[sg.b], w=[s_fm.b])
                oc_out += 1
                ci += 2 if rope else 1
            if 't' not in cfg.ab_parts:
                P.barrier()
                return
            tstg = [sb("tstg%d" % i, [128, 512], BF16, st) for i in range(3)]
            tstf = [sb("tstf%d" % i, [128, 32], F32, st) for i in range(2)]
            ti = 0
            for bi in range(NFM // 4, nblk):
                if bi + 2 < nblk and bi + 2 not in blkbuf:
                    load_blk(bi + 2)
                wb = blkbuf[bi]
                c0 = bi * 512 - W_FM
                n = min(512, W_TM - c0)
                for tt in range(T // 128):
                    pa = ps_next()

                    def mmt(e, pa=pa, wb=wb, tt=tt, n=n):
                        for kc in range(16):
                            ins = e.matmul(pa.t[:, 0:n], lhsT=hT.t[:, kc, tt * 128:(tt + 1) * 128],
                                           rhs=wb.t[:, kc, 0:n], start=(kc == 0), stop=(kc == 15))
                        return ins
                    P.op("pe", mmt, r=[wb.b, hT.b], w=[pa.b])
                    segs = []
                    for (nm, a, b_) in (("va", 0, 256), ("vn", 256, 1280), ("ab", 1280, 1312)):
                        lo, hi = max(a, c0), min(b_, c0 + n)
                        if lo < hi:
                            segs.append((nm, lo - a, hi - a, lo - c0, hi - c0))
                    for (nm, d0, d1, p0, p1) in segs:
                        if nm == "ab":
                            tf = tstf[ti % 2]
                            if tt % 2:
                                P.op("act", lambda e, tf=tf, pa=pa, p0=p0, p1=p1: e.activation(
                                    out=tf.t[:, 0:32], in_=pa.t[:, p0:p1], func=AF.Copy), r=[pa.b], w=[tf.b])
                            else:
                                P.op("dve", lambda e, tf=tf, pa=pa, p0=p0, p1=p1: e.tensor_copy(
                                    out=tf.t[:, 0:32], in_=pa.t[:, p0:p1]), r=[pa.b], w=[tf.b])
                            P.dma("sp", s_ab.t[tt * 128:(tt + 1) * 128, :], tf.t[:], r=[tf.b], w=[s_ab.b])
                        else:
                            ts_ = tstg[ti % 3]
                            ti += 1
                            eng = "act" if tt % 2 else "dve"
                            if eng == "act":
                                P.op("act", lambda e, ts_=ts_, pa=pa, p0=p0, p1=p1: e.activation(
                                    out=ts_.t[:, 0:p1 - p0], in_=pa.t[:, p0:p1], func=AF.Copy), r=[pa.b], w=[ts_.b])
                            else:
                                P.op("dve", lambda e, ts_=ts_, pa=pa, p0=p0, p1=p1: e.tensor_copy(
                                    out=ts_.t[:, 0:p1 - p0], in_=pa.t[:, p0:p1]), r=[pa.b], w=[ts_.b])
                            dst = s_va if nm == "va" else s_vn
                            P.dma("sp", dst.t[tt * 128:(tt + 1) * 128, d0:d1], ts_.t[:, 0:p1 - p0], r=[ts_.b], w=[dst.b])
            P.barrier()

    def post_norm_residual(st, l, s, ig, mT, t0, n, last_out, tag, xt=None):
        row = 2 if t0 < NCTX else s
        sq = [sb(tag + "sq%d" % i, [128, 512], F32, st) for i in range(2)]
        rstd = sb(tag + "rstd", [128, 512], F32, st)
        dstv = xT.t[s].rearrange("(c p) t -> p c t", p=128)
        if xt is None:
            xt = sb(tag + "xt", [128, 16, 512], F32, st)
            src = (xin[s] if (l == 0 and tag == "D") else xT.t[s]).rearrange("(c p) t -> p c t", p=128)
            for j in range(4):
                P.dma("sp", xt.t[:, 4 * j:4 * j + 4, 0:n], src[:, 4 * j:4 * j + 4, t0:t0 + n], r=[xT.b], w=[xt.b])
        ps = ps_next()
        for kc in range(16):
            q = sq[kc % 2]
            P.op("act", lambda e, q=q, kc=kc: e.activation(out=q.t[:, 0:n], in_=mT.t[:, kc, 0:n], func=AF.Square),
                 r=[mT.b], w=[q.b])
            P.op("pe", lambda e, q=q, kc=kc: e.matmul(ps.t[:, 0:n], lhsT=ones_f.t[:], rhs=q.t[:, 0:n],
                                                      start=(kc == 0), stop=(kc == 15)), r=[q.b], w=[ps.b])
        P.op("act", lambda e: e.activation(out=rstd.t[:, 0:n], in_=ps.t[:, 0:n], func=AF.Sqrt, bias=epsc.t[:, 0:1],
                                           scale=1.0 / D), r=[ps.b, epsc.b], w=[rstd.b])
        P.op("dve", lambda e: e.reciprocal(out=rstd.t[:, 0:n], in_=rstd.t[:, 0:n]), r=[rstd.b], w=[rstd.b])
        for kc in range(16):
            P.op("pool", lambda e, kc=kc: e.tensor_tensor(out=mT.t[:, kc, 0:n], in0=mT.t[:, kc, 0:n], in1=rstd.t[:, 0:n],
                                                          op=ALU.mult), r=[mT.b, rstd.b], w=[mT.b])
            P.op("dve", lambda e, kc=kc: e.scalar_tensor_tensor(
                out=xt.t[:, kc, 0:n], in0=mT.t[:, kc, 0:n], scalar=coef.t[:, ig, kc, row:row + 1], in1=xt.t[:, kc, 0:n],
                op0=ALU.mult, op1=ALU.add), r=[mT.b, coef.b, xt.b], w=[xt.b])
        for j in range(4):
            P.dma("sp", dstv[:, 4 * j:4 * j + 4, t0:t0 + n], xt.t[:, 4 * j:4 * j + 4, 0:n], r=[xt.b], w=[xT.b])
        if last_out and t0 >= NCTX:
            yv = yout[s].rearrange("(c p) t -> p c t", p=128)
            for j in range(4):
                P.dma("sp", yv[:, 4 * j:4 * j + 4, t0 - NCTX:t0 - NCTX + n], xt.t[:, 4 * j:4 * j + 4, 0:n], r=[xt.b])

    def stage_D(l, s, tgs):
        gate_v = s_fm.t[OFF_GATE * 128:(OFF_GATE + 48) * 128, :].rearrange("(i c p) t -> p i c t", p=128, i=3)
        for (t0, n) in tgs:
            with ExitStack() as st:
                oT = sb("oT", [128, 3, 8, 512], BF16, st)
                yT = sb("yT", [128, 16, 512], BF16, st)
                mT = sb("mT", [128, 16, 512], F32, st)
                gtb = [sb("gt%d" % i, [128, 4, 512], BF16, st) for i in range(3)]
                wbb = [sb("wbb%d" % i, [128, 8, 512], BF16, st) for i in range(3)]
                acc = sb("accD", [128, 4, 512], F32, st)
                tm2 = [sb("tm2%d" % i, [128, 512], F32, st) for i in range(2)]
                ov = s_o.t.rearrange("i (c p) t -> p i c t", p=128)
                for i in range(3):
                    for j in range(2):
                        P.dma("sp", oT.t[:, i, 4 * j:4 * j + 4, 0:n], ov[:, i, 4 * j:4 * j + 4, t0:t0 + n], r=[s_o.b], w=[oT.b])
                wi = 0
                loaders = []
                for ob in range(4):
                    for i in range(3):
                        loaders.append(lambda wb, i=i, ob=ob: P.dma("pool", wb.t[:].rearrange("p a b -> p (a b)"), w_branch16.t[i, ob],
                                                                    r=[w_branch16.b], w=[wb.b]))
                for ob in range(4):
                    for kh in range(2):
                        loaders.append(lambda wb, kh=kh, ob=ob: P.dma("pool", wb.t[:].rearrange("p a b -> p (a b)"), w_out16.t[ob, kh],
                                                                      r=[w_out16.b], w=[wb.b]))
                pf = Prefetch(wbb, loaders, 2)
                for ob in range(4):
                    for i in range(3):
                        wb = pf.get(wi)
                        g = gtb[wi % 3]
                        wi += 1
                        P.dma("sp", g.t[:, :, 0:n], gate_v[:, i, 4 * ob:4 * ob + 4, t0:t0 + n], r=[s_fm.b], w=[g.b])
                        for oc in range(4):
                            pa = ps_next()

                            def mm(e, pa=pa, wb=wb, i=i, oc=oc):
                                for kc in range(8):
                                    ins = e.matmul(pa.t[:, 0:n], lhsT=wb.t[:, kc, oc * 128:(oc + 1) * 128], rhs=oT.t[:, i, kc, 0:n],
                                                   start=(kc == 0), stop=(kc == 7))
                                return ins
                            P.op("pe", mm, r=[wb.b, oT.b], w=[pa.b])
                            if i == 0:
                                P.op("dve", lambda e, pa=pa, g=g, oc=oc: e.tensor_tensor(
                                    out=acc.t[:, oc, 0:n], in0=pa.t[:, 0:n], in1=g.t[:, oc, 0:n], op=ALU.mult),
                                    r=[pa.b, g.b], w=[acc.b])
                            else:
                                t2 = tm2[oc % 2]
                                P.op("dve", lambda e, pa=pa, t2=t2, g=g, oc=oc: e.tensor_tensor(
                                    out=t2.t[:, 0:n], in0=pa.t[:, 0:n], in1=g.t[:, oc, 0:n], op=ALU.mult),
                                    r=[pa.b, g.b], w=[t2.b])
                                if i == 1:
                                    P.op("dve", lambda e, t2=t2, oc=oc: e.tensor_tensor(
                                        out=acc.t[:, oc, 0:n], in0=acc.t[:, oc, 0:n], in1=t2.t[:, 0:n], op=ALU.add),
                                        r=[acc.b, t2.b], w=[acc.b])
                                else:
                                    P.op("dve", lambda e, t2=t2, oc=oc, ob=ob: e.tensor_tensor(
                                        out=yT.t[:, 4 * ob + oc, 0:n], in0=acc.t[:, oc, 0:n], in1=t2.t[:, 0:n], op=ALU.add),
                                        r=[acc.b, t2.b], w=[yT.b])
                pacc = PSPool([0, 1, 2, 3])
                for ob in range(4):
                    pas = [pacc.next() for _ in range(4)]
                    for kh in range(2):
                        wb = pf.get(wi)
                        wi += 1
                        for oc in range(4):
                            pa = pas[oc]

                            def mm2(e, pa=pa, wb=wb, oc=oc, kh=kh):
                                for kc in range(8):
                                    ins = e.matmul(pa.t[:, 0:n], lhsT=wb.t[:, kc, oc * 128:(oc + 1) * 128], rhs=yT.t[:, 8 * kh + kc, 0:n],
                                                   start=(kh == 0 and kc == 0), stop=(kh == 1 and kc == 7))
                                return ins
                            P.op("pe", mm2, r=[wb.b, yT.b], w=[pa.b])
                    for oc in range(4):
                        P.op("act", lambda e, pa=pas[oc], oc=oc, ob=ob: e.activation(out=mT.t[:, 4 * ob + oc, 0:n], in_=pa.t[:, 0:n],
                                                                                    func=AF.Copy), r=[pas[oc].b], w=[mT.b])
                post_norm_residual(st, l, s, 2, mT, t0, n, False, "D")
                P.barrier()

    def stage_E(l, s, tgs):
        last = (l == L - 1)
        for (t0, n) in tgs:
            with ExitStack() as st:
                aT = sb("aT", [128, 64, 512], BF16, st)
                xt = sb("xtE", [128, 16, 512], F32, st)
                with ExitStack() as st2:
                    hT = sb("hE", [128, 16, 512], BF16, st2)
                    modulate_tg(st2, xT.t[s], s, 3, 4, hT, t0, n, [xt])
                    wub = [sb("wub%d" % i, [128, 16, 512], BF16, st2) for i in range(3)]
                    rl = [sb("rl%d" % i, [128, 512], F32, st2) for i in range(2)]
                    pfu = Prefetch(wub, [(lambda wb, blk=blk: P.dma("pool", wb.t[:].rearrange("p a b -> p (a b)"), w_up16.t[blk],
                                                                    r=[w_up16.b], w=[wb.b])) for blk in range(16)], 2)
                    for blk in range(16):
                        wb = pfu.get(blk)
                        for cc in range(4):
                            fc = blk * 4 + cc
                            pa = ps_next()

                            def mm(e, pa=pa, wb=wb, cc=cc):
                                for kc in range(16):
                                    ins = e.matmul(pa.t[:, 0:n], lhsT=wb.t[:, kc, cc * 128:(cc + 1) * 128],
                                                   rhs=hT.t[:, kc, 0:n], start=(kc == 0), stop=(kc == 15))
                                return ins
                            P.op("pe", mm, r=[wb.b, hT.b], w=[pa.b])
                            r_ = rl[fc % 2]
                            P.op("act", lambda e, pa=pa, r_=r_: e.activation(out=r_.t[:, 0:n], in_=pa.t[:, 0:n], func=AF.Relu),
                                 r=[pa.b], w=[r_.b])
                            P.op("dve", lambda e, r_=r_, fc=fc: e.tensor_tensor(out=aT.t[:, fc, 0:n], in0=r_.t[:, 0:n],
                                                                            in1=r_.t[:, 0:n], op=ALU.mult),
                                 r=[r_.b], w=[aT.b])
                    P.barrier()
                with ExitStack() as st2:
                    mT = sb("mE", [128, 16, 512], F32, st2)
                    wdb = [sb("wdb%d" % i, [128, 16, 512], BF16, st2) for i in range(3)]
                    pacc = PSPool([0, 1, 2, 3])
                    wi = 0
                    pfd = Prefetch(wdb, [(lambda wb, ob=ob, kq=kq: P.dma("pool", wb.t[:].rearrange("p a b -> p (a b)"), w_down16.t[ob, kq],
                                                                         r=[w_down16.b], w=[wb.b])) for ob in range(4) for kq in range(4)], 2)
                    for ob in range(4):
                        pas = [pacc.next() for _ in range(4)]
                        for kq in range(4):
                            wb = pfd.get(wi)
                            wi += 1
                            for oc in range(4):
                                pa = pas[oc]

                                def mm2(e, pa=pa, wb=wb, oc=oc, kq=kq):
                                    for kc in range(16):
                                        ins = e.matmul(pa.t[:, 0:n], lhsT=wb.t[:, kc, oc * 128:(oc + 1) * 128], rhs=aT.t[:, 16 * kq + kc, 0:n],
                                                       start=(kq == 0 and kc == 0), stop=(kq == 3 and kc == 15))
                                    return ins
                                P.op("pe", mm2, r=[wb.b, aT.b], w=[pa.b])
                        for oc in range(4):
                            P.op("act", lambda e, pa=pas[oc], oc=oc, ob=ob: e.activation(out=mT.t[:, 4 * ob + oc, 0:n], in_=pa.t[:, 0:n],
                                                                                        func=AF.Copy), r=[pas[oc].b], w=[mT.b])
                    post_norm_residual(st2, l, s, 5, mT, t0, n, last, "E", xt)
                    P.barrier()

    def modulate_tg(st, src, s, ia, ib, hT, t0, n, xt=None):
        class V:
            pass
        hv = TB(None)
        hv.b = hT.b

        class _T:
            def __getitem__(self, key):
                p, kc, sl = key
                return hT.t[p, kc, sl.start - t0:sl.stop - t0]
        hv.t = _T()
        modulate(st, src, s, ia, ib, hv, [(t0, n)], xt)


    SCALE = 1.0 / math.sqrt(HD)

    def stage_C1(l, s, with_ctx):
        with ExitStack() as st:
            qT = sb("qaT", [128, 8, T], BF16, st)
            kT = sb("kaT", [128, 2, T], BF16, st)
            va = sb("vaS", [128, 18, 256], BF16, st)
            oT = sb("oaT", [128, 8, T], BF16, st)
            esk = sb("esk", [128, 8], F32, st)
            eskB = sb("eskB", [128, 8, 128], F32, st)
            mk = sb("mkA", [128, 2, 4, 128], BF16, st)
            pTs = [sb("pTa%d" % i, [128, 512], BF16, st) for i in range(3)]
            den = sb("denA", [128, 512], F32, st)
            fmv = s_fm.t.rearrange("(c p) t -> p c t", p=128)
            for h in range(8):
                P.dma("sp", qT.t[:, h, :], fmv[:, OFF_QA + h, :], r=[s_fm.b], w=[qT.b])
            for h in range(2):
                P.dma("sp", kT.t[:, h, :], fmv[:, OFF_KA + h, :], r=[s_fm.b], w=[kT.b])
            vav = s_va.t.rearrange("(tt p) c -> p tt c", p=128)
            for j in range(3):
                P.dma("sp", va.t[:, 6 * j:6 * j + 6, :], vav[:, 6 * j:6 * j + 6, :], r=[s_va.b], w=[va.b])
            P.dma("sp", esk.t[:], sink[l:l + 1, :].partition_broadcast(128), w=[esk.b])
            P.op("act", lambda e: e.activation(out=esk.t[:], in_=esk.t[:], func=AF.Exp), r=[esk.b], w=[esk.b])
            P.op("dve", lambda e: e.tensor_copy(out=eskB.t[:], in_=esk.t[:].unsqueeze(2).broadcast_to([128, 8, 128])),
                 r=[esk.b], w=[eskB.b])
            for j in range(2):
                P.op("dve", lambda e, j=j: e.tensor_copy(out=mk.t[:, j], in_=cm.t[:, 1 + j, :].unsqueeze(1).broadcast_to([128, 4, 128])),
                     r=[cm.b], w=[mk.b])
            qblocks = ([(128 * j, None) for j in range(2)] if with_ctx else []) + [(256 + 128 * i, i) for i in range(16)]
            pacc = PSPool([0, 1, 2, 3])
            pss = PSPool([4, 5, 6, 7])
            pi = 0
            for g in range(2):
                for (t0, xi) in qblocks:
                    keys = [(0, None), (128, None)]
                    if xi is not None:
                        if xi > 0:
                            keys.append((256 + 128 * (xi - 1), 0))
                        keys.append((256 + 128 * xi, None))
                        if xi < 15:
                            keys.append((256 + 128 * (xi + 1), 1))
                    ps_o = pacc.next()
                    ps_d = pacc.next()
                    for idx, (k0, mki) in enumerate(keys):
                        ps_s = pss.next()
                        pT = pTs[pi % 3]
                        pi += 1
                        P.op("pe", lambda e, ps_s=ps_s, k0=k0: e.matmul(
                            ps_s.t[:, 0:512].rearrange("p (h q) -> p h q", h=4), lhsT=kT.t[:, g, k0:k0 + 128],
                            rhs=qT.t[:, 4 * g:4 * g + 4, t0:t0 + 128], start=True, stop=True), r=[kT.b, qT.b], w=[ps_s.b])
                        P.op("act", lambda e, ps_s=ps_s, pT=pT: e.activation(out=pT.t[:], in_=ps_s.t[:], func=AF.Exp, scale=SCALE),
                             r=[ps_s.b], w=[pT.b])
                        if mki is not None:
                            P.op("pool", lambda e, pT=pT, mki=mki: e.tensor_tensor(
                                out=pT.t[:], in0=pT.t[:], in1=mk.t[:, mki].rearrange("p h q -> p (h q)"), op=ALU.mult),
                                r=[pT.b, mk.b], w=[pT.b])
                        first, lastk = idx == 0, idx == len(keys) - 1
                        P.op("pe", lambda e, pT=pT: e.matmul(ps_d.t[:], lhsT=ones_b.t[:], rhs=pT.t[:], start=first, stop=lastk),
                             r=[pT.b, ones_b.b], w=[ps_d.b])
                        P.op("pe", lambda e, pT=pT, k0=k0: e.matmul(ps_o.t[:], lhsT=va.t[:, k0 // 128, g * 128:(g + 1) * 128],
                                                                    rhs=pT.t[:], start=first, stop=lastk),
                             r=[pT.b, va.b], w=[ps_o.b])
                    P.op("dve", lambda e: e.tensor_tensor(out=den.t[:], in0=ps_d.t[:],
                                                          in1=eskB.t[:, 4 * g:4 * g + 4, :].rearrange("p h q -> p (h q)"), op=ALU.add),
                         r=[ps_d.b, eskB.b], w=[den.b])
                    P.op("dve", lambda e: e.reciprocal(out=den.t[:], in_=den.t[:]), r=[den.b], w=[den.b])
                    P.op("dve", lambda e: e.tensor_tensor(out=oT.t[:, 4 * g:4 * g + 4, t0:t0 + 128],
                                                          in0=ps_o.t[:].rearrange("p (h q) -> p h q", h=4),
                                                          in1=den.t[:].rearrange("p (h q) -> p h q", h=4), op=ALU.mult),
                         r=[ps_o.b, den.b], w=[oT.b])
            ov = s_o.t[0].rearrange("(c p) t -> p c t", p=128)
            for h in range(8):
                P.dma("sp", ov[:, h, :], oT.t[:, h, :], r=[oT.b], w=[s_o.b])
            P.barrier()

    rpbpad = dscratch("rpbpad", [120, 127], F32)
    dbgC = dscratch("dbgC", [64, 7680], F32)

    def na_valid(r, kr):
        rs = min(max(r - 4, 0), 24)
        return rs <= kr <= rs + 7

    def stage_C2(l, s, with_ctx):
        with ExitStack() as st:
            Ctab = sb("Ctab", [64, 8, 15, 64], F32, st)
            zt = sb("zt", [120, 127], F32, st)
            P.op("dve", lambda e: e.memset(zt.t[:], 0.0), w=[zt.b])
            P.dma("sp", rpbpad.t[:, :], zt.t[:], r=[zt.b], w=[rpbpad.b])
            P.dma("sp", rpbpad.t[:, 48:79], rpb[l].rearrange("(r c) -> r c", c=31), w=[rpbpad.b])
            for h in range(8):
                src = bass.AP(tensor=rpbpad.t.tensor, offset=rpbpad.t.offset + h * 15 * 127, ap=[[1, 64], [127, 15], [1, 64]])
                P.dma("sp", Ctab.t[:, h], src, r=[rpbpad.b], w=[Ctab.b])
            P.op("dve", lambda e: e.tensor_tensor(
                out=Ctab.t[:].rearrange("p h a q -> p (h a) q"), in0=Ctab.t[:].rearrange("p h a q -> p (h a) q"),
                in1=cm.t[0:64, 3, 0:64].unsqueeze(1).broadcast_to([64, 120, 64]), op=ALU.add), r=[Ctab.b, cm.b], w=[Ctab.b])
            if cfg.debug:
                for h in range(8):
                    P.dma("sp", dbgC.t[:, h * 960:(h + 1) * 960], Ctab.t[:, h].rearrange("p a q -> p (a q)"), r=[Ctab.b], w=[dbgC.b])
            ctab_ap = Ctab.t[:]
            pstep = ctab_ap.ap[0][0]
            qh = [sb("qnh%d" % i, [128, T], BF16, st) for i in range(2)]
            kh = [sb("knh%d" % i, [128, T], BF16, st) for i in range(2)]
            v64 = [sb("v64_%d" % i, [128, 36, 128], BF16, st) for i in range(2)]
            pTr = [sb("pTr%d" % i, [128, 512], BF16, st) for i in range(3)]
            for t_ in v64 + pTr:
                P.op("pool", lambda e, t_=t_: e.memset(t_.t[:], 0.0), w=[t_.b])
            vcx = [sb("vcx%d" % i, [128, 2, 128], BF16, st) for i in range(2)]
            oh = [sb("onh%d" % i, [128, T], BF16, st) for i in range(2)]
            pTs = [sb("pTn%d" % i, [128, 512], BF16, st) for i in range(3)]
            sbs = [sb("sbn%d" % i, [64, 512], F32, st) for i in range(2)]
            den = sb("denN", [128, 512], F32, st)
            fmv = s_fm.t.rearrange("(c p) t -> p c t", p=128)
            v64v = s_vn.t.rearrange("(c p) d -> p c d", p=64)
            vcv = s_vn.t[0:256, :].rearrange("(c p) d -> p c d", p=128)
            ov = s_o.t[2].rearrange("(c p) t -> p c t", p=128)
            pi = 0
            bi_ = 0
            pacc = PSPool([0, 1, 2, 3])
            pss = PSPool([4, 5, 6, 7])
            for h in range(8):
                q_, k_, v_, vc_, o_ = qh[h % 2], kh[h % 2], v64[h % 2], vcx[h % 2], oh[h % 2]
                P.dma("sp", q_.t[:], fmv[:, OFF_QN + h, :], r=[s_fm.b], w=[q_.b])
                P.dma("sp", k_.t[:], fmv[:, OFF_KN + h, :], r=[s_fm.b], w=[k_.b])
                for j in range(3):
                    P.dma("sp", v_.t[0:64, 12 * j:12 * j + 12, :], v64v[:, 12 * j:12 * j + 12, h * 128:(h + 1) * 128],
                          r=[s_vn.b], w=[v_.b])
                P.dma("sp", vc_.t[:], vcv[:, :, h * 128:(h + 1) * 128], r=[s_vn.b], w=[vc_.b])
                groups = ([("c", 0, 256)] if with_ctx else []) + [("x", 256 + 512 * G, 512) for G in range(4)]
                for (gk, t0, n) in groups:
                    ps_o = pacc.next()
                    ps_d = pacc.next()
                    items = [("ctx", 0)]
                    if gk == "x":
                        G = (t0 - 256) // 512
                        for kr in range(32):
                            rr = [r for r in range(8 * G, 8 * G + 8) if na_valid(r, kr)]
                            if rr:
                                items.append(("row", kr, rr[0], rr[-1]))
                    items.append(("ctx", 1))
                    for it in items:
                        first, lastk = it is items[0], it is items[-1]
                        pT = pTs[pi % 3]
                        pi += 1
                        ps_s = pss.next()
                        if it[0] == "ctx":
                            cb = it[1]
                            P.op("pe", lambda e: e.matmul(ps_s.t[:, 0:n], lhsT=k_.t[:, cb * 128:(cb + 1) * 128], rhs=q_.t[:, t0:t0 + n],
                                                          start=True, stop=True), r=[k_.b, q_.b], w=[ps_s.b])
                            P.op("act", lambda e: e.activation(out=pT.t[:, 0:n], in_=ps_s.t[:, 0:n], func=AF.Exp, scale=SCALE),
                                 r=[ps_s.b], w=[pT.b])
                            P.op("pe", lambda e: e.matmul(ps_d.t[:, 0:n], lhsT=ones_b.t[:], rhs=pT.t[:, 0:n], start=first, stop=lastk),
                                 r=[pT.b, ones_b.b], w=[ps_d.b])
                            P.op("pe", lambda e: e.matmul(ps_o.t[:, 0:n], lhsT=vc_.t[:, cb, :], rhs=pT.t[:, 0:n], start=first, stop=lastk),
                                 r=[pT.b, vc_.b], w=[ps_o.b])
                        else:
                            _, kr, rlo, rhi = it
                            pT = pTr[pi % 3]
                            nr = rhi - rlo + 1
                            c0 = (rlo - 8 * G) * 64
                            nc_ = nr * 64
                            kt0 = 256 + 64 * kr
                            sbt = sbs[bi_ % 2]
                            bi_ += 1
                            P.op("pe", lambda e: e.matmul(ps_s.t[0:64, 0:nc_], lhsT=k_.t[:, kt0:kt0 + 64],
                                                          rhs=q_.t[:, t0 + c0:t0 + c0 + nc_], start=True, stop=True),
                                 r=[k_.b, q_.b], w=[ps_s.b])
                            a_start = 7 + kr - rlo
                            bias = bass.AP(tensor=ctab_ap.tensor, offset=ctab_ap.offset + (h * 15 + a_start) * 64 + 63,
                                           ap=[[pstep, 64], [-64, nr], [-1, 64]])
                            P.op("dve", lambda e: e.scalar_tensor_tensor(
                                out=sbt.t[:, 0:nc_].rearrange("p (r q) -> p r q", q=64),
                                in0=ps_s.t[0:64, 0:nc_].rearrange("p (r q) -> p r q", q=64), scalar=SCALE, in1=bias,
                                op0=ALU.mult, op1=ALU.add), r=[ps_s.b, Ctab.b], w=[sbt.b])
                            P.op("act", lambda e: e.activation(out=pT.t[0:64, 0:nc_], in_=sbt.t[:, 0:nc_], func=AF.Exp),
                                 r=[sbt.b], w=[pT.b])
                            P.op("pe", lambda e: e.matmul(ps_d.t[:, c0:c0 + nc_], lhsT=ones_b.t[:, :], rhs=pT.t[:, 0:nc_],
                                                          start=False, stop=False), r=[pT.b, ones_b.b], w=[ps_d.b])
                            P.op("pe", lambda e: e.matmul(ps_o.t[:, c0:c0 + nc_], lhsT=v_.t[:, 4 + kr, :], rhs=pT.t[:, 0:nc_],
                                                          start=False, stop=False), r=[pT.b, v_.b], w=[ps_o.b])
                    P.op("dve", lambda e: e.reciprocal(out=den.t[:, 0:n], in_=ps_d.t[:, 0:n]), r=[ps_d.b], w=[den.b])
                    P.op("dve", lambda e: e.tensor_tensor(out=o_.t[:, t0:t0 + n], in0=ps_o.t[:, 0:n], in1=den.t[:, 0:n], op=ALU.mult),
                         r=[ps_o.b, den.b], w=[o_.b])
                P.dma("sp", ov[:, h, :], o_.t[:], r=[o_.b], w=[s_o.b])
            P.barrier()


    NT_ = T // 128
    TGRP = [(0, 4), (4, 4), (8, 4), (12, 4), (16, 2)]

    def stage_C3(l, s):
        with ExitStack() as st:
            abT = sb("abT", [128, NT_, 32], F32, st)
            gG = sb("gG", [128, NT_, 16], F32, st)
            bG = sb("bG", [128, NT_, 16], F32, st)
            nA = sb("nA", [128, 16], F32, st)
            dtb = sb("dtb", [128, 16], F32, st)
            cwT = sb("cwT", [128, 5, 24], F32, st)
            gg = sb("ggdn", [128, 1], F32, st)
            abv = s_ab.t.rearrange("(tt p) j -> p tt j", p=128)
            for j in range(3):
                P.dma("sp", abT.t[:, 6 * j:6 * j + 6, :], abv[:, 6 * j:6 * j + 6, :], r=[s_ab.b], w=[abT.b])
            P.dma("sp", nA.t[:], a_log[l:l + 1, :].partition_broadcast(128), w=[nA.b])
            P.dma("sp", dtb.t[:], dt_bias[l:l + 1, :].partition_broadcast(128), w=[dtb.b])
            P.dma("sp", gg.t[:], g_gdn[l].rearrange("(p o) -> p o", o=1), w=[gg.b])
            P.op("act", lambda e: e.activation(out=nA.t[:], in_=nA.t[:], func=AF.Exp), r=[nA.b], w=[nA.b])
            P.op("dve", lambda e: e.tensor_scalar(out=nA.t[:], in0=nA.t[:], scalar1=-1.0, scalar2=None, op0=ALU.mult),
                 r=[nA.b], w=[nA.b])
            P.op("dve", lambda e: e.tensor_tensor(out=gG.t[:], in0=abT.t[:, :, 0:16],
                                                  in1=dtb.t[:].unsqueeze(1).broadcast_to([128, NT_, 16]), op=ALU.add),
                 r=[abT.b, dtb.b], w=[gG.b])
            P.op("act", lambda e: e.activation(out=gG.t[:], in_=gG.t[:], func=AF.Exp), r=[gG.b], w=[gG.b])
            P.op("act", lambda e: e.activation(out=gG.t[:], in_=gG.t[:], func=AF.Ln, bias=ones_f.t[:, 0:1], scale=1.0),
                 r=[gG.b, ones_f.b], w=[gG.b])
            P.op("dve", lambda e: e.tensor_tensor(out=gG.t[:], in0=gG.t[:],
                                                  in1=nA.t[:].unsqueeze(1).broadcast_to([128, NT_, 16]), op=ALU.mult),
                 r=[gG.b, nA.b], w=[gG.b])
            P.op("act", lambda e: e.activation(out=bG.t[:], in_=abT.t[:, :, 16:32], func=AF.Sigmoid), r=[abT.b], w=[bG.b])
            with ExitStack() as st2:
                tmpc = sb("cwtmp", [128, 128], F32, st2)
                P.dma("sp", tmpc.t[0:120, :], conv_w[l].rearrange("j (c p) -> (j c) p", p=128), w=[tmpc.b])
                pst = ps_next()
                P.op("pe", lambda e: e.transpose(out=pst.t[:, 0:120], in_=tmpc.t[0:120, :], identity=cm.t[0:120, 0, 0:120]),
                     r=[tmpc.b, cm.b], w=[pst.b])
                P.op("dve", lambda e: e.tensor_copy(out=cwT.t[:].rearrange("p j c -> p (j c)"), in_=pst.t[:, 0:120]),
                     r=[pst.b], w=[cwT.b])
                P.barrier()
            fmv = s_fm.t.rearrange("(c p) t -> p c t", p=128)
            ov = s_o.t[1].rearrange("(c p) t -> p c t", p=128)
            SEGS = [(0, NCTX), (NCTX, T)]
            idb = cm.t[:, 0, :]
            for h in range(8):
                with ExitStack() as sh:
                    qT = sb("gq", [128, T], F32, sh)
                    kT = sb("gk", [128, T], F32, sh)
                    k_tm = sb("gktm", [128, NT_, 128], F32, sh)
                    v_tm = sb("gvtm", [128, NT_, 128], F32, sh)
                    KK = sb("gKK", [128, NT_, 128], F32, sh)
                    QKT = sb("gQKT", [128, NT_, 128], F32, sh)
                    oacc = sb("goacc", [128, T], F32, sh)
                    with ExitStack() as s1:
                        raw = [sb("graw%d" % i, [128, T], BF16, s1) for i in range(3)]
                        acc = [sb("gacc%d" % i, [128, T], F32, s1) for i in range(2)]
                        vT = sb("gv", [128, T], F32, s1)
                        sqb = sb("gsq", [128, 512], F32, s1)
                        rnb = sb("grn", [128, 512], F32, s1)
                        for ci_, which in enumerate(("q", "k", "v")):
                            cidx = ci_ * 8 + h
                            rw = raw[ci_]
                            a_ = acc[ci_ % 2]
                            eng = "dve"
                            P.dma("sp", rw.t[:], fmv[:, OFF_QKVB + cidx, :], r=[s_fm.b], w=[rw.b])
                            P.op(eng, lambda e: e.tensor_scalar(out=a_.t[:], in0=rw.t[:], scalar1=cwT.t[:, 2, cidx:cidx + 1], scalar2=None,
                                                                op0=ALU.mult), r=[rw.b, cwT.b], w=[a_.b])
                            for j in (0, 1, 3, 4):
                                d_ = j - 2
                                for (s0, s1_) in SEGS:
                                    lo, hi = max(s0, s0 - d_), min(s1_, s1_ - d_)
                                    P.op(eng, lambda e: e.scalar_tensor_tensor(
                                        out=a_.t[:, lo:hi], in0=rw.t[:, lo + d_:hi + d_], scalar=cwT.t[:, j, cidx:cidx + 1],
                                        in1=a_.t[:, lo:hi], op0=ALU.mult, op1=ALU.add), r=[rw.b, cwT.b, a_.b], w=[a_.b])
                            dst = {"q": qT, "k": kT, "v": vT}[which]
                            if which == "v":
                                P.op("act", lambda e: e.activation(out=dst.t[:], in_=a_.t[:], func=AF.Silu), r=[a_.b], w=[dst.b])
                            else:
                                P.op("act", lambda e: e.activation(out=a_.t[:], in_=a_.t[:], func=AF.Silu), r=[a_.b], w=[a_.b])
                                for (t0, n) in TGS:
                                    P.op("pool", lambda e: e.tensor_tensor(out=sqb.t[:, 0:n], in0=a_.t[:, t0:t0 + n], in1=a_.t[:, t0:t0 + n],
                                                                          op=ALU.mult), r=[a_.b], w=[sqb.b])
                                    pq = ps_next()
                                    P.op("pe", lambda e: e.matmul(pq.t[:, 0:n], lhsT=ones_f.t[:], rhs=sqb.t[:, 0:n], start=True, stop=True),
                                         r=[sqb.b, ones_f.b], w=[pq.b])
                                    P.op("act", lambda e: e.activation(out=rnb.t[:, 0:n], in_=pq.t[:, 0:n], func=AF.Sqrt,
                                                                       bias=epsc.t[:, 0:1], scale=1.0), r=[pq.b, epsc.b], w=[rnb.b])
                                    P.op("dve", lambda e: e.reciprocal(out=rnb.t[:, 0:n], in_=rnb.t[:, 0:n]), r=[rnb.b], w=[rnb.b])
                                    if which == "q":
                                        P.op("dve", lambda e: e.scalar_tensor_tensor(
                                            out=dst.t[:, t0:t0 + n], in0=a_.t[:, t0:t0 + n], scalar=SCALE, in1=rnb.t[:, 0:n],
                                            op0=ALU.mult, op1=ALU.mult), r=[a_.b, rnb.b], w=[dst.b])
                                    else:
                                        P.op("dve", lambda e: e.tensor_tensor(out=dst.t[:, t0:t0 + n], in0=a_.t[:, t0:t0 + n],
                                                                              in1=rnb.t[:, 0:n], op=ALU.mult), r=[a_.b, rnb.b], w=[dst.b])
                        ei = 0
                        for (g0, gn) in TGRP:
                            for (src_, dst_) in ((kT, k_tm), (vT, v_tm)):
                                pt = ps_next()
                                for ti in range(gn):
                                    tt = g0 + ti
                                    P.op("pe", lambda e: e.transpose(out=pt.t[:, ti * 128:(ti + 1) * 128], in_=src_.t[:, tt * 128:(tt + 1) * 128],
                                                                     identity=idb), r=[src_.b, cm.b], w=[pt.b], inc=(ti == gn - 1))
                                ei += 1
                                if ei % 2:
                                    P.op("act", lambda e: e.activation(out=dst_.t[:, g0:g0 + gn, :].rearrange("p a b -> p (a b)"),
                                                                       in_=pt.t[:, 0:gn * 128], func=AF.Copy), r=[pt.b], w=[dst_.b])
                                else:
                                    P.op("dve", lambda e: e.tensor_copy(out=dst_.t[:, g0:g0 + gn, :].rearrange("p a b -> p (a b)"),
                                                                        in_=pt.t[:, 0:gn * 128]), r=[pt.b], w=[dst_.b])
                            for (rhs_, dst_) in ((kT, KK), (qT, QKT)):
                                pt = ps_next()
                                for ti in range(gn):
                                    tt = g0 + ti
                                    P.op("pe", lambda e: e.matmul(pt.t[:, ti * 128:(ti + 1) * 128], lhsT=kT.t[:, tt * 128:(tt + 1) * 128],
                                                                  rhs=rhs_.t[:, tt * 128:(tt + 1) * 128], start=True, stop=True),
                                         r=[kT.b, rhs_.b], w=[pt.b], inc=(ti == gn - 1))
                                ei += 1
                                if ei % 2:
                                    P.op("act", lambda e: e.activation(out=dst_.t[:, g0:g0 + gn, :].rearrange("p a b -> p (a b)"),
                                                                       in_=pt.t[:, 0:gn * 128], func=AF.Copy), r=[pt.b], w=[dst_.b])
                                else:
                                    P.op("dve", lambda e: e.tensor_copy(out=dst_.t[:, g0:g0 + gn, :].rearrange("p a b -> p (a b)"),
                                                                        in_=pt.t[:, 0:gn * 128]), r=[pt.b], w=[dst_.b])
                        P.barrier()
                    dirs = []
                    for di in range(2):
                        iU, iML, iMU, iSL = (4, 6, 7, 8) if di == 0 else (5, 7, 6, 9)
                        gcol = di * 8 + h
                        wT = sb("gwT%d" % di, [128, T], F32, sh)
                        qgT = sb("gqg%d" % di, [128, T], BF16, sh)
                        kd = sb("gkd%d" % di, [128, NT_, 128], BF16, sh)
                        attnT = sb("gat%d" % di, [128, NT_, 128], BF16, sh)
                        u_ = sb("gu%d" % di, [128, NT_, 128], F32, sh)
                        egl = sb("gegl%d" % di, [128, NT_, 2], F32, sh)
                        dirs.append((wT, qgT, kd, attnT, u_, egl))
                        with ExitStack() as s2:
                            gc = sb("ggc", [128, NT_], F32, s2)
                            egc = sb("gegc", [128, NT_], F32, s2)
                            ekd = sb("gekd", [128, NT_], F32, s2)
                            bsc = sb("gbsc", [128, NT_], F32, s2)
                            ghd = sb("gghd", [128, NT_], F32, s2)
                            bhd = sb("gbhd", [128, NT_], F32, s2)
                            kbg = sb("gkbg", [128, NT_, 128], F32, s2)
                            vb = sb("gvb", [128, NT_, 128], F32, s2)
                            Xs = sb("gX", [128, NT_, 128], F32, s2)
                            GU = sb("gGU", [128, 512], F32, s2)
                            Gb = sb("gGb", [128, 512], F32, s2)
                            EL = sb("gEL", [128, 512], F32, s2)
                            EU = sb("gEU", [128, 512], F32, s2)
                            Lf = sb("gLf", [128, 512], F32, s2)
                            Dg = sb("gDg", [128, 512], F32, s2)
                            nb = [[sb("gN%d_%d" % (a, b_), [128, 512], F32, s2) for b_ in range(2)] for a in range(2)]
                            Xb = [sb("gXb%d" % i, [128, 512], F32, s2) for i in range(2)]
                            P.op("dve", lambda e: e.tensor_copy(out=ghd.t[:], in_=gG.t[:, :, gcol]), r=[gG.b], w=[ghd.b])
                            P.op("dve", lambda e: e.tensor_copy(out=bhd.t[:], in_=bG.t[:, :, gcol]), r=[bG.b], w=[bhd.b])
                            pg = ps_next()
                            P.op("pe", lambda e: e.matmul(pg.t[:, 0:NT_], lhsT=cm.t[:, iU, :], rhs=ghd.t[:], start=True, stop=True),
                                 r=[cm.b, ghd.b], w=[pg.b], inc=False)
                            P.op("pe", lambda e: e.matmul(pg.t[:, 32:32 + NT_], lhsT=cm.t[:, 10, :], rhs=ghd.t[:], start=True, stop=True),
                                 r=[cm.b, ghd.b], w=[pg.b], inc=False)
                            P.op("pe", lambda e: e.matmul(pg.t[:, 64:64 + NT_], lhsT=cm.t[:, 11, :], rhs=ghd.t[:], start=True, stop=True),
                                 r=[cm.b, ghd.b], w=[pg.b], inc=False)
                            P.op("pe", lambda e: e.matmul(pg.t[:, 96:96 + NT_], lhsT=cm.t[:, 12, :], rhs=ghd.t[:], start=True, stop=True),
                                 r=[cm.b, ghd.b], w=[pg.b])
                            P.op("dve", lambda e: e.tensor_copy(out=gc.t[:], in_=pg.t[:, 0:NT_]), r=[pg.b], w=[gc.b])
                            P.op("dve", lambda e: e.tensor_tensor(out=ekd.t[:], in0=pg.t[:, 32:32 + NT_], in1=gc.t[:], op=ALU.subtract),
                                 r=[pg.b, gc.b], w=[ekd.b])
                            P.op("dve", lambda e: e.tensor_copy(out=egl.t[:, :, 0], in_=pg.t[:, 64:64 + NT_]), r=[pg.b], w=[egl.b])
                            P.op("dve", lambda e: e.tensor_copy(out=egl.t[:, :, 1], in_=pg.t[:, 96:96 + NT_]), r=[pg.b], w=[egl.b])
                            P.op("act", lambda e: e.activation(out=egc.t[:], in_=gc.t[:], func=AF.Exp), r=[gc.b], w=[egc.b])
                            P.op("act", lambda e: e.activation(out=ekd.t[:], in_=ekd.t[:], func=AF.Exp), r=[ekd.b], w=[ekd.b])
                            P.op("act", lambda e: e.activation(out=egl.t[:], in_=egl.t[:], func=AF.Exp), r=[egl.b], w=[egl.b])
                            P.op("dve", lambda e: e.tensor_tensor(out=bsc.t[:], in0=bhd.t[:], in1=egc.t[:], op=ALU.mult),
                                 r=[bhd.b, egc.b], w=[bsc.b])
                            bc3 = lambda t_: t_.t[:].unsqueeze(2).broadcast_to([128, NT_, 128])
                            P.op("pool", lambda e: e.tensor_tensor(out=kbg.t[:], in0=k_tm.t[:], in1=bc3(bsc), op=ALU.mult),
                                 r=[k_tm.b, bsc.b], w=[kbg.b])
                            P.op("pool", lambda e: e.tensor_tensor(out=vb.t[:], in0=v_tm.t[:], in1=bc3(bhd), op=ALU.mult),
                                 r=[v_tm.b, bhd.b], w=[vb.b])
                            P.op("pool", lambda e: e.tensor_tensor(out=kd.t[:], in0=k_tm.t[:], in1=bc3(ekd), op=ALU.mult),
                                 r=[k_tm.b, ekd.b], w=[kd.b])
                            pacc = PSPool([0, 1, 2, 3, 4, 5, 6, 7])
                            for (g0, gn) in TGRP:
                                W_ = gn * 128
                                g3 = lambda t_: t_.t[:, 0:W_].rearrange("p (a b) -> p a b", b=128)
                                gsl = ghd.t[:, g0:g0 + gn].unsqueeze(2).broadcast_to([128, gn, 128])
                                cmb = lambda i_: cm.t[:, i_, :].unsqueeze(1).broadcast_to([128, gn, 128])
                                P.op("dve", lambda e: e.tensor_tensor(out=g3(GU), in0=cmb(iU), in1=gsl, op=ALU.mult),
                                     r=[cm.b, ghd.b], w=[GU.b])
                                P.op("pool", lambda e: e.tensor_copy(out=g3(Gb), in_=gsl), r=[ghd.b], w=[Gb.b])
                                pD = pacc.next()
                                P.op("pe", lambda e: e.matmul(pD.t[:, 0:W_], lhsT=cm.t[:, iU, :], rhs=Gb.t[:, 0:W_], start=True, stop=False),
                                     r=[cm.b, Gb.b], w=[pD.b], inc=False)
                                P.op("pe", lambda e: e.matmul(pD.t[:, 0:W_], lhsT=negones.t[:], rhs=GU.t[:, 0:W_], start=False, stop=True),
                                     r=[negones.b, GU.b], w=[pD.b])
                                P.op("dve", lambda e: e.tensor_tensor(out=g3(EL), in0=pD.t[:, 0:W_].rearrange("p (a b) -> p a b", b=128),
                                                                      in1=cmb(iML), op=ALU.add), r=[pD.b, cm.b], w=[EL.b])
                                P.op("dve", lambda e: e.scalar_tensor_tensor(
                                    out=g3(EU), in0=pD.t[:, 0:W_].rearrange("p (a b) -> p a b", b=128), scalar=-1.0, in1=cmb(iMU),
                                    op0=ALU.mult, op1=ALU.add), r=[pD.b, cm.b], w=[EU.b])
                                P.op("act", lambda e: e.activation(out=EL.t[:, 0:W_], in_=EL.t[:, 0:W_], func=AF.Exp), r=[EL.b], w=[EL.b])
                                P.op("act", lambda e: e.activation(out=EU.t[:, 0:W_], in_=EU.t[:, 0:W_], func=AF.Exp), r=[EU.b], w=[EU.b])
                                P.op("pool", lambda e: e.tensor_tensor(out=g3(Lf), in0=g3(EL), in1=KK.t[:, g0:g0 + gn, :], op=ALU.mult),
                                     r=[EL.b, KK.b], w=[Lf.b])
                                P.op("pool", lambda e: e.tensor_tensor(out=g3(Lf), in0=g3(Lf), in1=cmb(iSL), op=ALU.mult),
                                     r=[Lf.b, cm.b], w=[Lf.b])
                                P.op("pool", lambda e: e.tensor_tensor(
                                    out=g3(Lf), in0=g3(Lf), in1=bhd.t[:, g0:g0 + gn].unsqueeze(2).broadcast_to([128, gn, 128]), op=ALU.mult),
                                    r=[Lf.b, bhd.b], w=[Lf.b])
                                P.op("dve", lambda e: e.tensor_tensor(out=attnT.t[:, g0:g0 + gn, :], in0=g3(EU), in1=QKT.t[:, g0:g0 + gn, :],
                                                                      op=ALU.mult), r=[EU.b, QKT.b], w=[attnT.b])
                                NT0, N0 = nb[1][0], nb[0][0]
                                P.op("dve", lambda e: e.tensor_scalar(out=NT0.t[:, 0:W_], in0=Lf.t[:, 0:W_], scalar1=-1.0, scalar2=None,
                                                                      op0=ALU.mult), r=[Lf.b], w=[NT0.b])
                                pT_ = pacc.next()
                                for ti in range(gn):
                                    P.op("pe", lambda e: e.transpose(out=pT_.t[:, ti * 128:(ti + 1) * 128], in_=Lf.t[:, ti * 128:(ti + 1) * 128],
                                                                     identity=idb), r=[Lf.b, cm.b], w=[pT_.b], inc=(ti == gn - 1))
                                P.op("act", lambda e: e.activation(out=N0.t[:, 0:W_], in_=pT_.t[:, 0:W_], func=AF.Copy, scale=-1.0),
                                     r=[pT_.b], w=[N0.b])
                                Xc = Xb[0]
                                P.op("dve", lambda e: e.scalar_tensor_tensor(
                                    out=g3(Xc), in0=pT_.t[:, 0:W_].rearrange("p (a b) -> p a b", b=128), scalar=-1.0, in1=cmb(0),
                                    op0=ALU.mult, op1=ALU.add), r=[pT_.b, cm.b], w=[Xc.b])
                                Nc, NTc = N0, NT0
                                for k_ in range(5):
                                    par = (k_ + 1) % 2
                                    Nn, NTn = nb[0][par], nb[1][par]
                                    Xn = Xb[(k_ + 1) % 2]
                                    pN = pacc.next()
                                    pNT = pacc.next()
                                    if k_ < 4:
                                        for ti in range(gn):
                                            sl = slice(ti * 128, (ti + 1) * 128)
                                            P.op("pe", lambda e: e.matmul(pN.t[:, sl], lhsT=NTc.t[:, sl], rhs=Nc.t[:, sl], start=True, stop=True),
                                                 r=[NTc.b, Nc.b], w=[pN.b], inc=(ti == gn - 1))
                                    for ti in range(gn):
                                        sl = slice(ti * 128, (ti + 1) * 128)
                                        P.op("pe", lambda e: e.matmul(pNT.t[:, sl], lhsT=Nc.t[:, sl], rhs=NTc.t[:, sl], start=True, stop=True),
                                             r=[NTc.b, Nc.b], w=[pNT.b], inc=(ti == gn - 1))
                                    if k_ < 4:
                                        P.op("act", lambda e: e.activation(out=Nn.t[:, 0:W_], in_=pN.t[:, 0:W_], func=AF.Copy),
                                             r=[pN.b], w=[Nn.b])
                                    P.op("dve", lambda e: e.tensor_copy(out=NTn.t[:, 0:W_], in_=pNT.t[:, 0:W_]), r=[pNT.b], w=[NTn.b])
                                    pX = pacc.next()
                                    for ti in range(gn):
                                        sl = slice(ti * 128, (ti + 1) * 128)
                                        P.op("pe", lambda e: e.matmul(pX.t[:, sl], lhsT=NTn.t[:, sl], rhs=Xc.t[:, sl], start=True, stop=True),
                                             r=[NTn.b, Xc.b], w=[pX.b], inc=(ti == gn - 1))
                                    if k_ < 4:
                                        P.op("dve", lambda e: e.tensor_tensor(out=Xn.t[:, 0:W_], in0=pX.t[:, 0:W_], in1=Xc.t[:, 0:W_], op=ALU.add),
                                             r=[pX.b, Xc.b], w=[Xn.b])
                                    else:
                                        P.op("dve", lambda e: e.tensor_tensor(
                                            out=Xs.t[:, g0:g0 + gn, :].rearrange("p a b -> p (a b)"), in0=pX.t[:, 0:W_], in1=Xc.t[:, 0:W_],
                                            op=ALU.add), r=[pX.b, Xc.b], w=[Xs.b])
                                    Nc, NTc, Xc = Nn, NTn, Xn
                                pu = pacc.next()
                                pw = pacc.next()
                                for ti in range(gn):
                                    tt = g0 + ti
                                    sl = slice(ti * 128, (ti + 1) * 128)
                                    P.op("pe", lambda e: e.matmul(pu.t[:, sl], lhsT=Xs.t[:, tt, :], rhs=vb.t[:, tt, :], start=True, stop=True),
                                         r=[Xs.b, vb.b], w=[pu.b], inc=(ti == gn - 1))
                                for ti in range(gn):
                                    tt = g0 + ti
                                    sl = slice(ti * 128, (ti + 1) * 128)
                                    P.op("pe", lambda e: e.matmul(pw.t[:, sl], lhsT=kbg.t[:, tt, :], rhs=Xs.t[:, tt, :], start=True, stop=True),
                                         r=[Xs.b, kbg.b], w=[pw.b], inc=(ti == gn - 1))
                                P.op("act", lambda e: e.activation(out=u_.t[:, g0:g0 + gn, :].rearrange("p a b -> p (a b)"), in_=pu.t[:, 0:W_],
                                                                   func=AF.Copy), r=[pu.b], w=[u_.b])
                                P.op("act", lambda e: e.activation(out=wT.t[:, g0 * 128:g0 * 128 + W_], in_=pw.t[:, 0:W_], func=AF.Copy),
                                     r=[pw.b], w=[wT.b])
                                P.op("pool", lambda e: e.tensor_tensor(
                                    out=g3(Dg), in0=cmb(0), in1=egc.t[:, g0:g0 + gn].unsqueeze(2).broadcast_to([128, gn, 128]), op=ALU.mult),
                                    r=[cm.b, egc.b], w=[Dg.b])
                                pq = pacc.next()
                                P.op("pe", lambda e: e.matmul(pq.t[:, 0:W_], lhsT=ones_f.t[:], rhs=Dg.t[:, 0:W_], start=True, stop=True),
                                     r=[ones_f.b, Dg.b], w=[pq.b])
                                P.op("dve", lambda e: e.tensor_tensor(out=qgT.t[:, g0 * 128:g0 * 128 + W_], in0=pq.t[:, 0:W_],
                                                                      in1=qT.t[:, g0 * 128:g0 * 128 + W_], op=ALU.mult),
                                     r=[pq.b, qT.b], w=[qgT.b])
                            P.barrier()
                    with ExitStack() as s3:
                        Sst = [sb("gS%d" % i, [128, 128], F32, s3) for i in range(2)]
                        Sbf = [sb("gSb%d" % i, [128, 128], BF16, s3) for i in range(2)]
                        vnw = [[sb("gvn%d_%d" % (i, j), [128, 128], BF16, s3) for j in range(2)] for i in range(2)]
                        for i in range(2):
                            P.op("dve", lambda e: e.memset(Sst[i].t[:], 0.0), w=[Sst[i].b])
                            P.op("dve", lambda e: e.memset(Sbf[i].t[:], 0.0), w=[Sbf[i].b])
                        order_f = list(range(36))
                        order_b = [3, 2, 1, 0] + list(range(35, 3, -1))
                        written = set()
                        for step in range(36):
                            for di in range(2):
                                c = (order_f, order_b)[di][step]
                                tt, half = c // 2, c % 2
                                r0 = half * 64
                                wT, qgT, kd, attnT, u_, egl = dirs[di]
                                S_, Sb_ = Sst[di], Sbf[di]
                                vn = vnw[di][step % 2]
                                pA, pB, pC = psum[3 * di], psum[3 * di + 1], psum[3 * di + 2]
                                tsl = slice(tt * 128, (tt + 1) * 128)
                                P.op("pe", lambda e: e.matmul(pA.t[:, 0:128], lhsT=wT.t[:, tsl], rhs=S_.t[:], start=True, stop=True),
                                     r=[wT.b, S_.b], w=[pA.b])
                                P.op("dve", lambda e: e.tensor_tensor(out=vn.t[r0:r0 + 64, :], in0=u_.t[r0:r0 + 64, tt, :],
                                                                      in1=pA.t[r0:r0 + 64, 0:128], op=ALU.subtract),
                                     r=[u_.b, pA.b], w=[vn.b])
                                P.op("pe", lambda e: e.matmul(pB.t[:, 0:128], lhsT=Sb_.t[:], rhs=qgT.t[:, tsl], start=True, stop=False),
                                     r=[Sb_.b, qgT.b], w=[pB.b], inc=False)
                                P.op("pe", lambda e: e.matmul(pB.t[:, 0:128], lhsT=vn.t[r0:r0 + 64, :], rhs=attnT.t[r0:r0 + 64, tt, :],
                                                              start=False, stop=True), r=[vn.b, attnT.b], w=[pB.b])
                                P.op("pe", lambda e: e.matmul(pC.t[:, 0:128], lhsT=kd.t[r0:r0 + 64, tt, :], rhs=vn.t[r0:r0 + 64, :],
                                                              start=True, stop=True), r=[kd.b, vn.b], w=[pC.b])
                                P.op("dve", lambda e: e.scalar_tensor_tensor(out=S_.t[:], in0=S_.t[:], scalar=egl.t[:, tt, half:half + 1],
                                                                             in1=pC.t[:, 0:128], op0=ALU.mult, op1=ALU.add),
                                     r=[S_.b, egl.b, pC.b], w=[S_.b])
                                P.op("act", lambda e: e.activation(out=Sb_.t[:], in_=S_.t[:], func=AF.Copy), r=[S_.b], w=[Sb_.b])
                                osl = slice(c * 64, c * 64 + 64)
                                if c not in written:
                                    written.add(c)
                                    P.op("act", lambda e: e.activation(out=oacc.t[:, osl], in_=pB.t[:, r0:r0 + 64], func=AF.Copy),
                                         r=[pB.b], w=[oacc.b])
                                else:
                                    P.op("dve", lambda e: e.tensor_tensor(out=oacc.t[:, osl], in0=oacc.t[:, osl], in1=pB.t[:, r0:r0 + 64],
                                                                          op=ALU.add), r=[pB.b, oacc.b], w=[oacc.b])
                        zs = sb("gzs", [128, T], BF16, s3)
                        ob = sb("gob", [128, T], BF16, s3)
                        sq2 = sb("gsq2", [128, 512], F32, s3)
                        rn2 = sb("grn2", [128, 512], F32, s3)
                        P.dma("sp", zs.t[:], fmv[:, OFF_ZB + h, :], r=[s_fm.b], w=[zs.b])
                        for (t0, n) in TGS:
                            P.op("pool", lambda e: e.tensor_tensor(out=sq2.t[:, 0:n], in0=oacc.t[:, t0:t0 + n], in1=oacc.t[:, t0:t0 + n],
                                                                  op=ALU.mult), r=[oacc.b], w=[sq2.b])
                            pq = ps_next()
                            P.op("pe", lambda e: e.matmul(pq.t[:, 0:n], lhsT=ones_f.t[:], rhs=sq2.t[:, 0:n], start=True, stop=True),
                                 r=[sq2.b, ones_f.b], w=[pq.b])
                            P.op("act", lambda e: e.activation(out=rn2.t[:, 0:n], in_=pq.t[:, 0:n], func=AF.Sqrt, bias=epsc.t[:, 0:1],
                                                               scale=1.0 / 128), r=[pq.b, epsc.b], w=[rn2.b])
                            P.op("dve", lambda e: e.reciprocal(out=rn2.t[:, 0:n], in_=rn2.t[:, 0:n]), r=[rn2.b], w=[rn2.b])
                            P.op("dve", lambda e: e.scalar_tensor_tensor(out=rn2.t[:, 0:n], in0=oacc.t[:, t0:t0 + n], scalar=gg.t[:, 0:1],
                                                                         in1=rn2.t[:, 0:n], op0=ALU.mult, op1=ALU.mult),
                                 r=[oacc.b, gg.b, rn2.b], w=[rn2.b])
                            P.op("pool", lambda e: e.tensor_tensor(out=ob.t[:, t0:t0 + n], in0=rn2.t[:, 0:n], in1=zs.t[:, t0:t0 + n],
                                                                  op=ALU.mult), r=[rn2.b, zs.b], w=[ob.b])
                        P.dma("sp", ov[:, h, :], ob.t[:], r=[ob.b], w=[s_o.b])
                        P.barrier()

    if "M" in cfg.stages:
        stage_M()
    for l in range(L):
        layer_coefs(l)
        if "D" in cfg.stages or "E" in cfg.stages:
            stage_W(l)
        last = (l == L - 1) and not cfg.force_ctx
        for s in range(NS):
            tgs = TGS[1:] if last else TGS
            if "B" in cfg.stages:
                stage_AB(l, s)
            if "C" in cfg.stages:
                if "A" in cfg.mixers:
                    stage_C1(l, s, not last)
                if "N" in cfg.mixers:
                    stage_C2(l, s, not last)
                if "B" in cfg.mixers:
                    stage_C3(l, s)
            if "D" in cfg.stages:
                stage_D(l, s, tgs)
            if "E" in cfg.stages:
                stage_E(l, s, tgs)
    P.barrier()
    gs.close()
    return nc, P


def rope_tables():
    t = np.arange(NX)
    nf = HD // 4
    inv = (10000.0 ** (-np.arange(nf, dtype=np.float32) / nf)).astype(np.float32)
    ang_r = (t // 64).astype(np.float32)[:, None] * inv
    ang_c = (t % 64).astype(np.float32)[:, None] * inv
    cosT = np.ones((128, T), np.float32)
    sinT = np.zeros((128, T), np.float32)
    for a, ang in enumerate((ang_r, ang_c)):
        c = np.cos(ang).T.astype(np.float32)
        s_ = np.sin(ang).T.astype(np.float32)
        cosT[a * 64:a * 64 + 32, NCTX:] = c
        cosT[a * 64 + 32:a * 64 + 64, NCTX:] = c
        sinT[a * 64:a * 64 + 32, NCTX:] = -s_
        sinT[a * 64 + 32:a * 64 + 64, NCTX:] = s_
    return cosT, sinT


def w_in_cols():
    qa0, ka0, va0 = 0, 1024, 1280
    qb0 = 1536
    zb0 = qb0 + 3072
    ab0 = zb0 + 1024
    qn0 = ab0 + 32
    kn0, vn0 = qn0 + 1024, qn0 + 2048
    g0 = qn0 + 3072
    perm = np.concatenate([np.arange(32, 64), np.arange(0, 32), np.arange(96, 128), np.arange(64, 96)])
    cols = []
    for h in range(8):
        base = qa0 + h * 128
        cols.append(base + np.arange(128))
        cols.append(base + perm)
    for h in range(2):
        base = ka0 + h * 128
        cols.append(base + np.arange(128))
        cols.append(base + perm)
    cols.append(np.arange(qb0, qb0 + 3072))
    cols.append(np.arange(zb0, zb0 + 1024))
    cols.append(np.arange(qn0, qn0 + 1024))
    cols.append(np.arange(kn0, kn0 + 1024))
    cols.append(np.arange(g0, g0 + 6144))
    cols.append(np.arange(va0, va0 + 256))
    cols.append(np.arange(vn0, vn0 + 1024))
    cols.append(np.arange(ab0, ab0 + 32))
    cols = np.concatenate(cols)
    assert cols.shape[0] == W_ALL
    return cols


def const_masks():
    m = np.zeros((128, 13, 128), np.float32)
    m[:, 0, :] = np.eye(128, dtype=np.float32)
    p = np.arange(128)[:, None]
    f = np.arange(128)[None, :]
    m[:, 1, :] = (p >= f)
    m[:, 2, :] = (p <= f)
    kc = np.arange(64)[:, None]
    qc = 63 - np.arange(64)[None, :]
    cs = np.clip(qc - 8, 0, 48)
    ok = (kc >= cs) & (kc < cs + 16)
    m[:64, 3, :64] = np.where(ok, 0.0, -30000.0)
    same = (p // 64) == (f // 64)
    m[:, 4, :] = same & (p <= f)
    m[:, 5, :] = same & (p >= f)
    m[:, 6, :] = np.where(same & (p >= f), 0.0, -30000.0)
    m[:, 7, :] = np.where(same & (p <= f), 0.0, -30000.0)
    m[:, 8, :] = same & (p > f)
    m[:, 9, :] = same & (p < f)
    m[:, 10, :] = same
    m[:, 11, :] = (p < 64) & (f >= 0)
    m[:, 12, :] = (p >= 64) & (f >= 0)
    return m


def _tile_w(w, kcb):
    lead = w.shape[:-2]
    K_, N_ = w.shape[-2:]
    kg = K_ // (128 * kcb)
    a = w.reshape(lead + (kg, kcb, 128, N_ // 512, 512))
    nl = len(lead)
    a = np.transpose(a, tuple(range(nl)) + (nl + 3, nl + 0, nl + 2, nl + 1, nl + 4))
    a = np.ascontiguousarray(a).reshape(lead + (N_ // 512, kg, 128, kcb * 512))
    if kg == 1:
        a = a.reshape(lead + (N_ // 512, 128, kcb * 512))
    return a


def host_inputs(inputs, n_cores=8):
    x = np.asarray(inputs["x"], np.float32)
    ctx = np.asarray(inputs["ctx"], np.float32)
    c = np.asarray(inputs["c"], np.float32)
    cols = w_in_cols()
    shared = {
        "w_mod": np.ascontiguousarray(inputs["w_mod"], np.float32),
        "b_mod": np.ascontiguousarray(inputs["b_mod"], np.float32),
        "gvec": np.ascontiguousarray(np.stack([inputs["g_pre_mix"], inputs["g_post_mix"], inputs["g_pre_mlp"],
                                               inputs["g_post_mlp"]], axis=1), np.float32),
        "w_in": np.ascontiguousarray(np.asarray(inputs["w_in"], np.float32)[:, :, cols]),
        "conv_w": np.ascontiguousarray(inputs["conv_w"], np.float32),
        "a_log": np.ascontiguousarray(np.asarray(inputs["a_log"], np.float32).reshape(-1, 16)),
        "dt_bias": np.ascontiguousarray(np.asarray(inputs["dt_bias"], np.float32).reshape(-1, 16)),
        "g_gdn": np.ascontiguousarray(inputs["g_gdn_out"], np.float32),
        "sink": np.ascontiguousarray(inputs["sink"], np.float32),
        "rpb": np.ascontiguousarray(np.asarray(inputs["rpb"], np.float32).reshape(np.asarray(inputs["rpb"]).shape[0], -1)),
        "w_branch": _tile_w(np.asarray(inputs["w_branch"], np.float32), 8),
        "w_out": _tile_w(np.asarray(inputs["w_out"], np.float32), 8),
        "w_up": _tile_w(np.asarray(inputs["w_up"], np.float32), 16),
        "w_down": _tile_w(np.asarray(inputs["w_down"], np.float32), 16),
    }
    cosT, sinT = rope_tables()
    shared["ropec"] = cosT
    shared["ropes"] = sinT
    shared["cmasks"] = const_masks()
    maps = []
    for core in range(n_cores):
        b0 = core * NSEQ
        xin = np.empty((NSEQ, D, T), np.float32)
        for s in range(NSEQ):
            xin[s, :, :NCTX] = ctx[b0 + s].T
            xin[s, :, NCTX:] = x[b0 + s].T
        c3 = np.stack([c[b0], c[b0 + 1], np.asarray(inputs["c_ctx"], np.float32)], axis=0)
        m = dict(shared)
        m["xin"] = xin
        m["c3"] = np.ascontiguousarray(c3)
        maps.append(m)
    return maps


_CACHE = {}


def kernel(**inputs):
    n_cores = 8
    if "nc" not in _CACHE:
        _CACHE["nc"] = build_program(Cfg())[0]
    nc = _CACHE["nc"]
    maps = host_inputs(inputs, n_cores)
    res = run_bass_kernel_spmd(nc, maps, core_ids=list(range(n_cores)))
    out = np.empty((16, NX, D), np.float32)
    for core in range(n_cores):
        y = res.results[core]["yout"]
        for s in range(NSEQ):
            out[core * NSEQ + s] = y[s].T
    return out
```

```python
import math
from contextlib import ExitStack

import numpy as np
import concourse.bass as bass
import concourse.mybir as mybir
from concourse.bass_utils import run_bass_kernel_spmd

F32 = mybir.dt.float32
BF16 = mybir.dt.bfloat16
AF = mybir.ActivationFunctionType
ALU = mybir.AluOpType
AX = mybir.AxisListType

D = 2048
NCTX = 256
NX = 2048
T = NCTX + NX
DEPTH = 4
NSEQ = 2
KC = D // 128
DFF = 4 * D
HD = 128
EPS = 1e-6
TGS = [(0, 256)] + [(256 + 512 * i, 512) for i in range(4)]
NFM = 116
W_FM = NFM * 128
W_TM = 256 + 1024 + 32
W_ALL = W_FM + W_TM


class Buf:
    __slots__ = ("name", "lw", "rd", "excl")

    def __init__(self, name=""):
        self.name = name
        self.lw = None
        self.rd = {}
        self.excl = False


class TB:
    def __init__(self, t, name=""):
        self.t = t
        self.b = Buf(name)


COMPUTE = ("pe", "dve", "act", "pool")
NDS = 24


class Prog:
    def __init__(self, nc):
        self.nc = nc
        self.E = {"pe": nc.tensor, "dve": nc.vector, "act": nc.scalar, "pool": nc.gpsimd, "sp": nc.sync}
        self.sems = []
        self.semidx = {}
        for e in COMPUTE:
            self.semidx[e] = len(self.sems)
            self.sems.append(nc.alloc_semaphore("s_" + e))
        self.cnt = {e: 0 for e in COMPUTE}
        self.pend = {e: None for e in COMPUTE}
        self.seen = {e: {} for e in self.E}
        self.dslots = []
        for i in range(NDS):
            self.dslots.append([len(self.sems), 0])
            self.sems.append(nc.alloc_semaphore("d%d" % i))
        self.dnext = 0
        self.nwaits = 0
        self.nops = 0

    def _wait(self, e, toks):
        need = {}
        for t in toks:
            if t is None:
                continue
            te, si, val = t
            if e == "pe" and te == "pe":
                continue
            assert val is not None, "wait on pending token"
            if self.seen[e].get(si, 0) >= val:
                continue
            if need.get(si, 0) < val:
                need[si] = val
        for si, val in need.items():
            self.E[e].wait_ge(self.sems[si], val)
            self.seen[e][si] = val
            self.nwaits += 1

    def _deps(self, e, r, w):
        deps = []
        for b in r:
            if b.lw is not None:
                deps.append(b.lw)
            if b.excl:
                for k, t in b.rd.items():
                    if k != e:
                        deps.append(t)
        for b in w:
            if b.lw is not None:
                deps.append(b.lw)
            for k, t in b.rd.items():
                if k == e and e in COMPUTE:
                    continue
                deps.append(t)
        return deps

    def op(self, e, fn, r=(), w=(), inc=True):
        self._wait(e, self._deps(e, r, w))
        ins = fn(self.E[e])
        self.nops += 1
        if inc:
            self.cnt[e] += 1
            ins.then_inc(self.sems[self.semidx[e]], 1)
            tok = self.pend[e]
            if tok is None:
                tok = [e, self.semidx[e], None]
            tok[2] = self.cnt[e]
            self.pend[e] = None
        else:
            tok = self.pend[e]
            if tok is None:
                tok = self.pend[e] = [e, self.semidx[e], None]
        for b in r:
            b.rd[e] = tok
        for b in w:
            b.lw = tok
            b.rd = {}
        return ins

    def dma(self, q, out, in_, r=(), w=(), **kw):
        deps = self._deps(q, r, w)
        slot = self.dslots[self.dnext]
        self.dnext = (self.dnext + 1) % NDS
        if slot[1] > 0:
            deps.append(["dma", slot[0], slot[1]])
        self._wait(q, deps)
        ins = self.E[q].dma_start(out=out, in_=in_, **kw)
        slot[1] += 16
        ins.then_inc(self.sems[slot[0]], 16)
        self.nops += 1
        tok = ["dma", slot[0], slot[1]]
        for b in r:
            b.rd[("dma", slot[0])] = tok
        for b in w:
            b.lw = tok
            b.rd = {}
        return ins

    def all_toks(self):
        toks = []
        for e in COMPUTE:
            assert self.pend[e] is None, "pending at barrier on " + e
            if self.cnt[e] > 0:
                toks.append(["x", self.semidx[e], self.cnt[e]])
        for slot in self.dslots:
            if slot[1] > 0:
                toks.append(["dma", slot[0], slot[1]])
        return toks

    def barrier(self, engines=None):
        toks = self.all_toks()
        for e in (engines or self.E):
            self._wait(e, toks)


class Cfg:
    def __init__(self, **kw):
        self.layers = DEPTH
        self.nseq = NSEQ
        self.debug = False
        self.stages = "MABCDE"
        self.inject = ()
        self.wdepth = DEPTH
        self.ab_parts = "mft"
        self.mixers = "ANB"
        self.force_ctx = False
        self.nfm = NFM
        self.__dict__.update(kw)


def build_program(cfg):
    nc = bass.Bass("TRN2", target_bir_lowering=False)
    P = Prog(nc)
    print("sbuf bytes remaining at start:", nc.sbuf_bytes_remaining)
    L = cfg.layers
    NS = cfg.nseq
    WD = cfg.wdepth

    def din(name, shape, dt=F32):
        return nc.dram_tensor(name, list(shape), dt, kind="ExternalInput").ap()

    def dscratch(name, shape, dt):
        if name in cfg.inject:
            kind = "ExternalInput"
        elif cfg.debug:
            kind = "ExternalOutput"
        else:
            kind = "Internal"
        return TB(nc.dram_tensor(name, list(shape), dt, kind=kind).ap(), name)

    xin = din("xin", [NSEQ, D, T])
    c3 = din("c3", [3, D])
    w_mod = din("w_mod", [WD, D, 6 * D])
    b_mod = din("b_mod", [WD, 6 * D])
    gvec = din("gvec", [WD, 4, D])
    w_in = din("w_in", [WD, D, W_ALL])
    conv_w = din("conv_w", [WD, 5, 3072])
    a_log = din("a_log", [WD, 16])
    dt_bias = din("dt_bias", [WD, 16])
    g_gdn = din("g_gdn", [WD, 128])
    sink = din("sink", [WD, 8])
    rpb = din("rpb", [WD, 8 * 15 * 31])
    w_branch = din("w_branch", [WD, 3, 4, 128, 8 * 512])
    w_out = din("w_out", [WD, 4, 2, 128, 8 * 512])
    w_up = din("w_up", [WD, 16, 128, 16 * 512])
    w_down = din("w_down", [WD, 4, 4, 128, 16 * 512])
    ropec = din("ropec", [128, T])
    ropes = din("ropes", [128, T])
    cmasks = din("cmasks", [128, 13, 128])
    yout = nc.dram_tensor("yout", [NSEQ, D, NX], F32, kind="ExternalOutput").ap()

    xT = dscratch("xT", [NSEQ, D, T], F32)
    s_fm = dscratch("s_fm", [W_FM, T], BF16)
    s_va = dscratch("s_va", [T, 256], BF16)
    s_vn = dscratch("s_vn", [T, 1024], BF16)
    s_ab = dscratch("s_ab", [T, 32], F32)
    s_o = dscratch("s_o", [3, 1024, T], BF16)
    w_up16 = dscratch("w_up16", [16, 128, 16 * 512], BF16)
    w_down16 = dscratch("w_down16", [4, 4, 128, 16 * 512], BF16)
    w_branch16 = dscratch("w_branch16", [3, 4, 128, 8 * 512], BF16)
    w_out16 = dscratch("w_out16", [4, 2, 128, 8 * 512], BF16)
    OFF_QA, OFF_KA, OFF_QKVB, OFF_ZB, OFF_QN, OFF_KN, OFF_GATE = 0, 8, 10, 34, 42, 50, 58
    NFM_OUT = 106

    gs = ExitStack()

    uniq = {"n": 0}

    def sb(name, shape, dt, stack=None):
        uniq["n"] += 1
        nm = "%s_%d" % (name, uniq["n"])
        return TB((stack or gs).enter_context(nc.sbuf_tensor(nm, list(shape), dt)), nm)

    psum = [TB(gs.enter_context(nc.psum_tensor("ps%d" % i, [128, 512], F32)), "ps%d" % i) for i in range(8)]
    for p_ in psum:
        p_.b.excl = True
    pstate = {"i": 0}

    def ps_next():
        p = psum[pstate["i"]]
        pstate["i"] = (pstate["i"] + 1) % 8
        return p

    class PSPool:
        def __init__(self, idx):
            self.idx = list(idx)
            self.i = 0

        def next(self):
            p = psum[self.idx[self.i]]
            self.i = (self.i + 1) % len(self.idx)
            return p

    ones_f = sb("ones_f", [128, 128], F32)
    ones_b = sb("ones_b", [128, 128], BF16)
    modT = sb("modT", [128, DEPTH, 96, 3], F32)
    gT = sb("gT", [128, DEPTH, 4, 16], F32)
    coef = sb("coef", [128, 6, 16, 3], F32)
    negones = sb("negones", [128, 128], F32)
    P.op("dve", lambda e: e.memset(negones.t[:], -1.0), w=[negones.b])
    epsc = sb("epsc", [128, 4], F32)
    P.op("dve", lambda e: e.memset(epsc.t[:], EPS), w=[epsc.b])
    P.op("dve", lambda e: e.memset(ones_f.t[:], 1.0), w=[ones_f.b])
    P.op("dve", lambda e: e.memset(ones_b.t[:], 1.0), w=[ones_b.b])
    cm = sb("cm", [128, 13, 128], F32)
    for j in range(13):
        P.dma("sp", cm.t[:, j, :], cmasks[:, j, :], w=[cm.b])
    ident = cm.t[:, 0, :]

    def load_T(dst_ap, src_rows, nrows, tag):
        with ExitStack() as st:
            tmp = sb("ldT" + tag, [128, 128], F32, st)
            P.dma("sp", tmp.t[0:nrows, :], src_rows, w=[tmp.b])
            ps = ps_next()
            P.op("pe", lambda e: e.transpose(out=ps.t[:, 0:nrows], in_=tmp.t[0:nrows, :], identity=cm.t[0:nrows, 0, 0:nrows]),
                 r=[tmp.b, cm.b], w=[ps.b])
            P.op("dve", lambda e: e.tensor_copy(out=dst_ap, in_=ps.t[:, 0:nrows]), r=[ps.b], w=[gT.b, modT.b])
            P.barrier()

    gv = gvec.rearrange("l k (c p) -> (l k c) p", p=128)
    gflat = gT.t[:].rearrange("p l k c -> p (l k c)")
    for j in range(0, L * 64, 128):
        nr = min(128, L * 64 - j)
        load_T(gflat[:, j:j + nr], gv[j:j + nr, :], nr, "g%d" % j)

    def stage_M():
        with ExitStack() as st:
            scT = sb("scT", [128, 3, 16], F32, st)
            wblk = [sb("wmblk%d" % i, [128, 16, 512], F32, st) for i in range(2)]
            brow = [sb("brow%d" % i, [1, 512], F32, st) for i in range(2)]
            load_T(scT.t[:].rearrange("p r c -> p (r c)"), c3.rearrange("r (c p) -> (r c) p", p=128), 48, "c3")
            P.op("act", lambda e: e.activation(out=scT.t[:], in_=scT.t[:], func=AF.Silu), r=[scT.b], w=[scT.b])
            it = 0
            for l in range(L):
                wv = w_mod[l].rearrange("(kc p) n -> p kc n", p=128)
                for blk in range(24):
                    wb = wblk[it % 2]
                    br = brow[it % 2]
                    it += 1
                    for j in range(4):
                        P.dma("sp", wb.t[:, 4 * j:4 * j + 4, :], wv[:, 4 * j:4 * j + 4, blk * 512:(blk + 1) * 512],
                              w=[wb.b])
                    P.dma("sp", br.t[:], b_mod[l:l + 1, blk * 512:(blk + 1) * 512], w=[br.b])
                    ps = ps_next()
                    for j in range(4):
                        def mm(e, j=j, wb=wb, br=br, ps=ps):
                            for kc in range(16):
                                e.matmul(ps.t[:, 3 * j:3 * j + 3], lhsT=wb.t[:, kc, j * 128:(j + 1) * 128],
                                         rhs=scT.t[:, :, kc], start=(kc == 0), stop=False)
                            return e.matmul(ps.t[:, 3 * j:3 * j + 3], lhsT=br.t[0:1, j * 128:(j + 1) * 128],
                                            rhs=ones_f.t[0:1, 0:3], start=False, stop=True)
                        P.op("pe", mm, r=[wb.b, br.b, scT.b, ones_f.b], w=[ps.b], inc=(j == 3))
                    P.op("dve", lambda e, ps=ps, l=l, blk=blk: e.tensor_copy(
                        out=modT.t[:, l, blk * 4:(blk + 1) * 4, :],
                        in_=ps.t[:, 0:12].rearrange("p (a b) -> p a b", b=3)), r=[ps.b], w=[modT.b])
            P.barrier()

    def stage_W(l):
        for blk in range(16):
            P.dma("pool", w_up16.t[blk], w_up[l, blk], w=[w_up16.b])
        for ob in range(4):
            for kq in range(4):
                P.dma("pool", w_down16.t[ob, kq], w_down[l, ob, kq], w=[w_down16.b])
        for i in range(3):
            for ob in range(4):
                P.dma("pool", w_branch16.t[i, ob], w_branch[l, i, ob], w=[w_branch16.b])
        for ob in range(4):
            for kh in range(2):
                P.dma("pool", w_out16.t[ob, kh], w_out[l, ob, kh], w=[w_out16.b])

    class Prefetch:
        def __init__(self, bufs, loaders, dist=2):
            self.bufs, self.loaders, self.dist, self.issued = bufs, loaders, dist, 0

        def get(self, k):
            while self.issued < len(self.loaders) and self.issued <= k + self.dist:
                self.loaders[self.issued](self.bufs[self.issued % len(self.bufs)])
                self.issued += 1
            return self.bufs[k % len(self.bufs)]

    def layer_coefs(l):
        def m(idx):
            return modT.t[:, l, idx * 16:(idx + 1) * 16, :]

        def g(k):
            return gT.t[:, l, k, :].unsqueeze(2).broadcast_to([128, 16, 3])
        for (dst, gi, mi, plus1) in ((0, 0, 1, True), (2, 1, 2, False), (3, 2, 4, True), (5, 3, 5, False)):
            if plus1:
                P.op("dve", lambda e, dst=dst, gi=gi, mi=mi: e.scalar_tensor_tensor(
                    out=coef.t[:, dst], in0=m(mi), scalar=1.0, in1=g(gi), op0=ALU.add, op1=ALU.mult),
                    r=[modT.b, gT.b], w=[coef.b])
            else:
                P.op("dve", lambda e, dst=dst, gi=gi, mi=mi: e.tensor_tensor(
                    out=coef.t[:, dst], in0=m(mi), in1=g(gi), op=ALU.mult), r=[modT.b, gT.b], w=[coef.b])
        P.op("dve", lambda e: e.tensor_copy(out=coef.t[:, 1], in_=m(0)), r=[modT.b], w=[coef.b])
        P.op("dve", lambda e: e.tensor_copy(out=coef.t[:, 4], in_=m(3)), r=[modT.b], w=[coef.b])

    def modulate(st, src, s, ia, ib, hT, tgs, xt=None):
        if xt is None:
            xt = [sb("mx%d" % i, [128, 16, 512], F32, st) for i in range(2)]
        sq = [sb("msq%d" % i, [128, 512], F32, st) for i in range(2)]
        rstd = sb("mrstd", [128, 512], F32, st)
        tmp = [sb("mtmp%d" % i, [128, 512], F32, st) for i in range(2)]
        srcv = src.rearrange("(c p) t -> p c t", p=128)
        for gi, (t0, n) in enumerate(tgs):
            row = 2 if t0 < NCTX else s
            x = xt[gi % len(xt)]
            for j in range(4):
                P.dma("sp", x.t[:, 4 * j:4 * j + 4, 0:n], srcv[:, 4 * j:4 * j + 4, t0:t0 + n], r=[xT.b], w=[x.b])
            ps = ps_next()
            for kc in range(16):
                q = sq[kc % 2]
                P.op("act", lambda e, q=q, x=x, kc=kc: e.activation(out=q.t[:, 0:n], in_=x.t[:, kc, 0:n], func=AF.Square),
                     r=[x.b], w=[q.b])
                P.op("pe", lambda e, q=q, kc=kc, ps=ps: e.matmul(ps.t[:, 0:n], lhsT=ones_f.t[:], rhs=q.t[:, 0:n],
                                                                 start=(kc == 0), stop=(kc == 15)),
                     r=[q.b, ones_f.b], w=[ps.b], inc=True)
            P.op("act", lambda e, ps=ps: e.activation(out=rstd.t[:, 0:n], in_=ps.t[:, 0:n], func=AF.Sqrt, bias=epsc.t[:, 0:1],
                                                      scale=1.0 / D), r=[ps.b, epsc.b], w=[rstd.b])
            P.op("dve", lambda e: e.reciprocal(out=rstd.t[:, 0:n], in_=rstd.t[:, 0:n]), r=[rstd.b], w=[rstd.b])
            for kc in range(16):
                tm = tmp[kc % 2]
                P.op("dve", lambda e, tm=tm, x=x, kc=kc: e.scalar_tensor_tensor(
                    out=tm.t[:, 0:n], in0=x.t[:, kc, 0:n], scalar=coef.t[:, ia, kc, row:row + 1], in1=rstd.t[:, 0:n],
                    op0=ALU.mult, op1=ALU.mult), r=[x.b, coef.b, rstd.b], w=[tm.b])
                P.op("act", lambda e, tm=tm, kc=kc: e.activation(
                    out=hT.t[:, kc, t0:t0 + n], in_=tm.t[:, 0:n], func=AF.Identity,
                    bias=coef.t[:, ib, kc, row:row + 1], scale=1.0), r=[tm.b, coef.b], w=[hT.b])

    def stage_AB(l, s):
        with ExitStack() as st:
            hT = sb("hT", [128, 16, T], BF16, st)
            with ExitStack() as st2:
                src = xin[s] if l == 0 else xT.t[s]
                modulate(st2, src, s, 0, 1, hT, TGS)
                P.barrier()
            if "f" not in cfg.ab_parts:
                return
            wblk = [sb("wblk%d" % i, [128, 16, 512], BF16, st) for i in range(3)]
            stg = [sb("stg%d" % i, [128, T], BF16, st) for i in range(4)]
            cosT = sb("cosT", [128, T], F32, st)
            sinT = sb("sinT", [128, T], F32, st)
            r1 = [sb("r1_%d" % i, [128, 512], F32, st) for i in range(2)]
            r2 = [sb("r2_%d" % i, [128, 512], F32, st) for i in range(2)]
            for j in range(3):
                a, b_ = j * 768, (j + 1) * 768
                P.dma("sp", cosT.t[:, a:b_], ropec[:, a:b_], w=[cosT.b])
                P.dma("sp", sinT.t[:, a:b_], ropes[:, a:b_], w=[sinT.b])
            wv = w_in[l].rearrange("(kc p) n -> p kc n", p=128)
            nblk = (W_ALL + 511) // 512
            blkbuf = {}

            def load_blk(bi):
                wb = wblk[bi % 3]
                c0 = bi * 512
                n = min(512, W_ALL - c0)
                for j in range(2):
                    P.dma("pool", wb.t[:, 8 * j:8 * j + 8, 0:n], wv[:, 8 * j:8 * j + 8, c0:c0 + n], w=[wb.b])
                blkbuf[bi] = wb

            load_blk(0)
            load_blk(1)
            si = 0
            evq = 0
            oc_out = 0
            ci = 0
            while ci < cfg.nfm:
                bi = ci // 4
                if ci % 4 == 0 and bi + 2 < nblk:
                    load_blk(bi + 2)
                wb = blkbuf[bi]
                rope = ci < 20
                kind = "plain"
                if ci >= 20 + 24 and ci < 20 + 32:
                    kind = "silu"
                if ci >= 20 + 48:
                    kind = "sigmoid"
                sg = stg[si % 4]
                si += 1
                for gi, (t0, n) in enumerate(TGS):
                    pa = ps_next()

                    def mm(e, pa=pa, wb=wb, cc=ci % 4, t0=t0, n=n):
                        for kc in range(16):
                            ins = e.matmul(pa.t[:, 0:n], lhsT=wb.t[:, kc, cc * 128:(cc + 1) * 128],
                                           rhs=hT.t[:, kc, t0:t0 + n], start=(kc == 0), stop=(kc == 15))
                        return ins
                    P.op("pe", mm, r=[wb.b, hT.b], w=[pa.b])
                    if rope:
                        pb = ps_next()
                        P.op("pe", lambda e, pb=pb, wb=wb, cc=ci % 4 + 1, t0=t0, n=n: mm(e, pb, wb, cc, t0, n),
                             r=[wb.b, hT.b], w=[pb.b])
                        a1 = r1[gi % 2]
                        a2 = r2[gi % 2]
                        P.op("dve", lambda e, a1=a1, pa=pa, t0=t0, n=n: e.tensor_tensor(
                            out=a1.t[:, 0:n], in0=pa.t[:, 0:n], in1=cosT.t[:, t0:t0 + n], op=ALU.mult),
                            r=[pa.b, cosT.b], w=[a1.b])
                        P.op("dve", lambda e, a2=a2, pb=pb, t0=t0, n=n: e.tensor_tensor(
                            out=a2.t[:, 0:n], in0=pb.t[:, 0:n], in1=sinT.t[:, t0:t0 + n], op=ALU.mult),
                            r=[pb.b, sinT.b], w=[a2.b])
                        P.op("pool", lambda e, a1=a1, a2=a2, sg=sg, t0=t0, n=n: e.tensor_tensor(
                            out=sg.t[:, t0:t0 + n], in0=a1.t[:, 0:n], in1=a2.t[:, 0:n], op=ALU.add),
                            r=[a1.b, a2.b], w=[sg.b])
                    elif kind == "plain":
                        evq += 1
                        if evq % 2:
                            P.op("dve", lambda e, pa=pa, sg=sg, t0=t0, n=n: e.tensor_copy(
                                out=sg.t[:, t0:t0 + n], in_=pa.t[:, 0:n]), r=[pa.b], w=[sg.b])
                        else:
                            P.op("act", lambda e, pa=pa, sg=sg, t0=t0, n=n: e.activation(
                                out=sg.t[:, t0:t0 + n], in_=pa.t[:, 0:n], func=AF.Copy), r=[pa.b], w=[sg.b])
                    else:
                        fn = AF.Silu if kind == "silu" else AF.Sigmoid
                        P.op("act", lambda e, pa=pa, sg=sg, t0=t0, n=n, fn=fn: e.activation(
                            out=sg.t[:, t0:t0 + n], in_=pa.t[:, 0:n], func=fn), r=[pa.b], w=[sg.b])
                P.dma("sp", s_fm.t[oc_out * 128:(oc_out + 1) * 128, :], sg.t[:], r=[sg.b], w=[s_fm.b])
                oc_out += 1
                ci += 2 if rope else 1
            if 't' not in cfg.ab_parts:
                P.barrier()
                return
            tstg = [sb("tstg%d" % i, [128, 512], BF16, st) for i in range(3)]
            tstf = [sb("tstf%d" % i, [128, 32], F32, st) for i in range(2)]
            ti = 0
            for bi in range(NFM // 4, nblk):
                if bi + 2 < nblk and bi + 2 not in blkbuf:
                    load_blk(bi + 2)
                wb = blkbuf[bi]
                c0 = bi * 512 - W_FM
                n = min(512, W_TM - c0)
                for tt in range(T // 128):
                    pa = ps_next()

                    def mmt(e, pa=pa, wb=wb, tt=tt, n=n):
                        for kc in range(16):
                            ins = e.matmul(pa.t[:, 0:n], lhsT=hT.t[:, kc, tt * 128:(tt + 1) * 128],
                                           rhs=wb.t[:, kc, 0:n], start=(kc == 0), stop=(kc == 15))
                        return ins
                    P.op("pe", mmt, r=[wb.b, hT.b], w=[pa.b])
                    segs = []
                    for (nm, a, b_) in (("va", 0, 256), ("vn", 256, 1280), ("ab", 1280, 1312)):
                        lo, hi = max(a, c0), min(b_, c0 + n)
                        if lo < hi:
                            segs.append((nm, lo - a, hi - a, lo - c0, hi - c0))
                    for (nm, d0, d1, p0, p1) in segs:
                        if nm == "ab":
                            tf = tstf[ti % 2]
                            if tt % 2:
                                P.op("act", lambda e, tf=tf, pa=pa, p0=p0, p1=p1: e.activation(
                                    out=tf.t[:, 0:32], in_=pa.t[:, p0:p1], func=AF.Copy), r=[pa.b], w=[tf.b])
                            else:
                                P.op("dve", lambda e, tf=tf, pa=pa, p0=p0, p1=p1: e.tensor_copy(
                                    out=tf.t[:, 0:32], in_=pa.t[:, p0:p1]), r=[pa.b], w=[tf.b])
                            P.dma("sp", s_ab.t[tt * 128:(tt + 1) * 128, :], tf.t[:], r=[tf.b], w=[s_ab.b])
                        else:
                            ts_ = tstg[ti % 3]
                            ti += 1
                            eng = "act" if tt % 2 else "dve"
                            if eng == "act":
                                P.op("act", lambda e, ts_=ts_, pa=pa, p0=p0, p1=p1: e.activation(
                                    out=ts_.t[:, 0:p1 - p0], in_=pa.t[:, p0:p1], func=AF.Copy), r=[pa.b], w=[ts_.b])
                            else:
                                P.op("dve", lambda e, ts_=ts_, pa=pa, p0=p0, p1=p1: e.tensor_copy(
                                    out=ts_.t[:, 0:p1 - p0], in_=pa.t[:, p0:p1]), r=[pa.b], w=[ts_.b])
                            dst = s_va if nm == "va" else s_vn
                            P.dma("sp", dst.t[tt * 128:(tt + 1) * 128, d0:d1], ts_.t[:, 0:p1 - p0], r=[ts_.b], w=[dst.b])
            P.barrier()

    def post_norm_residual(st, l, s, ig, mT, t0, n, last_out, tag, xt=None):
        row = 2 if t0 < NCTX else s
        sq = [sb(tag + "sq%d" % i, [128, 512], F32, st) for i in range(2)]
        rstd = sb(tag + "rstd", [128, 512], F32, st)
        dstv = xT.t[s].rearrange("(c p) t -> p c t", p=128)
        if xt is None:
            xt = sb(tag + "xt", [128, 16, 512], F32, st)
            src = (xin[s] if (l == 0 and tag == "D") else xT.t[s]).rearrange("(c p) t -> p c t", p=128)
            for j in range(4):
                P.dma("sp", xt.t[:, 4 * j:4 * j + 4, 0:n], src[:, 4 * j:4 * j + 4, t0:t0 + n], r=[xT.b], w=[xt.b])
        ps = ps_next()
        for kc in range(16):
            q = sq[kc % 2]
            P.op("act", lambda e, q=q, kc=kc: e.activation(out=q.t[:, 0:n], in_=mT.t[:, kc, 0:n], func=AF.Square),
                 r=[mT.b], w=[q.b])
            P.op("pe", lambda e, q=q, kc=kc: e.matmul(ps.t[:, 0:n], lhsT=ones_f.t[:], rhs=q.t[:, 0:n],
                                                      start=(kc == 0), stop=(kc == 15)), r=[q.b], w=[ps.b])
        P.op("act", lambda e: e.activation(out=rstd.t[:, 0:n], in_=ps.t[:, 0:n], func=AF.Sqrt, bias=epsc.t[:, 0:1],
                                           scale=1.0 / D), r=[ps.b, epsc.b], w=[rstd.b])
        P.op("dve", lambda e: e.reciprocal(out=rstd.t[:, 0:n], in_=rstd.t[:, 0:n]), r=[rstd.b], w=[rstd.b])
        for kc in range(16):
            P.op("pool", lambda e, kc=kc: e.tensor_tensor(out=mT.t[:, kc, 0:n], in0=mT.t[:, kc, 0:n], in1=rstd.t[:, 0:n],
                                                          op=ALU.mult), r=[mT.b, rstd.b], w=[mT.b])
            P.op("dve", lambda e, kc=kc: e.scalar_tensor_tensor(
                out=xt.t[:, kc, 0:n], in0=mT.t[:, kc, 0:n], scalar=coef.t[:, ig, kc, row:row + 1], in1=xt.t[:, kc, 0:n],
                op0=ALU.mult, op1=ALU.add), r=[mT.b, coef.b, xt.b], w=[xt.b])
        for j in range(4):
            P.dma("sp", dstv[:, 4 * j:4 * j + 4, t0:t0 + n], xt.t[:, 4 * j:4 * j + 4, 0:n], r=[xt.b], w=[xT.b])
        if last_out and t0 >= NCTX:
            yv = yout[s].rearrange("(c p) t -> p c t", p=128)
            for j in range(4):
                P.dma("sp", yv[:, 4 * j:4 * j + 4, t0 - NCTX:t0 - NCTX + n], xt.t[:, 4 * j:4 * j + 4, 0:n], r=[xt.b])

    def stage_D(l, s, tgs):
        gate_v = s_fm.t[OFF_GATE * 128:(OFF_GATE + 48) * 128, :].rearrange("(i c p) t -> p i c t", p=128, i=3)
        for (t0, n) in tgs:
            with ExitStack() as st:
                oT = sb("oT", [128, 3, 8, 512], BF16, st)
                yT = sb("yT", [128, 16, 512], BF16, st)
                mT = sb("mT", [128, 16, 512], F32, st)
                gtb = [sb("gt%d" % i, [128, 4, 512], BF16, st) for i in range(3)]
                wbb = [sb("wbb%d" % i, [128, 8, 512], BF16, st) for i in range(3)]
                acc = sb("accD", [128, 4, 512], F32, st)
                tm2 = [sb("tm2%d" % i, [128, 512], F32, st) for i in range(2)]
                ov = s_o.t.rearrange("i (c p) t -> p i c t", p=128)
                for i in range(3):
                    for j in range(2):
                        P.dma("sp", oT.t[:, i, 4 * j:4 * j + 4, 0:n], ov[:, i, 4 * j:4 * j + 4, t0:t0 + n], r=[s_o.b], w=[oT.b])
                wi = 0
                loaders = []
                for ob in range(4):
                    for i in range(3):
                        loaders.append(lambda wb, i=i, ob=ob: P.dma("pool", wb.t[:].rearrange("p a b -> p (a b)"), w_branch16.t[i, ob],
                                                                    r=[w_branch16.b], w=[wb.b]))
                for ob in range(4):
                    for kh in range(2):
                        loaders.append(lambda wb, kh=kh, ob=ob: P.dma("pool", wb.t[:].rearrange("p a b -> p (a b)"), w_out16.t[ob, kh],
                                                                      r=[w_out16.b], w=[wb.b]))
                pf = Prefetch(wbb, loaders, 2)
                for ob in range(4):
                    for i in range(3):
                        wb = pf.get(wi)
                        g = gtb[wi % 3]
                        wi += 1
                        P.dma("sp", g.t[:, :, 0:n], gate_v[:, i, 4 * ob:4 * ob + 4, t0:t0 + n], r=[s_fm.b], w=[g.b])
                        for oc in range(4):
                            pa = ps_next()

                            def mm(e, pa=pa, wb=wb, i=i, oc=oc):
                                for kc in range(8):
                                    ins = e.matmul(pa.t[:, 0:n], lhsT=wb.t[:, kc, oc * 128:(oc + 1) * 128], rhs=oT.t[:, i, kc, 0:n],
                                                   start=(kc == 0), stop=(kc == 7))
                                return ins
                            P.op("pe", mm, r=[wb.b, oT.b], w=[pa.b])
                            if i == 0:
                                P.op("dve", lambda e, pa=pa, g=g, oc=oc: e.tensor_tensor(
                                    out=acc.t[:, oc, 0:n], in0=pa.t[:, 0:n], in1=g.t[:, oc, 0:n], op=ALU.mult),
                                    r=[pa.b, g.b], w=[acc.b])
                            else:
                                t2 = tm2[oc % 2]
                                P.op("dve", lambda e, pa=pa, t2=t2, g=g, oc=oc: e.tensor_tensor(
                                    out=t2.t[:, 0:n], in0=pa.t[:, 0:n], in1=g.t[:, oc, 0:n], op=ALU.mult),
                                    r=[pa.b, g.b], w=[t2.b])
                                if i == 1:
                                    P.op("dve", lambda e, t2=t2, oc=oc: e.tensor_tensor(
                                        out=acc.t[:, oc, 0:n], in0=acc.t[:, oc, 0:n], in1=t2.t[:, 0:n], op=ALU.add),
                                        r=[acc.b, t2.b], w=[acc.b])
                                else:
                                    P.op("dve", lambda e, t2=t2, oc=oc, ob=ob: e.tensor_tensor(
                                        out=yT.t[:, 4 * ob + oc, 0:n], in0=acc.t[:, oc, 0:n], in1=t2.t[:, 0:n], op=ALU.add),
                                        r=[acc.b, t2.b], w=[yT.b])
                pacc = PSPool([0, 1, 2, 3])
                for ob in range(4):
                    pas = [pacc.next() for _ in range(4)]
                    for kh in range(2):
                        wb = pf.get(wi)
                        wi += 1
                        for oc in range(4):
                            pa = pas[oc]

                            def mm2(e, pa=pa, wb=wb, oc=oc, kh=kh):
                                for kc in range(8):
                                    ins = e.matmul(pa.t[:, 0:n], lhsT=wb.t[:, kc, oc * 128:(oc + 1) * 128], rhs=yT.t[:, 8 * kh + kc, 0:n],
                                                   start=(kh == 0 and kc == 0), stop=(kh == 1 and kc == 7))
                                return ins
                            P.op("pe", mm2, r=[wb.b, yT.b], w=[pa.b])
                    for oc in range(4):
                        P.op("act", lambda e, pa=pas[oc], oc=oc, ob=ob: e.activation(out=mT.t[:, 4 * ob + oc, 0:n], in_=pa.t[:, 0:n],
                                                                                    func=AF.Copy), r=[pas[oc].b], w=[mT.b])
                post_norm_residual(st, l, s, 2, mT, t0, n, False, "D")
                P.barrier()

    def stage_E(l, s, tgs):
        last = (l == L - 1)
        for (t0, n) in tgs:
            with ExitStack() as st:
                aT = sb("aT", [128, 64, 512], BF16, st)
                xt = sb("xtE", [128, 16, 512], F32, st)
                with ExitStack() as st2:
                    hT = sb("hE", [128, 16, 512], BF16, st2)
                    modulate_tg(st2, xT.t[s], s, 3, 4, hT, t0, n, [xt])
                    wub = [sb("wub%d" % i, [128, 16, 512], BF16, st2) for i in range(3)]
                    rl = [sb("rl%d" % i, [128, 512], F32, st2) for i in range(2)]
                    pfu = Prefetch(wub, [(lambda wb, blk=blk: P.dma("pool", wb.t[:].rearrange("p a b -> p (a b)"), w_up16.t[blk],
                                                                    r=[w_up16.b], w=[wb.b])) for blk in range(16)], 2)
                    for blk in range(16):
                        wb = pfu.get(blk)
                        for cc in range(4):
                            fc = blk * 4 + cc
                            pa = ps_next()

                            def mm(e, pa=pa, wb=wb, cc=cc):
                                for kc in range(16):
                                    ins = e.matmul(pa.t[:, 0:n], lhsT=wb.t[:, kc, cc * 128:(cc + 1) * 128],
                                                   rhs=hT.t[:, kc, 0:n], start=(kc == 0), stop=(kc == 15))
                                return ins
                            P.op("pe", mm, r=[wb.b, hT.b], w=[pa.b])
                            r_ = rl[fc % 2]
                            P.op("act", lambda e, pa=pa, r_=r_: e.activation(out=r_.t[:, 0:n], in_=pa.t[:, 0:n], func=AF.Relu),
                                 r=[pa.b], w=[r_.b])
                            P.op("dve", lambda e, r_=r_, fc=fc: e.tensor_tensor(out=aT.t[:, fc, 0:n], in0=r_.t[:, 0:n],
                                                                            in1=r_.t[:, 0:n], op=ALU.mult),
                                 r=[r_.b], w=[aT.b])
                    P.barrier()
                with ExitStack() as st2:
                    mT = sb("mE", [128, 16, 512], F32, st2)
                    wdb = [sb("wdb%d" % i, [128, 16, 512], BF16, st2) for i in range(3)]
                    pacc = PSPool([0, 1, 2, 3])
                    wi = 0
                    pfd = Prefetch(wdb, [(lambda wb, ob=ob, kq=kq: P.dma("pool", wb.t[:].rearrange("p a b -> p (a b)"), w_down16.t[ob, kq],
                                                                         r=[w_down16.b], w=[wb.b])) for ob in range(4) for kq in range(4)], 2)
                    for ob in range(4):
                        pas = [pacc.next() for _ in range(4)]
                        for kq in range(4):
                            wb = pfd.get(wi)
                            wi += 1
                            for oc in range(4):
                                pa = pas[oc]

                                def mm2(e, pa=pa, wb=wb, oc=oc, kq=kq):
                                    for kc in range(16):
                                        ins = e.matmul(pa.t[:, 0:n], lhsT=wb.t[:, kc, oc * 128:(oc + 1) * 128], rhs=aT.t[:, 16 * kq + kc, 0:n],
                                                       start=(kq == 0 and kc == 0), stop=(kq == 3 and kc == 15))
                                    return ins
                                P.op("pe", mm2, r=[wb.b, aT.b], w=[pa.b])
                        for oc in range(4):
                            P.op("act", lambda e, pa=pas[oc], oc=oc, ob=ob: e.activation(out=mT.t[:, 4 * ob + oc, 0:n], in_=pa.t[:, 0:n],
                                                                                        func=AF.Copy), r=[pas[oc].b], w=[mT.b])
                    post_norm_residual(st2, l, s, 5, mT, t0, n, last, "E", xt)
                    P.barrier()

    def modulate_tg(st, src, s, ia, ib, hT, t0, n, xt=None):
        class V:
            pass
        hv = TB(None)
        hv.b = hT.b

        class _T:
            def __getitem__(self, key):
                p, kc, sl = key
                return hT.t[p, kc, sl.start - t0:sl.stop - t0]
        hv.t = _T()
        modulate(st, src, s, ia, ib, hv, [(t0, n)], xt)


    SCALE = 1.0 / math.sqrt(HD)

    def stage_C1(l, s, with_ctx):
        with ExitStack() as st:
            qT = sb("qaT", [128, 8, T], BF16, st)
            kT = sb("kaT", [128, 2, T], BF16, st)
            va = sb("vaS", [128, 18, 256], BF16, st)
            oT = sb("oaT", [128, 8, T], BF16, st)
            esk = sb("esk", [128, 8], F32, st)
            eskB = sb("eskB", [128, 8, 128], F32, st)
            mk = sb("mkA", [128, 2, 4, 128], BF16, st)
            pTs = [sb("pTa%d" % i, [128, 512], BF16, st) for i in range(3)]
            den = sb("denA", [128, 512], F32, st)
            fmv = s_fm.t.rearrange("(c p) t -> p c t", p=128)
            for h in range(8):
                P.dma("sp", qT.t[:, h, :], fmv[:, OFF_QA + h, :], r=[s_fm.b], w=[qT.b])
            for h in range(2):
                P.dma("sp", kT.t[:, h, :], fmv[:, OFF_KA + h, :], r=[s_fm.b], w=[kT.b])
            vav = s_va.t.rearrange("(tt p) c -> p tt c", p=128)
            for j in range(3):
                P.dma("sp", va.t[:, 6 * j:6 * j + 6, :], vav[:, 6 * j:6 * j + 6, :], r=[s_va.b], w=[va.b])
            P.dma("sp", esk.t[:], sink[l:l + 1, :].partition_broadcast(128), w=[esk.b])
            P.op("act", lambda e: e.activation(out=esk.t[:], in_=esk.t[:], func=AF.Exp), r=[esk.b], w=[esk.b])
            P.op("dve", lambda e: e.tensor_copy(out=eskB.t[:], in_=esk.t[:].unsqueeze(2).broadcast_to([128, 8, 128])),
                 r=[esk.b], w=[eskB.b])
            for j in range(2):
                P.op("dve", lambda e, j=j: e.tensor_copy(out=mk.t[:, j], in_=cm.t[:, 1 + j, :].unsqueeze(1).broadcast_to([128, 4, 128])),
                     r=[cm.b], w=[mk.b])
            qblocks = ([(128 * j, None) for j in range(2)] if with_ctx else []) + [(256 + 128 * i, i) for i in range(16)]
            pacc = PSPool([0, 1, 2, 3])
            pss = PSPool([4, 5, 6, 7])
            pi = 0
            for g in range(2):
                for (t0, xi) in qblocks:
                    keys = [(0, None), (128, None)]
                    if xi is not None:
                        if xi > 0:
                            keys.append((256 + 128 * (xi - 1), 0))
                        keys.append((256 + 128 * xi, None))
                        if xi < 15:
                            keys.append((256 + 128 * (xi + 1), 1))
                    ps_o = pacc.next()
                    ps_d = pacc.next()
                    pss_of = {}

                    def emit_scores_a(ii):
                        k0 = keys[ii][0]
                        ps_s = pss.next()
                        pss_of[ii] = ps_s
                        P.op("pe", lambda e: e.matmul(
                            ps_s.t[:, 0:512].rearrange("p (h q) -> p h q", h=4), lhsT=kT.t[:, g, k0:k0 + 128],
                            rhs=qT.t[:, 4 * g:4 * g + 4, t0:t0 + 128], start=True, stop=True), r=[kT.b, qT.b], w=[ps_s.b])

                    for ii in range(min(2, len(keys))):
                        emit_scores_a(ii)
                    for idx, (k0, mki) in enumerate(keys):
                        if idx + 2 < len(keys):
                            emit_scores_a(idx + 2)
                        ps_s = pss_of.pop(idx)
                        pT = pTs[pi % 3]
                        pi += 1
                        P.op("act", lambda e, ps_s=ps_s, pT=pT: e.activation(out=pT.t[:], in_=ps_s.t[:], func=AF.Exp, scale=SCALE),
                             r=[ps_s.b], w=[pT.b])
                        if mki is not None:
                            P.op("pool", lambda e, pT=pT, mki=mki: e.tensor_tensor(
                                out=pT.t[:], in0=pT.t[:], in1=mk.t[:, mki].rearrange("p h q -> p (h q)"), op=ALU.mult),
                                r=[pT.b, mk.b], w=[pT.b])
                        first, lastk = idx == 0, idx == len(keys) - 1
                        P.op("pe", lambda e, pT=pT: e.matmul(ps_d.t[:], lhsT=ones_b.t[:], rhs=pT.t[:], start=first, stop=lastk),
                             r=[pT.b, ones_b.b], w=[ps_d.b])
                        P.op("pe", lambda e, pT=pT, k0=k0: e.matmul(ps_o.t[:], lhsT=va.t[:, k0 // 128, g * 128:(g + 1) * 128],
                                                                    rhs=pT.t[:], start=first, stop=lastk),
                             r=[pT.b, va.b], w=[ps_o.b])
                    P.op("dve", lambda e: e.tensor_tensor(out=den.t[:], in0=ps_d.t[:],
                                                          in1=eskB.t[:, 4 * g:4 * g + 4, :].rearrange("p h q -> p (h q)"), op=ALU.add),
                         r=[ps_d.b, eskB.b], w=[den.b])
                    P.op("dve", lambda e: e.reciprocal(out=den.t[:], in_=den.t[:]), r=[den.b], w=[den.b])
                    P.op("dve", lambda e: e.tensor_tensor(out=oT.t[:, 4 * g:4 * g + 4, t0:t0 + 128],
                                                          in0=ps_o.t[:].rearrange("p (h q) -> p h q", h=4),
                                                          in1=den.t[:].rearrange("p (h q) -> p h q", h=4), op=ALU.mult),
                         r=[ps_o.b, den.b], w=[oT.b])
            ov = s_o.t[0].rearrange("(c p) t -> p c t", p=128)
            for h in range(8):
                P.dma("sp", ov[:, h, :], oT.t[:, h, :], r=[oT.b], w=[s_o.b])
            P.barrier()

    rpbpad = dscratch("rpbpad", [120, 127], F32)
    dbgC = dscratch("dbgC", [64, 7680], F32)

    def na_valid(r, kr):
        rs = min(max(r - 4, 0), 24)
        return rs <= kr <= rs + 7

    def stage_C2(l, s, with_ctx):
        with ExitStack() as st:
            Ctab = sb("Ctab", [64, 8, 15, 64], F32, st)
            zt = sb("zt", [120, 127], F32, st)
            P.op("dve", lambda e: e.memset(zt.t[:], 0.0), w=[zt.b])
            P.dma("sp", rpbpad.t[:, :], zt.t[:], r=[zt.b], w=[rpbpad.b])
            P.dma("sp", rpbpad.t[:, 48:79], rpb[l].rearrange("(r c) -> r c", c=31), w=[rpbpad.b])
            for h in range(8):
                src = bass.AP(tensor=rpbpad.t.tensor, offset=rpbpad.t.offset + h * 15 * 127, ap=[[1, 64], [127, 15], [1, 64]])
                P.dma("sp", Ctab.t[:, h], src, r=[rpbpad.b], w=[Ctab.b])
            P.op("dve", lambda e: e.tensor_tensor(
                out=Ctab.t[:].rearrange("p h a q -> p (h a) q"), in0=Ctab.t[:].rearrange("p h a q -> p (h a) q"),
                in1=cm.t[0:64, 3, 0:64].unsqueeze(1).broadcast_to([64, 120, 64]), op=ALU.add), r=[Ctab.b, cm.b], w=[Ctab.b])
            if cfg.debug:
                for h in range(8):
                    P.dma("sp", dbgC.t[:, h * 960:(h + 1) * 960], Ctab.t[:, h].rearrange("p a q -> p (a q)"), r=[Ctab.b], w=[dbgC.b])
            ctab_ap = Ctab.t[:]
            pstep = ctab_ap.ap[0][0]
            qh = [sb("qnh%d" % i, [128, T], BF16, st) for i in range(2)]
            kh = [sb("knh%d" % i, [128, T], BF16, st) for i in range(2)]
            v64 = [sb("v64_%d" % i, [128, 36, 128], BF16, st) for i in range(2)]
            pTr = [sb("pTr%d" % i, [128, 512], BF16, st) for i in range(3)]
            for t_ in v64 + pTr:
                P.op("pool", lambda e, t_=t_: e.memset(t_.t[:], 0.0), w=[t_.b])
            vcx = [sb("vcx%d" % i, [128, 2, 128], BF16, st) for i in range(2)]
            oh = [sb("onh%d" % i, [128, T], BF16, st) for i in range(2)]
            pTs = [sb("pTn%d" % i, [128, 512], BF16, st) for i in range(3)]
            sbs = [sb("sbn%d" % i, [64, 512], F32, st) for i in range(3)]
            den = sb("denN", [128, 512], F32, st)
            fmv = s_fm.t.rearrange("(c p) t -> p c t", p=128)
            v64v = s_vn.t.rearrange("(c p) d -> p c d", p=64)
            vcv = s_vn.t[0:256, :].rearrange("(c p) d -> p c d", p=128)
            ov = s_o.t[2].rearrange("(c p) t -> p c t", p=128)
            pi = 0
            bi_ = 0
            pacc = PSPool([0, 1, 2, 3])
            pss = PSPool([4, 5, 6, 7])
            for h in range(8):
                q_, k_, v_, vc_, o_ = qh[h % 2], kh[h % 2], v64[h % 2], vcx[h % 2], oh[h % 2]
                P.dma("sp", q_.t[:], fmv[:, OFF_QN + h, :], r=[s_fm.b], w=[q_.b])
                P.dma("sp", k_.t[:], fmv[:, OFF_KN + h, :], r=[s_fm.b], w=[k_.b])
                for j in range(3):
                    P.dma("sp", v_.t[0:64, 12 * j:12 * j + 12, :], v64v[:, 12 * j:12 * j + 12, h * 128:(h + 1) * 128],
                          r=[s_vn.b], w=[v_.b])
                P.dma("sp", vc_.t[:], vcv[:, :, h * 128:(h + 1) * 128], r=[s_vn.b], w=[vc_.b])
                groups = ([("c", 0, 256)] if with_ctx else []) + [("x", 256 + 512 * G, 512) for G in range(4)]
                for (gk, t0, n) in groups:
                    ps_o = pacc.next()
                    ps_d = pacc.next()
                    items = [("ctx", 0)]
                    if gk == "x":
                        G = (t0 - 256) // 512
                        for kr in range(32):
                            rr = [r for r in range(8 * G, 8 * G + 8) if na_valid(r, kr)]
                            if rr:
                                items.append(("row", kr, rr[0], rr[-1]))
                    items.append(("ctx", 1))
                    LOOK = 2
                    pss_of = {}

                    def emit_scores(ii):
                        it = items[ii]
                        ps_s = pss.next()
                        pss_of[ii] = ps_s
                        if it[0] == "ctx":
                            cb = it[1]
                            P.op("pe", lambda e: e.matmul(ps_s.t[:, 0:n], lhsT=k_.t[:, cb * 128:(cb + 1) * 128], rhs=q_.t[:, t0:t0 + n],
                                                          start=True, stop=True), r=[k_.b, q_.b], w=[ps_s.b])
                        else:
                            _, kr, rlo, rhi = it
                            nr = rhi - rlo + 1
                            c0 = (rlo - 8 * G) * 64
                            nc_ = nr * 64
                            kt0 = 256 + 64 * kr
                            P.op("pe", lambda e: e.matmul(ps_s.t[0:64, 0:nc_], lhsT=k_.t[:, kt0:kt0 + 64],
                                                          rhs=q_.t[:, t0 + c0:t0 + c0 + nc_], start=True, stop=True),
                                 r=[k_.b, q_.b], w=[ps_s.b])

                    for ii in range(min(LOOK, len(items))):
                        emit_scores(ii)
                    for ii, it in enumerate(items):
                        if ii + LOOK < len(items):
                            emit_scores(ii + LOOK)
                        first, lastk = ii == 0, ii == len(items) - 1
                        pT = pTs[pi % 3]
                        pi += 1
                        ps_s = pss_of.pop(ii)
                        if it[0] == "ctx":
                            cb = it[1]
                            P.op("act", lambda e: e.activation(out=pT.t[:, 0:n], in_=ps_s.t[:, 0:n], func=AF.Exp, scale=SCALE),
                                 r=[ps_s.b], w=[pT.b])
                            P.op("pe", lambda e: e.matmul(ps_d.t[:, 0:n], lhsT=ones_b.t[:], rhs=pT.t[:, 0:n], start=first, stop=lastk),
                                 r=[pT.b, ones_b.b], w=[ps_d.b])
                            P.op("pe", lambda e: e.matmul(ps_o.t[:, 0:n], lhsT=vc_.t[:, cb, :], rhs=pT.t[:, 0:n], start=first, stop=lastk),
                                 r=[pT.b, vc_.b], w=[ps_o.b])
                        else:
                            _, kr, rlo, rhi = it
                            pT = pTr[pi % 3]
                            nr = rhi - rlo + 1
                            c0 = (rlo - 8 * G) * 64
                            nc_ = nr * 64
                            sbt = sbs[bi_ % 3]
                            bi_ += 1
                            a_start = 7 + kr - rlo
                            bias = bass.AP(tensor=ctab_ap.tensor, offset=ctab_ap.offset + (h * 15 + a_start) * 64 + 63,
                                           ap=[[pstep, 64], [-64, nr], [-1, 64]])
                            P.op("dve", lambda e: e.scalar_tensor_tensor(
                                out=sbt.t[:, 0:nc_].rearrange("p (r q) -> p r q", q=64),
                                in0=ps_s.t[0:64, 0:nc_].rearrange("p (r q) -> p r q", q=64), scalar=SCALE, in1=bias,
                                op0=ALU.mult, op1=ALU.add), r=[ps_s.b, Ctab.b], w=[sbt.b])
                            P.op("act", lambda e: e.activation(out=pT.t[0:64, 0:nc_], in_=sbt.t[:, 0:nc_], func=AF.Exp),
                                 r=[sbt.b], w=[pT.b])
                            P.op("pe", lambda e: e.matmul(ps_d.t[:, c0:c0 + nc_], lhsT=ones_b.t[:, :], rhs=pT.t[:, 0:nc_],
                                                          start=False, stop=False), r=[pT.b, ones_b.b], w=[ps_d.b])
                            P.op("pe", lambda e: e.matmul(ps_o.t[:, c0:c0 + nc_], lhsT=v_.t[:, 4 + kr, :], rhs=pT.t[:, 0:nc_],
                                                          start=False, stop=False), r=[pT.b, v_.b], w=[ps_o.b])
                    P.op("dve", lambda e: e.reciprocal(out=den.t[:, 0:n], in_=ps_d.t[:, 0:n]), r=[ps_d.b], w=[den.b])
                    P.op("dve", lambda e: e.tensor_tensor(out=o_.t[:, t0:t0 + n], in0=ps_o.t[:, 0:n], in1=den.t[:, 0:n], op=ALU.mult),
                         r=[ps_o.b, den.b], w=[o_.b])
                P.dma("sp", ov[:, h, :], o_.t[:], r=[o_.b], w=[s_o.b])
            P.barrier()


    NT_ = T // 128
    TGRP = [(0, 4), (4, 4), (8, 4), (12, 4), (16, 2)]

    def stage_C3(l, s):
        with ExitStack() as st:
            abT = sb("abT", [128, NT_, 32], F32, st)
            gG = sb("gG", [128, NT_, 16], F32, st)
            bG = sb("bG", [128, NT_, 16], F32, st)
            nA = sb("nA", [128, 16], F32, st)
            dtb = sb("dtb", [128, 16], F32, st)
            cwT = sb("cwT", [128, 5, 24], F32, st)
            gg = sb("ggdn", [128, 1], F32, st)
            abv = s_ab.t.rearrange("(tt p) j -> p tt j", p=128)
            for j in range(3):
                P.dma("sp", abT.t[:, 6 * j:6 * j + 6, :], abv[:, 6 * j:6 * j + 6, :], r=[s_ab.b], w=[abT.b])
            P.dma("sp", nA.t[:], a_log[l:l + 1, :].partition_broadcast(128), w=[nA.b])
            P.dma("sp", dtb.t[:], dt_bias[l:l + 1, :].partition_broadcast(128), w=[dtb.b])
            P.dma("sp", gg.t[:], g_gdn[l].rearrange("(p o) -> p o", o=1), w=[gg.b])
            P.op("act", lambda e: e.activation(out=nA.t[:], in_=nA.t[:], func=AF.Exp), r=[nA.b], w=[nA.b])
            P.op("dve", lambda e: e.tensor_scalar(out=nA.t[:], in0=nA.t[:], scalar1=-1.0, scalar2=None, op0=ALU.mult),
                 r=[nA.b], w=[nA.b])
            P.op("dve", lambda e: e.tensor_tensor(out=gG.t[:], in0=abT.t[:, :, 0:16],
                                                  in1=dtb.t[:].unsqueeze(1).broadcast_to([128, NT_, 16]), op=ALU.add),
                 r=[abT.b, dtb.b], w=[gG.b])
            P.op("act", lambda e: e.activation(out=gG.t[:], in_=gG.t[:], func=AF.Exp), r=[gG.b], w=[gG.b])
            P.op("act", lambda e: e.activation(out=gG.t[:], in_=gG.t[:], func=AF.Ln, bias=ones_f.t[:, 0:1], scale=1.0),
                 r=[gG.b, ones_f.b], w=[gG.b])
            P.op("dve", lambda e: e.tensor_tensor(out=gG.t[:], in0=gG.t[:],
                                                  in1=nA.t[:].unsqueeze(1).broadcast_to([128, NT_, 16]), op=ALU.mult),
                 r=[gG.b, nA.b], w=[gG.b])
            P.op("act", lambda e: e.activation(out=bG.t[:], in_=abT.t[:, :, 16:32], func=AF.Sigmoid), r=[abT.b], w=[bG.b])
            with ExitStack() as st2:
                tmpc = sb("cwtmp", [128, 128], F32, st2)
                P.dma("sp", tmpc.t[0:120, :], conv_w[l].rearrange("j (c p) -> (j c) p", p=128), w=[tmpc.b])
                pst = ps_next()
                P.op("pe", lambda e: e.transpose(out=pst.t[:, 0:120], in_=tmpc.t[0:120, :], identity=cm.t[0:120, 0, 0:120]),
                     r=[tmpc.b, cm.b], w=[pst.b])
                P.op("dve", lambda e: e.tensor_copy(out=cwT.t[:].rearrange("p j c -> p (j c)"), in_=pst.t[:, 0:120]),
                     r=[pst.b], w=[cwT.b])
                P.barrier()
            fmv = s_fm.t.rearrange("(c p) t -> p c t", p=128)
            ov = s_o.t[1].rearrange("(c p) t -> p c t", p=128)
            SEGS = [(0, NCTX), (NCTX, T)]
            idb = cm.t[:, 0, :]
            for h in range(8):
                with ExitStack() as sh:
                    qT = sb("gq", [128, T], F32, sh)
                    k_tm = sb("gktm", [128, NT_, 128], F32, sh)
                    v_tm = sb("gvtm", [128, NT_, 128], F32, sh)
                    KK = sb("gKK", [128, NT_, 128], F32, sh)
                    QKT = sb("gQKT", [128, NT_, 128], F32, sh)
                    with ExitStack() as s1:
                        kT = sb("gk", [128, T], F32, s1)
                        raw = [sb("graw%d" % i, [128, T], BF16, s1) for i in range(3)]
                        acc = [sb("gacc%d" % i, [128, T], F32, s1) for i in range(2)]
                        vT = sb("gv", [128, T], F32, s1)
                        sqb = sb("gsq", [128, 512], F32, s1)
                        rnb = sb("grn", [128, 512], F32, s1)
                        for ci_, which in enumerate(("q", "k", "v")):
                            cidx = ci_ * 8 + h
                            rw = raw[ci_]
                            a_ = acc[ci_ % 2]
                            eng = "dve"
                            P.dma("sp", rw.t[:], fmv[:, OFF_QKVB + cidx, :], r=[s_fm.b], w=[rw.b])
                            P.op(eng, lambda e: e.tensor_scalar(out=a_.t[:], in0=rw.t[:], scalar1=cwT.t[:, 2, cidx:cidx + 1], scalar2=None,
                                                                op0=ALU.mult), r=[rw.b, cwT.b], w=[a_.b])
                            for j in (0, 1, 3, 4):
                                d_ = j - 2
                                for (s0, s1_) in SEGS:
                                    lo, hi = max(s0, s0 - d_), min(s1_, s1_ - d_)
                                    P.op(eng, lambda e: e.scalar_tensor_tensor(
                                        out=a_.t[:, lo:hi], in0=rw.t[:, lo + d_:hi + d_], scalar=cwT.t[:, j, cidx:cidx + 1],
                                        in1=a_.t[:, lo:hi], op0=ALU.mult, op1=ALU.add), r=[rw.b, cwT.b, a_.b], w=[a_.b])
                            dst = {"q": qT, "k": kT, "v": vT}[which]
                            if which == "v":
                                P.op("act", lambda e: e.activation(out=dst.t[:], in_=a_.t[:], func=AF.Silu), r=[a_.b], w=[dst.b])
                            else:
                                P.op("act", lambda e: e.activation(out=a_.t[:], in_=a_.t[:], func=AF.Silu), r=[a_.b], w=[a_.b])
                                for (t0, n) in TGS:
                                    P.op("pool", lambda e: e.tensor_tensor(out=sqb.t[:, 0:n], in0=a_.t[:, t0:t0 + n], in1=a_.t[:, t0:t0 + n],
                                                                          op=ALU.mult), r=[a_.b], w=[sqb.b])
                                    pq = ps_next()
                                    P.op("pe", lambda e: e.matmul(pq.t[:, 0:n], lhsT=ones_f.t[:], rhs=sqb.t[:, 0:n], start=True, stop=True),
                                         r=[sqb.b, ones_f.b], w=[pq.b])
                                    P.op("act", lambda e: e.activation(out=rnb.t[:, 0:n], in_=pq.t[:, 0:n], func=AF.Sqrt,
                                                                       bias=epsc.t[:, 0:1], scale=1.0), r=[pq.b, epsc.b], w=[rnb.b])
                                    P.op("dve", lambda e: e.reciprocal(out=rnb.t[:, 0:n], in_=rnb.t[:, 0:n]), r=[rnb.b], w=[rnb.b])
                                    if which == "q":
                                        P.op("dve", lambda e: e.scalar_tensor_tensor(
                                            out=dst.t[:, t0:t0 + n], in0=a_.t[:, t0:t0 + n], scalar=SCALE, in1=rnb.t[:, 0:n],
                                            op0=ALU.mult, op1=ALU.mult), r=[a_.b, rnb.b], w=[dst.b])
                                    else:
                                        P.op("dve", lambda e: e.tensor_tensor(out=dst.t[:, t0:t0 + n], in0=a_.t[:, t0:t0 + n],
                                                                              in1=rnb.t[:, 0:n], op=ALU.mult), r=[a_.b, rnb.b], w=[dst.b])
                        ei = 0
                        for (g0, gn) in TGRP:
                            for (src_, dst_) in ((kT, k_tm), (vT, v_tm)):
                                pt = ps_next()
                                for ti in range(gn):
                                    tt = g0 + ti
                                    P.op("pe", lambda e: e.transpose(out=pt.t[:, ti * 128:(ti + 1) * 128], in_=src_.t[:, tt * 128:(tt + 1) * 128],
                                                                     identity=idb), r=[src_.b, cm.b], w=[pt.b], inc=(ti == gn - 1))
                                ei += 1
                                if ei % 2:
                                    P.op("act", lambda e: e.activation(out=dst_.t[:, g0:g0 + gn, :].rearrange("p a b -> p (a b)"),
                                                                       in_=pt.t[:, 0:gn * 128], func=AF.Copy), r=[pt.b], w=[dst_.b])
                                else:
                                    P.op("dve", lambda e: e.tensor_copy(out=dst_.t[:, g0:g0 + gn, :].rearrange("p a b -> p (a b)"),
                                                                        in_=pt.t[:, 0:gn * 128]), r=[pt.b], w=[dst_.b])
                            for (rhs_, dst_) in ((kT, KK), (qT, QKT)):
                                pt = ps_next()
                                for ti in range(gn):
                                    tt = g0 + ti
                                    P.op("pe", lambda e: e.matmul(pt.t[:, ti * 128:(ti + 1) * 128], lhsT=kT.t[:, tt * 128:(tt + 1) * 128],
                                                                  rhs=rhs_.t[:, tt * 128:(tt + 1) * 128], start=True, stop=True),
                                         r=[kT.b, rhs_.b], w=[pt.b], inc=(ti == gn - 1))
                                ei += 1
                                if ei % 2:
                                    P.op("act", lambda e: e.activation(out=dst_.t[:, g0:g0 + gn, :].rearrange("p a b -> p (a b)"),
                                                                       in_=pt.t[:, 0:gn * 128], func=AF.Copy), r=[pt.b], w=[dst_.b])
                                else:
                                    P.op("dve", lambda e: e.tensor_copy(out=dst_.t[:, g0:g0 + gn, :].rearrange("p a b -> p (a b)"),
                                                                        in_=pt.t[:, 0:gn * 128]), r=[pt.b], w=[dst_.b])
                        P.barrier()
                    dirs = []
                    for di in range(2):
                        iU, iML, iMU, iSL = (4, 6, 7, 8) if di == 0 else (5, 7, 6, 9)
                        gcol = di * 8 + h
                        wT = sb("gwT%d" % di, [128, T], F32, sh)
                        qgT = sb("gqg%d" % di, [128, T], BF16, sh)
                        kd = sb("gkd%d" % di, [128, NT_, 128], BF16, sh)
                        attnT = sb("gat%d" % di, [128, NT_, 128], BF16, sh)
                        u_ = sb("gu%d" % di, [128, NT_, 128], F32, sh)
                        egl = sb("gegl%d" % di, [128, NT_, 2], F32, sh)
                        dirs.append((wT, qgT, kd, attnT, u_, egl))
                        with ExitStack() as s2:
                            gc = sb("ggc", [128, NT_], F32, s2)
                            egc = sb("gegc", [128, NT_], F32, s2)
                            ekd = sb("gekd", [128, NT_], F32, s2)
                            bsc = sb("gbsc", [128, NT_], F32, s2)
                            ghd = sb("gghd", [128, NT_], F32, s2)
                            bhd = sb("gbhd", [128, NT_], F32, s2)
                            P.op("dve", lambda e: e.tensor_copy(out=ghd.t[:], in_=gG.t[:, :, gcol]), r=[gG.b], w=[ghd.b])
                            P.op("dve", lambda e: e.tensor_copy(out=bhd.t[:], in_=bG.t[:, :, gcol]), r=[bG.b], w=[bhd.b])
                            pg = ps_next()
                            P.op("pe", lambda e: e.matmul(pg.t[:, 0:NT_], lhsT=cm.t[:, iU, :], rhs=ghd.t[:], start=True, stop=True),
                                 r=[cm.b, ghd.b], w=[pg.b], inc=False)
                            P.op("pe", lambda e: e.matmul(pg.t[:, 32:32 + NT_], lhsT=cm.t[:, 10, :], rhs=ghd.t[:], start=True, stop=True),
                                 r=[cm.b, ghd.b], w=[pg.b], inc=False)
                            P.op("pe", lambda e: e.matmul(pg.t[:, 64:64 + NT_], lhsT=cm.t[:, 11, :], rhs=ghd.t[:], start=True, stop=True),
                                 r=[cm.b, ghd.b], w=[pg.b], inc=False)
                            P.op("pe", lambda e: e.matmul(pg.t[:, 96:96 + NT_], lhsT=cm.t[:, 12, :], rhs=ghd.t[:], start=True, stop=True),
                                 r=[cm.b, ghd.b], w=[pg.b])
                            P.op("dve", lambda e: e.tensor_copy(out=gc.t[:], in_=pg.t[:, 0:NT_]), r=[pg.b], w=[gc.b])
                            P.op("dve", lambda e: e.tensor_tensor(out=ekd.t[:], in0=pg.t[:, 32:32 + NT_], in1=gc.t[:], op=ALU.subtract),
                                 r=[pg.b, gc.b], w=[ekd.b])
                            P.op("dve", lambda e: e.tensor_copy(out=egl.t[:, :, 0], in_=pg.t[:, 64:64 + NT_]), r=[pg.b], w=[egl.b])
                            P.op("dve", lambda e: e.tensor_copy(out=egl.t[:, :, 1], in_=pg.t[:, 96:96 + NT_]), r=[pg.b], w=[egl.b])
                            P.op("act", lambda e: e.activation(out=egc.t[:], in_=gc.t[:], func=AF.Exp), r=[gc.b], w=[egc.b])
                            P.op("act", lambda e: e.activation(out=ekd.t[:], in_=ekd.t[:], func=AF.Exp), r=[ekd.b], w=[ekd.b])
                            P.op("act", lambda e: e.activation(out=egl.t[:], in_=egl.t[:], func=AF.Exp), r=[egl.b], w=[egl.b])
                            P.op("dve", lambda e: e.tensor_tensor(out=bsc.t[:], in0=bhd.t[:], in1=egc.t[:], op=ALU.mult),
                                 r=[bhd.b, egc.b], w=[bsc.b])
                            P.op("pool", lambda e: e.tensor_tensor(out=kd.t[:], in0=k_tm.t[:],
                                                                  in1=ekd.t[:].unsqueeze(2).broadcast_to([128, NT_, 128]), op=ALU.mult),
                                 r=[k_tm.b, ekd.b], w=[kd.b])
                            pacc = PSPool([0, 1, 2, 3, 4, 5, 6, 7])
                            slots = []
                            for si in range(2):
                                B = {}
                                for nm in ("GU", "Gb", "EL", "EU", "Lf", "Dg", "kbg", "vb", "X", "N00", "N01", "N10", "N11", "X0", "X1"):
                                    B[nm] = sb("g%s_%d" % (nm, si), [128, 512], F32, s2)
                                slots.append(B)

                            def group_gen(g0, gn, B):
                                W_ = gn * 128
                                GU, Gb, EL, EU, Lf, Dg, kbg, vb, Xg = (B[k_] for k_ in ("GU", "Gb", "EL", "EU", "Lf", "Dg", "kbg", "vb", "X"))
                                nb = [[B["N00"], B["N01"]], [B["N10"], B["N11"]]]
                                Xb = [B["X0"], B["X1"]]
                                g3 = lambda t_: t_.t[:, 0:W_].rearrange("p (a b) -> p a b", b=128)
                                p3 = lambda p_: p_.t[:, 0:W_].rearrange("p (a b) -> p a b", b=128)
                                gsl = ghd.t[:, g0:g0 + gn].unsqueeze(2).broadcast_to([128, gn, 128])
                                bcg = lambda t_: t_.t[:, g0:g0 + gn].unsqueeze(2).broadcast_to([128, gn, 128])
                                cmb = lambda i_: cm.t[:, i_, :].unsqueeze(1).broadcast_to([128, gn, 128])
                                P.op("dve", lambda e: e.tensor_tensor(out=g3(GU), in0=cmb(iU), in1=gsl, op=ALU.mult), r=[cm.b, ghd.b], w=[GU.b])
                                P.op("pool", lambda e: e.tensor_copy(out=g3(Gb), in_=gsl), r=[ghd.b], w=[Gb.b])
                                P.op("pool", lambda e: e.tensor_tensor(out=g3(kbg), in0=k_tm.t[:, g0:g0 + gn, :], in1=bcg(bsc), op=ALU.mult),
                                     r=[k_tm.b, bsc.b], w=[kbg.b])
                                P.op("pool", lambda e: e.tensor_tensor(out=g3(vb), in0=v_tm.t[:, g0:g0 + gn, :], in1=bcg(bhd), op=ALU.mult),
                                     r=[v_tm.b, bhd.b], w=[vb.b])
                                P.op("pool", lambda e: e.tensor_tensor(out=g3(Dg), in0=cmb(0), in1=bcg(egc), op=ALU.mult),
                                     r=[cm.b, egc.b], w=[Dg.b])
                                pD = pacc.next()
                                P.op("pe", lambda e: e.matmul(pD.t[:, 0:W_], lhsT=cm.t[:, iU, :], rhs=Gb.t[:, 0:W_], start=True, stop=False),
                                     r=[cm.b, Gb.b], w=[pD.b], inc=False)
                                P.op("pe", lambda e: e.matmul(pD.t[:, 0:W_], lhsT=negones.t[:], rhs=GU.t[:, 0:W_], start=False, stop=True),
                                     r=[negones.b, GU.b], w=[pD.b])
                                pq = pacc.next()
                                P.op("pe", lambda e: e.matmul(pq.t[:, 0:W_], lhsT=ones_f.t[:], rhs=Dg.t[:, 0:W_], start=True, stop=True),
                                     r=[ones_f.b, Dg.b], w=[pq.b])
                                yield
                                P.op("dve", lambda e: e.tensor_tensor(out=g3(EL), in0=p3(pD), in1=cmb(iML), op=ALU.add), r=[pD.b, cm.b], w=[EL.b])
                                P.op("dve", lambda e: e.scalar_tensor_tensor(out=g3(EU), in0=p3(pD), scalar=-1.0, in1=cmb(iMU),
                                                                             op0=ALU.mult, op1=ALU.add), r=[pD.b, cm.b], w=[EU.b])
                                P.op("dve", lambda e: e.tensor_tensor(out=qgT.t[:, g0 * 128:g0 * 128 + W_], in0=pq.t[:, 0:W_],
                                                                      in1=qT.t[:, g0 * 128:g0 * 128 + W_], op=ALU.mult),
                                     r=[pq.b, qT.b], w=[qgT.b])
                                P.op("act", lambda e: e.activation(out=EL.t[:, 0:W_], in_=EL.t[:, 0:W_], func=AF.Exp), r=[EL.b], w=[EL.b])
                                P.op("act", lambda e: e.activation(out=EU.t[:, 0:W_], in_=EU.t[:, 0:W_], func=AF.Exp), r=[EU.b], w=[EU.b])
                                P.op("pool", lambda e: e.tensor_tensor(out=g3(Lf), in0=g3(EL), in1=KK.t[:, g0:g0 + gn, :], op=ALU.mult),
                                     r=[EL.b, KK.b], w=[Lf.b])
                                P.op("pool", lambda e: e.tensor_tensor(out=g3(Lf), in0=g3(Lf), in1=cmb(iSL), op=ALU.mult),
                                     r=[Lf.b, cm.b], w=[Lf.b])
                                P.op("pool", lambda e: e.tensor_tensor(out=g3(Lf), in0=g3(Lf), in1=bcg(bhd), op=ALU.mult),
                                     r=[Lf.b, bhd.b], w=[Lf.b])
                                P.op("dve", lambda e: e.tensor_tensor(out=attnT.t[:, g0:g0 + gn, :], in0=g3(EU), in1=QKT.t[:, g0:g0 + gn, :],
                                                                      op=ALU.mult), r=[EU.b, QKT.b], w=[attnT.b])
                                NT0, N0 = nb[1][0], nb[0][0]
                                P.op("dve", lambda e: e.tensor_scalar(out=NT0.t[:, 0:W_], in0=Lf.t[:, 0:W_], scalar1=-1.0, scalar2=None,
                                                                      op0=ALU.mult), r=[Lf.b], w=[NT0.b])
                                pT_ = pacc.next()
                                for ti in range(gn):
                                    P.op("pe", lambda e: e.transpose(out=pT_.t[:, ti * 128:(ti + 1) * 128], in_=Lf.t[:, ti * 128:(ti + 1) * 128],
                                                                     identity=idb), r=[Lf.b, cm.b], w=[pT_.b], inc=(ti == gn - 1))
                                yield
                                P.op("act", lambda e: e.activation(out=N0.t[:, 0:W_], in_=pT_.t[:, 0:W_], func=AF.Copy, scale=-1.0),
                                     r=[pT_.b], w=[N0.b])
                                Xc = Xb[0]
                                P.op("dve", lambda e: e.scalar_tensor_tensor(out=g3(Xc), in0=p3(pT_), scalar=-1.0, in1=cmb(0),
                                                                             op0=ALU.mult, op1=ALU.add), r=[pT_.b, cm.b], w=[Xc.b])
                                Nc, NTc = N0, NT0
                                for k_ in range(5):
                                    par = (k_ + 1) % 2
                                    Nn, NTn = nb[0][par], nb[1][par]
                                    Xn = Xb[(k_ + 1) % 2]
                                    pN = pacc.next() if k_ < 4 else None
                                    pNT = pacc.next()
                                    for ti in range(gn):
                                        sl = slice(ti * 128, (ti + 1) * 128)
                                        P.op("pe", lambda e: e.matmul(pNT.t[:, sl], lhsT=Nc.t[:, sl], rhs=NTc.t[:, sl], start=True, stop=True),
                                             r=[NTc.b, Nc.b], w=[pNT.b], inc=(ti == gn - 1))
                                    if k_ < 4:
                                        for ti in range(gn):
                                            sl = slice(ti * 128, (ti + 1) * 128)
                                            P.op("pe", lambda e: e.matmul(pN.t[:, sl], lhsT=NTc.t[:, sl], rhs=Nc.t[:, sl], start=True, stop=True),
                                                 r=[NTc.b, Nc.b], w=[pN.b], inc=(ti == gn - 1))
                                    yield
                                    P.op("dve", lambda e: e.tensor_copy(out=NTn.t[:, 0:W_], in_=pNT.t[:, 0:W_]), r=[pNT.b], w=[NTn.b])
                                    if k_ < 4:
                                        P.op("act", lambda e: e.activation(out=Nn.t[:, 0:W_], in_=pN.t[:, 0:W_], func=AF.Copy),
                                             r=[pN.b], w=[Nn.b])
                                    pX = pacc.next()
                                    for ti in range(gn):
                                        sl = slice(ti * 128, (ti + 1) * 128)
                                        P.op("pe", lambda e: e.matmul(pX.t[:, sl], lhsT=NTn.t[:, sl], rhs=Xc.t[:, sl], start=True, stop=True),
                                             r=[NTn.b, Xc.b], w=[pX.b], inc=(ti == gn - 1))
                                    yield
                                    dstX = Xn if k_ < 4 else Xg
                                    P.op("dve", lambda e: e.tensor_tensor(out=dstX.t[:, 0:W_], in0=pX.t[:, 0:W_], in1=Xc.t[:, 0:W_], op=ALU.add),
                                         r=[pX.b, Xc.b], w=[dstX.b])
                                    Nc, NTc, Xc = Nn, NTn, Xn
                                pu = pacc.next()
                                pw = pacc.next()
                                for ti in range(gn):
                                    sl = slice(ti * 128, (ti + 1) * 128)
                                    P.op("pe", lambda e: e.matmul(pu.t[:, sl], lhsT=Xg.t[:, sl], rhs=vb.t[:, sl], start=True, stop=True),
                                         r=[Xg.b, vb.b], w=[pu.b], inc=(ti == gn - 1))
                                for ti in range(gn):
                                    sl = slice(ti * 128, (ti + 1) * 128)
                                    P.op("pe", lambda e: e.matmul(pw.t[:, sl], lhsT=kbg.t[:, sl], rhs=Xg.t[:, sl], start=True, stop=True),
                                         r=[Xg.b, kbg.b], w=[pw.b], inc=(ti == gn - 1))
                                yield
                                P.op("act", lambda e: e.activation(out=u_.t[:, g0:g0 + gn, :].rearrange("p a b -> p (a b)"), in_=pu.t[:, 0:W_],
                                                                   func=AF.Copy), r=[pu.b], w=[u_.b])
                                P.op("act", lambda e: e.activation(out=wT.t[:, g0 * 128:g0 * 128 + W_], in_=pw.t[:, 0:W_], func=AF.Copy),
                                     r=[pw.b], w=[wT.b])

                            for pair in ((0, 1), (2, 3), (4,)):
                                gens = [group_gen(TGRP[gi][0], TGRP[gi][1], slots[k_]) for k_, gi in enumerate(pair)]
                                while gens:
                                    for g_ in list(gens):
                                        try:
                                            next(g_)
                                        except StopIteration:
                                            gens.remove(g_)
                            P.barrier()
                    with ExitStack() as s3:
                        oacc = sb("goacc", [128, T], F32, s3)
                        Sst = [sb("gS%d" % i, [128, 128], F32, s3) for i in range(2)]
                        Sbf = [sb("gSb%d" % i, [128, 128], BF16, s3) for i in range(2)]
                        vnw = [[sb("gvn%d_%d" % (i, j), [128, 128], BF16, s3) for j in range(2)] for i in range(2)]
                        for i in range(2):
                            P.op("dve", lambda e: e.memset(Sst[i].t[:], 0.0), w=[Sst[i].b])
                            P.op("dve", lambda e: e.memset(Sbf[i].t[:], 0.0), w=[Sbf[i].b])
                        order_f = list(range(36))
                        order_b = [3, 2, 1, 0] + list(range(35, 3, -1))
                        written = set()
                        for step in range(36):
                            for di in range(2):
                                c = (order_f, order_b)[di][step]
                                tt, half = c // 2, c % 2
                                r0 = half * 64
                                wT, qgT, kd, attnT, u_, egl = dirs[di]
                                S_, Sb_ = Sst[di], Sbf[di]
                                vn = vnw[di][step % 2]
                                pA, pB, pC = psum[3 * di], psum[3 * di + 1], psum[3 * di + 2]
                                tsl = slice(tt * 128, (tt + 1) * 128)
                                P.op("pe", lambda e: e.matmul(pA.t[:, 0:128], lhsT=wT.t[:, tsl], rhs=S_.t[:], start=True, stop=True),
                                     r=[wT.b, S_.b], w=[pA.b])
                                P.op("dve", lambda e: e.tensor_tensor(out=vn.t[r0:r0 + 64, :], in0=u_.t[r0:r0 + 64, tt, :],
                                                                      in1=pA.t[r0:r0 + 64, 0:128], op=ALU.subtract),
                                     r=[u_.b, pA.b], w=[vn.b])
                                P.op("pe", lambda e: e.matmul(pB.t[:, 0:128], lhsT=Sb_.t[:], rhs=qgT.t[:, tsl], start=True, stop=False),
                                     r=[Sb_.b, qgT.b], w=[pB.b], inc=False)
                                P.op("pe", lambda e: e.matmul(pB.t[:, 0:128], lhsT=vn.t[r0:r0 + 64, :], rhs=attnT.t[r0:r0 + 64, tt, :],
                                                              start=False, stop=True), r=[vn.b, attnT.b], w=[pB.b])
                                P.op("pe", lambda e: e.matmul(pC.t[:, 0:128], lhsT=kd.t[r0:r0 + 64, tt, :], rhs=vn.t[r0:r0 + 64, :],
                                                              start=True, stop=True), r=[kd.b, vn.b], w=[pC.b])
                                P.op("dve", lambda e: e.scalar_tensor_tensor(out=S_.t[:], in0=S_.t[:], scalar=egl.t[:, tt, half:half + 1],
                                                                             in1=pC.t[:, 0:128], op0=ALU.mult, op1=ALU.add),
                                     r=[S_.b, egl.b, pC.b], w=[S_.b])
                                P.op("act", lambda e: e.activation(out=Sb_.t[:], in_=S_.t[:], func=AF.Copy), r=[S_.b], w=[Sb_.b])
                                osl = slice(c * 64, c * 64 + 64)
                                if c not in written:
                                    written.add(c)
                                    P.op("act", lambda e: e.activation(out=oacc.t[:, osl], in_=pB.t[:, r0:r0 + 64], func=AF.Copy),
                                         r=[pB.b], w=[oacc.b])
                                else:
                                    P.op("dve", lambda e: e.tensor_tensor(out=oacc.t[:, osl], in0=oacc.t[:, osl], in1=pB.t[:, r0:r0 + 64],
                                                                          op=ALU.add), r=[pB.b, oacc.b], w=[oacc.b])
                        zs = sb("gzs", [128, T], BF16, s3)
                        ob = sb("gob", [128, T], BF16, s3)
                        sq2 = sb("gsq2", [128, 512], F32, s3)
                        rn2 = sb("grn2", [128, 512], F32, s3)
                        P.dma("sp", zs.t[:], fmv[:, OFF_ZB + h, :], r=[s_fm.b], w=[zs.b])
                        for (t0, n) in TGS:
                            P.op("pool", lambda e: e.tensor_tensor(out=sq2.t[:, 0:n], in0=oacc.t[:, t0:t0 + n], in1=oacc.t[:, t0:t0 + n],
                                                                  op=ALU.mult), r=[oacc.b], w=[sq2.b])
                            pq = ps_next()
                            P.op("pe", lambda e: e.matmul(pq.t[:, 0:n], lhsT=ones_f.t[:], rhs=sq2.t[:, 0:n], start=True, stop=True),
                                 r=[sq2.b, ones_f.b], w=[pq.b])
                            P.op("act", lambda e: e.activation(out=rn2.t[:, 0:n], in_=pq.t[:, 0:n], func=AF.Sqrt, bias=epsc.t[:, 0:1],
                                                               scale=1.0 / 128), r=[pq.b, epsc.b], w=[rn2.b])
                            P.op("dve", lambda e: e.reciprocal(out=rn2.t[:, 0:n], in_=rn2.t[:, 0:n]), r=[rn2.b], w=[rn2.b])
                            P.op("dve", lambda e: e.scalar_tensor_tensor(out=rn2.t[:, 0:n], in0=oacc.t[:, t0:t0 + n], scalar=gg.t[:, 0:1],
                                                                         in1=rn2.t[:, 0:n], op0=ALU.mult, op1=ALU.mult),
                                 r=[oacc.b, gg.b, rn2.b], w=[rn2.b])
                            P.op("pool", lambda e: e.tensor_tensor(out=ob.t[:, t0:t0 + n], in0=rn2.t[:, 0:n], in1=zs.t[:, t0:t0 + n],
                                                                  op=ALU.mult), r=[rn2.b, zs.b], w=[ob.b])
                        P.dma("sp", ov[:, h, :], ob.t[:], r=[ob.b], w=[s_o.b])
                        P.barrier()

    if "M" in cfg.stages:
        stage_M()
    for l in range(L):
        layer_coefs(l)
        if "D" in cfg.stages or "E" in cfg.stages:
            stage_W(l)
        last = (l == L - 1) and not cfg.force_ctx
        for s in range(NS):
            tgs = TGS[1:] if last else TGS
            if "B" in cfg.stages:
                stage_AB(l, s)
            if "C" in cfg.stages:
                if "A" in cfg.mixers:
                    stage_C1(l, s, not last)
                if "N" in cfg.mixers:
                    stage_C2(l, s, not last)
                if "B" in cfg.mixers:
                    stage_C3(l, s)
            if "D" in cfg.stages:
                stage_D(l, s, tgs)
            if "E" in cfg.stages:
                stage_E(l, s, tgs)
    P.barrier()
    gs.close()
    return nc, P


def rope_tables():
    t = np.arange(NX)
    nf = HD // 4
    inv = (10000.0 ** (-np.arange(nf, dtype=np.float32) / nf)).astype(np.float32)
    ang_r = (t // 64).astype(np.float32)[:, None] * inv
    ang_c = (t % 64).astype(np.float32)[:, None] * inv
    cosT = np.ones((128, T), np.float32)
    sinT = np.zeros((128, T), np.float32)
    for a, ang in enumerate((ang_r, ang_c)):
        c = np.cos(ang).T.astype(np.float32)
        s_ = np.sin(ang).T.astype(np.float32)
        cosT[a * 64:a * 64 + 32, NCTX:] = c
        cosT[a * 64 + 32:a * 64 + 64, NCTX:] = c
        sinT[a * 64:a * 64 + 32, NCTX:] = -s_
        sinT[a * 64 + 32:a * 64 + 64, NCTX:] = s_
    return cosT, sinT


def w_in_cols():
    qa0, ka0, va0 = 0, 1024, 1280
    qb0 = 1536
    zb0 = qb0 + 3072
    ab0 = zb0 + 1024
    qn0 = ab0 + 32
    kn0, vn0 = qn0 + 1024, qn0 + 2048
    g0 = qn0 + 3072
    perm = np.concatenate([np.arange(32, 64), np.arange(0, 32), np.arange(96, 128), np.arange(64, 96)])
    cols = []
    for h in range(8):
        base = qa0 + h * 128
        cols.append(base + np.arange(128))
        cols.append(base + perm)
    for h in range(2):
        base = ka0 + h * 128
        cols.append(base + np.arange(128))
        cols.append(base + perm)
    cols.append(np.arange(qb0, qb0 + 3072))
    cols.append(np.arange(zb0, zb0 + 1024))
    cols.append(np.arange(qn0, qn0 + 1024))
    cols.append(np.arange(kn0, kn0 + 1024))
    cols.append(np.arange(g0, g0 + 6144))
    cols.append(np.arange(va0, va0 + 256))
    cols.append(np.arange(vn0, vn0 + 1024))
    cols.append(np.arange(ab0, ab0 + 32))
    cols = np.concatenate(cols)
    assert cols.shape[0] == W_ALL
    return cols


def const_masks():
    m = np.zeros((128, 13, 128), np.float32)
    m[:, 0, :] = np.eye(128, dtype=np.float32)
    p = np.arange(128)[:, None]
    f = np.arange(128)[None, :]
    m[:, 1, :] = (p >= f)
    m[:, 2, :] = (p <= f)
    kc = np.arange(64)[:, None]
    qc = 63 - np.arange(64)[None, :]
    cs = np.clip(qc - 8, 0, 48)
    ok = (kc >= cs) & (kc < cs + 16)
    m[:64, 3, :64] = np.where(ok, 0.0, -30000.0)
    same = (p // 64) == (f // 64)
    m[:, 4, :] = same & (p <= f)
    m[:, 5, :] = same & (p >= f)
    m[:, 6, :] = np.where(same & (p >= f), 0.0, -30000.0)
    m[:, 7, :] = np.where(same & (p <= f), 0.0, -30000.0)
    m[:, 8, :] = same & (p > f)
    m[:, 9, :] = same & (p < f)
    m[:, 10, :] = same
    m[:, 11, :] = (p < 64) & (f >= 0)
    m[:, 12, :] = (p >= 64) & (f >= 0)
    return m


def _tile_w(w, kcb):
    lead = w.shape[:-2]
    K_, N_ = w.shape[-2:]
    kg = K_ // (128 * kcb)
    a = w.reshape(lead + (kg, kcb, 128, N_ // 512, 512))
    nl = len(lead)
    a = np.transpose(a, tuple(range(nl)) + (nl + 3, nl + 0, nl + 2, nl + 1, nl + 4))
    a = np.ascontiguousarray(a).reshape(lead + (N_ // 512, kg, 128, kcb * 512))
    if kg == 1:
        a = a.reshape(lead + (N_ // 512, 128, kcb * 512))
    return a


def host_inputs(inputs, n_cores=8):
    x = np.asarray(inputs["x"], np.float32)
    ctx = np.asarray(inputs["ctx"], np.float32)
    c = np.asarray(inputs["c"], np.float32)
    cols = w_in_cols()
    shared = {
        "w_mod": np.ascontiguousarray(inputs["w_mod"], np.float32),
        "b_mod": np.ascontiguousarray(inputs["b_mod"], np.float32),
        "gvec": np.ascontiguousarray(np.stack([inputs["g_pre_mix"], inputs["g_post_mix"], inputs["g_pre_mlp"],
                                               inputs["g_post_mlp"]], axis=1), np.float32),
        "w_in": np.ascontiguousarray(np.asarray(inputs["w_in"], np.float32)[:, :, cols]),
        "conv_w": np.ascontiguousarray(inputs["conv_w"], np.float32),
        "a_log": np.ascontiguousarray(np.asarray(inputs["a_log"], np.float32).reshape(-1, 16)),
        "dt_bias": np.ascontiguousarray(np.asarray(inputs["dt_bias"], np.float32).reshape(-1, 16)),
        "g_gdn": np.ascontiguousarray(inputs["g_gdn_out"], np.float32),
        "sink": np.ascontiguousarray(inputs["sink"], np.float32),
        "rpb": np.ascontiguousarray(np.asarray(inputs["rpb"], np.float32).reshape(np.asarray(inputs["rpb"]).shape[0], -1)),
        "w_branch": _tile_w(np.asarray(inputs["w_branch"], np.float32), 8),
        "w_out": _tile_w(np.asarray(inputs["w_out"], np.float32), 8),
        "w_up": _tile_w(np.asarray(inputs["w_up"], np.float32), 16),
        "w_down": _tile_w(np.asarray(inputs["w_down"], np.float32), 16),
    }
    cosT, sinT = rope_tables()
    shared["ropec"] = cosT
    shared["ropes"] = sinT
    shared["cmasks"] = const_masks()
    maps = []
    for core in range(n_cores):
        b0 = core * NSEQ
        xin = np.empty((NSEQ, D, T), np.float32)
        for s in range(NSEQ):
            xin[s, :, :NCTX] = ctx[b0 + s].T
            xin[s, :, NCTX:] = x[b0 + s].T
        c3 = np.stack([c[b0], c[b0 + 1], np.asarray(inputs["c_ctx"], np.float32)], axis=0)
        m = dict(shared)
        m["xin"] = xin
        m["c3"] = np.ascontiguousarray(c3)
        maps.append(m)
    return maps


_CACHE = {}


def kernel(**inputs):
    n_cores = 8
    if "nc" not in _CACHE:
        _CACHE["nc"] = build_program(Cfg())[0]
    nc = _CACHE["nc"]
    maps = host_inputs(inputs, n_cores)
    res = run_bass_kernel_spmd(nc, maps, core_ids=list(range(n_cores)))
    out = np.empty((16, NX, D), np.float32)
    for core in range(n_cores):
        y = res.results[core]["yout"]
        for s in range(NSEQ):
            out[core * NSEQ + s] = y[s].T
    return out
```

```python
import math
from contextlib import ExitStack

import numpy as np
import concourse.bass as bass
import concourse.mybir as mybir
from concourse.bass_utils import run_bass_kernel_spmd

F32 = mybir.dt.float32
BF16 = mybir.dt.bfloat16
AF = mybir.ActivationFunctionType
ALU = mybir.AluOpType
AX = mybir.AxisListType

D = 2048
NCTX = 256
NX = 2048
T = NCTX + NX
DEPTH = 4
NSEQ = 2
KC = D // 128
DFF = 4 * D
HD = 128
EPS = 1e-6
TGS = [(0, 256)] + [(256 + 512 * i, 512) for i in range(4)]
NFM = 116
W_FM = NFM * 128
W_TM = 256 + 1024 + 32
W_ALL = W_FM + W_TM


class Buf:
    __slots__ = ("name", "lw", "rd", "excl")

    def __init__(self, name=""):
        self.name = name
        self.lw = None
        self.rd = {}
        self.excl = False


class TB:
    def __init__(self, t, name=""):
        self.t = t
        self.b = Buf(name)


COMPUTE = ("pe", "dve", "act", "pool")
NDS = 24


class Prog:
    def __init__(self, nc):
        self.nc = nc
        self.E = {"pe": nc.tensor, "dve": nc.vector, "act": nc.scalar, "pool": nc.gpsimd, "sp": nc.sync}
        self.sems = []
        self.semidx = {}
        for e in COMPUTE:
            self.semidx[e] = len(self.sems)
            self.sems.append(nc.alloc_semaphore("s_" + e))
        self.cnt = {e: 0 for e in COMPUTE}
        self.pend = {e: None for e in COMPUTE}
        self.seen = {e: {} for e in self.E}
        self.dslots = []
        for i in range(NDS):
            self.dslots.append([len(self.sems), 0])
            self.sems.append(nc.alloc_semaphore("d%d" % i))
        self.dnext = 0
        self.nwaits = 0
        self.nops = 0

    def _wait(self, e, toks):
        need = {}
        for t in toks:
            if t is None:
                continue
            te, si, val = t
            if e == "pe" and te == "pe":
                continue
            assert val is not None, "wait on pending token"
            if self.seen[e].get(si, 0) >= val:
                continue
            if need.get(si, 0) < val:
                need[si] = val
        for si, val in need.items():
            self.E[e].wait_ge(self.sems[si], val)
            self.seen[e][si] = val
            self.nwaits += 1

    def _deps(self, e, r, w):
        deps = []
        for b in r:
            if b.lw is not None:
                deps.append(b.lw)
            if b.excl:
                for k, t in b.rd.items():
                    if k != e:
                        deps.append(t)
        for b in w:
            if b.lw is not None:
                deps.append(b.lw)
            for k, t in b.rd.items():
                if k == e and e in COMPUTE:
                    continue
                deps.append(t)
        return deps

    def op(self, e, fn, r=(), w=(), inc=True):
        self._wait(e, self._deps(e, r, w))
        ins = fn(self.E[e])
        self.nops += 1
        if inc:
            self.cnt[e] += 1
            ins.then_inc(self.sems[self.semidx[e]], 1)
            tok = self.pend[e]
            if tok is None:
                tok = [e, self.semidx[e], None]
            tok[2] = self.cnt[e]
            self.pend[e] = None
        else:
            tok = self.pend[e]
            if tok is None:
                tok = self.pend[e] = [e, self.semidx[e], None]
        for b in r:
            b.rd[e] = tok
        for b in w:
            b.lw = tok
            b.rd = {}
        return ins

    def dma(self, q, out, in_, r=(), w=(), **kw):
        deps = self._deps(q, r, w)
        slot = self.dslots[self.dnext]
        self.dnext = (self.dnext + 1) % NDS
        if slot[1] > 0:
            deps.append(["dma", slot[0], slot[1]])
        self._wait(q, deps)
        ins = self.E[q].dma_start(out=out, in_=in_, **kw)
        slot[1] += 16
        ins.then_inc(self.sems[slot[0]], 16)
        self.nops += 1
        tok = ["dma", slot[0], slot[1]]
        for b in r:
            b.rd[("dma", slot[0])] = tok
        for b in w:
            b.lw = tok
            b.rd = {}
        return ins

    def all_toks(self):
        toks = []
        for e in COMPUTE:
            assert self.pend[e] is None, "pending at barrier on " + e
            if self.cnt[e] > 0:
                toks.append(["x", self.semidx[e], self.cnt[e]])
        for slot in self.dslots:
            if slot[1] > 0:
                toks.append(["dma", slot[0], slot[1]])
        return toks

    def barrier(self, engines=None):
        toks = self.all_toks()
        for e in (engines or self.E):
            self._wait(e, toks)


class Cfg:
    def __init__(self, **kw):
        self.layers = DEPTH
        self.nseq = NSEQ
        self.debug = False
        self.stages = "MABCDE"
        self.inject = ()
        self.wdepth = DEPTH
        self.ab_parts = "mft"
        self.mixers = "ANB"
        self.force_ctx = False
        self.nfm = NFM
        self.__dict__.update(kw)


def build_program(cfg):
    nc = bass.Bass("TRN2", target_bir_lowering=False)
    P = Prog(nc)
    print("sbuf bytes remaining at start:", nc.sbuf_bytes_remaining)
    L = cfg.layers
    NS = cfg.nseq
    WD = cfg.wdepth

    def din(name, shape, dt=F32):
        return nc.dram_tensor(name, list(shape), dt, kind="ExternalInput").ap()

    def dscratch(name, shape, dt):
        if name in cfg.inject:
            kind = "ExternalInput"
        elif cfg.debug:
            kind = "ExternalOutput"
        else:
            kind = "Internal"
        return TB(nc.dram_tensor(name, list(shape), dt, kind=kind).ap(), name)

    xin = din("xin", [NSEQ, D, T])
    c3 = din("c3", [3, D])
    w_mod = din("w_mod", [WD, D, 6 * D])
    b_mod = din("b_mod", [WD, 6 * D])
    gvec = din("gvec", [WD, 4, D])
    w_in = din("w_in", [WD, D, W_ALL])
    conv_w = din("conv_w", [WD, 5, 3072])
    a_log = din("a_log", [WD, 16])
    dt_bias = din("dt_bias", [WD, 16])
    g_gdn = din("g_gdn", [WD, 128])
    sink = din("sink", [WD, 8])
    rpb = din("rpb", [WD, 8 * 15 * 31])
    w_branch = din("w_branch", [WD, 3, 4, 128, 8 * 512])
    w_out = din("w_out", [WD, 4, 2, 128, 8 * 512])
    w_up = din("w_up", [WD, 16, 128, 16 * 512])
    w_down = din("w_down", [WD, 4, 4, 128, 16 * 512])
    ropec = din("ropec", [128, T])
    ropes = din("ropes", [128, T])
    cmasks = din("cmasks", [128, 13, 128])
    yout = nc.dram_tensor("yout", [NSEQ, D, NX], F32, kind="ExternalOutput").ap()

    xT = dscratch("xT", [NSEQ, D, T], F32)
    s_fm = dscratch("s_fm", [W_FM, T], BF16)
    s_va = dscratch("s_va", [T, 256], BF16)
    s_vn = dscratch("s_vn", [T, 1024], BF16)
    s_ab = dscratch("s_ab", [T, 32], F32)
    s_o = dscratch("s_o", [3, 1024, T], BF16)
    w_up16 = dscratch("w_up16", [16, 128, 16 * 512], BF16)
    w_down16 = dscratch("w_down16", [4, 4, 128, 16 * 512], BF16)
    w_branch16 = dscratch("w_branch16", [3, 4, 128, 8 * 512], BF16)
    w_out16 = dscratch("w_out16", [4, 2, 128, 8 * 512], BF16)
    OFF_QA, OFF_KA, OFF_QKVB, OFF_ZB, OFF_QN, OFF_KN, OFF_GATE = 0, 8, 10, 34, 42, 50, 58
    NFM_OUT = 106

    gs = ExitStack()

    uniq = {"n": 0}

    def sb(name, shape, dt, stack=None):
        uniq["n"] += 1
        nm = "%s_%d" % (name, uniq["n"])
        return TB((stack or gs).enter_context(nc.sbuf_tensor(nm, list(shape), dt)), nm)

    psum = [TB(gs.enter_context(nc.psum_tensor("ps%d" % i, [128, 512], F32)), "ps%d" % i) for i in range(8)]
    for p_ in psum:
        p_.b.excl = True
    pstate = {"i": 0}

    def ps_next():
        p = psum[pstate["i"]]
        pstate["i"] = (pstate["i"] + 1) % 8
        return p

    class PSPool:
        def __init__(self, idx):
            self.idx = list(idx)
            self.i = 0

        def next(self):
            p = psum[self.idx[self.i]]
            self.i = (self.i + 1) % len(self.idx)
            return p

    ones_f = sb("ones_f", [128, 128], F32)
    ones_b = sb("ones_b", [128, 128], BF16)
    modT = sb("modT", [128, DEPTH, 96, 3], F32)
    gT = sb("gT", [128, DEPTH, 4, 16], F32)
    coef = sb("coef", [128, 6, 16, 3], F32)
    negones = sb("negones", [128, 128], F32)
    P.op("dve", lambda e: e.memset(negones.t[:], -1.0), w=[negones.b])
    epsc = sb("epsc", [128, 4], F32)
    P.op("dve", lambda e: e.memset(epsc.t[:], EPS), w=[epsc.b])
    P.op("dve", lambda e: e.memset(ones_f.t[:], 1.0), w=[ones_f.b])
    P.op("dve", lambda e: e.memset(ones_b.t[:], 1.0), w=[ones_b.b])
    cm = sb("cm", [128, 13, 128], F32)
    for j in range(13):
        P.dma("sp", cm.t[:, j, :], cmasks[:, j, :], w=[cm.b])
    ident = cm.t[:, 0, :]

    def load_T(dst_ap, src_rows, nrows, tag):
        with ExitStack() as st:
            tmp = sb("ldT" + tag, [128, 128], F32, st)
            P.dma("sp", tmp.t[0:nrows, :], src_rows, w=[tmp.b])
            ps = ps_next()
            P.op("pe", lambda e: e.transpose(out=ps.t[:, 0:nrows], in_=tmp.t[0:nrows, :], identity=cm.t[0:nrows, 0, 0:nrows]),
                 r=[tmp.b, cm.b], w=[ps.b])
            P.op("dve", lambda e: e.tensor_copy(out=dst_ap, in_=ps.t[:, 0:nrows]), r=[ps.b], w=[gT.b, modT.b])
            P.barrier()

    gv = gvec.rearrange("l k (c p) -> (l k c) p", p=128)
    gflat = gT.t[:].rearrange("p l k c -> p (l k c)")
    for j in range(0, L * 64, 128):
        nr = min(128, L * 64 - j)
        load_T(gflat[:, j:j + nr], gv[j:j + nr, :], nr, "g%d" % j)

    def stage_M():
        with ExitStack() as st:
            scT = sb("scT", [128, 3, 16], F32, st)
            wblk = [sb("wmblk%d" % i, [128, 16, 512], F32, st) for i in range(2)]
            brow = [sb("brow%d" % i, [1, 512], F32, st) for i in range(2)]
            load_T(scT.t[:].rearrange("p r c -> p (r c)"), c3.rearrange("r (c p) -> (r c) p", p=128), 48, "c3")
            P.op("act", lambda e: e.activation(out=scT.t[:], in_=scT.t[:], func=AF.Silu), r=[scT.b], w=[scT.b])
            it = 0
            for l in range(L):
                wv = w_mod[l].rearrange("(kc p) n -> p kc n", p=128)
                for blk in range(24):
                    wb = wblk[it % 2]
                    br = brow[it % 2]
                    it += 1
                    for j in range(4):
                        P.dma("sp", wb.t[:, 4 * j:4 * j + 4, :], wv[:, 4 * j:4 * j + 4, blk * 512:(blk + 1) * 512],
                              w=[wb.b])
                    P.dma("sp", br.t[:], b_mod[l:l + 1, blk * 512:(blk + 1) * 512], w=[br.b])
                    ps = ps_next()
                    for j in range(4):
                        def mm(e, j=j, wb=wb, br=br, ps=ps):
                            for kc in range(16):
                                e.matmul(ps.t[:, 3 * j:3 * j + 3], lhsT=wb.t[:, kc, j * 128:(j + 1) * 128],
                                         rhs=scT.t[:, :, kc], start=(kc == 0), stop=False)
                            return e.matmul(ps.t[:, 3 * j:3 * j + 3], lhsT=br.t[0:1, j * 128:(j + 1) * 128],
                                            rhs=ones_f.t[0:1, 0:3], start=False, stop=True)
                        P.op("pe", mm, r=[wb.b, br.b, scT.b, ones_f.b], w=[ps.b], inc=(j == 3))
                    P.op("dve", lambda e, ps=ps, l=l, blk=blk: e.tensor_copy(
                        out=modT.t[:, l, blk * 4:(blk + 1) * 4, :],
                        in_=ps.t[:, 0:12].rearrange("p (a b) -> p a b", b=3)), r=[ps.b], w=[modT.b])
            P.barrier()

    def stage_W(l):
        for blk in range(16):
            P.dma("pool", w_up16.t[blk], w_up[l, blk], w=[w_up16.b])
        for ob in range(4):
            for kq in range(4):
                P.dma("pool", w_down16.t[ob, kq], w_down[l, ob, kq], w=[w_down16.b])
        for i in range(3):
            for ob in range(4):
                P.dma("pool", w_branch16.t[i, ob], w_branch[l, i, ob], w=[w_branch16.b])
        for ob in range(4):
            for kh in range(2):
                P.dma("pool", w_out16.t[ob, kh], w_out[l, ob, kh], w=[w_out16.b])

    class Prefetch:
        def __init__(self, bufs, loaders, dist=2):
            self.bufs, self.loaders, self.dist, self.issued = bufs, loaders, dist, 0

        def get(self, k):
            while self.issued < len(self.loaders) and self.issued <= k + self.dist:
                self.loaders[self.issued](self.bufs[self.issued % len(self.bufs)])
                self.issued += 1
            return self.bufs[k % len(self.bufs)]

    def layer_coefs(l):
        def m(idx):
            return modT.t[:, l, idx * 16:(idx + 1) * 16, :]

        def g(k):
            return gT.t[:, l, k, :].unsqueeze(2).broadcast_to([128, 16, 3])
        for (dst, gi, mi, plus1) in ((0, 0, 1, True), (2, 1, 2, False), (3, 2, 4, True), (5, 3, 5, False)):
            if plus1:
                P.op("dve", lambda e, dst=dst, gi=gi, mi=mi: e.scalar_tensor_tensor(
                    out=coef.t[:, dst], in0=m(mi), scalar=1.0, in1=g(gi), op0=ALU.add, op1=ALU.mult),
                    r=[modT.b, gT.b], w=[coef.b])
            else:
                P.op("dve", lambda e, dst=dst, gi=gi, mi=mi: e.tensor_tensor(
                    out=coef.t[:, dst], in0=m(mi), in1=g(gi), op=ALU.mult), r=[modT.b, gT.b], w=[coef.b])
        P.op("dve", lambda e: e.tensor_copy(out=coef.t[:, 1], in_=m(0)), r=[modT.b], w=[coef.b])
        P.op("dve", lambda e: e.tensor_copy(out=coef.t[:, 4], in_=m(3)), r=[modT.b], w=[coef.b])

    def modulate(st, src, s, ia, ib, hT, tgs, xt=None):
        if xt is None:
            xt = [sb("mx%d" % i, [128, 16, 512], F32, st) for i in range(2)]
        sq = [sb("msq%d" % i, [128, 512], F32, st) for i in range(2)]
        rstd = sb("mrstd", [128, 512], F32, st)
        tmp = [sb("mtmp%d" % i, [128, 512], F32, st) for i in range(2)]
        srcv = src.rearrange("(c p) t -> p c t", p=128)
        for gi, (t0, n) in enumerate(tgs):
            row = 2 if t0 < NCTX else s
            x = xt[gi % len(xt)]
            for j in range(4):
                P.dma("sp", x.t[:, 4 * j:4 * j + 4, 0:n], srcv[:, 4 * j:4 * j + 4, t0:t0 + n], r=[xT.b], w=[x.b])
            ps = ps_next()
            for kc in range(16):
                q = sq[kc % 2]
                P.op("act", lambda e, q=q, x=x, kc=kc: e.activation(out=q.t[:, 0:n], in_=x.t[:, kc, 0:n], func=AF.Square),
                     r=[x.b], w=[q.b])
                P.op("pe", lambda e, q=q, kc=kc, ps=ps: e.matmul(ps.t[:, 0:n], lhsT=ones_f.t[:], rhs=q.t[:, 0:n],
                                                                 start=(kc == 0), stop=(kc == 15)),
                     r=[q.b, ones_f.b], w=[ps.b], inc=True)
            P.op("act", lambda e, ps=ps: e.activation(out=rstd.t[:, 0:n], in_=ps.t[:, 0:n], func=AF.Sqrt, bias=epsc.t[:, 0:1],
                                                      scale=1.0 / D), r=[ps.b, epsc.b], w=[rstd.b])
            P.op("dve", lambda e: e.reciprocal(out=rstd.t[:, 0:n], in_=rstd.t[:, 0:n]), r=[rstd.b], w=[rstd.b])
            for kc in range(16):
                tm = tmp[kc % 2]
                P.op("dve", lambda e, tm=tm, x=x, kc=kc: e.scalar_tensor_tensor(
                    out=tm.t[:, 0:n], in0=x.t[:, kc, 0:n], scalar=coef.t[:, ia, kc, row:row + 1], in1=rstd.t[:, 0:n],
                    op0=ALU.mult, op1=ALU.mult), r=[x.b, coef.b, rstd.b], w=[tm.b])
                P.op("act", lambda e, tm=tm, kc=kc: e.activation(
                    out=hT.t[:, kc, t0:t0 + n], in_=tm.t[:, 0:n], func=AF.Identity,
                    bias=coef.t[:, ib, kc, row:row + 1], scale=1.0), r=[tm.b, coef.b], w=[hT.b])

    def stage_AB(l, s):
        with ExitStack() as st:
            hT = sb("hT", [128, 16, T], BF16, st)
            with ExitStack() as st2:
                src = xin[s] if l == 0 else xT.t[s]
                modulate(st2, src, s, 0, 1, hT, TGS)
                P.barrier()
            if "f" not in cfg.ab_parts:
                return
            wblk = [sb("wblk%d" % i, [128, 16, 512], BF16, st) for i in range(3)]
            stg = [sb("stg%d" % i, [128, T], BF16, st) for i in range(4)]
            cosT = sb("cosT", [128, T], F32, st)
            sinT = sb("sinT", [128, T], F32, st)
            r1 = [sb("r1_%d" % i, [128, 512], F32, st) for i in range(2)]
            r2 = [sb("r2_%d" % i, [128, 512], F32, st) for i in range(2)]
            for j in range(3):
                a, b_ = j * 768, (j + 1) * 768
                P.dma("sp", cosT.t[:, a:b_], ropec[:, a:b_], w=[cosT.b])
                P.dma("sp", sinT.t[:, a:b_], ropes[:, a:b_], w=[sinT.b])
            wv = w_in[l].rearrange("(kc p) n -> p kc n", p=128)
            nblk = (W_ALL + 511) // 512
            blkbuf = {}

            def load_blk(bi):
                wb = wblk[bi % 3]
                c0 = bi * 512
                n = min(512, W_ALL - c0)
                for j in range(2):
                    P.dma("pool", wb.t[:, 8 * j:8 * j + 8, 0:n], wv[:, 8 * j:8 * j + 8, c0:c0 + n], w=[wb.b])
                blkbuf[bi] = wb

            load_blk(0)
            load_blk(1)
            si = 0
            evq = 0
            oc_out = 0
            ci = 0
            while ci < cfg.nfm:
                bi = ci // 4
                if ci % 4 == 0 and bi + 2 < nblk:
                    load_blk(bi + 2)
                wb = blkbuf[bi]
                rope = ci < 20
                kind = "plain"
                if ci >= 20 + 24 and ci < 20 + 32:
                    kind = "silu"
                if ci >= 20 + 48:
                    kind = "sigmoid"
                sg = stg[si % 4]
                si += 1
                for gi, (t0, n) in enumerate(TGS):
                    pa = ps_next()

                    def mm(e, pa=pa, wb=wb, cc=ci % 4, t0=t0, n=n):
                        for kc in range(16):
                            ins = e.matmul(pa.t[:, 0:n], lhsT=wb.t[:, kc, cc * 128:(cc + 1) * 128],
                                           rhs=hT.t[:, kc, t0:t0 + n], start=(kc == 0), stop=(kc == 15))
                        return ins
                    P.op("pe", mm, r=[wb.b, hT.b], w=[pa.b])
                    if rope:
                        pb = ps_next()
                        P.op("pe", lambda e, pb=pb, wb=wb, cc=ci % 4 + 1, t0=t0, n=n: mm(e, pb, wb, cc, t0, n),
                             r=[wb.b, hT.b], w=[pb.b])
                        a1 = r1[gi % 2]
                        a2 = r2[gi % 2]
                        P.op("dve", lambda e, a1=a1, pa=pa, t0=t0, n=n: e.tensor_tensor(
                            out=a1.t[:, 0:n], in0=pa.t[:, 0:n], in1=cosT.t[:, t0:t0 + n], op=ALU.mult),
                            r=[pa.b, cosT.b], w=[a1.b])
                        P.op("dve", lambda e, a2=a2, pb=pb, t0=t0, n=n: e.tensor_tensor(
                            out=a2.t[:, 0:n], in0=pb.t[:, 0:n], in1=sinT.t[:, t0:t0 + n], op=ALU.mult),
                            r=[pb.b, sinT.b], w=[a2.b])
                        P.op("pool", lambda e, a1=a1, a2=a2, sg=sg, t0=t0, n=n: e.tensor_tensor(
                            out=sg.t[:, t0:t0 + n], in0=a1.t[:, 0:n], in1=a2.t[:, 0:n], op=ALU.add),
                            r=[a1.b, a2.b], w=[sg.b])
                    elif kind == "plain":
                        evq += 1
                        if evq % 2:
                            P.op("dve", lambda e, pa=pa, sg=sg, t0=t0, n=n: e.tensor_copy(
                                out=sg.t[:, t0:t0 + n], in_=pa.t[:, 0:n]), r=[pa.b], w=[sg.b])
                        else:
                            P.op("act", lambda e, pa=pa, sg=sg, t0=t0, n=n: e.activation(
                                out=sg.t[:, t0:t0 + n], in_=pa.t[:, 0:n], func=AF.Copy), r=[pa.b], w=[sg.b])
                    else:
                        fn = AF.Silu if kind == "silu" else AF.Sigmoid
                        P.op("act", lambda e, pa=pa, sg=sg, t0=t0, n=n, fn=fn: e.activation(
                            out=sg.t[:, t0:t0 + n], in_=pa.t[:, 0:n], func=fn), r=[pa.b], w=[sg.b])
                P.dma("sp", s_fm.t[oc_out * 128:(oc_out + 1) * 128, :], sg.t[:], r=[sg.b], w=[s_fm.b])
                oc_out += 1
                ci += 2 if rope else 1
            if 't' not in cfg.ab_parts:
                P.barrier()
                return
            tstg = [sb("tstg%d" % i, [128, 512], BF16, st) for i in range(3)]
            tstf = [sb("tstf%d" % i, [128, 32], F32, st) for i in range(2)]
            ti = 0
            for bi in range(NFM // 4, nblk):
                if bi + 2 < nblk and bi + 2 not in blkbuf:
                    load_blk(bi + 2)
                wb = blkbuf[bi]
                c0 = bi * 512 - W_FM
                n = min(512, W_TM - c0)
                for tt in range(T // 128):
                    pa = ps_next()

                    def mmt(e, pa=pa, wb=wb, tt=tt, n=n):
                        for kc in range(16):
                            ins = e.matmul(pa.t[:, 0:n], lhsT=hT.t[:, kc, tt * 128:(tt + 1) * 128],
                                           rhs=wb.t[:, kc, 0:n], start=(kc == 0), stop=(kc == 15))
                        return ins
                    P.op("pe", mmt, r=[wb.b, hT.b], w=[pa.b])
                    segs = []
                    for (nm, a, b_) in (("va", 0, 256), ("vn", 256, 1280), ("ab", 1280, 1312)):
                        lo, hi = max(a, c0), min(b_, c0 + n)
                        if lo < hi:
                            segs.append((nm, lo - a, hi - a, lo - c0, hi - c0))
                    for (nm, d0, d1, p0, p1) in segs:
                        if nm == "ab":
                            tf = tstf[ti % 2]
                            if tt % 2:
                                P.op("act", lambda e, tf=tf, pa=pa, p0=p0, p1=p1: e.activation(
                                    out=tf.t[:, 0:32], in_=pa.t[:, p0:p1], func=AF.Copy), r=[pa.b], w=[tf.b])
                            else:
                                P.op("dve", lambda e, tf=tf, pa=pa, p0=p0, p1=p1: e.tensor_copy(
                                    out=tf.t[:, 0:32], in_=pa.t[:, p0:p1]), r=[pa.b], w=[tf.b])
                            P.dma("sp", s_ab.t[tt * 128:(tt + 1) * 128, :], tf.t[:], r=[tf.b], w=[s_ab.b])
                        else:
                            ts_ = tstg[ti % 3]
                            ti += 1
                            eng = "act" if tt % 2 else "dve"
                            if eng == "act":
                                P.op("act", lambda e, ts_=ts_, pa=pa, p0=p0, p1=p1: e.activation(
                                    out=ts_.t[:, 0:p1 - p0], in_=pa.t[:, p0:p1], func=AF.Copy), r=[pa.b], w=[ts_.b])
                            else:
                                P.op("dve", lambda e, ts_=ts_, pa=pa, p0=p0, p1=p1: e.tensor_copy(
                                    out=ts_.t[:, 0:p1 - p0], in_=pa.t[:, p0:p1]), r=[pa.b], w=[ts_.b])
                            dst = s_va if nm == "va" else s_vn
                            P.dma("sp", dst.t[tt * 128:(tt + 1) * 128, d0:d1], ts_.t[:, 0:p1 - p0], r=[ts_.b], w=[dst.b])
            P.barrier()

    def post_norm_residual(st, l, s, ig, mT, t0, n, last_out, tag, xt=None):
        row = 2 if t0 < NCTX else s
        sq = [sb(tag + "sq%d" % i, [128, 512], F32, st) for i in range(2)]
        rstd = sb(tag + "rstd", [128, 512], F32, st)
        dstv = xT.t[s].rearrange("(c p) t -> p c t", p=128)
        if xt is None:
            xt = sb(tag + "xt", [128, 16, 512], F32, st)
            src = (xin[s] if (l == 0 and tag == "D") else xT.t[s]).rearrange("(c p) t -> p c t", p=128)
            for j in range(4):
                P.dma("sp", xt.t[:, 4 * j:4 * j + 4, 0:n], src[:, 4 * j:4 * j + 4, t0:t0 + n], r=[xT.b], w=[xt.b])
        ps = ps_next()
        for kc in range(16):
            q = sq[kc % 2]
            P.op("act", lambda e, q=q, kc=kc: e.activation(out=q.t[:, 0:n], in_=mT.t[:, kc, 0:n], func=AF.Square),
                 r=[mT.b], w=[q.b])
            P.op("pe", lambda e, q=q, kc=kc: e.matmul(ps.t[:, 0:n], lhsT=ones_f.t[:], rhs=q.t[:, 0:n],
                                                      start=(kc == 0), stop=(kc == 15)), r=[q.b], w=[ps.b])
        P.op("act", lambda e: e.activation(out=rstd.t[:, 0:n], in_=ps.t[:, 0:n], func=AF.Sqrt, bias=epsc.t[:, 0:1],
                                           scale=1.0 / D), r=[ps.b, epsc.b], w=[rstd.b])
        P.op("dve", lambda e: e.reciprocal(out=rstd.t[:, 0:n], in_=rstd.t[:, 0:n]), r=[rstd.b], w=[rstd.b])
        for kc in range(16):
            P.op("pool", lambda e, kc=kc: e.tensor_tensor(out=mT.t[:, kc, 0:n], in0=mT.t[:, kc, 0:n], in1=rstd.t[:, 0:n],
                                                          op=ALU.mult), r=[mT.b, rstd.b], w=[mT.b])
            P.op("dve", lambda e, kc=kc: e.scalar_tensor_tensor(
                out=xt.t[:, kc, 0:n], in0=mT.t[:, kc, 0:n], scalar=coef.t[:, ig, kc, row:row + 1], in1=xt.t[:, kc, 0:n],
                op0=ALU.mult, op1=ALU.add), r=[mT.b, coef.b, xt.b], w=[xt.b])
        for j in range(4):
            P.dma("sp", dstv[:, 4 * j:4 * j + 4, t0:t0 + n], xt.t[:, 4 * j:4 * j + 4, 0:n], r=[xt.b], w=[xT.b])
        if last_out and t0 >= NCTX:
            yv = yout[s].rearrange("(c p) t -> p c t", p=128)
            for j in range(4):
                P.dma("sp", yv[:, 4 * j:4 * j + 4, t0 - NCTX:t0 - NCTX + n], xt.t[:, 4 * j:4 * j + 4, 0:n], r=[xt.b])

    def stage_D(l, s, tgs):
        gate_v = s_fm.t[OFF_GATE * 128:(OFF_GATE + 48) * 128, :].rearrange("(i c p) t -> p i c t", p=128, i=3)
        for (t0, n) in tgs:
            with ExitStack() as st:
                oT = sb("oT", [128, 3, 8, 512], BF16, st)
                yT = sb("yT", [128, 16, 512], BF16, st)
                mT = sb("mT", [128, 16, 512], F32, st)
                gtb = [sb("gt%d" % i, [128, 4, 512], BF16, st) for i in range(3)]
                wbb = [sb("wbb%d" % i, [128, 8, 512], BF16, st) for i in range(3)]
                acc = sb("accD", [128, 4, 512], F32, st)
                tm2 = [sb("tm2%d" % i, [128, 512], F32, st) for i in range(2)]
                ov = s_o.t.rearrange("i (c p) t -> p i c t", p=128)
                for i in range(3):
                    for j in range(2):
                        P.dma("sp", oT.t[:, i, 4 * j:4 * j + 4, 0:n], ov[:, i, 4 * j:4 * j + 4, t0:t0 + n], r=[s_o.b], w=[oT.b])
                wi = 0
                loaders = []
                for ob in range(4):
                    for i in range(3):
                        loaders.append(lambda wb, i=i, ob=ob: P.dma("pool", wb.t[:].rearrange("p a b -> p (a b)"), w_branch16.t[i, ob],
                                                                    r=[w_branch16.b], w=[wb.b]))
                for ob in range(4):
                    for kh in range(2):
                        loaders.append(lambda wb, kh=kh, ob=ob: P.dma("pool", wb.t[:].rearrange("p a b -> p (a b)"), w_out16.t[ob, kh],
                                                                      r=[w_out16.b], w=[wb.b]))
                pf = Prefetch(wbb, loaders, 2)
                for ob in range(4):
                    for i in range(3):
                        wb = pf.get(wi)
                        g = gtb[wi % 3]
                        wi += 1
                        P.dma("sp", g.t[:, :, 0:n], gate_v[:, i, 4 * ob:4 * ob + 4, t0:t0 + n], r=[s_fm.b], w=[g.b])
                        for oc in range(4):
                            pa = ps_next()

                            def mm(e, pa=pa, wb=wb, i=i, oc=oc):
                                for kc in range(8):
                                    ins = e.matmul(pa.t[:, 0:n], lhsT=wb.t[:, kc, oc * 128:(oc + 1) * 128], rhs=oT.t[:, i, kc, 0:n],
                                                   start=(kc == 0), stop=(kc == 7))
                                return ins
                            P.op("pe", mm, r=[wb.b, oT.b], w=[pa.b])
                            if i == 0:
                                P.op("dve", lambda e, pa=pa, g=g, oc=oc: e.tensor_tensor(
                                    out=acc.t[:, oc, 0:n], in0=pa.t[:, 0:n], in1=g.t[:, oc, 0:n], op=ALU.mult),
                                    r=[pa.b, g.b], w=[acc.b])
                            else:
                                t2 = tm2[oc % 2]
                                P.op("dve", lambda e, pa=pa, t2=t2, g=g, oc=oc: e.tensor_tensor(
                                    out=t2.t[:, 0:n], in0=pa.t[:, 0:n], in1=g.t[:, oc, 0:n], op=ALU.mult),
                                    r=[pa.b, g.b], w=[t2.b])
                                if i == 1:
                                    P.op("dve", lambda e, t2=t2, oc=oc: e.tensor_tensor(
                                        out=acc.t[:, oc, 0:n], in0=acc.t[:, oc, 0:n], in1=t2.t[:, 0:n], op=ALU.add),
                                        r=[acc.b, t2.b], w=[acc.b])
                                else:
                                    P.op("dve", lambda e, t2=t2, oc=oc, ob=ob: e.tensor_tensor(
                                        out=yT.t[:, 4 * ob + oc, 0:n], in0=acc.t[:, oc, 0:n], in1=t2.t[:, 0:n], op=ALU.add),
                                        r=[acc.b, t2.b], w=[yT.b])
                pacc = PSPool([0, 1, 2, 3])
                for ob in range(4):
                    pas = [pacc.next() for _ in range(4)]
                    for kh in range(2):
                        wb = pf.get(wi)
                        wi += 1
                        for oc in range(4):
                            pa = pas[oc]

                            def mm2(e, pa=pa, wb=wb, oc=oc, kh=kh):
                                for kc in range(8):
                                    ins = e.matmul(pa.t[:, 0:n], lhsT=wb.t[:, kc, oc * 128:(oc + 1) * 128], rhs=yT.t[:, 8 * kh + kc, 0:n],
                                                   start=(kh == 0 and kc == 0), stop=(kh == 1 and kc == 7))
                                return ins
                            P.op("pe", mm2, r=[wb.b, yT.b], w=[pa.b])
                    for oc in range(4):
                        P.op("act", lambda e, pa=pas[oc], oc=oc, ob=ob: e.activation(out=mT.t[:, 4 * ob + oc, 0:n], in_=pa.t[:, 0:n],
                                                                                    func=AF.Copy), r=[pas[oc].b], w=[mT.b])
                post_norm_residual(st, l, s, 2, mT, t0, n, False, "D")
                P.barrier()

    def stage_E(l, s, tgs):
        last = (l == L - 1)
        for (t0, n) in tgs:
            with ExitStack() as st:
                aT = sb("aT", [128, 64, 512], BF16, st)
                xt = sb("xtE", [128, 16, 512], F32, st)
                with ExitStack() as st2:
                    hT = sb("hE", [128, 16, 512], BF16, st2)
                    modulate_tg(st2, xT.t[s], s, 3, 4, hT, t0, n, [xt])
                    wub = [sb("wub%d" % i, [128, 16, 512], BF16, st2) for i in range(3)]
                    rl = [sb("rl%d" % i, [128, 512], F32, st2) for i in range(2)]
                    pfu = Prefetch(wub, [(lambda wb, blk=blk: P.dma("pool", wb.t[:].rearrange("p a b -> p (a b)"), w_up16.t[blk],
                                                                    r=[w_up16.b], w=[wb.b])) for blk in range(16)], 2)
                    for blk in range(16):
                        wb = pfu.get(blk)
                        for cc in range(4):
                            fc = blk * 4 + cc
                            pa = ps_next()

                            def mm(e, pa=pa, wb=wb, cc=cc):
                                for kc in range(16):
                                    ins = e.matmul(pa.t[:, 0:n], lhsT=wb.t[:, kc, cc * 128:(cc + 1) * 128],
                                                   rhs=hT.t[:, kc, 0:n], start=(kc == 0), stop=(kc == 15))
                                return ins
                            P.op("pe", mm, r=[wb.b, hT.b], w=[pa.b])
                            r_ = rl[fc % 2]
                            P.op("act", lambda e, pa=pa, r_=r_: e.activation(out=r_.t[:, 0:n], in_=pa.t[:, 0:n], func=AF.Relu),
                                 r=[pa.b], w=[r_.b])
                            P.op("dve", lambda e, r_=r_, fc=fc: e.tensor_tensor(out=aT.t[:, fc, 0:n], in0=r_.t[:, 0:n],
                                                                            in1=r_.t[:, 0:n], op=ALU.mult),
                                 r=[r_.b], w=[aT.b])
                    P.barrier()
                with ExitStack() as st2:
                    mT = sb("mE", [128, 16, 512], F32, st2)
                    wdb = [sb("wdb%d" % i, [128, 16, 512], BF16, st2) for i in range(3)]
                    pacc = PSPool([0, 1, 2, 3])
                    wi = 0
                    pfd = Prefetch(wdb, [(lambda wb, ob=ob, kq=kq: P.dma("pool", wb.t[:].rearrange("p a b -> p (a b)"), w_down16.t[ob, kq],
                                                                         r=[w_down16.b], w=[wb.b])) for ob in range(4) for kq in range(4)], 2)
                    for ob in range(4):
                        pas = [pacc.next() for _ in range(4)]
                        for kq in range(4):
                            wb = pfd.get(wi)
                            wi += 1
                            for oc in range(4):
                                pa = pas[oc]

                                def mm2(e, pa=pa, wb=wb, oc=oc, kq=kq):
                                    for kc in range(16):
                                        ins = e.matmul(pa.t[:, 0:n], lhsT=wb.t[:, kc, oc * 128:(oc + 1) * 128], rhs=aT.t[:, 16 * kq + kc, 0:n],
                                                       start=(kq == 0 and kc == 0), stop=(kq == 3 and kc == 15))
                                    return ins
                                P.op("pe", mm2, r=[wb.b, aT.b], w=[pa.b])
                        for oc in range(4):
                            P.op("act", lambda e, pa=pas[oc], oc=oc, ob=ob: e.activation(out=mT.t[:, 4 * ob + oc, 0:n], in_=pa.t[:, 0:n],
                                                                                        func=AF.Copy), r=[pas[oc].b], w=[mT.b])
                    post_norm_residual(st2, l, s, 5, mT, t0, n, last, "E", xt)
                    P.barrier()

    def modulate_tg(st, src, s, ia, ib, hT, t0, n, xt=None):
        class V:
            pass
        hv = TB(None)
        hv.b = hT.b

        class _T:
            def __getitem__(self, key):
                p, kc, sl = key
                return hT.t[p, kc, sl.start - t0:sl.stop - t0]
        hv.t = _T()
        modulate(st, src, s, ia, ib, hv, [(t0, n)], xt)


    SCALE = 1.0 / math.sqrt(HD)

    def stage_C1(l, s, with_ctx):
        with ExitStack() as st:
            qT = sb("qaT", [128, 8, T], BF16, st)
            kT = sb("kaT", [128, 2, T], BF16, st)
            va = sb("vaS", [128, 18, 256], BF16, st)
            oT = sb("oaT", [128, 8, T], BF16, st)
            esk = sb("esk", [128, 8], F32, st)
            eskB = sb("eskB", [128, 8, 128], F32, st)
            mk = sb("mkA", [128, 2, 4, 128], BF16, st)
            pTs = [sb("pTa%d" % i, [128, 512], BF16, st) for i in range(3)]
            den = sb("denA", [128, 512], F32, st)
            fmv = s_fm.t.rearrange("(c p) t -> p c t", p=128)
            for h in range(8):
                P.dma("sp", qT.t[:, h, :], fmv[:, OFF_QA + h, :], r=[s_fm.b], w=[qT.b])
            for h in range(2):
                P.dma("sp", kT.t[:, h, :], fmv[:, OFF_KA + h, :], r=[s_fm.b], w=[kT.b])
            vav = s_va.t.rearrange("(tt p) c -> p tt c", p=128)
            for j in range(3):
                P.dma("sp", va.t[:, 6 * j:6 * j + 6, :], vav[:, 6 * j:6 * j + 6, :], r=[s_va.b], w=[va.b])
            P.dma("sp", esk.t[:], sink[l:l + 1, :].partition_broadcast(128), w=[esk.b])
            P.op("act", lambda e: e.activation(out=esk.t[:], in_=esk.t[:], func=AF.Exp), r=[esk.b], w=[esk.b])
            P.op("dve", lambda e: e.tensor_copy(out=eskB.t[:], in_=esk.t[:].unsqueeze(2).broadcast_to([128, 8, 128])),
                 r=[esk.b], w=[eskB.b])
            for j in range(2):
                P.op("dve", lambda e, j=j: e.tensor_copy(out=mk.t[:, j], in_=cm.t[:, 1 + j, :].unsqueeze(1).broadcast_to([128, 4, 128])),
                     r=[cm.b], w=[mk.b])
            qblocks = ([(128 * j, None) for j in range(2)] if with_ctx else []) + [(256 + 128 * i, i) for i in range(16)]
            pacc = PSPool([0, 1, 2, 3])
            pss = PSPool([4, 5, 6, 7])
            pi = 0
            for g in range(2):
                for (t0, xi) in qblocks:
                    keys = [(0, None), (128, None)]
                    if xi is not None:
                        if xi > 0:
                            keys.append((256 + 128 * (xi - 1), 0))
                        keys.append((256 + 128 * xi, None))
                        if xi < 15:
                            keys.append((256 + 128 * (xi + 1), 1))
                    ps_o = pacc.next()
                    ps_d = pacc.next()
                    pss_of = {}

                    def emit_scores_a(ii):
                        k0 = keys[ii][0]
                        ps_s = pss.next()
                        pss_of[ii] = ps_s
                        P.op("pe", lambda e: e.matmul(
                            ps_s.t[:, 0:512].rearrange("p (h q) -> p h q", h=4), lhsT=kT.t[:, g, k0:k0 + 128],
                            rhs=qT.t[:, 4 * g:4 * g + 4, t0:t0 + 128], start=True, stop=True), r=[kT.b, qT.b], w=[ps_s.b])

                    for ii in range(min(2, len(keys))):
                        emit_scores_a(ii)
                    for idx, (k0, mki) in enumerate(keys):
                        if idx + 2 < len(keys):
                            emit_scores_a(idx + 2)
                        ps_s = pss_of.pop(idx)
                        pT = pTs[pi % 3]
                        pi += 1
                        P.op("act", lambda e, ps_s=ps_s, pT=pT: e.activation(out=pT.t[:], in_=ps_s.t[:], func=AF.Exp, scale=SCALE),
                             r=[ps_s.b], w=[pT.b])
                        if mki is not None:
                            P.op("pool", lambda e, pT=pT, mki=mki: e.tensor_tensor(
                                out=pT.t[:], in0=pT.t[:], in1=mk.t[:, mki].rearrange("p h q -> p (h q)"), op=ALU.mult),
                                r=[pT.b, mk.b], w=[pT.b])
                        first, lastk = idx == 0, idx == len(keys) - 1
                        P.op("pe", lambda e, pT=pT: e.matmul(ps_d.t[:], lhsT=ones_b.t[:], rhs=pT.t[:], start=first, stop=lastk),
                             r=[pT.b, ones_b.b], w=[ps_d.b])
                        P.op("pe", lambda e, pT=pT, k0=k0: e.matmul(ps_o.t[:], lhsT=va.t[:, k0 // 128, g * 128:(g + 1) * 128],
                                                                    rhs=pT.t[:], start=first, stop=lastk),
                             r=[pT.b, va.b], w=[ps_o.b])
                    P.op("dve", lambda e: e.tensor_tensor(out=den.t[:], in0=ps_d.t[:],
                                                          in1=eskB.t[:, 4 * g:4 * g + 4, :].rearrange("p h q -> p (h q)"), op=ALU.add),
                         r=[ps_d.b, eskB.b], w=[den.b])
                    P.op("dve", lambda e: e.reciprocal(out=den.t[:], in_=den.t[:]), r=[den.b], w=[den.b])
                    P.op("dve", lambda e: e.tensor_tensor(out=oT.t[:, 4 * g:4 * g + 4, t0:t0 + 128],
                                                          in0=ps_o.t[:].rearrange("p (h q) -> p h q", h=4),
                                                          in1=den.t[:].rearrange("p (h q) -> p h q", h=4), op=ALU.mult),
                         r=[ps_o.b, den.b], w=[oT.b])
            ov = s_o.t[0].rearrange("(c p) t -> p c t", p=128)
            for h in range(8):
                P.dma("sp", ov[:, h, :], oT.t[:, h, :], r=[oT.b], w=[s_o.b])
            P.barrier()

    rpbpad = dscratch("rpbpad", [120, 127], F32)
    dbgC = dscratch("dbgC", [64, 7680], F32)

    def na_valid(r, kr):
        rs = min(max(r - 4, 0), 24)
        return rs <= kr <= rs + 7

    def stage_C2(l, s, with_ctx):
        with ExitStack() as st:
            Ctab = sb("Ctab", [64, 8, 15, 64], F32, st)
            zt = sb("zt", [120, 127], F32, st)
            P.op("dve", lambda e: e.memset(zt.t[:], 0.0), w=[zt.b])
            P.dma("sp", rpbpad.t[:, :], zt.t[:], r=[zt.b], w=[rpbpad.b])
            P.dma("sp", rpbpad.t[:, 48:79], rpb[l].rearrange("(r c) -> r c", c=31), w=[rpbpad.b])
            for h in range(8):
                src = bass.AP(tensor=rpbpad.t.tensor, offset=rpbpad.t.offset + h * 15 * 127, ap=[[1, 64], [127, 15], [1, 64]])
                P.dma("sp", Ctab.t[:, h], src, r=[rpbpad.b], w=[Ctab.b])
            P.op("dve", lambda e: e.tensor_tensor(
                out=Ctab.t[:].rearrange("p h a q -> p (h a) q"), in0=Ctab.t[:].rearrange("p h a q -> p (h a) q"),
                in1=cm.t[0:64, 3, 0:64].unsqueeze(1).broadcast_to([64, 120, 64]), op=ALU.add), r=[Ctab.b, cm.b], w=[Ctab.b])
            if cfg.debug:
                for h in range(8):
                    P.dma("sp", dbgC.t[:, h * 960:(h + 1) * 960], Ctab.t[:, h].rearrange("p a q -> p (a q)"), r=[Ctab.b], w=[dbgC.b])
            ctab_ap = Ctab.t[:]
            pstep = ctab_ap.ap[0][0]
            qh = [sb("qnh%d" % i, [128, T], BF16, st) for i in range(2)]
            kh = [sb("knh%d" % i, [128, T], BF16, st) for i in range(2)]
            v64 = [sb("v64_%d" % i, [128, 36, 128], BF16, st) for i in range(2)]
            pTr = [sb("pTr%d" % i, [128, 512], BF16, st) for i in range(3)]
            for t_ in v64 + pTr:
                P.op("pool", lambda e, t_=t_: e.memset(t_.t[:], 0.0), w=[t_.b])
            vcx = [sb("vcx%d" % i, [128, 2, 128], BF16, st) for i in range(2)]
            oh = [sb("onh%d" % i, [128, T], BF16, st) for i in range(2)]
            pTs = [sb("pTn%d" % i, [128, 512], BF16, st) for i in range(3)]
            sbs = [sb("sbn%d" % i, [64, 512], F32, st) for i in range(3)]
            den = sb("denN", [128, 512], F32, st)
            fmv = s_fm.t.rearrange("(c p) t -> p c t", p=128)
            v64v = s_vn.t.rearrange("(c p) d -> p c d", p=64)
            vcv = s_vn.t[0:256, :].rearrange("(c p) d -> p c d", p=128)
            ov = s_o.t[2].rearrange("(c p) t -> p c t", p=128)
            pi = 0
            bi_ = 0
            pacc = PSPool([0, 1, 2, 3])
            pss = PSPool([4, 5, 6, 7])
            for h in range(8):
                q_, k_, v_, vc_, o_ = qh[h % 2], kh[h % 2], v64[h % 2], vcx[h % 2], oh[h % 2]
                P.dma("sp", q_.t[:], fmv[:, OFF_QN + h, :], r=[s_fm.b], w=[q_.b])
                P.dma("sp", k_.t[:], fmv[:, OFF_KN + h, :], r=[s_fm.b], w=[k_.b])
                for j in range(3):
                    P.dma("sp", v_.t[0:64, 12 * j:12 * j + 12, :], v64v[:, 12 * j:12 * j + 12, h * 128:(h + 1) * 128],
                          r=[s_vn.b], w=[v_.b])
                P.dma("sp", vc_.t[:], vcv[:, :, h * 128:(h + 1) * 128], r=[s_vn.b], w=[vc_.b])
                groups = ([("c", 0, 256)] if with_ctx else []) + [("x", 256 + 512 * G, 512) for G in range(4)]
                for (gk, t0, n) in groups:
                    ps_o = pacc.next()
                    ps_d = pacc.next()
                    items = [("ctx", 0)]
                    if gk == "x":
                        G = (t0 - 256) // 512
                        for kr in range(32):
                            rr = [r for r in range(8 * G, 8 * G + 8) if na_valid(r, kr)]
                            if rr:
                                items.append(("row", kr, rr[0], rr[-1]))
                    items.append(("ctx", 1))
                    LOOK = 2
                    pss_of = {}

                    def emit_scores(ii):
                        it = items[ii]
                        ps_s = pss.next()
                        pss_of[ii] = ps_s
                        if it[0] == "ctx":
                            cb = it[1]
                            P.op("pe", lambda e: e.matmul(ps_s.t[:, 0:n], lhsT=k_.t[:, cb * 128:(cb + 1) * 128], rhs=q_.t[:, t0:t0 + n],
                                                          start=True, stop=True), r=[k_.b, q_.b], w=[ps_s.b])
                        else:
                            _, kr, rlo, rhi = it
                            nr = rhi - rlo + 1
                            c0 = (rlo - 8 * G) * 64
                            nc_ = nr * 64
                            kt0 = 256 + 64 * kr
                            P.op("pe", lambda e: e.matmul(ps_s.t[0:64, 0:nc_], lhsT=k_.t[:, kt0:kt0 + 64],
                                                          rhs=q_.t[:, t0 + c0:t0 + c0 + nc_], start=True, stop=True),
                                 r=[k_.b, q_.b], w=[ps_s.b])

                    for ii in range(min(LOOK, len(items))):
                        emit_scores(ii)
                    for ii, it in enumerate(items):
                        if ii + LOOK < len(items):
                            emit_scores(ii + LOOK)
                        first, lastk = ii == 0, ii == len(items) - 1
                        pT = pTs[pi % 3]
                        pi += 1
                        ps_s = pss_of.pop(ii)
                        if it[0] == "ctx":
                            cb = it[1]
                            P.op("act", lambda e: e.activation(out=pT.t[:, 0:n], in_=ps_s.t[:, 0:n], func=AF.Exp, scale=SCALE),
                                 r=[ps_s.b], w=[pT.b])
                            P.op("pe", lambda e: e.matmul(ps_d.t[:, 0:n], lhsT=ones_b.t[:], rhs=pT.t[:, 0:n], start=first, stop=lastk),
                                 r=[pT.b, ones_b.b], w=[ps_d.b])
                            P.op("pe", lambda e: e.matmul(ps_o.t[:, 0:n], lhsT=vc_.t[:, cb, :], rhs=pT.t[:, 0:n], start=first, stop=lastk),
                                 r=[pT.b, vc_.b], w=[ps_o.b])
                        else:
                            _, kr, rlo, rhi = it
                            pT = pTr[pi % 3]
                            nr = rhi - rlo + 1
                            c0 = (rlo - 8 * G) * 64
                            nc_ = nr * 64
                            sbt = sbs[bi_ % 3]
                            bi_ += 1
                            a_start = 7 + kr - rlo
                            bias = bass.AP(tensor=ctab_ap.tensor, offset=ctab_ap.offset + (h * 15 + a_start) * 64 + 63,
                                           ap=[[pstep, 64], [-64, nr], [-1, 64]])
                            P.op("dve", lambda e: e.scalar_tensor_tensor(
                                out=sbt.t[:, 0:nc_].rearrange("p (r q) -> p r q", q=64),
                                in0=ps_s.t[0:64, 0:nc_].rearrange("p (r q) -> p r q", q=64), scalar=SCALE, in1=bias,
                                op0=ALU.mult, op1=ALU.add), r=[ps_s.b, Ctab.b], w=[sbt.b])
                            P.op("act", lambda e: e.activation(out=pT.t[0:64, 0:nc_], in_=sbt.t[:, 0:nc_], func=AF.Exp),
                                 r=[sbt.b], w=[pT.b])
                            P.op("pe", lambda e: e.matmul(ps_d.t[:, c0:c0 + nc_], lhsT=ones_b.t[:, :], rhs=pT.t[:, 0:nc_],
                                                          start=False, stop=False), r=[pT.b, ones_b.b], w=[ps_d.b])
                            P.op("pe", lambda e: e.matmul(ps_o.t[:, c0:c0 + nc_], lhsT=v_.t[:, 4 + kr, :], rhs=pT.t[:, 0:nc_],
                                                          start=False, stop=False), r=[pT.b, v_.b], w=[ps_o.b])
                    P.op("dve", lambda e: e.reciprocal(out=den.t[:, 0:n], in_=ps_d.t[:, 0:n]), r=[ps_d.b], w=[den.b])
                    P.op("dve", lambda e: e.tensor_tensor(out=o_.t[:, t0:t0 + n], in0=ps_o.t[:, 0:n], in1=den.t[:, 0:n], op=ALU.mult),
                         r=[ps_o.b, den.b], w=[o_.b])
                P.dma("sp", ov[:, h, :], o_.t[:], r=[o_.b], w=[s_o.b])
            P.barrier()


    NT_ = T // 128
    TGRP = [(0, 4), (4, 4), (8, 4), (12, 4), (16, 2)]

    def stage_C3(l, s):
        with ExitStack() as st:
            abT = sb("abT", [128, NT_, 32], F32, st)
            gG = sb("gG", [128, NT_, 16], F32, st)
            bG = sb("bG", [128, NT_, 16], F32, st)
            nA = sb("nA", [128, 16], F32, st)
            dtb = sb("dtb", [128, 16], F32, st)
            cwT = sb("cwT", [128, 5, 24], F32, st)
            gg = sb("ggdn", [128, 1], F32, st)
            abv = s_ab.t.rearrange("(tt p) j -> p tt j", p=128)
            for j in range(3):
                P.dma("sp", abT.t[:, 6 * j:6 * j + 6, :], abv[:, 6 * j:6 * j + 6, :], r=[s_ab.b], w=[abT.b])
            P.dma("sp", nA.t[:], a_log[l:l + 1, :].partition_broadcast(128), w=[nA.b])
            P.dma("sp", dtb.t[:], dt_bias[l:l + 1, :].partition_broadcast(128), w=[dtb.b])
            P.dma("sp", gg.t[:], g_gdn[l].rearrange("(p o) -> p o", o=1), w=[gg.b])
            P.op("act", lambda e: e.activation(out=nA.t[:], in_=nA.t[:], func=AF.Exp), r=[nA.b], w=[nA.b])
            P.op("dve", lambda e: e.tensor_scalar(out=nA.t[:], in0=nA.t[:], scalar1=-1.0, scalar2=None, op0=ALU.mult),
                 r=[nA.b], w=[nA.b])
            P.op("dve", lambda e: e.tensor_tensor(out=gG.t[:], in0=abT.t[:, :, 0:16],
                                                  in1=dtb.t[:].unsqueeze(1).broadcast_to([128, NT_, 16]), op=ALU.add),
                 r=[abT.b, dtb.b], w=[gG.b])
            P.op("act", lambda e: e.activation(out=gG.t[:], in_=gG.t[:], func=AF.Exp), r=[gG.b], w=[gG.b])
            P.op("act", lambda e: e.activation(out=gG.t[:], in_=gG.t[:], func=AF.Ln, bias=ones_f.t[:, 0:1], scale=1.0),
                 r=[gG.b, ones_f.b], w=[gG.b])
            P.op("dve", lambda e: e.tensor_tensor(out=gG.t[:], in0=gG.t[:],
                                                  in1=nA.t[:].unsqueeze(1).broadcast_to([128, NT_, 16]), op=ALU.mult),
                 r=[gG.b, nA.b], w=[gG.b])
            P.op("act", lambda e: e.activation(out=bG.t[:], in_=abT.t[:, :, 16:32], func=AF.Sigmoid), r=[abT.b], w=[bG.b])
            with ExitStack() as st2:
                tmpc = sb("cwtmp", [128, 128], F32, st2)
                P.dma("sp", tmpc.t[0:120, :], conv_w[l].rearrange("j (c p) -> (j c) p", p=128), w=[tmpc.b])
                pst = ps_next()
                P.op("pe", lambda e: e.transpose(out=pst.t[:, 0:120], in_=tmpc.t[0:120, :], identity=cm.t[0:120, 0, 0:120]),
                     r=[tmpc.b, cm.b], w=[pst.b])
                P.op("dve", lambda e: e.tensor_copy(out=cwT.t[:].rearrange("p j c -> p (j c)"), in_=pst.t[:, 0:120]),
                     r=[pst.b], w=[cwT.b])
                P.barrier()
            fmv = s_fm.t.rearrange("(c p) t -> p c t", p=128)
            ov = s_o.t[1].rearrange("(c p) t -> p c t", p=128)
            SEGS = [(0, NCTX), (NCTX, T)]
            idb = cm.t[:, 0, :]
            for h in range(8):
                with ExitStack() as sh:
                    qT = sb("gq", [128, T], F32, sh)
                    k_tm = sb("gktm", [128, NT_, 128], F32, sh)
                    v_tm = sb("gvtm", [128, NT_, 128], F32, sh)
                    KK = sb("gKK", [128, NT_, 128], F32, sh)
                    QKT = sb("gQKT", [128, NT_, 128], F32, sh)
                    with ExitStack() as s1:
                        kT = sb("gk", [128, T], F32, s1)
                        raw = [sb("graw%d" % i, [128, T], BF16, s1) for i in range(3)]
                        acc = [sb("gacc%d" % i, [128, T], F32, s1) for i in range(2)]
                        vT = sb("gv", [128, T], F32, s1)
                        sqbs = [sb("gsq%d" % i, [128, 512], F32, s1) for i in range(2)]
                        rnbs = [sb("grn%d" % i, [128, 512], F32, s1) for i in range(2)]
                        for ci_, which in enumerate(("q", "k", "v")):
                            cidx = ci_ * 8 + h
                            rw = raw[ci_]
                            a_ = acc[ci_ % 2]
                            eng = "dve"
                            P.dma("sp", rw.t[:], fmv[:, OFF_QKVB + cidx, :], r=[s_fm.b], w=[rw.b])
                            P.op(eng, lambda e: e.tensor_scalar(out=a_.t[:], in0=rw.t[:], scalar1=cwT.t[:, 2, cidx:cidx + 1], scalar2=None,
                                                                op0=ALU.mult), r=[rw.b, cwT.b], w=[a_.b])
                            for j in (0, 1, 3, 4):
                                d_ = j - 2
                                for (s0, s1_) in SEGS:
                                    lo, hi = max(s0, s0 - d_), min(s1_, s1_ - d_)
                                    P.op(eng, lambda e: e.scalar_tensor_tensor(
                                        out=a_.t[:, lo:hi], in0=rw.t[:, lo + d_:hi + d_], scalar=cwT.t[:, j, cidx:cidx + 1],
                                        in1=a_.t[:, lo:hi], op0=ALU.mult, op1=ALU.add), r=[rw.b, cwT.b, a_.b], w=[a_.b])
                            dst = {"q": qT, "k": kT, "v": vT}[which]
                            if which == "v":
                                P.op("act", lambda e: e.activation(out=dst.t[:], in_=a_.t[:], func=AF.Silu), r=[a_.b], w=[dst.b])
                            else:
                                P.op("act", lambda e: e.activation(out=a_.t[:], in_=a_.t[:], func=AF.Silu), r=[a_.b], w=[a_.b])
                                for tgi, (t0, n) in enumerate(TGS):
                                    sqb, rnb = sqbs[tgi % 2], rnbs[tgi % 2]
                                    P.op("pool", lambda e: e.tensor_tensor(out=sqb.t[:, 0:n], in0=a_.t[:, t0:t0 + n], in1=a_.t[:, t0:t0 + n],
                                                                          op=ALU.mult), r=[a_.b], w=[sqb.b])
                                    pq = ps_next()
                                    P.op("pe", lambda e: e.matmul(pq.t[:, 0:n], lhsT=ones_f.t[:], rhs=sqb.t[:, 0:n], start=True, stop=True),
                                         r=[sqb.b, ones_f.b], w=[pq.b])
                                    P.op("act", lambda e: e.activation(out=rnb.t[:, 0:n], in_=pq.t[:, 0:n], func=AF.Sqrt,
                                                                       bias=epsc.t[:, 0:1], scale=1.0), r=[pq.b, epsc.b], w=[rnb.b])
                                    P.op("dve", lambda e: e.reciprocal(out=rnb.t[:, 0:n], in_=rnb.t[:, 0:n]), r=[rnb.b], w=[rnb.b])
                                    if which == "q":
                                        P.op("dve", lambda e: e.scalar_tensor_tensor(
                                            out=dst.t[:, t0:t0 + n], in0=a_.t[:, t0:t0 + n], scalar=SCALE, in1=rnb.t[:, 0:n],
                                            op0=ALU.mult, op1=ALU.mult), r=[a_.b, rnb.b], w=[dst.b])
                                    else:
                                        P.op("dve", lambda e: e.tensor_tensor(out=dst.t[:, t0:t0 + n], in0=a_.t[:, t0:t0 + n],
                                                                              in1=rnb.t[:, 0:n], op=ALU.mult), r=[a_.b, rnb.b], w=[dst.b])
                        ei = 0
                        for (g0, gn) in TGRP:
                            for (src_, dst_) in ((kT, k_tm), (vT, v_tm)):
                                pt = ps_next()
                                for ti in range(gn):
                                    tt = g0 + ti
                                    P.op("pe", lambda e: e.transpose(out=pt.t[:, ti * 128:(ti + 1) * 128], in_=src_.t[:, tt * 128:(tt + 1) * 128],
                                                                     identity=idb), r=[src_.b, cm.b], w=[pt.b], inc=(ti == gn - 1))
                                ei += 1
                                if ei % 2:
                                    P.op("act", lambda e: e.activation(out=dst_.t[:, g0:g0 + gn, :].rearrange("p a b -> p (a b)"),
                                                                       in_=pt.t[:, 0:gn * 128], func=AF.Copy), r=[pt.b], w=[dst_.b])
                                else:
                                    P.op("dve", lambda e: e.tensor_copy(out=dst_.t[:, g0:g0 + gn, :].rearrange("p a b -> p (a b)"),
                                                                        in_=pt.t[:, 0:gn * 128]), r=[pt.b], w=[dst_.b])
                            for (rhs_, dst_) in ((kT, KK), (qT, QKT)):
                                pt = ps_next()
                                for ti in range(gn):
                                    tt = g0 + ti
                                    P.op("pe", lambda e: e.matmul(pt.t[:, ti * 128:(ti + 1) * 128], lhsT=kT.t[:, tt * 128:(tt + 1) * 128],
                                                                  rhs=rhs_.t[:, tt * 128:(tt + 1) * 128], start=True, stop=True),
                                         r=[kT.b, rhs_.b], w=[pt.b], inc=(ti == gn - 1))
                                ei += 1
                                if ei % 2:
                                    P.op("act", lambda e: e.activation(out=dst_.t[:, g0:g0 + gn, :].rearrange("p a b -> p (a b)"),
                                                                       in_=pt.t[:, 0:gn * 128], func=AF.Copy), r=[pt.b], w=[dst_.b])
                                else:
                                    P.op("dve", lambda e: e.tensor_copy(out=dst_.t[:, g0:g0 + gn, :].rearrange("p a b -> p (a b)"),
                                                                        in_=pt.t[:, 0:gn * 128]), r=[pt.b], w=[dst_.b])
                        P.barrier()
                    dirs = []
                    for di in range(2):
                        iU, iML, iMU, iSL = (4, 6, 7, 8) if di == 0 else (5, 7, 6, 9)
                        gcol = di * 8 + h
                        wT = sb("gwT%d" % di, [128, T], F32, sh)
                        qgT = sb("gqg%d" % di, [128, T], BF16, sh)
                        kd = sb("gkd%d" % di, [128, NT_, 128], BF16, sh)
                        attnT = sb("gat%d" % di, [128, NT_, 128], BF16, sh)
                        u_ = sb("gu%d" % di, [128, NT_, 128], F32, sh)
                        egl = sb("gegl%d" % di, [128, NT_, 2], F32, sh)
                        dirs.append((wT, qgT, kd, attnT, u_, egl))
                        with ExitStack() as s2:
                            gc = sb("ggc", [128, NT_], F32, s2)
                            egc = sb("gegc", [128, NT_], F32, s2)
                            ekd = sb("gekd", [128, NT_], F32, s2)
                            bsc = sb("gbsc", [128, NT_], F32, s2)
                            ghd = sb("gghd", [128, NT_], F32, s2)
                            bhd = sb("gbhd", [128, NT_], F32, s2)
                            P.op("dve", lambda e: e.tensor_copy(out=ghd.t[:], in_=gG.t[:, :, gcol]), r=[gG.b], w=[ghd.b])
                            P.op("dve", lambda e: e.tensor_copy(out=bhd.t[:], in_=bG.t[:, :, gcol]), r=[bG.b], w=[bhd.b])
                            pg = ps_next()
                            P.op("pe", lambda e: e.matmul(pg.t[:, 0:NT_], lhsT=cm.t[:, iU, :], rhs=ghd.t[:], start=True, stop=True),
                                 r=[cm.b, ghd.b], w=[pg.b], inc=False)
                            P.op("pe", lambda e: e.matmul(pg.t[:, 32:32 + NT_], lhsT=cm.t[:, 10, :], rhs=ghd.t[:], start=True, stop=True),
                                 r=[cm.b, ghd.b], w=[pg.b], inc=False)
                            P.op("pe", lambda e: e.matmul(pg.t[:, 64:64 + NT_], lhsT=cm.t[:, 11, :], rhs=ghd.t[:], start=True, stop=True),
                                 r=[cm.b, ghd.b], w=[pg.b], inc=False)
                            P.op("pe", lambda e: e.matmul(pg.t[:, 96:96 + NT_], lhsT=cm.t[:, 12, :], rhs=ghd.t[:], start=True, stop=True),
                                 r=[cm.b, ghd.b], w=[pg.b])
                            P.op("dve", lambda e: e.tensor_copy(out=gc.t[:], in_=pg.t[:, 0:NT_]), r=[pg.b], w=[gc.b])
                            P.op("dve", lambda e: e.tensor_tensor(out=ekd.t[:], in0=pg.t[:, 32:32 + NT_], in1=gc.t[:], op=ALU.subtract),
                                 r=[pg.b, gc.b], w=[ekd.b])
                            P.op("dve", lambda e: e.tensor_copy(out=egl.t[:, :, 0], in_=pg.t[:, 64:64 + NT_]), r=[pg.b], w=[egl.b])
                            P.op("dve", lambda e: e.tensor_copy(out=egl.t[:, :, 1], in_=pg.t[:, 96:96 + NT_]), r=[pg.b], w=[egl.b])
                            P.op("act", lambda e: e.activation(out=egc.t[:], in_=gc.t[:], func=AF.Exp), r=[gc.b], w=[egc.b])
                            P.op("act", lambda e: e.activation(out=ekd.t[:], in_=ekd.t[:], func=AF.Exp), r=[ekd.b], w=[ekd.b])
                            P.op("act", lambda e: e.activation(out=egl.t[:], in_=egl.t[:], func=AF.Exp), r=[egl.b], w=[egl.b])
                            P.op("dve", lambda e: e.tensor_tensor(out=bsc.t[:], in0=bhd.t[:], in1=egc.t[:], op=ALU.mult),
                                 r=[bhd.b, egc.b], w=[bsc.b])
                            P.op("pool", lambda e: e.tensor_tensor(out=kd.t[:], in0=k_tm.t[:],
                                                                  in1=ekd.t[:].unsqueeze(2).broadcast_to([128, NT_, 128]), op=ALU.mult),
                                 r=[k_tm.b, ekd.b], w=[kd.b])
                            pacc = PSPool([0, 1, 2, 3, 4, 5, 6, 7])
                            slots = []
                            for si in range(2):
                                B = {}
                                for nm in ("GU", "Gb", "EL", "EU", "Lf", "Dg", "kbg", "vb", "X", "N00", "N01", "N10", "N11", "X0", "X1"):
                                    B[nm] = sb("g%s_%d" % (nm, si), [128, 512], F32, s2)
                                slots.append(B)

                            def group_gen(g0, gn, B):
                                W_ = gn * 128
                                GU, Gb, EL, EU, Lf, Dg, kbg, vb, Xg = (B[k_] for k_ in ("GU", "Gb", "EL", "EU", "Lf", "Dg", "kbg", "vb", "X"))
                                nb = [[B["N00"], B["N01"]], [B["N10"], B["N11"]]]
                                Xb = [B["X0"], B["X1"]]
                                g3 = lambda t_: t_.t[:, 0:W_].rearrange("p (a b) -> p a b", b=128)
                                p3 = lambda p_: p_.t[:, 0:W_].rearrange("p (a b) -> p a b", b=128)
                                gsl = ghd.t[:, g0:g0 + gn].unsqueeze(2).broadcast_to([128, gn, 128])
                                bcg = lambda t_: t_.t[:, g0:g0 + gn].unsqueeze(2).broadcast_to([128, gn, 128])
                                cmb = lambda i_: cm.t[:, i_, :].unsqueeze(1).broadcast_to([128, gn, 128])
                                P.op("dve", lambda e: e.tensor_tensor(out=g3(GU), in0=cmb(iU), in1=gsl, op=ALU.mult), r=[cm.b, ghd.b], w=[GU.b])
                                P.op("pool", lambda e: e.tensor_copy(out=g3(Gb), in_=gsl), r=[ghd.b], w=[Gb.b])
                                P.op("pool", lambda e: e.tensor_tensor(out=g3(kbg), in0=k_tm.t[:, g0:g0 + gn, :], in1=bcg(bsc), op=ALU.mult),
                                     r=[k_tm.b, bsc.b], w=[kbg.b])
                                P.op("pool", lambda e: e.tensor_tensor(out=g3(vb), in0=v_tm.t[:, g0:g0 + gn, :], in1=bcg(bhd), op=ALU.mult),
                                     r=[v_tm.b, bhd.b], w=[vb.b])
                                P.op("pool", lambda e: e.tensor_tensor(out=g3(Dg), in0=cmb(0), in1=bcg(egc), op=ALU.mult),
                                     r=[cm.b, egc.b], w=[Dg.b])
                                pD = pacc.next()
                                P.op("pe", lambda e: e.matmul(pD.t[:, 0:W_], lhsT=cm.t[:, iU, :], rhs=Gb.t[:, 0:W_], start=True, stop=False),
                                     r=[cm.b, Gb.b], w=[pD.b], inc=False)
                                P.op("pe", lambda e: e.matmul(pD.t[:, 0:W_], lhsT=negones.t[:], rhs=GU.t[:, 0:W_], start=False, stop=True),
                                     r=[negones.b, GU.b], w=[pD.b])
                                pq = pacc.next()
                                P.op("pe", lambda e: e.matmul(pq.t[:, 0:W_], lhsT=ones_f.t[:], rhs=Dg.t[:, 0:W_], start=True, stop=True),
                                     r=[ones_f.b, Dg.b], w=[pq.b])
                                yield
                                P.op("dve", lambda e: e.tensor_tensor(out=g3(EL), in0=p3(pD), in1=cmb(iML), op=ALU.add), r=[pD.b, cm.b], w=[EL.b])
                                P.op("dve", lambda e: e.scalar_tensor_tensor(out=g3(EU), in0=p3(pD), scalar=-1.0, in1=cmb(iMU),
                                                                             op0=ALU.mult, op1=ALU.add), r=[pD.b, cm.b], w=[EU.b])
                                P.op("dve", lambda e: e.tensor_tensor(out=qgT.t[:, g0 * 128:g0 * 128 + W_], in0=pq.t[:, 0:W_],
                                                                      in1=qT.t[:, g0 * 128:g0 * 128 + W_], op=ALU.mult),
                                     r=[pq.b, qT.b], w=[qgT.b])
                                P.op("act", lambda e: e.activation(out=EL.t[:, 0:W_], in_=EL.t[:, 0:W_], func=AF.Exp), r=[EL.b], w=[EL.b])
                                P.op("act", lambda e: e.activation(out=EU.t[:, 0:W_], in_=EU.t[:, 0:W_], func=AF.Exp), r=[EU.b], w=[EU.b])
                                P.op("pool", lambda e: e.tensor_tensor(out=g3(Lf), in0=g3(EL), in1=KK.t[:, g0:g0 + gn, :], op=ALU.mult),
                                     r=[EL.b, KK.b], w=[Lf.b])
                                P.op("pool", lambda e: e.tensor_tensor(out=g3(Lf), in0=g3(Lf), in1=cmb(iSL), op=ALU.mult),
                                     r=[Lf.b, cm.b], w=[Lf.b])
                                P.op("pool", lambda e: e.tensor_tensor(out=g3(Lf), in0=g3(Lf), in1=bcg(bhd), op=ALU.mult),
                                     r=[Lf.b, bhd.b], w=[Lf.b])
                                P.op("dve", lambda e: e.tensor_tensor(out=attnT.t[:, g0:g0 + gn, :], in0=g3(EU), in1=QKT.t[:, g0:g0 + gn, :],
                                                                      op=ALU.mult), r=[EU.b, QKT.b], w=[attnT.b])
                                NT0, N0 = nb[1][0], nb[0][0]
                                P.op("dve", lambda e: e.tensor_scalar(out=NT0.t[:, 0:W_], in0=Lf.t[:, 0:W_], scalar1=-1.0, scalar2=None,
                                                                      op0=ALU.mult), r=[Lf.b], w=[NT0.b])
                                pT_ = pacc.next()
                                for ti in range(gn):
                                    P.op("pe", lambda e: e.transpose(out=pT_.t[:, ti * 128:(ti + 1) * 128], in_=Lf.t[:, ti * 128:(ti + 1) * 128],
                                                                     identity=idb), r=[Lf.b, cm.b], w=[pT_.b], inc=(ti == gn - 1))
                                yield
                                P.op("act", lambda e: e.activation(out=N0.t[:, 0:W_], in_=pT_.t[:, 0:W_], func=AF.Copy, scale=-1.0),
                                     r=[pT_.b], w=[N0.b])
                                Xc = Xb[0]
                                P.op("dve", lambda e: e.scalar_tensor_tensor(out=g3(Xc), in0=p3(pT_), scalar=-1.0, in1=cmb(0),
                                                                             op0=ALU.mult, op1=ALU.add), r=[pT_.b, cm.b], w=[Xc.b])
                                Nc, NTc = N0, NT0
                                for k_ in range(5):
                                    par = (k_ + 1) % 2
                                    Nn, NTn = nb[0][par], nb[1][par]
                                    Xn = Xb[(k_ + 1) % 2]
                                    pN = pacc.next() if k_ < 4 else None
                                    pNT = pacc.next()
                                    for ti in range(gn):
                                        sl = slice(ti * 128, (ti + 1) * 128)
                                        P.op("pe", lambda e: e.matmul(pNT.t[:, sl], lhsT=Nc.t[:, sl], rhs=NTc.t[:, sl], start=True, stop=True),
                                             r=[NTc.b, Nc.b], w=[pNT.b], inc=(ti == gn - 1))
                                    if k_ < 4:
                                        for ti in range(gn):
                                            sl = slice(ti * 128, (ti + 1) * 128)
                                            P.op("pe", lambda e: e.matmul(pN.t[:, sl], lhsT=NTc.t[:, sl], rhs=Nc.t[:, sl], start=True, stop=True),
                                                 r=[NTc.b, Nc.b], w=[pN.b], inc=(ti == gn - 1))
                                    yield
                                    P.op("dve", lambda e: e.tensor_copy(out=NTn.t[:, 0:W_], in_=pNT.t[:, 0:W_]), r=[pNT.b], w=[NTn.b])
                                    if k_ < 4:
                                        P.op("act", lambda e: e.activation(out=Nn.t[:, 0:W_], in_=pN.t[:, 0:W_], func=AF.Copy),
                                             r=[pN.b], w=[Nn.b])
                                    pX = pacc.next()
                                    for ti in range(gn):
                                        sl = slice(ti * 128, (ti + 1) * 128)
                                        P.op("pe", lambda e: e.matmul(pX.t[:, sl], lhsT=NTn.t[:, sl], rhs=Xc.t[:, sl], start=True, stop=True),
                                             r=[NTn.b, Xc.b], w=[pX.b], inc=(ti == gn - 1))
                                    yield
                                    dstX = Xn if k_ < 4 else Xg
                                    P.op("dve", lambda e: e.tensor_tensor(out=dstX.t[:, 0:W_], in0=pX.t[:, 0:W_], in1=Xc.t[:, 0:W_], op=ALU.add),
                                         r=[pX.b, Xc.b], w=[dstX.b])
                                    Nc, NTc, Xc = Nn, NTn, Xn
                                pu = pacc.next()
                                pw = pacc.next()
                                for ti in range(gn):
                                    sl = slice(ti * 128, (ti + 1) * 128)
                                    P.op("pe", lambda e: e.matmul(pu.t[:, sl], lhsT=Xg.t[:, sl], rhs=vb.t[:, sl], start=True, stop=True),
                                         r=[Xg.b, vb.b], w=[pu.b], inc=(ti == gn - 1))
                                for ti in range(gn):
                                    sl = slice(ti * 128, (ti + 1) * 128)
                                    P.op("pe", lambda e: e.matmul(pw.t[:, sl], lhsT=kbg.t[:, sl], rhs=Xg.t[:, sl], start=True, stop=True),
                                         r=[Xg.b, kbg.b], w=[pw.b], inc=(ti == gn - 1))
                                yield
                                P.op("act", lambda e: e.activation(out=u_.t[:, g0:g0 + gn, :].rearrange("p a b -> p (a b)"), in_=pu.t[:, 0:W_],
                                                                   func=AF.Copy), r=[pu.b], w=[u_.b])
                                P.op("act", lambda e: e.activation(out=wT.t[:, g0 * 128:g0 * 128 + W_], in_=pw.t[:, 0:W_], func=AF.Copy),
                                     r=[pw.b], w=[wT.b])

                            for pair in ((0, 1), (2, 3), (4,)):
                                gens = [group_gen(TGRP[gi][0], TGRP[gi][1], slots[k_]) for k_, gi in enumerate(pair)]
                                while gens:
                                    for g_ in list(gens):
                                        try:
                                            next(g_)
                                        except StopIteration:
                                            gens.remove(g_)
                            P.barrier()
                    with ExitStack() as s3:
                        oacc = sb("goacc", [128, T], F32, s3)
                        otmp = sb("gotmp", [128, 64], F32, s3)
                        Sst = [sb("gS%d" % i, [128, 128], F32, s3) for i in range(2)]
                        Sbf = [sb("gSb%d" % i, [128, 128], BF16, s3) for i in range(2)]
                        vnw = [[sb("gvn%d_%d" % (i, j), [128, 128], BF16, s3) for j in range(2)] for i in range(2)]
                        for i in range(2):
                            P.op("dve", lambda e: e.memset(Sst[i].t[:], 0.0), w=[Sst[i].b])
                            P.op("dve", lambda e: e.memset(Sbf[i].t[:], 0.0), w=[Sbf[i].b])
                        order_f = list(range(36))
                        order_b = [3, 2, 1, 0] + list(range(35, 3, -1))
                        written = set()
                        for step in range(36):
                            ctxs = []
                            for di in range(2):
                                c = (order_f, order_b)[di][step]
                                tt, half = c // 2, c % 2
                                ctxs.append(dict(c=c, tt=tt, half=half, r0=half * 64, d=dirs[di], S_=Sst[di], Sb_=Sbf[di],
                                                 vn=vnw[di][step % 2], pA=psum[3 * di], pB=psum[3 * di + 1], pC=psum[3 * di + 2],
                                                 tsl=slice(tt * 128, (tt + 1) * 128)))
                            for X in ctxs:
                                wT = X["d"][0]
                                P.op("pe", lambda e: e.matmul(X["pA"].t[:, 0:128], lhsT=wT.t[:, X["tsl"]], rhs=X["S_"].t[:], start=True, stop=True),
                                     r=[wT.b, X["S_"].b], w=[X["pA"].b])
                            for X in ctxs:
                                u_ = X["d"][4]
                                r0, tt, vn = X["r0"], X["tt"], X["vn"]
                                P.op("dve", lambda e: e.tensor_tensor(out=vn.t[r0:r0 + 64, :], in0=u_.t[r0:r0 + 64, tt, :],
                                                                      in1=X["pA"].t[r0:r0 + 64, 0:128], op=ALU.subtract),
                                     r=[u_.b, X["pA"].b], w=[vn.b])
                            for X in ctxs:
                                wT, qgT, kd, attnT, u_, egl = X["d"]
                                r0, tt, vn, pB, pC, Sb_ = X["r0"], X["tt"], X["vn"], X["pB"], X["pC"], X["Sb_"]
                                P.op("pe", lambda e: e.matmul(pC.t[:, 0:128], lhsT=kd.t[r0:r0 + 64, tt, :], rhs=vn.t[r0:r0 + 64, :],
                                                              start=True, stop=True), r=[kd.b, vn.b], w=[pC.b])
                                P.op("pe", lambda e: e.matmul(pB.t[:, 0:128], lhsT=Sb_.t[:], rhs=qgT.t[:, X["tsl"]], start=True, stop=False),
                                     r=[Sb_.b, qgT.b], w=[pB.b], inc=False)
                                P.op("pe", lambda e: e.matmul(pB.t[:, 0:128], lhsT=vn.t[r0:r0 + 64, :], rhs=attnT.t[r0:r0 + 64, tt, :],
                                                              start=False, stop=True), r=[vn.b, attnT.b], w=[pB.b])
                            for X in ctxs:
                                egl = X["d"][5]
                                S_, pC, tt, half = X["S_"], X["pC"], X["tt"], X["half"]
                                P.op("dve", lambda e: e.scalar_tensor_tensor(out=S_.t[:], in0=S_.t[:], scalar=egl.t[:, tt, half:half + 1],
                                                                             in1=pC.t[:, 0:128], op0=ALU.mult, op1=ALU.add),
                                     r=[S_.b, egl.b, pC.b], w=[S_.b])
                            for X in ctxs:
                                S_, Sb_, pB, c, r0 = X["S_"], X["Sb_"], X["pB"], X["c"], X["r0"]
                                P.op("act", lambda e: e.activation(out=Sb_.t[:], in_=S_.t[:], func=AF.Copy), r=[S_.b], w=[Sb_.b])
                                osl = slice(c * 64, c * 64 + 64)
                                if c not in written:
                                    written.add(c)
                                    P.op("act", lambda e: e.activation(out=oacc.t[:, osl], in_=pB.t[:, r0:r0 + 64], func=AF.Copy),
                                         r=[pB.b], w=[oacc.b])
                                else:
                                    P.op("act", lambda e: e.activation(out=otmp.t[:, 0:64], in_=pB.t[:, r0:r0 + 64], func=AF.Copy),
                                         r=[pB.b], w=[otmp.b])
                                    P.op("pool", lambda e: e.tensor_tensor(out=oacc.t[:, osl], in0=oacc.t[:, osl], in1=otmp.t[:, 0:64],
                                                                          op=ALU.add), r=[otmp.b, oacc.b], w=[oacc.b])
                        zs = sb("gzs", [128, T], BF16, s3)
                        ob = sb("gob", [128, T], BF16, s3)
                        sq2 = sb("gsq2", [128, 512], F32, s3)
                        rn2 = sb("grn2", [128, 512], F32, s3)
                        P.dma("sp", zs.t[:], fmv[:, OFF_ZB + h, :], r=[s_fm.b], w=[zs.b])
                        for (t0, n) in TGS:
                            P.op("pool", lambda e: e.tensor_tensor(out=sq2.t[:, 0:n], in0=oacc.t[:, t0:t0 + n], in1=oacc.t[:, t0:t0 + n],
                                                                  op=ALU.mult), r=[oacc.b], w=[sq2.b])
                            pq = ps_next()
                            P.op("pe", lambda e: e.matmul(pq.t[:, 0:n], lhsT=ones_f.t[:], rhs=sq2.t[:, 0:n], start=True, stop=True),
                                 r=[sq2.b, ones_f.b], w=[pq.b])
                            P.op("act", lambda e: e.activation(out=rn2.t[:, 0:n], in_=pq.t[:, 0:n], func=AF.Sqrt, bias=epsc.t[:, 0:1],
                                                               scale=1.0 / 128), r=[pq.b, epsc.b], w=[rn2.b])
                            P.op("dve", lambda e: e.reciprocal(out=rn2.t[:, 0:n], in_=rn2.t[:, 0:n]), r=[rn2.b], w=[rn2.b])
                            P.op("dve", lambda e: e.scalar_tensor_tensor(out=rn2.t[:, 0:n], in0=oacc.t[:, t0:t0 + n], scalar=gg.t[:, 0:1],
                                                                         in1=rn2.t[:, 0:n], op0=ALU.mult, op1=ALU.mult),
                                 r=[oacc.b, gg.b, rn2.b], w=[rn2.b])
                            P.op("pool", lambda e: e.tensor_tensor(out=ob.t[:, t0:t0 + n], in0=rn2.t[:, 0:n], in1=zs.t[:, t0:t0 + n],
                                                                  op=ALU.mult), r=[rn2.b, zs.b], w=[ob.b])
                        P.dma("sp", ov[:, h, :], ob.t[:], r=[ob.b], w=[s_o.b])
                        P.barrier()

    if "M" in cfg.stages:
        stage_M()
    for l in range(L):
        layer_coefs(l)
        if "D" in cfg.stages or "E" in cfg.stages:
            stage_W(l)
        last = (l == L - 1) and not cfg.force_ctx
        for s in range(NS):
            tgs = TGS[1:] if last else TGS
            if "B" in cfg.stages:
                stage_AB(l, s)
            if "C" in cfg.stages:
                if "A" in cfg.mixers:
                    stage_C1(l, s, not last)
                if "N" in cfg.mixers:
                    stage_C2(l, s, not last)
                if "B" in cfg.mixers:
                    stage_C3(l, s)
            if "D" in cfg.stages:
                stage_D(l, s, tgs)
            if "E" in cfg.stages:
                stage_E(l, s, tgs)
    P.barrier()
    gs.close()
    return nc, P


def rope_tables():
    t = np.arange(NX)
    nf = HD // 4
    inv = (10000.0 ** (-np.arange(nf, dtype=np.float32) / nf)).astype(np.float32)
    ang_r = (t // 64).astype(np.float32)[:, None] * inv
    ang_c = (t % 64).astype(np.float32)[:, None] * inv
    cosT = np.ones((128, T), np.float32)
    sinT = np.zeros((128, T), np.float32)
    for a, ang in enumerate((ang_r, ang_c)):
        c = np.cos(ang).T.astype(np.float32)
        s_ = np.sin(ang).T.astype(np.float32)
        cosT[a * 64:a * 64 + 32, NCTX:] = c
        cosT[a * 64 + 32:a * 64 + 64, NCTX:] = c
        sinT[a * 64:a * 64 + 32, NCTX:] = -s_
        sinT[a * 64 + 32:a * 64 + 64, NCTX:] = s_
    return cosT, sinT


def w_in_cols():
    qa0, ka0, va0 = 0, 1024, 1280
    qb0 = 1536
    zb0 = qb0 + 3072
    ab0 = zb0 + 1024
    qn0 = ab0 + 32
    kn0, vn0 = qn0 + 1024, qn0 + 2048
    g0 = qn0 + 3072
    perm = np.concatenate([np.arange(32, 64), np.arange(0, 32), np.arange(96, 128), np.arange(64, 96)])
    cols = []
    for h in range(8):
        base = qa0 + h * 128
        cols.append(base + np.arange(128))
        cols.append(base + perm)
    for h in range(2):
        base = ka0 + h * 128
        cols.append(base + np.arange(128))
        cols.append(base + perm)
    cols.append(np.arange(qb0, qb0 + 3072))
    cols.append(np.arange(zb0, zb0 + 1024))
    cols.append(np.arange(qn0, qn0 + 1024))
    cols.append(np.arange(kn0, kn0 + 1024))
    cols.append(np.arange(g0, g0 + 6144))
    cols.append(np.arange(va0, va0 + 256))
    cols.append(np.arange(vn0, vn0 + 1024))
    cols.append(np.arange(ab0, ab0 + 32))
    cols = np.concatenate(cols)
    assert cols.shape[0] == W_ALL
    return cols


def const_masks():
    m = np.zeros((128, 13, 128), np.float32)
    m[:, 0, :] = np.eye(128, dtype=np.float32)
    p = np.arange(128)[:, None]
    f = np.arange(128)[None, :]
    m[:, 1, :] = (p >= f)
    m[:, 2, :] = (p <= f)
    kc = np.arange(64)[:, None]
    qc = 63 - np.arange(64)[None, :]
    cs = np.clip(qc - 8, 0, 48)
    ok = (kc >= cs) & (kc < cs + 16)
    m[:64, 3, :64] = np.where(ok, 0.0, -30000.0)
    same = (p // 64) == (f // 64)
    m[:, 4, :] = same & (p <= f)
    m[:, 5, :] = same & (p >= f)
    m[:, 6, :] = np.where(same & (p >= f), 0.0, -30000.0)
    m[:, 7, :] = np.where(same & (p <= f), 0.0, -30000.0)
    m[:, 8, :] = same & (p > f)
    m[:, 9, :] = same & (p < f)
    m[:, 10, :] = same
    m[:, 11, :] = (p < 64) & (f >= 0)
    m[:, 12, :] = (p >= 64) & (f >= 0)
    return m


def _tile_w(w, kcb):
    lead = w.shape[:-2]
    K_, N_ = w.shape[-2:]
    kg = K_ // (128 * kcb)
    a = w.reshape(lead + (kg, kcb, 128, N_ // 512, 512))
    nl = len(lead)
    a = np.transpose(a, tuple(range(nl)) + (nl + 3, nl + 0, nl + 2, nl + 1, nl + 4))
    a = np.ascontiguousarray(a).reshape(lead + (N_ // 512, kg, 128, kcb * 512))
    if kg == 1:
        a = a.reshape(lead + (N_ // 512, 128, kcb * 512))
    return a


def host_inputs(inputs, n_cores=8):
    x = np.asarray(inputs["x"], np.float32)
    ctx = np.asarray(inputs["ctx"], np.float32)
    c = np.asarray(inputs["c"], np.float32)
    cols = w_in_cols()
    shared = {
        "w_mod": np.ascontiguousarray(inputs["w_mod"], np.float32),
        "b_mod": np.ascontiguousarray(inputs["b_mod"], np.float32),
        "gvec": np.ascontiguousarray(np.stack([inputs["g_pre_mix"], inputs["g_post_mix"], inputs["g_pre_mlp"],
                                               inputs["g_post_mlp"]], axis=1), np.float32),
        "w_in": np.ascontiguousarray(np.asarray(inputs["w_in"], np.float32)[:, :, cols]),
        "conv_w": np.ascontiguousarray(inputs["conv_w"], np.float32),
        "a_log": np.ascontiguousarray(np.asarray(inputs["a_log"], np.float32).reshape(-1, 16)),
        "dt_bias": np.ascontiguousarray(np.asarray(inputs["dt_bias"], np.float32).reshape(-1, 16)),
        "g_gdn": np.ascontiguousarray(inputs["g_gdn_out"], np.float32),
        "sink": np.ascontiguousarray(inputs["sink"], np.float32),
        "rpb": np.ascontiguousarray(np.asarray(inputs["rpb"], np.float32).reshape(np.asarray(inputs["rpb"]).shape[0], -1)),
        "w_branch": _tile_w(np.asarray(inputs["w_branch"], np.float32), 8),
        "w_out": _tile_w(np.asarray(inputs["w_out"], np.float32), 8),
        "w_up": _tile_w(np.asarray(inputs["w_up"], np.float32), 16),
        "w_down": _tile_w(np.asarray(inputs["w_down"], np.float32), 16),
    }
    cosT, sinT = rope_tables()
    shared["ropec"] = cosT
    shared["ropes"] = sinT
    shared["cmasks"] = const_masks()
    maps = []
    for core in range(n_cores):
        b0 = core * NSEQ
        xin = np.empty((NSEQ, D, T), np.float32)
        for s in range(NSEQ):
            xin[s, :, :NCTX] = ctx[b0 + s].T
            xin[s, :, NCTX:] = x[b0 + s].T
        c3 = np.stack([c[b0], c[b0 + 1], np.asarray(inputs["c_ctx"], np.float32)], axis=0)
        m = dict(shared)
        m["xin"] = xin
        m["c3"] = np.ascontiguousarray(c3)
        maps.append(m)
    return maps


_CACHE = {}


def kernel(**inputs):
    n_cores = 8
    if "nc" not in _CACHE:
        _CACHE["nc"] = build_program(Cfg())[0]
    nc = _CACHE["nc"]
    maps = host_inputs(inputs, n_cores)
    res = run_bass_kernel_spmd(nc, maps, core_ids=list(range(n_cores)))
    out = np.empty((16, NX, D), np.float32)
    for core in range(n_cores):
        y = res.results[core]["yout"]
        for s in range(NSEQ):
            out[core * NSEQ + s] = y[s].T
    return out
```

```python
import math
from contextlib import ExitStack

import numpy as np
import concourse.bass as bass
import concourse.mybir as mybir
from concourse.bass_utils import run_bass_kernel_spmd

F32 = mybir.dt.float32
BF16 = mybir.dt.bfloat16
AF = mybir.ActivationFunctionType
ALU = mybir.AluOpType
AX = mybir.AxisListType

D = 2048
NCTX = 256
NX = 2048
T = NCTX + NX
DEPTH = 4
NSEQ = 2
KC = D // 128
DFF = 4 * D
HD = 128
EPS = 1e-6
TGS = [(0, 256)] + [(256 + 512 * i, 512) for i in range(4)]
NFM = 116
W_FM = NFM * 128
W_TM = 256 + 1024 + 32
W_ALL = W_FM + W_TM


class Buf:
    __slots__ = ("name", "lw", "rd", "excl")

    def __init__(self, name=""):
        self.name = name
        self.lw = None
        self.rd = {}
        self.excl = False


class TB:
    def __init__(self, t, name=""):
        self.t = t
        self.b = Buf(name)


COMPUTE = ("pe", "dve", "act", "pool")
NDS = 24


class Prog:
    def __init__(self, nc):
        self.nc = nc
        self.E = {"pe": nc.tensor, "dve": nc.vector, "act": nc.scalar, "pool": nc.gpsimd, "sp": nc.sync}
        self.sems = []
        self.semidx = {}
        for e in COMPUTE:
            self.semidx[e] = len(self.sems)
            self.sems.append(nc.alloc_semaphore("s_" + e))
        self.cnt = {e: 0 for e in COMPUTE}
        self.pend = {e: None for e in COMPUTE}
        self.seen = {e: {} for e in self.E}
        self.dslots = []
        for i in range(NDS):
            self.dslots.append([len(self.sems), 0])
            self.sems.append(nc.alloc_semaphore("d%d" % i))
        self.dnext = 0
        self.nwaits = 0
        self.nops = 0

    def _wait(self, e, toks):
        need = {}
        for t in toks:
            if t is None:
                continue
            te, si, val = t
            if e == "pe" and te == "pe":
                continue
            assert val is not None, "wait on pending token"
            if self.seen[e].get(si, 0) >= val:
                continue
            if need.get(si, 0) < val:
                need[si] = val
        for si, val in need.items():
            self.E[e].wait_ge(self.sems[si], val)
            self.seen[e][si] = val
            self.nwaits += 1

    def _deps(self, e, r, w):
        deps = []
        for b in r:
            if b.lw is not None:
                deps.append(b.lw)
            if b.excl:
                for k, t in b.rd.items():
                    if k != e:
                        deps.append(t)
        for b in w:
            if b.lw is not None:
                deps.append(b.lw)
            for k, t in b.rd.items():
                if k == e and e in COMPUTE:
                    continue
                deps.append(t)
        return deps

    def op(self, e, fn, r=(), w=(), inc=True):
        self._wait(e, self._deps(e, r, w))
        ins = fn(self.E[e])
        self.nops += 1
        if inc:
            self.cnt[e] += 1
            ins.then_inc(self.sems[self.semidx[e]], 1)
            tok = self.pend[e]
            if tok is None:
                tok = [e, self.semidx[e], None]
            tok[2] = self.cnt[e]
            self.pend[e] = None
        else:
            tok = self.pend[e]
            if tok is None:
                tok = self.pend[e] = [e, self.semidx[e], None]
        for b in r:
            b.rd[e] = tok
        for b in w:
            b.lw = tok
            b.rd = {}
        return ins

    def dma(self, q, out, in_, r=(), w=(), **kw):
        deps = self._deps(q, r, w)
        slot = self.dslots[self.dnext]
        self.dnext = (self.dnext + 1) % NDS
        if slot[1] > 0:
            deps.append(["dma", slot[0], slot[1]])
        self._wait(q, deps)
        ins = self.E[q].dma_start(out=out, in_=in_, **kw)
        slot[1] += 16
        ins.then_inc(self.sems[slot[0]], 16)
        self.nops += 1
        tok = ["dma", slot[0], slot[1]]
        for b in r:
            b.rd[("dma", slot[0])] = tok
        for b in w:
            b.lw = tok
            b.rd = {}
        return ins

    def all_toks(self):
        toks = []
        for e in COMPUTE:
            assert self.pend[e] is None, "pending at barrier on " + e
            if self.cnt[e] > 0:
                toks.append(["x", self.semidx[e], self.cnt[e]])
        for slot in self.dslots:
            if slot[1] > 0:
                toks.append(["dma", slot[0], slot[1]])
        return toks

    def barrier(self, engines=None):
        toks = self.all_toks()
        for e in (engines or self.E):
            self._wait(e, toks)


class Cfg:
    def __init__(self, **kw):
        self.layers = DEPTH
        self.nseq = NSEQ
        self.debug = False
        self.stages = "MABCDE"
        self.inject = ()
        self.wdepth = DEPTH
        self.ab_parts = "mft"
        self.mixers = "ANB"
        self.force_ctx = False
        self.nfm = NFM
        self.__dict__.update(kw)


def build_program(cfg):
    nc = bass.Bass("TRN2", target_bir_lowering=False)
    P = Prog(nc)
    print("sbuf bytes remaining at start:", nc.sbuf_bytes_remaining)
    L = cfg.layers
    NS = cfg.nseq
    WD = cfg.wdepth

    def din(name, shape, dt=F32):
        return nc.dram_tensor(name, list(shape), dt, kind="ExternalInput").ap()

    def dscratch(name, shape, dt):
        if name in cfg.inject:
            kind = "ExternalInput"
        elif cfg.debug:
            kind = "ExternalOutput"
        else:
            kind = "Internal"
        return TB(nc.dram_tensor(name, list(shape), dt, kind=kind).ap(), name)

    xin = din("xin", [NSEQ, D, T])
    c3 = din("c3", [3, D])
    w_mod = din("w_mod", [WD, D, 6 * D])
    b_mod = din("b_mod", [WD, 6 * D])
    gvec = din("gvec", [WD, 4, D])
    w_in = din("w_in", [WD, D, W_ALL])
    conv_w = din("conv_w", [WD, 5, 3072])
    a_log = din("a_log", [WD, 16])
    dt_bias = din("dt_bias", [WD, 16])
    g_gdn = din("g_gdn", [WD, 128])
    sink = din("sink", [WD, 8])
    rpb = din("rpb", [WD, 8 * 15 * 31])
    w_branch = din("w_branch", [WD, 3, 4, 128, 8 * 512])
    w_out = din("w_out", [WD, 4, 2, 128, 8 * 512])
    w_up = din("w_up", [WD, 16, 128, 16 * 512])
    w_down = din("w_down", [WD, 4, 4, 128, 16 * 512])
    ropec = din("ropec", [128, T])
    ropes = din("ropes", [128, T])
    cmasks = din("cmasks", [128, 13, 128])
    yout = nc.dram_tensor("yout", [NSEQ, D, NX], F32, kind="ExternalOutput").ap()

    xT = dscratch("xT", [NSEQ, D, T], F32)
    s_fm = dscratch("s_fm", [W_FM, T], BF16)
    s_va = dscratch("s_va", [T, 256], BF16)
    s_vn = dscratch("s_vn", [T, 1024], BF16)
    s_ab = dscratch("s_ab", [T, 32], F32)
    s_o = dscratch("s_o", [3, 1024, T], BF16)
    w_up16 = dscratch("w_up16", [16, 128, 16 * 512], BF16)
    w_down16 = dscratch("w_down16", [4, 4, 128, 16 * 512], BF16)
    w_branch16 = dscratch("w_branch16", [3, 4, 128, 8 * 512], BF16)
    w_out16 = dscratch("w_out16", [4, 2, 128, 8 * 512], BF16)
    OFF_QA, OFF_KA, OFF_QKVB, OFF_ZB, OFF_QN, OFF_KN, OFF_GATE = 0, 8, 10, 34, 42, 50, 58
    NFM_OUT = 106

    gs = ExitStack()

    uniq = {"n": 0}

    def sb(name, shape, dt, stack=None):
        uniq["n"] += 1
        nm = "%s_%d" % (name, uniq["n"])
        return TB((stack or gs).enter_context(nc.sbuf_tensor(nm, list(shape), dt)), nm)

    psum = [TB(gs.enter_context(nc.psum_tensor("ps%d" % i, [128, 512], F32)), "ps%d" % i) for i in range(8)]
    for p_ in psum:
        p_.b.excl = True
    pstate = {"i": 0}

    def ps_next():
        p = psum[pstate["i"]]
        pstate["i"] = (pstate["i"] + 1) % 8
        return p

    class PSPool:
        def __init__(self, idx):
            self.idx = list(idx)
            self.i = 0

        def next(self):
            p = psum[self.idx[self.i]]
            self.i = (self.i + 1) % len(self.idx)
            return p

    ones_f = sb("ones_f", [128, 128], F32)
    ones_b = sb("ones_b", [128, 128], BF16)
    modT = sb("modT", [128, DEPTH, 96, 3], F32)
    gT = sb("gT", [128, DEPTH, 4, 16], F32)
    coef = sb("coef", [128, 6, 16, 3], F32)
    negones = sb("negones", [128, 128], F32)
    P.op("dve", lambda e: e.memset(negones.t[:], -1.0), w=[negones.b])
    epsc = sb("epsc", [128, 4], F32)
    P.op("dve", lambda e: e.memset(epsc.t[:], EPS), w=[epsc.b])
    P.op("dve", lambda e: e.memset(ones_f.t[:], 1.0), w=[ones_f.b])
    P.op("dve", lambda e: e.memset(ones_b.t[:], 1.0), w=[ones_b.b])
    cm = sb("cm", [128, 13, 128], F32)
    for j in range(13):
        P.dma("sp", cm.t[:, j, :], cmasks[:, j, :], w=[cm.b])
    ident = cm.t[:, 0, :]

    def load_T(dst_ap, src_rows, nrows, tag):
        with ExitStack() as st:
            tmp = sb("ldT" + tag, [128, 128], F32, st)
            P.dma("sp", tmp.t[0:nrows, :], src_rows, w=[tmp.b])
            ps = ps_next()
            P.op("pe", lambda e: e.transpose(out=ps.t[:, 0:nrows], in_=tmp.t[0:nrows, :], identity=cm.t[0:nrows, 0, 0:nrows]),
                 r=[tmp.b, cm.b], w=[ps.b])
            P.op("dve", lambda e: e.tensor_copy(out=dst_ap, in_=ps.t[:, 0:nrows]), r=[ps.b], w=[gT.b, modT.b])
            P.barrier()

    gv = gvec.rearrange("l k (c p) -> (l k c) p", p=128)
    gflat = gT.t[:].rearrange("p l k c -> p (l k c)")
    for j in range(0, L * 64, 128):
        nr = min(128, L * 64 - j)
        load_T(gflat[:, j:j + nr], gv[j:j + nr, :], nr, "g%d" % j)

    def stage_M():
        with ExitStack() as st:
            scT = sb("scT", [128, 3, 16], F32, st)
            wblk = [sb("wmblk%d" % i, [128, 16, 512], F32, st) for i in range(2)]
            brow = [sb("brow%d" % i, [1, 512], F32, st) for i in range(2)]
            load_T(scT.t[:].rearrange("p r c -> p (r c)"), c3.rearrange("r (c p) -> (r c) p", p=128), 48, "c3")
            P.op("act", lambda e: e.activation(out=scT.t[:], in_=scT.t[:], func=AF.Silu), r=[scT.b], w=[scT.b])
            it = 0
            for l in range(L):
                wv = w_mod[l].rearrange("(kc p) n -> p kc n", p=128)
                for blk in range(24):
                    wb = wblk[it % 2]
                    br = brow[it % 2]
                    it += 1
                    for j in range(4):
                        P.dma("sp", wb.t[:, 4 * j:4 * j + 4, :], wv[:, 4 * j:4 * j + 4, blk * 512:(blk + 1) * 512],
                              w=[wb.b])
                    P.dma("sp", br.t[:], b_mod[l:l + 1, blk * 512:(blk + 1) * 512], w=[br.b])
                    ps = ps_next()
                    for j in range(4):
                        def mm(e, j=j, wb=wb, br=br, ps=ps):
                            for kc in range(16):
                                e.matmul(ps.t[:, 3 * j:3 * j + 3], lhsT=wb.t[:, kc, j * 128:(j + 1) * 128],
                                         rhs=scT.t[:, :, kc], start=(kc == 0), stop=False)
                            return e.matmul(ps.t[:, 3 * j:3 * j + 3], lhsT=br.t[0:1, j * 128:(j + 1) * 128],
                                            rhs=ones_f.t[0:1, 0:3], start=False, stop=True)
                        P.op("pe", mm, r=[wb.b, br.b, scT.b, ones_f.b], w=[ps.b], inc=(j == 3))
                    P.op("dve", lambda e, ps=ps, l=l, blk=blk: e.tensor_copy(
                        out=modT.t[:, l, blk * 4:(blk + 1) * 4, :],
                        in_=ps.t[:, 0:12].rearrange("p (a b) -> p a b", b=3)), r=[ps.b], w=[modT.b])
            P.barrier()

    def stage_W(l):
        for blk in range(16):
            P.dma("pool", w_up16.t[blk], w_up[l, blk], w=[w_up16.b])
        for ob in range(4):
            for kq in range(4):
                P.dma("pool", w_down16.t[ob, kq], w_down[l, ob, kq], w=[w_down16.b])
        for i in range(3):
            for ob in range(4):
                P.dma("pool", w_branch16.t[i, ob], w_branch[l, i, ob], w=[w_branch16.b])
        for ob in range(4):
            for kh in range(2):
                P.dma("pool", w_out16.t[ob, kh], w_out[l, ob, kh], w=[w_out16.b])

    class Prefetch:
        def __init__(self, bufs, loaders, dist=2):
            self.bufs, self.loaders, self.dist, self.issued = bufs, loaders, dist, 0

        def get(self, k):
            while self.issued < len(self.loaders) and self.issued <= k + self.dist:
                self.loaders[self.issued](self.bufs[self.issued % len(self.bufs)])
                self.issued += 1
            return self.bufs[k % len(self.bufs)]

    def layer_coefs(l):
        def m(idx):
            return modT.t[:, l, idx * 16:(idx + 1) * 16, :]

        def g(k):
            return gT.t[:, l, k, :].unsqueeze(2).broadcast_to([128, 16, 3])
        for (dst, gi, mi, plus1) in ((0, 0, 1, True), (2, 1, 2, False), (3, 2, 4, True), (5, 3, 5, False)):
            if plus1:
                P.op("dve", lambda e, dst=dst, gi=gi, mi=mi: e.scalar_tensor_tensor(
                    out=coef.t[:, dst], in0=m(mi), scalar=1.0, in1=g(gi), op0=ALU.add, op1=ALU.mult),
                    r=[modT.b, gT.b], w=[coef.b])
            else:
                P.op("dve", lambda e, dst=dst, gi=gi, mi=mi: e.tensor_tensor(
                    out=coef.t[:, dst], in0=m(mi), in1=g(gi), op=ALU.mult), r=[modT.b, gT.b], w=[coef.b])
        P.op("dve", lambda e: e.tensor_copy(out=coef.t[:, 1], in_=m(0)), r=[modT.b], w=[coef.b])
        P.op("dve", lambda e: e.tensor_copy(out=coef.t[:, 4], in_=m(3)), r=[modT.b], w=[coef.b])

    def modulate(st, src, s, ia, ib, hT, tgs, xt=None):
        if xt is None:
            xt = [sb("mx%d" % i, [128, 16, 512], F32, st) for i in range(2)]
        sq = [sb("msq%d" % i, [128, 512], F32, st) for i in range(2)]
        rstd = sb("mrstd", [128, 512], F32, st)
        tmp = [sb("mtmp%d" % i, [128, 512], F32, st) for i in range(2)]
        srcv = src.rearrange("(c p) t -> p c t", p=128)
        for gi, (t0, n) in enumerate(tgs):
            row = 2 if t0 < NCTX else s
            x = xt[gi % len(xt)]
            for j in range(4):
                P.dma("sp", x.t[:, 4 * j:4 * j + 4, 0:n], srcv[:, 4 * j:4 * j + 4, t0:t0 + n], r=[xT.b], w=[x.b])
            ps = ps_next()
            for kc in range(16):
                q = sq[kc % 2]
                P.op("act", lambda e, q=q, x=x, kc=kc: e.activation(out=q.t[:, 0:n], in_=x.t[:, kc, 0:n], func=AF.Square),
                     r=[x.b], w=[q.b])
                P.op("pe", lambda e, q=q, kc=kc, ps=ps: e.matmul(ps.t[:, 0:n], lhsT=ones_f.t[:], rhs=q.t[:, 0:n],
                                                                 start=(kc == 0), stop=(kc == 15)),
                     r=[q.b, ones_f.b], w=[ps.b], inc=True)
            P.op("act", lambda e, ps=ps: e.activation(out=rstd.t[:, 0:n], in_=ps.t[:, 0:n], func=AF.Sqrt, bias=epsc.t[:, 0:1],
                                                      scale=1.0 / D), r=[ps.b, epsc.b], w=[rstd.b])
            P.op("dve", lambda e: e.reciprocal(out=rstd.t[:, 0:n], in_=rstd.t[:, 0:n]), r=[rstd.b], w=[rstd.b])
            for kc in range(16):
                tm = tmp[kc % 2]
                P.op("dve", lambda e, tm=tm, x=x, kc=kc: e.scalar_tensor_tensor(
                    out=tm.t[:, 0:n], in0=x.t[:, kc, 0:n], scalar=coef.t[:, ia, kc, row:row + 1], in1=rstd.t[:, 0:n],
                    op0=ALU.mult, op1=ALU.mult), r=[x.b, coef.b, rstd.b], w=[tm.b])
                P.op("act", lambda e, tm=tm, kc=kc: e.activation(
                    out=hT.t[:, kc, t0:t0 + n], in_=tm.t[:, 0:n], func=AF.Identity,
                    bias=coef.t[:, ib, kc, row:row + 1], scale=1.0), r=[tm.b, coef.b], w=[hT.b])

    def stage_AB(l, s):
        with ExitStack() as st:
            hT = sb("hT", [128, 16, T], BF16, st)
            with ExitStack() as st2:
                src = xin[s] if l == 0 else xT.t[s]
                modulate(st2, src, s, 0, 1, hT, TGS)
                P.barrier()
            if "f" not in cfg.ab_parts:
                return
            wblk = [sb("wblk%d" % i, [128, 16, 512], BF16, st) for i in range(3)]
            stg = [sb("stg%d" % i, [128, T], BF16, st) for i in range(4)]
            cosT = sb("cosT", [128, T], F32, st)
            sinT = sb("sinT", [128, T], F32, st)
            r1 = [sb("r1_%d" % i, [128, 512], F32, st) for i in range(2)]
            r2 = [sb("r2_%d" % i, [128, 512], F32, st) for i in range(2)]
            for j in range(3):
                a, b_ = j * 768, (j + 1) * 768
                P.dma("sp", cosT.t[:, a:b_], ropec[:, a:b_], w=[cosT.b])
                P.dma("sp", sinT.t[:, a:b_], ropes[:, a:b_], w=[sinT.b])
            wv = w_in[l].rearrange("(kc p) n -> p kc n", p=128)
            nblk = (W_ALL + 511) // 512
            blkbuf = {}

            def load_blk(bi):
                wb = wblk[bi % 3]
                c0 = bi * 512
                n = min(512, W_ALL - c0)
                for j in range(2):
                    P.dma("pool", wb.t[:, 8 * j:8 * j + 8, 0:n], wv[:, 8 * j:8 * j + 8, c0:c0 + n], w=[wb.b])
                blkbuf[bi] = wb

            load_blk(0)
            load_blk(1)
            si = 0
            evq = 0
            oc_out = 0
            ci = 0
            while ci < cfg.nfm:
                bi = ci // 4
                if ci % 4 == 0 and bi + 2 < nblk:
                    load_blk(bi + 2)
                wb = blkbuf[bi]
                rope = ci < 20
                kind = "plain"
                if ci >= 20 + 24 and ci < 20 + 32:
                    kind = "silu"
                if ci >= 20 + 48:
                    kind = "sigmoid"
                sg = stg[si % 4]
                si += 1
                for gi, (t0, n) in enumerate(TGS):
                    pa = ps_next()

                    def mm(e, pa=pa, wb=wb, cc=ci % 4, t0=t0, n=n):
                        for kc in range(16):
                            ins = e.matmul(pa.t[:, 0:n], lhsT=wb.t[:, kc, cc * 128:(cc + 1) * 128],
                                           rhs=hT.t[:, kc, t0:t0 + n], start=(kc == 0), stop=(kc == 15))
                        return ins
                    P.op("pe", mm, r=[wb.b, hT.b], w=[pa.b])
                    if rope:
                        pb = ps_next()
                        P.op("pe", lambda e, pb=pb, wb=wb, cc=ci % 4 + 1, t0=t0, n=n: mm(e, pb, wb, cc, t0, n),
                             r=[wb.b, hT.b], w=[pb.b])
                        a1 = r1[gi % 2]
                        a2 = r2[gi % 2]
                        P.op("dve", lambda e, a1=a1, pa=pa, t0=t0, n=n: e.tensor_tensor(
                            out=a1.t[:, 0:n], in0=pa.t[:, 0:n], in1=cosT.t[:, t0:t0 + n], op=ALU.mult),
                            r=[pa.b, cosT.b], w=[a1.b])
                        P.op("dve", lambda e, a2=a2, pb=pb, t0=t0, n=n: e.tensor_tensor(
                            out=a2.t[:, 0:n], in0=pb.t[:, 0:n], in1=sinT.t[:, t0:t0 + n], op=ALU.mult),
                            r=[pb.b, sinT.b], w=[a2.b])
                        P.op("pool", lambda e, a1=a1, a2=a2, sg=sg, t0=t0, n=n: e.tensor_tensor(
                            out=sg.t[:, t0:t0 + n], in0=a1.t[:, 0:n], in1=a2.t[:, 0:n], op=ALU.add),
                            r=[a1.b, a2.b], w=[sg.b])
                    elif kind == "plain":
                        evq += 1
                        if evq % 2:
                            P.op("dve", lambda e, pa=pa, sg=sg, t0=t0, n=n: e.tensor_copy(
                                out=sg.t[:, t0:t0 + n], in_=pa.t[:, 0:n]), r=[pa.b], w=[sg.b])
                        else:
                            P.op("act", lambda e, pa=pa, sg=sg, t0=t0, n=n: e.activation(
                                out=sg.t[:, t0:t0 + n], in_=pa.t[:, 0:n], func=AF.Copy), r=[pa.b], w=[sg.b])
                    else:
                        fn = AF.Silu if kind == "silu" else AF.Sigmoid
                        P.op("act", lambda e, pa=pa, sg=sg, t0=t0, n=n, fn=fn: e.activation(
                            out=sg.t[:, t0:t0 + n], in_=pa.t[:, 0:n], func=fn), r=[pa.b], w=[sg.b])
                P.dma("sp", s_fm.t[oc_out * 128:(oc_out + 1) * 128, :], sg.t[:], r=[sg.b], w=[s_fm.b])
                oc_out += 1
                ci += 2 if rope else 1
            if 't' not in cfg.ab_parts:
                P.barrier()
                return
            tstg = [sb("tstg%d" % i, [128, 512], BF16, st) for i in range(3)]
            tstf = [sb("tstf%d" % i, [128, 32], F32, st) for i in range(2)]
            ti = 0
            for bi in range(NFM // 4, nblk):
                if bi + 2 < nblk and bi + 2 not in blkbuf:
                    load_blk(bi + 2)
                wb = blkbuf[bi]
                c0 = bi * 512 - W_FM
                n = min(512, W_TM - c0)
                for tt in range(T // 128):
                    pa = ps_next()

                    def mmt(e, pa=pa, wb=wb, tt=tt, n=n):
                        for kc in range(16):
                            ins = e.matmul(pa.t[:, 0:n], lhsT=hT.t[:, kc, tt * 128:(tt + 1) * 128],
                                           rhs=wb.t[:, kc, 0:n], start=(kc == 0), stop=(kc == 15))
                        return ins
                    P.op("pe", mmt, r=[wb.b, hT.b], w=[pa.b])
                    segs = []
                    for (nm, a, b_) in (("va", 0, 256), ("vn", 256, 1280), ("ab", 1280, 1312)):
                        lo, hi = max(a, c0), min(b_, c0 + n)
                        if lo < hi:
                            segs.append((nm, lo - a, hi - a, lo - c0, hi - c0))
                    for (nm, d0, d1, p0, p1) in segs:
                        if nm == "ab":
                            tf = tstf[ti % 2]
                            if tt % 2:
                                P.op("act", lambda e, tf=tf, pa=pa, p0=p0, p1=p1: e.activation(
                                    out=tf.t[:, 0:32], in_=pa.t[:, p0:p1], func=AF.Copy), r=[pa.b], w=[tf.b])
                            else:
                                P.op("dve", lambda e, tf=tf, pa=pa, p0=p0, p1=p1: e.tensor_copy(
                                    out=tf.t[:, 0:32], in_=pa.t[:, p0:p1]), r=[pa.b], w=[tf.b])
                            P.dma("sp", s_ab.t[tt * 128:(tt + 1) * 128, :], tf.t[:], r=[tf.b], w=[s_ab.b])
                        else:
                            ts_ = tstg[ti % 3]
                            ti += 1
                            eng = "act" if tt % 2 else "dve"
                            if eng == "act":
                                P.op("act", lambda e, ts_=ts_, pa=pa, p0=p0, p1=p1: e.activation(
                                    out=ts_.t[:, 0:p1 - p0], in_=pa.t[:, p0:p1], func=AF.Copy), r=[pa.b], w=[ts_.b])
                            else:
                                P.op("dve", lambda e, ts_=ts_, pa=pa, p0=p0, p1=p1: e.tensor_copy(
                                    out=ts_.t[:, 0:p1 - p0], in_=pa.t[:, p0:p1]), r=[pa.b], w=[ts_.b])
                            dst = s_va if nm == "va" else s_vn
                            P.dma("sp", dst.t[tt * 128:(tt + 1) * 128, d0:d1], ts_.t[:, 0:p1 - p0], r=[ts_.b], w=[dst.b])
            P.barrier()

    def post_norm_residual(st, l, s, ig, mT, t0, n, last_out, tag, xt=None):
        row = 2 if t0 < NCTX else s
        sq = [sb(tag + "sq%d" % i, [128, 512], F32, st) for i in range(2)]
        rstd = sb(tag + "rstd", [128, 512], F32, st)
        dstv = xT.t[s].rearrange("(c p) t -> p c t", p=128)
        if xt is None:
            xt = sb(tag + "xt", [128, 16, 512], F32, st)
            src = (xin[s] if (l == 0 and tag == "D") else xT.t[s]).rearrange("(c p) t -> p c t", p=128)
            for j in range(4):
                P.dma("sp", xt.t[:, 4 * j:4 * j + 4, 0:n], src[:, 4 * j:4 * j + 4, t0:t0 + n], r=[xT.b], w=[xt.b])
        ps = ps_next()
        for kc in range(16):
            q = sq[kc % 2]
            P.op("act", lambda e, q=q, kc=kc: e.activation(out=q.t[:, 0:n], in_=mT.t[:, kc, 0:n], func=AF.Square),
                 r=[mT.b], w=[q.b])
            P.op("pe", lambda e, q=q, kc=kc: e.matmul(ps.t[:, 0:n], lhsT=ones_f.t[:], rhs=q.t[:, 0:n],
                                                      start=(kc == 0), stop=(kc == 15)), r=[q.b], w=[ps.b])
        P.op("act", lambda e: e.activation(out=rstd.t[:, 0:n], in_=ps.t[:, 0:n], func=AF.Sqrt, bias=epsc.t[:, 0:1],
                                           scale=1.0 / D), r=[ps.b, epsc.b], w=[rstd.b])
        P.op("dve", lambda e: e.reciprocal(out=rstd.t[:, 0:n], in_=rstd.t[:, 0:n]), r=[rstd.b], w=[rstd.b])
        for kc in range(16):
            P.op("pool", lambda e, kc=kc: e.tensor_tensor(out=mT.t[:, kc, 0:n], in0=mT.t[:, kc, 0:n], in1=rstd.t[:, 0:n],
                                                          op=ALU.mult), r=[mT.b, rstd.b], w=[mT.b])
            P.op("dve", lambda e, kc=kc: e.scalar_tensor_tensor(
                out=xt.t[:, kc, 0:n], in0=mT.t[:, kc, 0:n], scalar=coef.t[:, ig, kc, row:row + 1], in1=xt.t[:, kc, 0:n],
                op0=ALU.mult, op1=ALU.add), r=[mT.b, coef.b, xt.b], w=[xt.b])
        for j in range(4):
            P.dma("sp", dstv[:, 4 * j:4 * j + 4, t0:t0 + n], xt.t[:, 4 * j:4 * j + 4, 0:n], r=[xt.b], w=[xT.b])
        if last_out and t0 >= NCTX:
            yv = yout[s].rearrange("(c p) t -> p c t", p=128)
            for j in range(4):
                P.dma("sp", yv[:, 4 * j:4 * j + 4, t0 - NCTX:t0 - NCTX + n], xt.t[:, 4 * j:4 * j + 4, 0:n], r=[xt.b])

    def stage_D(l, s, tgs):
        gate_v = s_fm.t[OFF_GATE * 128:(OFF_GATE + 48) * 128, :].rearrange("(i c p) t -> p i c t", p=128, i=3)
        for (t0, n) in tgs:
            with ExitStack() as st:
                oT = sb("oT", [128, 3, 8, 512], BF16, st)
                yT = sb("yT", [128, 16, 512], BF16, st)
                mT = sb("mT", [128, 16, 512], F32, st)
                gtb = [sb("gt%d" % i, [128, 4, 512], BF16, st) for i in range(3)]
                wbb = [sb("wbb%d" % i, [128, 8, 512], BF16, st) for i in range(3)]
                acc = sb("accD", [128, 4, 512], F32, st)
                tm2 = [sb("tm2%d" % i, [128, 512], F32, st) for i in range(2)]
                ov = s_o.t.rearrange("i (c p) t -> p i c t", p=128)
                for i in range(3):
                    for j in range(2):
                        P.dma("sp", oT.t[:, i, 4 * j:4 * j + 4, 0:n], ov[:, i, 4 * j:4 * j + 4, t0:t0 + n], r=[s_o.b], w=[oT.b])
                wi = 0
                loaders = []
                for ob in range(4):
                    for i in range(3):
                        loaders.append(lambda wb, i=i, ob=ob: P.dma("pool", wb.t[:].rearrange("p a b -> p (a b)"), w_branch16.t[i, ob],
                                                                    r=[w_branch16.b], w=[wb.b]))
                for ob in range(4):
                    for kh in range(2):
                        loaders.append(lambda wb, kh=kh, ob=ob: P.dma("pool", wb.t[:].rearrange("p a b -> p (a b)"), w_out16.t[ob, kh],
                                                                      r=[w_out16.b], w=[wb.b]))
                pf = Prefetch(wbb, loaders, 2)
                for ob in range(4):
                    for i in range(3):
                        wb = pf.get(wi)
                        g = gtb[wi % 3]
                        wi += 1
                        P.dma("sp", g.t[:, :, 0:n], gate_v[:, i, 4 * ob:4 * ob + 4, t0:t0 + n], r=[s_fm.b], w=[g.b])
                        for oc in range(4):
                            pa = ps_next()

                            def mm(e, pa=pa, wb=wb, i=i, oc=oc):
                                for kc in range(8):
                                    ins = e.matmul(pa.t[:, 0:n], lhsT=wb.t[:, kc, oc * 128:(oc + 1) * 128], rhs=oT.t[:, i, kc, 0:n],
                                                   start=(kc == 0), stop=(kc == 7))
                                return ins
                            P.op("pe", mm, r=[wb.b, oT.b], w=[pa.b])
                            if i == 0:
                                P.op("dve", lambda e, pa=pa, g=g, oc=oc: e.tensor_tensor(
                                    out=acc.t[:, oc, 0:n], in0=pa.t[:, 0:n], in1=g.t[:, oc, 0:n], op=ALU.mult),
                                    r=[pa.b, g.b], w=[acc.b])
                            else:
                                t2 = tm2[oc % 2]
                                P.op("dve", lambda e, pa=pa, t2=t2, g=g, oc=oc: e.tensor_tensor(
                                    out=t2.t[:, 0:n], in0=pa.t[:, 0:n], in1=g.t[:, oc, 0:n], op=ALU.mult),
                                    r=[pa.b, g.b], w=[t2.b])
                                if i == 1:
                                    P.op("dve", lambda e, t2=t2, oc=oc: e.tensor_tensor(
                                        out=acc.t[:, oc, 0:n], in0=acc.t[:, oc, 0:n], in1=t2.t[:, 0:n], op=ALU.add),
                                        r=[acc.b, t2.b], w=[acc.b])
                                else:
                                    P.op("dve", lambda e, t2=t2, oc=oc, ob=ob: e.tensor_tensor(
                                        out=yT.t[:, 4 * ob + oc, 0:n], in0=acc.t[:, oc, 0:n], in1=t2.t[:, 0:n], op=ALU.add),
                                        r=[acc.b, t2.b], w=[yT.b])
                pacc = PSPool([0, 1, 2, 3])
                for ob in range(4):
                    pas = [pacc.next() for _ in range(4)]
                    for kh in range(2):
                        wb = pf.get(wi)
                        wi += 1
                        for oc in range(4):
                            pa = pas[oc]

                            def mm2(e, pa=pa, wb=wb, oc=oc, kh=kh):
                                for kc in range(8):
                                    ins = e.matmul(pa.t[:, 0:n], lhsT=wb.t[:, kc, oc * 128:(oc + 1) * 128], rhs=yT.t[:, 8 * kh + kc, 0:n],
                                                   start=(kh == 0 and kc == 0), stop=(kh == 1 and kc == 7))
                                return ins
                            P.op("pe", mm2, r=[wb.b, yT.b], w=[pa.b])
                    for oc in range(4):
                        P.op("act", lambda e, pa=pas[oc], oc=oc, ob=ob: e.activation(out=mT.t[:, 4 * ob + oc, 0:n], in_=pa.t[:, 0:n],
                                                                                    func=AF.Copy), r=[pas[oc].b], w=[mT.b])
                post_norm_residual(st, l, s, 2, mT, t0, n, False, "D")
                P.barrier()

    def stage_E(l, s, tgs):
        last = (l == L - 1)
        for (t0, n) in tgs:
            with ExitStack() as st:
                aT = sb("aT", [128, 64, 512], BF16, st)
                xt = sb("xtE", [128, 16, 512], F32, st)
                with ExitStack() as st2:
                    hT = sb("hE", [128, 16, 512], BF16, st2)
                    modulate_tg(st2, xT.t[s], s, 3, 4, hT, t0, n, [xt])
                    wub = [sb("wub%d" % i, [128, 16, 512], BF16, st2) for i in range(3)]
                    rl = [sb("rl%d" % i, [128, 512], F32, st2) for i in range(2)]
                    pfu = Prefetch(wub, [(lambda wb, blk=blk: P.dma("pool", wb.t[:].rearrange("p a b -> p (a b)"), w_up16.t[blk],
                                                                    r=[w_up16.b], w=[wb.b])) for blk in range(16)], 2)
                    for blk in range(16):
                        wb = pfu.get(blk)
                        for cc in range(4):
                            fc = blk * 4 + cc
                            pa = ps_next()

                            def mm(e, pa=pa, wb=wb, cc=cc):
                                for kc in range(16):
                                    ins = e.matmul(pa.t[:, 0:n], lhsT=wb.t[:, kc, cc * 128:(cc + 1) * 128],
                                                   rhs=hT.t[:, kc, 0:n], start=(kc == 0), stop=(kc == 15))
                                return ins
                            P.op("pe", mm, r=[wb.b, hT.b], w=[pa.b])
                            r_ = rl[fc % 2]
                            P.op("act", lambda e, pa=pa, r_=r_: e.activation(out=r_.t[:, 0:n], in_=pa.t[:, 0:n], func=AF.Relu),
                                 r=[pa.b], w=[r_.b])
                            P.op("dve", lambda e, r_=r_, fc=fc: e.tensor_tensor(out=aT.t[:, fc, 0:n], in0=r_.t[:, 0:n],
                                                                            in1=r_.t[:, 0:n], op=ALU.mult),
                                 r=[r_.b], w=[aT.b])
                    P.barrier()
                with ExitStack() as st2:
                    mT = sb("mE", [128, 16, 512], F32, st2)
                    wdb = [sb("wdb%d" % i, [128, 16, 512], BF16, st2) for i in range(3)]
                    pacc = PSPool([0, 1, 2, 3])
                    wi = 0
                    pfd = Prefetch(wdb, [(lambda wb, ob=ob, kq=kq: P.dma("pool", wb.t[:].rearrange("p a b -> p (a b)"), w_down16.t[ob, kq],
                                                                         r=[w_down16.b], w=[wb.b])) for ob in range(4) for kq in range(4)], 2)
                    for ob in range(4):
                        pas = [pacc.next() for _ in range(4)]
                        for kq in range(4):
                            wb = pfd.get(wi)
                            wi += 1
                            for oc in range(4):
                                pa = pas[oc]

                                def mm2(e, pa=pa, wb=wb, oc=oc, kq=kq):
                                    for kc in range(16):
                                        ins = e.matmul(pa.t[:, 0:n], lhsT=wb.t[:, kc, oc * 128:(oc + 1) * 128], rhs=aT.t[:, 16 * kq + kc, 0:n],
                                                       start=(kq == 0 and kc == 0), stop=(kq == 3 and kc == 15))
                                    return ins
                                P.op("pe", mm2, r=[wb.b, aT.b], w=[pa.b])
                        for oc in range(4):
                            P.op("act", lambda e, pa=pas[oc], oc=oc, ob=ob: e.activation(out=mT.t[:, 4 * ob + oc, 0:n], in_=pa.t[:, 0:n],
                                                                                        func=AF.Copy), r=[pas[oc].b], w=[mT.b])
                    post_norm_residual(st2, l, s, 5, mT, t0, n, last, "E", xt)
                    P.barrier()

    def modulate_tg(st, src, s, ia, ib, hT, t0, n, xt=None):
        class V:
            pass
        hv = TB(None)
        hv.b = hT.b

        class _T:
            def __getitem__(self, key):
                p, kc, sl = key
                return hT.t[p, kc, sl.start - t0:sl.stop - t0]
        hv.t = _T()
        modulate(st, src, s, ia, ib, hv, [(t0, n)], xt)


    SCALE = 1.0 / math.sqrt(HD)

    def stage_C1(l, s, with_ctx):
        with ExitStack() as st:
            qT = sb("qaT", [128, 8, T], BF16, st)
            kT = sb("kaT", [128, 2, T], BF16, st)
            va = sb("vaS", [128, 18, 256], BF16, st)
            oT = sb("oaT", [128, 8, T], BF16, st)
            esk = sb("esk", [128, 8], F32, st)
            eskB = sb("eskB", [128, 8, 128], F32, st)
            mk = sb("mkA", [128, 2, 4, 128], BF16, st)
            pTs = [sb("pTa%d" % i, [128, 512], BF16, st) for i in range(3)]
            den = sb("denA", [128, 512], F32, st)
            fmv = s_fm.t.rearrange("(c p) t -> p c t", p=128)
            for h in range(8):
                P.dma("sp", qT.t[:, h, :], fmv[:, OFF_QA + h, :], r=[s_fm.b], w=[qT.b])
            for h in range(2):
                P.dma("sp", kT.t[:, h, :], fmv[:, OFF_KA + h, :], r=[s_fm.b], w=[kT.b])
            vav = s_va.t.rearrange("(tt p) c -> p tt c", p=128)
            for j in range(3):
                P.dma("sp", va.t[:, 6 * j:6 * j + 6, :], vav[:, 6 * j:6 * j + 6, :], r=[s_va.b], w=[va.b])
            P.dma("sp", esk.t[:], sink[l:l + 1, :].partition_broadcast(128), w=[esk.b])
            P.op("act", lambda e: e.activation(out=esk.t[:], in_=esk.t[:], func=AF.Exp), r=[esk.b], w=[esk.b])
            P.op("dve", lambda e: e.tensor_copy(out=eskB.t[:], in_=esk.t[:].unsqueeze(2).broadcast_to([128, 8, 128])),
                 r=[esk.b], w=[eskB.b])
            for j in range(2):
                P.op("dve", lambda e, j=j: e.tensor_copy(out=mk.t[:, j], in_=cm.t[:, 1 + j, :].unsqueeze(1).broadcast_to([128, 4, 128])),
                     r=[cm.b], w=[mk.b])
            qblocks = ([(128 * j, None) for j in range(2)] if with_ctx else []) + [(256 + 128 * i, i) for i in range(16)]
            pacc = PSPool([0, 1, 2, 3])
            pss = PSPool([4, 5, 6, 7])
            pi = 0
            for g in range(2):
                for (t0, xi) in qblocks:
                    keys = [(0, None), (128, None)]
                    if xi is not None:
                        if xi > 0:
                            keys.append((256 + 128 * (xi - 1), 0))
                        keys.append((256 + 128 * xi, None))
                        if xi < 15:
                            keys.append((256 + 128 * (xi + 1), 1))
                    ps_o = pacc.next()
                    ps_d = pacc.next()
                    pss_of = {}

                    def emit_scores_a(ii):
                        k0 = keys[ii][0]
                        ps_s = pss.next()
                        pss_of[ii] = ps_s
                        P.op("pe", lambda e: e.matmul(
                            ps_s.t[:, 0:512].rearrange("p (h q) -> p h q", h=4), lhsT=kT.t[:, g, k0:k0 + 128],
                            rhs=qT.t[:, 4 * g:4 * g + 4, t0:t0 + 128], start=True, stop=True), r=[kT.b, qT.b], w=[ps_s.b])

                    for ii in range(min(2, len(keys))):
                        emit_scores_a(ii)
                    for idx, (k0, mki) in enumerate(keys):
                        if idx + 2 < len(keys):
                            emit_scores_a(idx + 2)
                        ps_s = pss_of.pop(idx)
                        pT = pTs[pi % 3]
                        pi += 1
                        P.op("act", lambda e, ps_s=ps_s, pT=pT: e.activation(out=pT.t[:], in_=ps_s.t[:], func=AF.Exp, scale=SCALE),
                             r=[ps_s.b], w=[pT.b])
                        if mki is not None:
                            P.op("pool", lambda e, pT=pT, mki=mki: e.tensor_tensor(
                                out=pT.t[:], in0=pT.t[:], in1=mk.t[:, mki].rearrange("p h q -> p (h q)"), op=ALU.mult),
                                r=[pT.b, mk.b], w=[pT.b])
                        first, lastk = idx == 0, idx == len(keys) - 1
                        P.op("pe", lambda e, pT=pT: e.matmul(ps_d.t[:], lhsT=ones_b.t[:], rhs=pT.t[:], start=first, stop=lastk),
                             r=[pT.b, ones_b.b], w=[ps_d.b])
                        P.op("pe", lambda e, pT=pT, k0=k0: e.matmul(ps_o.t[:], lhsT=va.t[:, k0 // 128, g * 128:(g + 1) * 128],
                                                                    rhs=pT.t[:], start=first, stop=lastk),
                             r=[pT.b, va.b], w=[ps_o.b])
                    P.op("dve", lambda e: e.tensor_tensor(out=den.t[:], in0=ps_d.t[:],
                                                          in1=eskB.t[:, 4 * g:4 * g + 4, :].rearrange("p h q -> p (h q)"), op=ALU.add),
                         r=[ps_d.b, eskB.b], w=[den.b])
                    P.op("dve", lambda e: e.reciprocal(out=den.t[:], in_=den.t[:]), r=[den.b], w=[den.b])
                    P.op("dve", lambda e: e.tensor_tensor(out=oT.t[:, 4 * g:4 * g + 4, t0:t0 + 128],
                                                          in0=ps_o.t[:].rearrange("p (h q) -> p h q", h=4),
                                                          in1=den.t[:].rearrange("p (h q) -> p h q", h=4), op=ALU.mult),
                         r=[ps_o.b, den.b], w=[oT.b])
            ov = s_o.t[0].rearrange("(c p) t -> p c t", p=128)
            for h in range(8):
                P.dma("sp", ov[:, h, :], oT.t[:, h, :], r=[oT.b], w=[s_o.b])
            P.barrier()

    rpbpad = dscratch("rpbpad", [120, 127], F32)
    dbgC = dscratch("dbgC", [64, 7680], F32)

    def na_valid(r, kr):
        rs = min(max(r - 4, 0), 24)
        return rs <= kr <= rs + 7

    def stage_C2(l, s, with_ctx):
        with ExitStack() as st:
            Ctab = sb("Ctab", [64, 8, 15, 64], F32, st)
            zt = sb("zt", [120, 127], F32, st)
            P.op("dve", lambda e: e.memset(zt.t[:], 0.0), w=[zt.b])
            P.dma("sp", rpbpad.t[:, :], zt.t[:], r=[zt.b], w=[rpbpad.b])
            P.dma("sp", rpbpad.t[:, 48:79], rpb[l].rearrange("(r c) -> r c", c=31), w=[rpbpad.b])
            for h in range(8):
                src = bass.AP(tensor=rpbpad.t.tensor, offset=rpbpad.t.offset + h * 15 * 127, ap=[[1, 64], [127, 15], [1, 64]])
                P.dma("sp", Ctab.t[:, h], src, r=[rpbpad.b], w=[Ctab.b])
            P.op("dve", lambda e: e.tensor_tensor(
                out=Ctab.t[:].rearrange("p h a q -> p (h a) q"), in0=Ctab.t[:].rearrange("p h a q -> p (h a) q"),
                in1=cm.t[0:64, 3, 0:64].unsqueeze(1).broadcast_to([64, 120, 64]), op=ALU.add), r=[Ctab.b, cm.b], w=[Ctab.b])
            if cfg.debug:
                for h in range(8):
                    P.dma("sp", dbgC.t[:, h * 960:(h + 1) * 960], Ctab.t[:, h].rearrange("p a q -> p (a q)"), r=[Ctab.b], w=[dbgC.b])
            ctab_ap = Ctab.t[:]
            pstep = ctab_ap.ap[0][0]
            qh = [sb("qnh%d" % i, [128, T], BF16, st) for i in range(2)]
            kh = [sb("knh%d" % i, [128, T], BF16, st) for i in range(2)]
            v64 = [sb("v64_%d" % i, [128, 36, 128], BF16, st) for i in range(2)]
            pTr = [sb("pTr%d" % i, [128, 512], BF16, st) for i in range(3)]
            for t_ in v64 + pTr:
                P.op("pool", lambda e, t_=t_: e.memset(t_.t[:], 0.0), w=[t_.b])
            vcx = [sb("vcx%d" % i, [128, 2, 128], BF16, st) for i in range(2)]
            oh = [sb("onh%d" % i, [128, T], BF16, st) for i in range(2)]
            pTs = [sb("pTn%d" % i, [128, 512], BF16, st) for i in range(3)]
            sbs = [sb("sbn%d" % i, [64, 512], F32, st) for i in range(3)]
            den = sb("denN", [128, 512], F32, st)
            fmv = s_fm.t.rearrange("(c p) t -> p c t", p=128)
            v64v = s_vn.t.rearrange("(c p) d -> p c d", p=64)
            vcv = s_vn.t[0:256, :].rearrange("(c p) d -> p c d", p=128)
            ov = s_o.t[2].rearrange("(c p) t -> p c t", p=128)
            pi = 0
            bi_ = 0
            pacc = PSPool([0, 1, 2, 3])
            pss = PSPool([4, 5, 6, 7])
            for h in range(8):
                q_, k_, v_, vc_, o_ = qh[h % 2], kh[h % 2], v64[h % 2], vcx[h % 2], oh[h % 2]
                P.dma("sp", q_.t[:], fmv[:, OFF_QN + h, :], r=[s_fm.b], w=[q_.b])
                P.dma("sp", k_.t[:], fmv[:, OFF_KN + h, :], r=[s_fm.b], w=[k_.b])
                for j in range(3):
                    P.dma("sp", v_.t[0:64, 12 * j:12 * j + 12, :], v64v[:, 12 * j:12 * j + 12, h * 128:(h + 1) * 128],
                          r=[s_vn.b], w=[v_.b])
                P.dma("sp", vc_.t[:], vcv[:, :, h * 128:(h + 1) * 128], r=[s_vn.b], w=[vc_.b])
                groups = ([("c", 0, 256)] if with_ctx else []) + [("x", 256 + 512 * G, 512) for G in range(4)]
                for (gk, t0, n) in groups:
                    ps_o = pacc.next()
                    ps_d = pacc.next()
                    items = [("ctx", 0)]
                    if gk == "x":
                        G = (t0 - 256) // 512
                        for kr in range(32):
                            rr = [r for r in range(8 * G, 8 * G + 8) if na_valid(r, kr)]
                            if rr:
                                items.append(("row", kr, rr[0], rr[-1]))
                    items.append(("ctx", 1))
                    LOOK = 2
                    pss_of = {}

                    def emit_scores(ii):
                        it = items[ii]
                        ps_s = pss.next()
                        pss_of[ii] = ps_s
                        if it[0] == "ctx":
                            cb = it[1]
                            P.op("pe", lambda e: e.matmul(ps_s.t[:, 0:n], lhsT=k_.t[:, cb * 128:(cb + 1) * 128], rhs=q_.t[:, t0:t0 + n],
                                                          start=True, stop=True), r=[k_.b, q_.b], w=[ps_s.b])
                        else:
                            _, kr, rlo, rhi = it
                            nr = rhi - rlo + 1
                            c0 = (rlo - 8 * G) * 64
                            nc_ = nr * 64
                            kt0 = 256 + 64 * kr
                            P.op("pe", lambda e: e.matmul(ps_s.t[0:64, 0:nc_], lhsT=k_.t[:, kt0:kt0 + 64],
                                                          rhs=q_.t[:, t0 + c0:t0 + c0 + nc_], start=True, stop=True),
                                 r=[k_.b, q_.b], w=[ps_s.b])

                    for ii in range(min(LOOK, len(items))):
                        emit_scores(ii)
                    for ii, it in enumerate(items):
                        if ii + LOOK < len(items):
                            emit_scores(ii + LOOK)
                        first, lastk = ii == 0, ii == len(items) - 1
                        pT = pTs[pi % 3]
                        pi += 1
                        ps_s = pss_of.pop(ii)
                        if it[0] == "ctx":
                            cb = it[1]
                            P.op("act", lambda e: e.activation(out=pT.t[:, 0:n], in_=ps_s.t[:, 0:n], func=AF.Exp, scale=SCALE),
                                 r=[ps_s.b], w=[pT.b])
                            P.op("pe", lambda e: e.matmul(ps_d.t[:, 0:n], lhsT=ones_b.t[:], rhs=pT.t[:, 0:n], start=first, stop=lastk),
                                 r=[pT.b, ones_b.b], w=[ps_d.b])
                            P.op("pe", lambda e: e.matmul(ps_o.t[:, 0:n], lhsT=vc_.t[:, cb, :], rhs=pT.t[:, 0:n], start=first, stop=lastk),
                                 r=[pT.b, vc_.b], w=[ps_o.b])
                        else:
                            _, kr, rlo, rhi = it
                            pT = pTr[pi % 3]
                            nr = rhi - rlo + 1
                            c0 = (rlo - 8 * G) * 64
                            nc_ = nr * 64
                            sbt = sbs[bi_ % 3]
                            bi_ += 1
                            a_start = 7 + kr - rlo
                            bias = bass.AP(tensor=ctab_ap.tensor, offset=ctab_ap.offset + (h * 15 + a_start) * 64 + 63,
                                           ap=[[pstep, 64], [-64, nr], [-1, 64]])
                            P.op("dve", lambda e: e.scalar_tensor_tensor(
                                out=sbt.t[:, 0:nc_].rearrange("p (r q) -> p r q", q=64),
                                in0=ps_s.t[0:64, 0:nc_].rearrange("p (r q) -> p r q", q=64), scalar=SCALE, in1=bias,
                                op0=ALU.mult, op1=ALU.add), r=[ps_s.b, Ctab.b], w=[sbt.b])
                            P.op("act", lambda e: e.activation(out=pT.t[0:64, 0:nc_], in_=sbt.t[:, 0:nc_], func=AF.Exp),
                                 r=[sbt.b], w=[pT.b])
                            P.op("pe", lambda e: e.matmul(ps_d.t[:, c0:c0 + nc_], lhsT=ones_b.t[:, :], rhs=pT.t[:, 0:nc_],
                                                          start=False, stop=False), r=[pT.b, ones_b.b], w=[ps_d.b])
                            P.op("pe", lambda e: e.matmul(ps_o.t[:, c0:c0 + nc_], lhsT=v_.t[:, 4 + kr, :], rhs=pT.t[:, 0:nc_],
                                                          start=False, stop=False), r=[pT.b, v_.b], w=[ps_o.b])
                    P.op("dve", lambda e: e.reciprocal(out=den.t[:, 0:n], in_=ps_d.t[:, 0:n]), r=[ps_d.b], w=[den.b])
                    P.op("dve", lambda e: e.tensor_tensor(out=o_.t[:, t0:t0 + n], in0=ps_o.t[:, 0:n], in1=den.t[:, 0:n], op=ALU.mult),
                         r=[ps_o.b, den.b], w=[o_.b])
                P.dma("sp", ov[:, h, :], o_.t[:], r=[o_.b], w=[s_o.b])
            P.barrier()


    NT_ = T // 128
    TGRP = [(0, 4), (4, 4), (8, 4), (12, 4), (16, 2)]

    def stage_C3(l, s):
        with ExitStack() as st:
            abT = sb("abT", [128, NT_, 32], F32, st)
            gG = sb("gG", [128, NT_, 16], F32, st)
            bG = sb("bG", [128, NT_, 16], F32, st)
            nA = sb("nA", [128, 16], F32, st)
            dtb = sb("dtb", [128, 16], F32, st)
            cwT = sb("cwT", [128, 5, 24], F32, st)
            gg = sb("ggdn", [128, 1], F32, st)
            abv = s_ab.t.rearrange("(tt p) j -> p tt j", p=128)
            for j in range(3):
                P.dma("sp", abT.t[:, 6 * j:6 * j + 6, :], abv[:, 6 * j:6 * j + 6, :], r=[s_ab.b], w=[abT.b])
            P.dma("sp", nA.t[:], a_log[l:l + 1, :].partition_broadcast(128), w=[nA.b])
            P.dma("sp", dtb.t[:], dt_bias[l:l + 1, :].partition_broadcast(128), w=[dtb.b])
            P.dma("sp", gg.t[:], g_gdn[l].rearrange("(p o) -> p o", o=1), w=[gg.b])
            P.op("act", lambda e: e.activation(out=nA.t[:], in_=nA.t[:], func=AF.Exp), r=[nA.b], w=[nA.b])
            P.op("dve", lambda e: e.tensor_scalar(out=nA.t[:], in0=nA.t[:], scalar1=-1.0, scalar2=None, op0=ALU.mult),
                 r=[nA.b], w=[nA.b])
            P.op("dve", lambda e: e.tensor_tensor(out=gG.t[:], in0=abT.t[:, :, 0:16],
                                                  in1=dtb.t[:].unsqueeze(1).broadcast_to([128, NT_, 16]), op=ALU.add),
                 r=[abT.b, dtb.b], w=[gG.b])
            P.op("act", lambda e: e.activation(out=gG.t[:], in_=gG.t[:], func=AF.Exp), r=[gG.b], w=[gG.b])
            P.op("act", lambda e: e.activation(out=gG.t[:], in_=gG.t[:], func=AF.Ln, bias=ones_f.t[:, 0:1], scale=1.0),
                 r=[gG.b, ones_f.b], w=[gG.b])
            P.op("dve", lambda e: e.tensor_tensor(out=gG.t[:], in0=gG.t[:],
                                                  in1=nA.t[:].unsqueeze(1).broadcast_to([128, NT_, 16]), op=ALU.mult),
                 r=[gG.b, nA.b], w=[gG.b])
            P.op("act", lambda e: e.activation(out=bG.t[:], in_=abT.t[:, :, 16:32], func=AF.Sigmoid), r=[abT.b], w=[bG.b])
            with ExitStack() as st2:
                tmpc = sb("cwtmp", [128, 128], F32, st2)
                P.dma("sp", tmpc.t[0:120, :], conv_w[l].rearrange("j (c p) -> (j c) p", p=128), w=[tmpc.b])
                pst = ps_next()
                P.op("pe", lambda e: e.transpose(out=pst.t[:, 0:120], in_=tmpc.t[0:120, :], identity=cm.t[0:120, 0, 0:120]),
                     r=[tmpc.b, cm.b], w=[pst.b])
                P.op("dve", lambda e: e.tensor_copy(out=cwT.t[:].rearrange("p j c -> p (j c)"), in_=pst.t[:, 0:120]),
                     r=[pst.b], w=[cwT.b])
                P.barrier()
            fmv = s_fm.t.rearrange("(c p) t -> p c t", p=128)
            ov = s_o.t[1].rearrange("(c p) t -> p c t", p=128)
            SEGS = [(0, NCTX), (NCTX, T)]
            idb = cm.t[:, 0, :]
            for h in range(8):
                with ExitStack() as sh:
                    qT = sb("gq", [128, T], F32, sh)
                    k_tm = sb("gktm", [128, NT_, 128], F32, sh)
                    v_tm = sb("gvtm", [128, NT_, 128], F32, sh)
                    KK = sb("gKK", [128, NT_, 128], F32, sh)
                    QKT = sb("gQKT", [128, NT_, 128], F32, sh)
                    with ExitStack() as s1:
                        kT = sb("gk", [128, T], F32, s1)
                        raw = [sb("graw%d" % i, [128, T], BF16, s1) for i in range(3)]
                        acc = [sb("gacc%d" % i, [128, T], F32, s1) for i in range(2)]
                        vT = sb("gv", [128, T], F32, s1)
                        sqbs = [sb("gsq%d" % i, [128, 512], F32, s1) for i in range(2)]
                        rnbs = [sb("grn%d" % i, [128, 512], F32, s1) for i in range(2)]
                        for ci_, which in enumerate(("q", "k", "v")):
                            cidx = ci_ * 8 + h
                            rw = raw[ci_]
                            a_ = acc[ci_ % 2]
                            eng = "dve"
                            P.dma("sp", rw.t[:], fmv[:, OFF_QKVB + cidx, :], r=[s_fm.b], w=[rw.b])
                            P.op(eng, lambda e: e.tensor_scalar(out=a_.t[:], in0=rw.t[:], scalar1=cwT.t[:, 2, cidx:cidx + 1], scalar2=None,
                                                                op0=ALU.mult), r=[rw.b, cwT.b], w=[a_.b])
                            for j in (0, 1, 3, 4):
                                d_ = j - 2
                                for (s0, s1_) in SEGS:
                                    lo, hi = max(s0, s0 - d_), min(s1_, s1_ - d_)
                                    P.op(eng, lambda e: e.scalar_tensor_tensor(
                                        out=a_.t[:, lo:hi], in0=rw.t[:, lo + d_:hi + d_], scalar=cwT.t[:, j, cidx:cidx + 1],
                                        in1=a_.t[:, lo:hi], op0=ALU.mult, op1=ALU.add), r=[rw.b, cwT.b, a_.b], w=[a_.b])
                            dst = {"q": qT, "k": kT, "v": vT}[which]
                            if which == "v":
                                P.op("act", lambda e: e.activation(out=dst.t[:], in_=a_.t[:], func=AF.Silu), r=[a_.b], w=[dst.b])
                            else:
                                P.op("act", lambda e: e.activation(out=a_.t[:], in_=a_.t[:], func=AF.Silu), r=[a_.b], w=[a_.b])
                                for tgi, (t0, n) in enumerate(TGS):
                                    sqb, rnb = sqbs[tgi % 2], rnbs[tgi % 2]
                                    P.op("pool", lambda e: e.tensor_tensor(out=sqb.t[:, 0:n], in0=a_.t[:, t0:t0 + n], in1=a_.t[:, t0:t0 + n],
                                                                          op=ALU.mult), r=[a_.b], w=[sqb.b])
                                    pq = ps_next()
                                    P.op("pe", lambda e: e.matmul(pq.t[:, 0:n], lhsT=ones_f.t[:], rhs=sqb.t[:, 0:n], start=True, stop=True),
                                         r=[sqb.b, ones_f.b], w=[pq.b])
                                    P.op("act", lambda e: e.activation(out=rnb.t[:, 0:n], in_=pq.t[:, 0:n], func=AF.Sqrt,
                                                                       bias=epsc.t[:, 0:1], scale=1.0), r=[pq.b, epsc.b], w=[rnb.b])
                                    P.op("dve", lambda e: e.reciprocal(out=rnb.t[:, 0:n], in_=rnb.t[:, 0:n]), r=[rnb.b], w=[rnb.b])
                                    if which == "q":
                                        P.op("dve", lambda e: e.scalar_tensor_tensor(
                                            out=dst.t[:, t0:t0 + n], in0=a_.t[:, t0:t0 + n], scalar=SCALE, in1=rnb.t[:, 0:n],
                                            op0=ALU.mult, op1=ALU.mult), r=[a_.b, rnb.b], w=[dst.b])
                                    else:
                                        P.op("dve", lambda e: e.tensor_tensor(out=dst.t[:, t0:t0 + n], in0=a_.t[:, t0:t0 + n],
                                                                              in1=rnb.t[:, 0:n], op=ALU.mult), r=[a_.b, rnb.b], w=[dst.b])
                        ei = 0
                        for (g0, gn) in TGRP:
                            for (src_, dst_) in ((kT, k_tm), (vT, v_tm)):
                                pt = ps_next()
                                for ti in range(gn):
                                    tt = g0 + ti
                                    P.op("pe", lambda e: e.transpose(out=pt.t[:, ti * 128:(ti + 1) * 128], in_=src_.t[:, tt * 128:(tt + 1) * 128],
                                                                     identity=idb), r=[src_.b, cm.b], w=[pt.b], inc=(ti == gn - 1))
                                ei += 1
                                if ei % 2:
                                    P.op("act", lambda e: e.activation(out=dst_.t[:, g0:g0 + gn, :].rearrange("p a b -> p (a b)"),
                                                                       in_=pt.t[:, 0:gn * 128], func=AF.Copy), r=[pt.b], w=[dst_.b])
                                else:
                                    P.op("dve", lambda e: e.tensor_copy(out=dst_.t[:, g0:g0 + gn, :].rearrange("p a b -> p (a b)"),
                                                                        in_=pt.t[:, 0:gn * 128]), r=[pt.b], w=[dst_.b])
                            for (rhs_, dst_) in ((kT, KK), (qT, QKT)):
                                pt = ps_next()
                                for ti in range(gn):
                                    tt = g0 + ti
                                    P.op("pe", lambda e: e.matmul(pt.t[:, ti * 128:(ti + 1) * 128], lhsT=kT.t[:, tt * 128:(tt + 1) * 128],
                                                                  rhs=rhs_.t[:, tt * 128:(tt + 1) * 128], start=True, stop=True),
                                         r=[kT.b, rhs_.b], w=[pt.b], inc=(ti == gn - 1))
                                ei += 1
                                if ei % 2:
                                    P.op("act", lambda e: e.activation(out=dst_.t[:, g0:g0 + gn, :].rearrange("p a b -> p (a b)"),
                                                                       in_=pt.t[:, 0:gn * 128], func=AF.Copy), r=[pt.b], w=[dst_.b])
                                else:
                                    P.op("dve", lambda e: e.tensor_copy(out=dst_.t[:, g0:g0 + gn, :].rearrange("p a b -> p (a b)"),
                                                                        in_=pt.t[:, 0:gn * 128]), r=[pt.b], w=[dst_.b])
                        P.barrier()
                    dirs = []
                    pre = []
                    for di in range(2):
                        pre.append((sb("gwT%d" % di, [128, T], F32, sh), sb("gqg%d" % di, [128, T], BF16, sh),
                                    sb("gkd%d" % di, [128, NT_, 128], BF16, sh), sb("gat%d" % di, [128, NT_, 128], BF16, sh),
                                    sb("gu%d" % di, [128, NT_, 128], F32, sh), sb("gegl%d" % di, [128, NT_, 2], F32, sh)))
                    s2 = ExitStack()
                    jobs = []
                    slots = []
                    for si in range(2):
                        B = {}
                        for nm in ("GU", "Gb", "EL", "EU", "Lf", "Dg", "kbg", "vb", "X", "N00", "N01", "N10", "N11", "X0", "X1"):
                            B[nm] = sb("g%s_%d" % (nm, si), [128, 512], F32, s2)
                        slots.append(B)
                    pacc = PSPool([0, 1, 2, 3, 4, 5, 6, 7])
                    for di in range(2):
                        iU, iML, iMU, iSL = (4, 6, 7, 8) if di == 0 else (5, 7, 6, 9)
                        gcol = di * 8 + h
                        wT, qgT, kd, attnT, u_, egl = pre[di]
                        dirs.append((wT, qgT, kd, attnT, u_, egl))
                        if True:
                            gc = sb("ggc", [128, NT_], F32, s2)
                            egc = sb("gegc", [128, NT_], F32, s2)
                            ekd = sb("gekd", [128, NT_], F32, s2)
                            bsc = sb("gbsc", [128, NT_], F32, s2)
                            ghd = sb("gghd", [128, NT_], F32, s2)
                            bhd = sb("gbhd", [128, NT_], F32, s2)
                            P.op("dve", lambda e: e.tensor_copy(out=ghd.t[:], in_=gG.t[:, :, gcol]), r=[gG.b], w=[ghd.b])
                            P.op("dve", lambda e: e.tensor_copy(out=bhd.t[:], in_=bG.t[:, :, gcol]), r=[bG.b], w=[bhd.b])
                            pg = ps_next()
                            P.op("pe", lambda e: e.matmul(pg.t[:, 0:NT_], lhsT=cm.t[:, iU, :], rhs=ghd.t[:], start=True, stop=True),
                                 r=[cm.b, ghd.b], w=[pg.b], inc=False)
                            P.op("pe", lambda e: e.matmul(pg.t[:, 32:32 + NT_], lhsT=cm.t[:, 10, :], rhs=ghd.t[:], start=True, stop=True),
                                 r=[cm.b, ghd.b], w=[pg.b], inc=False)
                            P.op("pe", lambda e: e.matmul(pg.t[:, 64:64 + NT_], lhsT=cm.t[:, 11, :], rhs=ghd.t[:], start=True, stop=True),
                                 r=[cm.b, ghd.b], w=[pg.b], inc=False)
                            P.op("pe", lambda e: e.matmul(pg.t[:, 96:96 + NT_], lhsT=cm.t[:, 12, :], rhs=ghd.t[:], start=True, stop=True),
                                 r=[cm.b, ghd.b], w=[pg.b])
                            P.op("dve", lambda e: e.tensor_copy(out=gc.t[:], in_=pg.t[:, 0:NT_]), r=[pg.b], w=[gc.b])
                            P.op("dve", lambda e: e.tensor_tensor(out=ekd.t[:], in0=pg.t[:, 32:32 + NT_], in1=gc.t[:], op=ALU.subtract),
                                 r=[pg.b, gc.b], w=[ekd.b])
                            P.op("dve", lambda e: e.tensor_copy(out=egl.t[:, :, 0], in_=pg.t[:, 64:64 + NT_]), r=[pg.b], w=[egl.b])
                            P.op("dve", lambda e: e.tensor_copy(out=egl.t[:, :, 1], in_=pg.t[:, 96:96 + NT_]), r=[pg.b], w=[egl.b])
                            P.op("act", lambda e: e.activation(out=egc.t[:], in_=gc.t[:], func=AF.Exp), r=[gc.b], w=[egc.b])
                            P.op("act", lambda e: e.activation(out=ekd.t[:], in_=ekd.t[:], func=AF.Exp), r=[ekd.b], w=[ekd.b])
                            P.op("act", lambda e: e.activation(out=egl.t[:], in_=egl.t[:], func=AF.Exp), r=[egl.b], w=[egl.b])
                            P.op("dve", lambda e: e.tensor_tensor(out=bsc.t[:], in0=bhd.t[:], in1=egc.t[:], op=ALU.mult),
                                 r=[bhd.b, egc.b], w=[bsc.b])
                            P.op("pool", lambda e: e.tensor_tensor(out=kd.t[:], in0=k_tm.t[:],
                                                                  in1=ekd.t[:].unsqueeze(2).broadcast_to([128, NT_, 128]), op=ALU.mult),
                                 r=[k_tm.b, ekd.b], w=[kd.b])
                            def group_gen(g0, gn, B, iU=iU, iML=iML, iMU=iMU, iSL=iSL, ghd=ghd, bhd=bhd, bsc=bsc, egc=egc,
                                          wT=wT, qgT=qgT, attnT=attnT, u_=u_):
                                W_ = gn * 128
                                GU, Gb, EL, EU, Lf, Dg, kbg, vb, Xg = (B[k_] for k_ in ("GU", "Gb", "EL", "EU", "Lf", "Dg", "kbg", "vb", "X"))
                                nb = [[B["N00"], B["N01"]], [B["N10"], B["N11"]]]
                                Xb = [B["X0"], B["X1"]]
                                g3 = lambda t_: t_.t[:, 0:W_].rearrange("p (a b) -> p a b", b=128)
                                p3 = lambda p_: p_.t[:, 0:W_].rearrange("p (a b) -> p a b", b=128)
                                gsl = ghd.t[:, g0:g0 + gn].unsqueeze(2).broadcast_to([128, gn, 128])
                                bcg = lambda t_: t_.t[:, g0:g0 + gn].unsqueeze(2).broadcast_to([128, gn, 128])
                                cmb = lambda i_: cm.t[:, i_, :].unsqueeze(1).broadcast_to([128, gn, 128])
                                P.op("dve", lambda e: e.tensor_tensor(out=g3(GU), in0=cmb(iU), in1=gsl, op=ALU.mult), r=[cm.b, ghd.b], w=[GU.b])
                                P.op("pool", lambda e: e.tensor_copy(out=g3(Gb), in_=gsl), r=[ghd.b], w=[Gb.b])
                                P.op("pool", lambda e: e.tensor_tensor(out=g3(kbg), in0=k_tm.t[:, g0:g0 + gn, :], in1=bcg(bsc), op=ALU.mult),
                                     r=[k_tm.b, bsc.b], w=[kbg.b])
                                P.op("pool", lambda e: e.tensor_tensor(out=g3(vb), in0=v_tm.t[:, g0:g0 + gn, :], in1=bcg(bhd), op=ALU.mult),
                                     r=[v_tm.b, bhd.b], w=[vb.b])
                                P.op("pool", lambda e: e.tensor_tensor(out=g3(Dg), in0=cmb(0), in1=bcg(egc), op=ALU.mult),
                                     r=[cm.b, egc.b], w=[Dg.b])
                                pD = pacc.next()
                                P.op("pe", lambda e: e.matmul(pD.t[:, 0:W_], lhsT=cm.t[:, iU, :], rhs=Gb.t[:, 0:W_], start=True, stop=False),
                                     r=[cm.b, Gb.b], w=[pD.b], inc=False)
                                P.op("pe", lambda e: e.matmul(pD.t[:, 0:W_], lhsT=negones.t[:], rhs=GU.t[:, 0:W_], start=False, stop=True),
                                     r=[negones.b, GU.b], w=[pD.b])
                                pq = pacc.next()
                                P.op("pe", lambda e: e.matmul(pq.t[:, 0:W_], lhsT=ones_f.t[:], rhs=Dg.t[:, 0:W_], start=True, stop=True),
                                     r=[ones_f.b, Dg.b], w=[pq.b])
                                yield
                                P.op("dve", lambda e: e.tensor_tensor(out=g3(EL), in0=p3(pD), in1=cmb(iML), op=ALU.add), r=[pD.b, cm.b], w=[EL.b])
                                P.op("dve", lambda e: e.scalar_tensor_tensor(out=g3(EU), in0=p3(pD), scalar=-1.0, in1=cmb(iMU),
                                                                             op0=ALU.mult, op1=ALU.add), r=[pD.b, cm.b], w=[EU.b])
                                P.op("dve", lambda e: e.tensor_tensor(out=qgT.t[:, g0 * 128:g0 * 128 + W_], in0=pq.t[:, 0:W_],
                                                                      in1=qT.t[:, g0 * 128:g0 * 128 + W_], op=ALU.mult),
                                     r=[pq.b, qT.b], w=[qgT.b])
                                P.op("act", lambda e: e.activation(out=EL.t[:, 0:W_], in_=EL.t[:, 0:W_], func=AF.Exp), r=[EL.b], w=[EL.b])
                                P.op("act", lambda e: e.activation(out=EU.t[:, 0:W_], in_=EU.t[:, 0:W_], func=AF.Exp), r=[EU.b], w=[EU.b])
                                P.op("pool", lambda e: e.tensor_tensor(out=g3(Lf), in0=g3(EL), in1=KK.t[:, g0:g0 + gn, :], op=ALU.mult),
                                     r=[EL.b, KK.b], w=[Lf.b])
                                P.op("pool", lambda e: e.tensor_tensor(out=g3(Lf), in0=g3(Lf), in1=cmb(iSL), op=ALU.mult),
                                     r=[Lf.b, cm.b], w=[Lf.b])
                                P.op("pool", lambda e: e.tensor_tensor(out=g3(Lf), in0=g3(Lf), in1=bcg(bhd), op=ALU.mult),
                                     r=[Lf.b, bhd.b], w=[Lf.b])
                                P.op("dve", lambda e: e.tensor_tensor(out=attnT.t[:, g0:g0 + gn, :], in0=g3(EU), in1=QKT.t[:, g0:g0 + gn, :],
                                                                      op=ALU.mult), r=[EU.b, QKT.b], w=[attnT.b])
                                NT0, N0 = nb[1][0], nb[0][0]
                                P.op("dve", lambda e: e.tensor_scalar(out=NT0.t[:, 0:W_], in0=Lf.t[:, 0:W_], scalar1=-1.0, scalar2=None,
                                                                      op0=ALU.mult), r=[Lf.b], w=[NT0.b])
                                pT_ = pacc.next()
                                for ti in range(gn):
                                    P.op("pe", lambda e: e.transpose(out=pT_.t[:, ti * 128:(ti + 1) * 128], in_=Lf.t[:, ti * 128:(ti + 1) * 128],
                                                                     identity=idb), r=[Lf.b, cm.b], w=[pT_.b], inc=(ti == gn - 1))
                                yield
                                P.op("act", lambda e: e.activation(out=N0.t[:, 0:W_], in_=pT_.t[:, 0:W_], func=AF.Copy, scale=-1.0),
                                     r=[pT_.b], w=[N0.b])
                                Xc = Xb[0]
                                P.op("dve", lambda e: e.scalar_tensor_tensor(out=g3(Xc), in0=p3(pT_), scalar=-1.0, in1=cmb(0),
                                                                             op0=ALU.mult, op1=ALU.add), r=[pT_.b, cm.b], w=[Xc.b])
                                Nc, NTc = N0, NT0
                                for k_ in range(5):
                                    par = (k_ + 1) % 2
                                    Nn, NTn = nb[0][par], nb[1][par]
                                    Xn = Xb[(k_ + 1) % 2]
                                    pN = pacc.next() if k_ < 4 else None
                                    pNT = pacc.next()
                                    for ti in range(gn):
                                        sl = slice(ti * 128, (ti + 1) * 128)
                                        P.op("pe", lambda e: e.matmul(pNT.t[:, sl], lhsT=Nc.t[:, sl], rhs=NTc.t[:, sl], start=True, stop=True),
                                             r=[NTc.b, Nc.b], w=[pNT.b], inc=(ti == gn - 1))
                                    if k_ < 4:
                                        for ti in range(gn):
                                            sl = slice(ti * 128, (ti + 1) * 128)
                                            P.op("pe", lambda e: e.matmul(pN.t[:, sl], lhsT=NTc.t[:, sl], rhs=Nc.t[:, sl], start=True, stop=True),
                                                 r=[NTc.b, Nc.b], w=[pN.b], inc=(ti == gn - 1))
                                    yield
                                    P.op("dve", lambda e: e.tensor_copy(out=NTn.t[:, 0:W_], in_=pNT.t[:, 0:W_]), r=[pNT.b], w=[NTn.b])
                                    if k_ < 4:
                                        P.op("act", lambda e: e.activation(out=Nn.t[:, 0:W_], in_=pN.t[:, 0:W_], func=AF.Copy),
                                             r=[pN.b], w=[Nn.b])
                                    pX = pacc.next()
                                    for ti in range(gn):
                                        sl = slice(ti * 128, (ti + 1) * 128)
                                        P.op("pe", lambda e: e.matmul(pX.t[:, sl], lhsT=NTn.t[:, sl], rhs=Xc.t[:, sl], start=True, stop=True),
                                             r=[NTn.b, Xc.b], w=[pX.b], inc=(ti == gn - 1))
                                    yield
                                    dstX = Xn if k_ < 4 else Xg
                                    P.op("dve", lambda e: e.tensor_tensor(out=dstX.t[:, 0:W_], in0=pX.t[:, 0:W_], in1=Xc.t[:, 0:W_], op=ALU.add),
                                         r=[pX.b, Xc.b], w=[dstX.b])
                                    Nc, NTc, Xc = Nn, NTn, Xn
                                pu = pacc.next()
                                pw = pacc.next()
                                for ti in range(gn):
                                    sl = slice(ti * 128, (ti + 1) * 128)
                                    P.op("pe", lambda e: e.matmul(pu.t[:, sl], lhsT=Xg.t[:, sl], rhs=vb.t[:, sl], start=True, stop=True),
                                         r=[Xg.b, vb.b], w=[pu.b], inc=(ti == gn - 1))
                                for ti in range(gn):
                                    sl = slice(ti * 128, (ti + 1) * 128)
                                    P.op("pe", lambda e: e.matmul(pw.t[:, sl], lhsT=kbg.t[:, sl], rhs=Xg.t[:, sl], start=True, stop=True),
                                         r=[Xg.b, kbg.b], w=[pw.b], inc=(ti == gn - 1))
                                yield
                                P.op("act", lambda e: e.activation(out=u_.t[:, g0:g0 + gn, :].rearrange("p a b -> p (a b)"), in_=pu.t[:, 0:W_],
                                                                   func=AF.Copy), r=[pu.b], w=[u_.b])
                                P.op("act", lambda e: e.activation(out=wT.t[:, g0 * 128:g0 * 128 + W_], in_=pw.t[:, 0:W_], func=AF.Copy),
                                     r=[pw.b], w=[wT.b])

                            for (g0_, gn_) in TGRP:
                                jobs.append((group_gen, g0_, gn_))
                    active = []
                    free = [0, 1]
                    ji = 0
                    while ji < len(jobs) or active:
                        while free and ji < len(jobs):
                            si = free.pop(0)
                            gf, g0_, gn_ = jobs[ji]
                            ji += 1
                            active.append([gf(g0_, gn_, slots[si]), si])
                        for a_ in list(active):
                            try:
                                next(a_[0])
                            except StopIteration:
                                active.remove(a_)
                                free.append(a_[1])
                    P.barrier()
                    s2.close()
                    with ExitStack() as s3:
                        oacc = sb("goacc", [128, T], F32, s3)
                        otmp = sb("gotmp", [128, 64], F32, s3)
                        Sst = [sb("gS%d" % i, [128, 128], F32, s3) for i in range(2)]
                        Sbf = [sb("gSb%d" % i, [128, 128], BF16, s3) for i in range(2)]
                        vnw = [[sb("gvn%d_%d" % (i, j), [128, 128], BF16, s3) for j in range(2)] for i in range(2)]
                        for i in range(2):
                            P.op("dve", lambda e: e.memset(Sst[i].t[:], 0.0), w=[Sst[i].b])
                            P.op("dve", lambda e: e.memset(Sbf[i].t[:], 0.0), w=[Sbf[i].b])
                        order_f = list(range(36))
                        order_b = [3, 2, 1, 0] + list(range(35, 3, -1))
                        written = set()
                        for step in range(36):
                            ctxs = []
                            for di in range(2):
                                c = (order_f, order_b)[di][step]
                                tt, half = c // 2, c % 2
                                ctxs.append(dict(c=c, tt=tt, half=half, r0=half * 64, d=dirs[di], S_=Sst[di], Sb_=Sbf[di],
                                                 vn=vnw[di][step % 2], pA=psum[3 * di], pB=psum[3 * di + 1], pC=psum[3 * di + 2],
                                                 tsl=slice(tt * 128, (tt + 1) * 128)))
                            for X in ctxs:
                                wT = X["d"][0]
                                P.op("pe", lambda e: e.matmul(X["pA"].t[:, 0:128], lhsT=wT.t[:, X["tsl"]], rhs=X["S_"].t[:], start=True, stop=True),
                                     r=[wT.b, X["S_"].b], w=[X["pA"].b])
                            for X in ctxs:
                                u_ = X["d"][4]
                                r0, tt, vn = X["r0"], X["tt"], X["vn"]
                                P.op("dve", lambda e: e.tensor_tensor(out=vn.t[r0:r0 + 64, :], in0=u_.t[r0:r0 + 64, tt, :],
                                                                      in1=X["pA"].t[r0:r0 + 64, 0:128], op=ALU.subtract),
                                     r=[u_.b, X["pA"].b], w=[vn.b])
                            for X in ctxs:
                                wT, qgT, kd, attnT, u_, egl = X["d"]
                                r0, tt, vn, pB, pC, Sb_ = X["r0"], X["tt"], X["vn"], X["pB"], X["pC"], X["Sb_"]
                                P.op("pe", lambda e: e.matmul(pC.t[:, 0:128], lhsT=kd.t[r0:r0 + 64, tt, :], rhs=vn.t[r0:r0 + 64, :],
                                                              start=True, stop=True), r=[kd.b, vn.b], w=[pC.b])
                                P.op("pe", lambda e: e.matmul(pB.t[:, 0:128], lhsT=Sb_.t[:], rhs=qgT.t[:, X["tsl"]], start=True, stop=False),
                                     r=[Sb_.b, qgT.b], w=[pB.b], inc=False)
                                P.op("pe", lambda e: e.matmul(pB.t[:, 0:128], lhsT=vn.t[r0:r0 + 64, :], rhs=attnT.t[r0:r0 + 64, tt, :],
                                                              start=False, stop=True), r=[vn.b, attnT.b], w=[pB.b])
                            for X in ctxs:
                                egl = X["d"][5]
                                S_, pC, tt, half = X["S_"], X["pC"], X["tt"], X["half"]
                                P.op("dve", lambda e: e.scalar_tensor_tensor(out=S_.t[:], in0=S_.t[:], scalar=egl.t[:, tt, half:half + 1],
                                                                             in1=pC.t[:, 0:128], op0=ALU.mult, op1=ALU.add),
                                     r=[S_.b, egl.b, pC.b], w=[S_.b])
                            for X in ctxs:
                                S_, Sb_, pB, c, r0 = X["S_"], X["Sb_"], X["pB"], X["c"], X["r0"]
                                P.op("act", lambda e: e.activation(out=Sb_.t[:], in_=S_.t[:], func=AF.Copy), r=[S_.b], w=[Sb_.b])
                                osl = slice(c * 64, c * 64 + 64)
                                if c not in written:
                                    written.add(c)
                                    P.op("act", lambda e: e.activation(out=oacc.t[:, osl], in_=pB.t[:, r0:r0 + 64], func=AF.Copy),
                                         r=[pB.b], w=[oacc.b])
                                else:
                                    P.op("act", lambda e: e.activation(out=otmp.t[:, 0:64], in_=pB.t[:, r0:r0 + 64], func=AF.Copy),
                                         r=[pB.b], w=[otmp.b])
                                    P.op("pool", lambda e: e.tensor_tensor(out=oacc.t[:, osl], in0=oacc.t[:, osl], in1=otmp.t[:, 0:64],
                                                                          op=ALU.add), r=[otmp.b, oacc.b], w=[oacc.b])
                        zs = sb("gzs", [128, T], BF16, s3)
                        ob = sb("gob", [128, T], BF16, s3)
                        sq2 = sb("gsq2", [128, 512], F32, s3)
                        rn2 = sb("grn2", [128, 512], F32, s3)
                        P.dma("sp", zs.t[:], fmv[:, OFF_ZB + h, :], r=[s_fm.b], w=[zs.b])
                        for (t0, n) in TGS:
                            P.op("pool", lambda e: e.tensor_tensor(out=sq2.t[:, 0:n], in0=oacc.t[:, t0:t0 + n], in1=oacc.t[:, t0:t0 + n],
                                                                  op=ALU.mult), r=[oacc.b], w=[sq2.b])
                            pq = ps_next()
                            P.op("pe", lambda e: e.matmul(pq.t[:, 0:n], lhsT=ones_f.t[:], rhs=sq2.t[:, 0:n], start=True, stop=True),
                                 r=[sq2.b, ones_f.b], w=[pq.b])
                            P.op("act", lambda e: e.activation(out=rn2.t[:, 0:n], in_=pq.t[:, 0:n], func=AF.Sqrt, bias=epsc.t[:, 0:1],
                                                               scale=1.0 / 128), r=[pq.b, epsc.b], w=[rn2.b])
                            P.op("dve", lambda e: e.reciprocal(out=rn2.t[:, 0:n], in_=rn2.t[:, 0:n]), r=[rn2.b], w=[rn2.b])
                            P.op("dve", lambda e: e.scalar_tensor_tensor(out=rn2.t[:, 0:n], in0=oacc.t[:, t0:t0 + n], scalar=gg.t[:, 0:1],
                                                                         in1=rn2.t[:, 0:n], op0=ALU.mult, op1=ALU.mult),
                                 r=[oacc.b, gg.b, rn2.b], w=[rn2.b])
                            P.op("pool", lambda e: e.tensor_tensor(out=ob.t[:, t0:t0 + n], in0=rn2.t[:, 0:n], in1=zs.t[:, t0:t0 + n],
                                                                  op=ALU.mult), r=[rn2.b, zs.b], w=[ob.b])
                        P.dma("sp", ov[:, h, :], ob.t[:], r=[ob.b], w=[s_o.b])
                        P.barrier()

    if "M" in cfg.stages:
        stage_M()
    for l in range(L):
        layer_coefs(l)
        last = (l == L - 1) and not cfg.force_ctx
        for s in range(NS):
            tgs = TGS[1:] if last else TGS
            if "B" in cfg.stages:
                stage_AB(l, s)
            if s == 0 and ("D" in cfg.stages or "E" in cfg.stages):
                stage_W(l)
            if "C" in cfg.stages:
                if "A" in cfg.mixers:
                    stage_C1(l, s, not last)
                if "N" in cfg.mixers:
                    stage_C2(l, s, not last)
                if "B" in cfg.mixers:
                    stage_C3(l, s)
            if "D" in cfg.stages:
                stage_D(l, s, tgs)
            if "E" in cfg.stages:
                stage_E(l, s, tgs)
    P.barrier()
    gs.close()
    return nc, P


def rope_tables():
    t = np.arange(NX)
    nf = HD // 4
    inv = (10000.0 ** (-np.arange(nf, dtype=np.float32) / nf)).astype(np.float32)
    ang_r = (t // 64).astype(np.float32)[:, None] * inv
    ang_c = (t % 64).astype(np.float32)[:, None] * inv
    cosT = np.ones((128, T), np.float32)
    sinT = np.zeros((128, T), np.float32)
    for a, ang in enumerate((ang_r, ang_c)):
        c = np.cos(ang).T.astype(np.float32)
        s_ = np.sin(ang).T.astype(np.float32)
        cosT[a * 64:a * 64 + 32, NCTX:] = c
        cosT[a * 64 + 32:a * 64 + 64, NCTX:] = c
        sinT[a * 64:a * 64 + 32, NCTX:] = -s_
        sinT[a * 64 + 32:a * 64 + 64, NCTX:] = s_
    return cosT, sinT


def w_in_cols():
    qa0, ka0, va0 = 0, 1024, 1280
    qb0 = 1536
    zb0 = qb0 + 3072
    ab0 = zb0 + 1024
    qn0 = ab0 + 32
    kn0, vn0 = qn0 + 1024, qn0 + 2048
    g0 = qn0 + 3072
    perm = np.concatenate([np.arange(32, 64), np.arange(0, 32), np.arange(96, 128), np.arange(64, 96)])
    cols = []
    for h in range(8):
        base = qa0 + h * 128
        cols.append(base + np.arange(128))
        cols.append(base + perm)
    for h in range(2):
        base = ka0 + h * 128
        cols.append(base + np.arange(128))
        cols.append(base + perm)
    cols.append(np.arange(qb0, qb0 + 3072))
    cols.append(np.arange(zb0, zb0 + 1024))
    cols.append(np.arange(qn0, qn0 + 1024))
    cols.append(np.arange(kn0, kn0 + 1024))
    cols.append(np.arange(g0, g0 + 6144))
    cols.append(np.arange(va0, va0 + 256))
    cols.append(np.arange(vn0, vn0 + 1024))
    cols.append(np.arange(ab0, ab0 + 32))
    cols = np.concatenate(cols)
    assert cols.shape[0] == W_ALL
    return cols


def const_masks():
    m = np.zeros((128, 13, 128), np.float32)
    m[:, 0, :] = np.eye(128, dtype=np.float32)
    p = np.arange(128)[:, None]
    f = np.arange(128)[None, :]
    m[:, 1, :] = (p >= f)
    m[:, 2, :] = (p <= f)
    kc = np.arange(64)[:, None]
    qc = 63 - np.arange(64)[None, :]
    cs = np.clip(qc - 8, 0, 48)
    ok = (kc >= cs) & (kc < cs + 16)
    m[:64, 3, :64] = np.where(ok, 0.0, -30000.0)
    same = (p // 64) == (f // 64)
    m[:, 4, :] = same & (p <= f)
    m[:, 5, :] = same & (p >= f)
    m[:, 6, :] = np.where(same & (p >= f), 0.0, -30000.0)
    m[:, 7, :] = np.where(same & (p <= f), 0.0, -30000.0)
    m[:, 8, :] = same & (p > f)
    m[:, 9, :] = same & (p < f)
    m[:, 10, :] = same
    m[:, 11, :] = (p < 64) & (f >= 0)
    m[:, 12, :] = (p >= 64) & (f >= 0)
    return m


def _tile_w(w, kcb):
    lead = w.shape[:-2]
    K_, N_ = w.shape[-2:]
    kg = K_ // (128 * kcb)
    a = w.reshape(lead + (kg, kcb, 128, N_ // 512, 512))
    nl = len(lead)
    a = np.transpose(a, tuple(range(nl)) + (nl + 3, nl + 0, nl + 2, nl + 1, nl + 4))
    a = np.ascontiguousarray(a).reshape(lead + (N_ // 512, kg, 128, kcb * 512))
    if kg == 1:
        a = a.reshape(lead + (N_ // 512, 128, kcb * 512))
    return a


def host_inputs(inputs, n_cores=8):
    x = np.asarray(inputs["x"], np.float32)
    ctx = np.asarray(inputs["ctx"], np.float32)
    c = np.asarray(inputs["c"], np.float32)
    cols = w_in_cols()
    shared = {
        "w_mod": np.ascontiguousarray(inputs["w_mod"], np.float32),
        "b_mod": np.ascontiguousarray(inputs["b_mod"], np.float32),
        "gvec": np.ascontiguousarray(np.stack([inputs["g_pre_mix"], inputs["g_post_mix"], inputs["g_pre_mlp"],
                                               inputs["g_post_mlp"]], axis=1), np.float32),
        "w_in": np.ascontiguousarray(np.asarray(inputs["w_in"], np.float32)[:, :, cols]),
        "conv_w": np.ascontiguousarray(inputs["conv_w"], np.float32),
        "a_log": np.ascontiguousarray(np.asarray(inputs["a_log"], np.float32).reshape(-1, 16)),
        "dt_bias": np.ascontiguousarray(np.asarray(inputs["dt_bias"], np.float32).reshape(-1, 16)),
        "g_gdn": np.ascontiguousarray(inputs["g_gdn_out"], np.float32),
        "sink": np.ascontiguousarray(inputs["sink"], np.float32),
        "rpb": np.ascontiguousarray(np.asarray(inputs["rpb"], np.float32).reshape(np.asarray(inputs["rpb"]).shape[0], -1)),
        "w_branch": _tile_w(np.asarray(inputs["w_branch"], np.float32), 8),
        "w_out": _tile_w(np.asarray(inputs["w_out"], np.float32), 8),
        "w_up": _tile_w(np.asarray(inputs["w_up"], np.float32), 16),
        "w_down": _tile_w(np.asarray(inputs["w_down"], np.float32), 16),
    }
    cosT, sinT = rope_tables()
    shared["ropec"] = cosT
    shared["ropes"] = sinT
    shared["cmasks"] = const_masks()
    maps = []
    for core in range(n_cores):
        b0 = core * NSEQ
        xin = np.empty((NSEQ, D, T), np.float32)
        for s in range(NSEQ):
            xin[s, :, :NCTX] = ctx[b0 + s].T
            xin[s, :, NCTX:] = x[b0 + s].T
        c3 = np.stack([c[b0], c[b0 + 1], np.asarray(inputs["c_ctx"], np.float32)], axis=0)
        m = dict(shared)
        m["xin"] = xin
        m["c3"] = np.ascontiguousarray(c3)
        maps.append(m)
    return maps


_CACHE = {}


def kernel(**inputs):
    n_cores = 8
    if "nc" not in _CACHE:
        _CACHE["nc"] = build_program(Cfg())[0]
    nc = _CACHE["nc"]
    maps = host_inputs(inputs, n_cores)
    res = run_bass_kernel_spmd(nc, maps, core_ids=list(range(n_cores)))
    out = np.empty((16, NX, D), np.float32)
    for core in range(n_cores):
        y = res.results[core]["yout"]
        for s in range(NSEQ):
            out[core * NSEQ + s] = y[s].T
    return out
```

```python
import math
from contextlib import ExitStack

import numpy as np
import concourse.bass as bass
import concourse.mybir as mybir
from concourse.bass_utils import run_bass_kernel_spmd

F32 = mybir.dt.float32
BF16 = mybir.dt.bfloat16
AF = mybir.ActivationFunctionType
ALU = mybir.AluOpType
AX = mybir.AxisListType

D = 2048
NCTX = 256
NX = 2048
T = NCTX + NX
DEPTH = 4
NSEQ = 2
KC = D // 128
DFF = 4 * D
HD = 128
EPS = 1e-6
TGS = [(0, 256)] + [(256 + 512 * i, 512) for i in range(4)]
NFM = 116
W_FM = NFM * 128
W_TM = 256 + 1024 + 32
W_ALL = W_FM + W_TM


class Buf:
    __slots__ = ("name", "lw", "rd", "excl")

    def __init__(self, name=""):
        self.name = name
        self.lw = None
        self.rd = {}
        self.excl = False


class TB:
    def __init__(self, t, name=""):
        self.t = t
        self.b = Buf(name)


COMPUTE = ("pe", "dve", "act", "pool")
NDS = 24


class Prog:
    def __init__(self, nc):
        self.nc = nc
        self.E = {"pe": nc.tensor, "dve": nc.vector, "act": nc.scalar, "pool": nc.gpsimd, "sp": nc.sync}
        self.sems = []
        self.semidx = {}
        for e in COMPUTE:
            self.semidx[e] = len(self.sems)
            self.sems.append(nc.alloc_semaphore("s_" + e))
        self.cnt = {e: 0 for e in COMPUTE}
        self.pend = {e: None for e in COMPUTE}
        self.seen = {e: {} for e in self.E}
        self.dslots = []
        for i in range(NDS):
            self.dslots.append([len(self.sems), 0])
            self.sems.append(nc.alloc_semaphore("d%d" % i))
        self.dnext = 0
        self.nwaits = 0
        self.nops = 0

    def _wait(self, e, toks):
        need = {}
        for t in toks:
            if t is None:
                continue
            te, si, val = t
            if e == "pe" and te == "pe":
                continue
            assert val is not None, "wait on pending token"
            if self.seen[e].get(si, 0) >= val:
                continue
            if need.get(si, 0) < val:
                need[si] = val
        for si, val in need.items():
            self.E[e].wait_ge(self.sems[si], val)
            self.seen[e][si] = val
            self.nwaits += 1

    def _deps(self, e, r, w):
        deps = []
        for b in r:
            if b.lw is not None:
                deps.append(b.lw)
            if b.excl:
                for k, t in b.rd.items():
                    if k != e:
                        deps.append(t)
        for b in w:
            if b.lw is not None:
                deps.append(b.lw)
            for k, t in b.rd.items():
                if k == e and e in COMPUTE:
                    continue
                deps.append(t)
        return deps

    def op(self, e, fn, r=(), w=(), inc=True):
        self._wait(e, self._deps(e, r, w))
        ins = fn(self.E[e])
        self.nops += 1
        if inc:
            self.cnt[e] += 1
            ins.then_inc(self.sems[self.semidx[e]], 1)
            tok = self.pend[e]
            if tok is None:
                tok = [e, self.semidx[e], None]
            tok[2] = self.cnt[e]
            self.pend[e] = None
        else:
            tok = self.pend[e]
            if tok is None:
                tok = self.pend[e] = [e, self.semidx[e], None]
        for b in r:
            b.rd[e] = tok
        for b in w:
            b.lw = tok
            b.rd = {}
        return ins

    def dma(self, q, out, in_, r=(), w=(), **kw):
        deps = self._deps(q, r, w)
        slot = self.dslots[self.dnext]
        self.dnext = (self.dnext + 1) % NDS
        if slot[1] > 0:
            deps.append(["dma", slot[0], slot[1]])
        self._wait(q, deps)
        ins = self.E[q].dma_start(out=out, in_=in_, **kw)
        slot[1] += 16
        ins.then_inc(self.sems[slot[0]], 16)
        self.nops += 1
        tok = ["dma", slot[0], slot[1]]
        for b in r:
            b.rd[("dma", slot[0])] = tok
        for b in w:
            b.lw = tok
            b.rd = {}
        return ins

    def all_toks(self):
        toks = []
        for e in COMPUTE:
            assert self.pend[e] is None, "pending at barrier on " + e
            if self.cnt[e] > 0:
                toks.append(["x", self.semidx[e], self.cnt[e]])
        for slot in self.dslots:
            if slot[1] > 0:
                toks.append(["dma", slot[0], slot[1]])
        return toks

    def barrier(self, engines=None):
        toks = self.all_toks()
        for e in (engines or self.E):
            self._wait(e, toks)


class Cfg:
    def __init__(self, **kw):
        self.layers = DEPTH
        self.nseq = NSEQ
        self.debug = False
        self.stages = "MABCDE"
        self.inject = ()
        self.wdepth = DEPTH
        self.ab_parts = "mft"
        self.mixers = "ANB"
        self.force_ctx = False
        self.nfm = NFM
        self.__dict__.update(kw)


def build_program(cfg):
    nc = bass.Bass("TRN2", target_bir_lowering=False)
    P = Prog(nc)
    print("sbuf bytes remaining at start:", nc.sbuf_bytes_remaining)
    L = cfg.layers
    NS = cfg.nseq
    WD = cfg.wdepth

    def din(name, shape, dt=F32):
        return nc.dram_tensor(name, list(shape), dt, kind="ExternalInput").ap()

    def dscratch(name, shape, dt):
        if name in cfg.inject:
            kind = "ExternalInput"
        elif cfg.debug:
            kind = "ExternalOutput"
        else:
            kind = "Internal"
        return TB(nc.dram_tensor(name, list(shape), dt, kind=kind).ap(), name)

    xin = din("xin", [NSEQ, D, T])
    c3 = din("c3", [3, D])
    w_mod = din("w_mod", [WD, D, 6 * D])
    b_mod = din("b_mod", [WD, 6 * D])
    gvec = din("gvec", [WD, 4, D])
    w_in = din("w_in", [WD, D, W_ALL])
    conv_w = din("conv_w", [WD, 5, 3072])
    a_log = din("a_log", [WD, 16])
    dt_bias = din("dt_bias", [WD, 16])
    g_gdn = din("g_gdn", [WD, 128])
    sink = din("sink", [WD, 8])
    rpb = din("rpb", [WD, 8 * 15 * 31])
    w_branch = din("w_branch", [WD, 3, 4, 128, 8 * 512])
    w_out = din("w_out", [WD, 4, 2, 128, 8 * 512])
    w_up = din("w_up", [WD, 16, 128, 16 * 512])
    w_down = din("w_down", [WD, 4, 4, 128, 16 * 512])
    ropec = din("ropec", [128, T])
    ropes = din("ropes", [128, T])
    cmasks = din("cmasks", [128, 13, 128])
    yout = nc.dram_tensor("yout", [NSEQ, D, NX], F32, kind="ExternalOutput").ap()

    xT = dscratch("xT", [NSEQ, D, T], F32)
    s_fm = dscratch("s_fm", [W_FM, T], BF16)
    s_va = dscratch("s_va", [T, 256], BF16)
    s_vn = dscratch("s_vn", [T, 1024], BF16)
    s_ab = dscratch("s_ab", [T, 32], F32)
    s_o = dscratch("s_o", [3, 1024, T], BF16)
    w_up16 = dscratch("w_up16", [16, 128, 16 * 512], BF16)
    w_down16 = dscratch("w_down16", [4, 4, 128, 16 * 512], BF16)
    w_branch16 = dscratch("w_branch16", [3, 4, 128, 8 * 512], BF16)
    w_out16 = dscratch("w_out16", [4, 2, 128, 8 * 512], BF16)
    OFF_QA, OFF_KA, OFF_QKVB, OFF_ZB, OFF_QN, OFF_KN, OFF_GATE = 0, 8, 10, 34, 42, 50, 58
    NFM_OUT = 106

    gs = ExitStack()

    uniq = {"n": 0}

    def sb(name, shape, dt, stack=None):
        uniq["n"] += 1
        nm = "%s_%d" % (name, uniq["n"])
        return TB((stack or gs).enter_context(nc.sbuf_tensor(nm, list(shape), dt)), nm)

    psum = [TB(gs.enter_context(nc.psum_tensor("ps%d" % i, [128, 512], F32)), "ps%d" % i) for i in range(8)]
    for p_ in psum:
        p_.b.excl = True
    pstate = {"i": 0}

    def ps_next():
        p = psum[pstate["i"]]
        pstate["i"] = (pstate["i"] + 1) % 8
        return p

    class PSPool:
        def __init__(self, idx):
            self.idx = list(idx)
            self.i = 0

        def next(self):
            p = psum[self.idx[self.i]]
            self.i = (self.i + 1) % len(self.idx)
            return p

    ones_f = sb("ones_f", [128, 128], F32)
    ones_b = sb("ones_b", [128, 128], BF16)
    modT = sb("modT", [128, DEPTH, 96, 3], F32)
    gT = sb("gT", [128, DEPTH, 4, 16], F32)
    coef = sb("coef", [128, 6, 16, 3], F32)
    negones = sb("negones", [128, 128], F32)
    P.op("dve", lambda e: e.memset(negones.t[:], -1.0), w=[negones.b])
    epsc = sb("epsc", [128, 4], F32)
    P.op("dve", lambda e: e.memset(epsc.t[:], EPS), w=[epsc.b])
    P.op("dve", lambda e: e.memset(ones_f.t[:], 1.0), w=[ones_f.b])
    P.op("dve", lambda e: e.memset(ones_b.t[:], 1.0), w=[ones_b.b])
    cm = sb("cm", [128, 13, 128], F32)
    for j in range(13):
        P.dma("sp", cm.t[:, j, :], cmasks[:, j, :], w=[cm.b])
    ident = cm.t[:, 0, :]

    def load_T(dst_ap, src_rows, nrows, tag):
        with ExitStack() as st:
            tmp = sb("ldT" + tag, [128, 128], F32, st)
            P.dma("sp", tmp.t[0:nrows, :], src_rows, w=[tmp.b])
            ps = ps_next()
            P.op("pe", lambda e: e.transpose(out=ps.t[:, 0:nrows], in_=tmp.t[0:nrows, :], identity=cm.t[0:nrows, 0, 0:nrows]),
                 r=[tmp.b, cm.b], w=[ps.b])
            P.op("dve", lambda e: e.tensor_copy(out=dst_ap, in_=ps.t[:, 0:nrows]), r=[ps.b], w=[gT.b, modT.b])
            P.barrier()

    gv = gvec.rearrange("l k (c p) -> (l k c) p", p=128)
    gflat = gT.t[:].rearrange("p l k c -> p (l k c)")
    for j in range(0, L * 64, 128):
        nr = min(128, L * 64 - j)
        load_T(gflat[:, j:j + nr], gv[j:j + nr, :], nr, "g%d" % j)

    def stage_M():
        with ExitStack() as st:
            scT = sb("scT", [128, 3, 16], F32, st)
            wblk = [sb("wmblk%d" % i, [128, 16, 512], F32, st) for i in range(2)]
            brow = [sb("brow%d" % i, [1, 512], F32, st) for i in range(2)]
            load_T(scT.t[:].rearrange("p r c -> p (r c)"), c3.rearrange("r (c p) -> (r c) p", p=128), 48, "c3")
            P.op("act", lambda e: e.activation(out=scT.t[:], in_=scT.t[:], func=AF.Silu), r=[scT.b], w=[scT.b])
            it = 0
            for l in range(L):
                wv = w_mod[l].rearrange("(kc p) n -> p kc n", p=128)
                for blk in range(24):
                    wb = wblk[it % 2]
                    br = brow[it % 2]
                    it += 1
                    for j in range(4):
                        P.dma("sp", wb.t[:, 4 * j:4 * j + 4, :], wv[:, 4 * j:4 * j + 4, blk * 512:(blk + 1) * 512],
                              w=[wb.b])
                    P.dma("sp", br.t[:], b_mod[l:l + 1, blk * 512:(blk + 1) * 512], w=[br.b])
                    ps = ps_next()
                    for j in range(4):
                        def mm(e, j=j, wb=wb, br=br, ps=ps):
                            for kc in range(16):
                                e.matmul(ps.t[:, 3 * j:3 * j + 3], lhsT=wb.t[:, kc, j * 128:(j + 1) * 128],
                                         rhs=scT.t[:, :, kc], start=(kc == 0), stop=False)
                            return e.matmul(ps.t[:, 3 * j:3 * j + 3], lhsT=br.t[0:1, j * 128:(j + 1) * 128],
                                            rhs=ones_f.t[0:1, 0:3], start=False, stop=True)
                        P.op("pe", mm, r=[wb.b, br.b, scT.b, ones_f.b], w=[ps.b], inc=(j == 3))
                    P.op("dve", lambda e, ps=ps, l=l, blk=blk: e.tensor_copy(
                        out=modT.t[:, l, blk * 4:(blk + 1) * 4, :],
                        in_=ps.t[:, 0:12].rearrange("p (a b) -> p a b", b=3)), r=[ps.b], w=[modT.b])
            P.barrier()

    def stage_W(l):
        for blk in range(16):
            P.dma("pool", w_up16.t[blk], w_up[l, blk], w=[w_up16.b])
        for ob in range(4):
            for kq in range(4):
                P.dma("pool", w_down16.t[ob, kq], w_down[l, ob, kq], w=[w_down16.b])
        for i in range(3):
            for ob in range(4):
                P.dma("pool", w_branch16.t[i, ob], w_branch[l, i, ob], w=[w_branch16.b])
        for ob in range(4):
            for kh in range(2):
                P.dma("pool", w_out16.t[ob, kh], w_out[l, ob, kh], w=[w_out16.b])

    class Prefetch:
        def __init__(self, bufs, loaders, dist=2):
            self.bufs, self.loaders, self.dist, self.issued = bufs, loaders, dist, 0

        def get(self, k):
            while self.issued < len(self.loaders) and self.issued <= k + self.dist:
                self.loaders[self.issued](self.bufs[self.issued % len(self.bufs)])
                self.issued += 1
            return self.bufs[k % len(self.bufs)]

    def layer_coefs(l):
        def m(idx):
            return modT.t[:, l, idx * 16:(idx + 1) * 16, :]

        def g(k):
            return gT.t[:, l, k, :].unsqueeze(2).broadcast_to([128, 16, 3])
        for (dst, gi, mi, plus1) in ((0, 0, 1, True), (2, 1, 2, False), (3, 2, 4, True), (5, 3, 5, False)):
            if plus1:
                P.op("dve", lambda e, dst=dst, gi=gi, mi=mi: e.scalar_tensor_tensor(
                    out=coef.t[:, dst], in0=m(mi), scalar=1.0, in1=g(gi), op0=ALU.add, op1=ALU.mult),
                    r=[modT.b, gT.b], w=[coef.b])
            else:
                P.op("dve", lambda e, dst=dst, gi=gi, mi=mi: e.tensor_tensor(
                    out=coef.t[:, dst], in0=m(mi), in1=g(gi), op=ALU.mult), r=[modT.b, gT.b], w=[coef.b])
        P.op("dve", lambda e: e.tensor_copy(out=coef.t[:, 1], in_=m(0)), r=[modT.b], w=[coef.b])
        P.op("dve", lambda e: e.tensor_copy(out=coef.t[:, 4], in_=m(3)), r=[modT.b], w=[coef.b])

    def modulate(st, src, s, ia, ib, hT, tgs, xt=None):
        if xt is None:
            xt = [sb("mx%d" % i, [128, 16, 512], F32, st) for i in range(2)]
        sq = [sb("msq%d" % i, [128, 512], F32, st) for i in range(2)]
        rstd = sb("mrstd", [128, 512], F32, st)
        tmp = [sb("mtmp%d" % i, [128, 512], F32, st) for i in range(2)]
        srcv = src.rearrange("(c p) t -> p c t", p=128)
        for gi, (t0, n) in enumerate(tgs):
            row = 2 if t0 < NCTX else s
            x = xt[gi % len(xt)]
            for j in range(4):
                P.dma("sp", x.t[:, 4 * j:4 * j + 4, 0:n], srcv[:, 4 * j:4 * j + 4, t0:t0 + n], r=[xT.b], w=[x.b])
            ps = ps_next()
            for kc in range(16):
                q = sq[kc % 2]
                P.op("act", lambda e, q=q, x=x, kc=kc: e.activation(out=q.t[:, 0:n], in_=x.t[:, kc, 0:n], func=AF.Square),
                     r=[x.b], w=[q.b])
                P.op("pe", lambda e, q=q, kc=kc, ps=ps: e.matmul(ps.t[:, 0:n], lhsT=ones_f.t[:], rhs=q.t[:, 0:n],
                                                                 start=(kc == 0), stop=(kc == 15)),
                     r=[q.b, ones_f.b], w=[ps.b], inc=True)
            P.op("act", lambda e, ps=ps: e.activation(out=rstd.t[:, 0:n], in_=ps.t[:, 0:n], func=AF.Sqrt, bias=epsc.t[:, 0:1],
                                                      scale=1.0 / D), r=[ps.b, epsc.b], w=[rstd.b])
            P.op("dve", lambda e: e.reciprocal(out=rstd.t[:, 0:n], in_=rstd.t[:, 0:n]), r=[rstd.b], w=[rstd.b])
            for kc in range(16):
                tm = tmp[kc % 2]
                P.op("dve", lambda e, tm=tm, x=x, kc=kc: e.scalar_tensor_tensor(
                    out=tm.t[:, 0:n], in0=x.t[:, kc, 0:n], scalar=coef.t[:, ia, kc, row:row + 1], in1=rstd.t[:, 0:n],
                    op0=ALU.mult, op1=ALU.mult), r=[x.b, coef.b, rstd.b], w=[tm.b])
                P.op("act", lambda e, tm=tm, kc=kc: e.activation(
                    out=hT.t[:, kc, t0:t0 + n], in_=tm.t[:, 0:n], func=AF.Identity,
                    bias=coef.t[:, ib, kc, row:row + 1], scale=1.0), r=[tm.b, coef.b], w=[hT.b])

    def stage_AB(l, s):
        with ExitStack() as st:
            hT = sb("hT", [128, 16, T], BF16, st)
            with ExitStack() as st2:
                src = xin[s] if l == 0 else xT.t[s]
                modulate(st2, src, s, 0, 1, hT, TGS)
                P.barrier()
            if "f" not in cfg.ab_parts:
                return
            wblk = [sb("wblk%d" % i, [128, 16, 512], BF16, st) for i in range(3)]
            stg = [sb("stg%d" % i, [128, T], BF16, st) for i in range(4)]
            cosT = sb("cosT", [128, T], F32, st)
            sinT = sb("sinT", [128, T], F32, st)
            r1 = [sb("r1_%d" % i, [128, 512], F32, st) for i in range(2)]
            r2 = [sb("r2_%d" % i, [128, 512], F32, st) for i in range(2)]
            for j in range(3):
                a, b_ = j * 768, (j + 1) * 768
                P.dma("sp", cosT.t[:, a:b_], ropec[:, a:b_], w=[cosT.b])
                P.dma("sp", sinT.t[:, a:b_], ropes[:, a:b_], w=[sinT.b])
            wv = w_in[l].rearrange("(kc p) n -> p kc n", p=128)
            nblk = (W_ALL + 511) // 512
            blkbuf = {}

            def load_blk(bi):
                wb = wblk[bi % 3]
                c0 = bi * 512
                n = min(512, W_ALL - c0)
                for j in range(2):
                    P.dma("pool", wb.t[:, 8 * j:8 * j + 8, 0:n], wv[:, 8 * j:8 * j + 8, c0:c0 + n], w=[wb.b])
                blkbuf[bi] = wb

            load_blk(0)
            load_blk(1)
            si = 0
            evq = 0
            oc_out = 0
            ci = 0
            while ci < cfg.nfm:
                bi = ci // 4
                if ci % 4 == 0 and bi + 2 < nblk:
                    load_blk(bi + 2)
                wb = blkbuf[bi]
                rope = ci < 20
                kind = "plain"
                if ci >= 20 + 24 and ci < 20 + 32:
                    kind = "silu"
                if ci >= 20 + 48:
                    kind = "sigmoid"
                sg = stg[si % 4]
                si += 1
                for gi, (t0, n) in enumerate(TGS):
                    pa = ps_next()

                    def mm(e, pa=pa, wb=wb, cc=ci % 4, t0=t0, n=n):
                        for kc in range(16):
                            ins = e.matmul(pa.t[:, 0:n], lhsT=wb.t[:, kc, cc * 128:(cc + 1) * 128],
                                           rhs=hT.t[:, kc, t0:t0 + n], start=(kc == 0), stop=(kc == 15))
                        return ins
                    P.op("pe", mm, r=[wb.b, hT.b], w=[pa.b])
                    if rope:
                        pb = ps_next()
                        P.op("pe", lambda e, pb=pb, wb=wb, cc=ci % 4 + 1, t0=t0, n=n: mm(e, pb, wb, cc, t0, n),
                             r=[wb.b, hT.b], w=[pb.b])
                        a1 = r1[gi % 2]
                        a2 = r2[gi % 2]
                        P.op("dve", lambda e, a1=a1, pa=pa, t0=t0, n=n: e.tensor_tensor(
                            out=a1.t[:, 0:n], in0=pa.t[:, 0:n], in1=cosT.t[:, t0:t0 + n], op=ALU.mult),
                            r=[pa.b, cosT.b], w=[a1.b])
                        P.op("dve", lambda e, a2=a2, pb=pb, t0=t0, n=n: e.tensor_tensor(
                            out=a2.t[:, 0:n], in0=pb.t[:, 0:n], in1=sinT.t[:, t0:t0 + n], op=ALU.mult),
                            r=[pb.b, sinT.b], w=[a2.b])
                        P.op("pool", lambda e, a1=a1, a2=a2, sg=sg, t0=t0, n=n: e.tensor_tensor(
                            out=sg.t[:, t0:t0 + n], in0=a1.t[:, 0:n], in1=a2.t[:, 0:n], op=ALU.add),
                            r=[a1.b, a2.b], w=[sg.b])
                    elif kind == "plain":
                        evq += 1
                        if evq % 2:
                            P.op("dve", lambda e, pa=pa, sg=sg, t0=t0, n=n: e.tensor_copy(
                                out=sg.t[:, t0:t0 + n], in_=pa.t[:, 0:n]), r=[pa.b], w=[sg.b])
                        else:
                            P.op("act", lambda e, pa=pa, sg=sg, t0=t0, n=n: e.activation(
                                out=sg.t[:, t0:t0 + n], in_=pa.t[:, 0:n], func=AF.Copy), r=[pa.b], w=[sg.b])
                    else:
                        fn = AF.Silu if kind == "silu" else AF.Sigmoid
                        P.op("act", lambda e, pa=pa, sg=sg, t0=t0, n=n, fn=fn: e.activation(
                            out=sg.t[:, t0:t0 + n], in_=pa.t[:, 0:n], func=fn), r=[pa.b], w=[sg.b])
                P.dma("sp", s_fm.t[oc_out * 128:(oc_out + 1) * 128, :], sg.t[:], r=[sg.b], w=[s_fm.b])
                oc_out += 1
                ci += 2 if rope else 1
            if 't' not in cfg.ab_parts:
                P.barrier()
                return
            tstg = [sb("tstg%d" % i, [128, 512], BF16, st) for i in range(3)]
            tstf = [sb("tstf%d" % i, [128, 32], F32, st) for i in range(2)]
            ti = 0
            for bi in range(NFM // 4, nblk):
                if bi + 2 < nblk and bi + 2 not in blkbuf:
                    load_blk(bi + 2)
                wb = blkbuf[bi]
                c0 = bi * 512 - W_FM
                n = min(512, W_TM - c0)
                for tt in range(T // 128):
                    pa = ps_next()

                    def mmt(e, pa=pa, wb=wb, tt=tt, n=n):
                        for kc in range(16):
                            ins = e.matmul(pa.t[:, 0:n], lhsT=hT.t[:, kc, tt * 128:(tt + 1) * 128],
                                           rhs=wb.t[:, kc, 0:n], start=(kc == 0), stop=(kc == 15))
                        return ins
                    P.op("pe", mmt, r=[wb.b, hT.b], w=[pa.b])
                    segs = []
                    for (nm, a, b_) in (("va", 0, 256), ("vn", 256, 1280), ("ab", 1280, 1312)):
                        lo, hi = max(a, c0), min(b_, c0 + n)
                        if lo < hi:
                            segs.append((nm, lo - a, hi - a, lo - c0, hi - c0))
                    for (nm, d0, d1, p0, p1) in segs:
                        if nm == "ab":
                            tf = tstf[ti % 2]
                            if tt % 2:
                                P.op("act", lambda e, tf=tf, pa=pa, p0=p0, p1=p1: e.activation(
                                    out=tf.t[:, 0:32], in_=pa.t[:, p0:p1], func=AF.Copy), r=[pa.b], w=[tf.b])
                            else:
                                P.op("dve", lambda e, tf=tf, pa=pa, p0=p0, p1=p1: e.tensor_copy(
                                    out=tf.t[:, 0:32], in_=pa.t[:, p0:p1]), r=[pa.b], w=[tf.b])
                            P.dma("sp", s_ab.t[tt * 128:(tt + 1) * 128, :], tf.t[:], r=[tf.b], w=[s_ab.b])
                        else:
                            ts_ = tstg[ti % 3]
                            ti += 1
                            eng = "act" if tt % 2 else "dve"
                            if eng == "act":
                                P.op("act", lambda e, ts_=ts_, pa=pa, p0=p0, p1=p1: e.activation(
                                    out=ts_.t[:, 0:p1 - p0], in_=pa.t[:, p0:p1], func=AF.Copy), r=[pa.b], w=[ts_.b])
                            else:
                                P.op("dve", lambda e, ts_=ts_, pa=pa, p0=p0, p1=p1: e.tensor_copy(
                                    out=ts_.t[:, 0:p1 - p0], in_=pa.t[:, p0:p1]), r=[pa.b], w=[ts_.b])
                            dst = s_va if nm == "va" else s_vn
                            P.dma("sp", dst.t[tt * 128:(tt + 1) * 128, d0:d1], ts_.t[:, 0:p1 - p0], r=[ts_.b], w=[dst.b])
            P.barrier()

    def post_norm_residual(st, l, s, ig, mT, t0, n, last_out, tag, xt=None):
        row = 2 if t0 < NCTX else s
        sq = [sb(tag + "sq%d" % i, [128, 512], F32, st) for i in range(2)]
        rstd = sb(tag + "rstd", [128, 512], F32, st)
        dstv = xT.t[s].rearrange("(c p) t -> p c t", p=128)
        if xt is None:
            xt = sb(tag + "xt", [128, 16, 512], F32, st)
            src = (xin[s] if (l == 0 and tag == "D") else xT.t[s]).rearrange("(c p) t -> p c t", p=128)
            for j in range(4):
                P.dma("sp", xt.t[:, 4 * j:4 * j + 4, 0:n], src[:, 4 * j:4 * j + 4, t0:t0 + n], r=[xT.b], w=[xt.b])
        ps = ps_next()
        for kc in range(16):
            q = sq[kc % 2]
            P.op("act", lambda e, q=q, kc=kc: e.activation(out=q.t[:, 0:n], in_=mT.t[:, kc, 0:n], func=AF.Square),
                 r=[mT.b], w=[q.b])
            P.op("pe", lambda e, q=q, kc=kc: e.matmul(ps.t[:, 0:n], lhsT=ones_f.t[:], rhs=q.t[:, 0:n],
                                                      start=(kc == 0), stop=(kc == 15)), r=[q.b], w=[ps.b])
        P.op("act", lambda e: e.activation(out=rstd.t[:, 0:n], in_=ps.t[:, 0:n], func=AF.Sqrt, bias=epsc.t[:, 0:1],
                                           scale=1.0 / D), r=[ps.b, epsc.b], w=[rstd.b])
        P.op("dve", lambda e: e.reciprocal(out=rstd.t[:, 0:n], in_=rstd.t[:, 0:n]), r=[rstd.b], w=[rstd.b])
        for kc in range(16):
            P.op("pool", lambda e, kc=kc: e.tensor_tensor(out=mT.t[:, kc, 0:n], in0=mT.t[:, kc, 0:n], in1=rstd.t[:, 0:n],
                                                          op=ALU.mult), r=[mT.b, rstd.b], w=[mT.b])
            P.op("dve", lambda e, kc=kc: e.scalar_tensor_tensor(
                out=xt.t[:, kc, 0:n], in0=mT.t[:, kc, 0:n], scalar=coef.t[:, ig, kc, row:row + 1], in1=xt.t[:, kc, 0:n],
                op0=ALU.mult, op1=ALU.add), r=[mT.b, coef.b, xt.b], w=[xt.b])
        for j in range(4):
            P.dma("sp", dstv[:, 4 * j:4 * j + 4, t0:t0 + n], xt.t[:, 4 * j:4 * j + 4, 0:n], r=[xt.b], w=[xT.b])
        if last_out and t0 >= NCTX:
            yv = yout[s].rearrange("(c p) t -> p c t", p=128)
            for j in range(4):
                P.dma("sp", yv[:, 4 * j:4 * j + 4, t0 - NCTX:t0 - NCTX + n], xt.t[:, 4 * j:4 * j + 4, 0:n], r=[xt.b])

    def stage_D(l, s, tgs):
        gate_v = s_fm.t[OFF_GATE * 128:(OFF_GATE + 48) * 128, :].rearrange("(i c p) t -> p i c t", p=128, i=3)
        for (t0, n) in tgs:
            with ExitStack() as st:
                oT = sb("oT", [128, 3, 8, 512], BF16, st)
                yT = sb("yT", [128, 16, 512], BF16, st)
                mT = sb("mT", [128, 16, 512], F32, st)
                gtb = [sb("gt%d" % i, [128, 4, 512], BF16, st) for i in range(3)]
                wbb = [sb("wbb%d" % i, [128, 8, 512], BF16, st) for i in range(3)]
                acc = sb("accD", [128, 4, 512], F32, st)
                tm2 = [sb("tm2%d" % i, [128, 512], F32, st) for i in range(2)]
                ov = s_o.t.rearrange("i (c p) t -> p i c t", p=128)
                for i in range(3):
                    for j in range(2):
                        P.dma("sp", oT.t[:, i, 4 * j:4 * j + 4, 0:n], ov[:, i, 4 * j:4 * j + 4, t0:t0 + n], r=[s_o.b], w=[oT.b])
                wi = 0
                loaders = []
                for ob in range(4):
                    for i in range(3):
                        loaders.append(lambda wb, i=i, ob=ob: P.dma("pool", wb.t[:].rearrange("p a b -> p (a b)"), w_branch16.t[i, ob],
                                                                    r=[w_branch16.b], w=[wb.b]))
                for ob in range(4):
                    for kh in range(2):
                        loaders.append(lambda wb, kh=kh, ob=ob: P.dma("pool", wb.t[:].rearrange("p a b -> p (a b)"), w_out16.t[ob, kh],
                                                                      r=[w_out16.b], w=[wb.b]))
                pf = Prefetch(wbb, loaders, 2)
                for ob in range(4):
                    for i in range(3):
                        wb = pf.get(wi)
                        g = gtb[wi % 3]
                        wi += 1
                        P.dma("sp", g.t[:, :, 0:n], gate_v[:, i, 4 * ob:4 * ob + 4, t0:t0 + n], r=[s_fm.b], w=[g.b])
                        for oc in range(4):
                            pa = ps_next()

                            def mm(e, pa=pa, wb=wb, i=i, oc=oc):
                                for kc in range(8):
                                    ins = e.matmul(pa.t[:, 0:n], lhsT=wb.t[:, kc, oc * 128:(oc + 1) * 128], rhs=oT.t[:, i, kc, 0:n],
                                                   start=(kc == 0), stop=(kc == 7))
                                return ins
                            P.op("pe", mm, r=[wb.b, oT.b], w=[pa.b])
                            if i == 0:
                                P.op("dve", lambda e, pa=pa, g=g, oc=oc: e.tensor_tensor(
                                    out=acc.t[:, oc, 0:n], in0=pa.t[:, 0:n], in1=g.t[:, oc, 0:n], op=ALU.mult),
                                    r=[pa.b, g.b], w=[acc.b])
                            else:
                                t2 = tm2[oc % 2]
                                P.op("dve", lambda e, pa=pa, t2=t2, g=g, oc=oc: e.tensor_tensor(
                                    out=t2.t[:, 0:n], in0=pa.t[:, 0:n], in1=g.t[:, oc, 0:n], op=ALU.mult),
                                    r=[pa.b, g.b], w=[t2.b])
                                if i == 1:
                                    P.op("dve", lambda e, t2=t2, oc=oc: e.tensor_tensor(
                                        out=acc.t[:, oc, 0:n], in0=acc.t[:, oc, 0:n], in1=t2.t[:, 0:n], op=ALU.add),
                                        r=[acc.b, t2.b], w=[acc.b])
                                else:
                                    P.op("dve", lambda e, t2=t2, oc=oc, ob=ob: e.tensor_tensor(
                                        out=yT.t[:, 4 * ob + oc, 0:n], in0=acc.t[:, oc, 0:n], in1=t2.t[:, 0:n], op=ALU.add),
                                        r=[acc.b, t2.b], w=[yT.b])
                pacc = PSPool([0, 1, 2, 3])
                for ob in range(4):
                    pas = [pacc.next() for _ in range(4)]
                    for kh in range(2):
                        wb = pf.get(wi)
                        wi += 1
                        for oc in range(4):
                            pa = pas[oc]

                            def mm2(e, pa=pa, wb=wb, oc=oc, kh=kh):
                                for kc in range(8):
                                    ins = e.matmul(pa.t[:, 0:n], lhsT=wb.t[:, kc, oc * 128:(oc + 1) * 128], rhs=yT.t[:, 8 * kh + kc, 0:n],
                                                   start=(kh == 0 and kc == 0), stop=(kh == 1 and kc == 7))
                                return ins
                            P.op("pe", mm2, r=[wb.b, yT.b], w=[pa.b])
                    for oc in range(4):
                        P.op("act", lambda e, pa=pas[oc], oc=oc, ob=ob: e.activation(out=mT.t[:, 4 * ob + oc, 0:n], in_=pa.t[:, 0:n],
                                                                                    func=AF.Copy), r=[pas[oc].b], w=[mT.b])
                post_norm_residual(st, l, s, 2, mT, t0, n, False, "D")
                P.barrier()

    def stage_E(l, s, tgs):
        last = (l == L - 1)
        for (t0, n) in tgs:
            with ExitStack() as st:
                aT = sb("aT", [128, 64, 512], BF16, st)
                xt = sb("xtE", [128, 16, 512], F32, st)
                with ExitStack() as st2:
                    hT = sb("hE", [128, 16, 512], BF16, st2)
                    modulate_tg(st2, xT.t[s], s, 3, 4, hT, t0, n, [xt])
                    wub = [sb("wub%d" % i, [128, 16, 512], BF16, st2) for i in range(3)]
                    rl = [sb("rl%d" % i, [128, 512], F32, st2) for i in range(2)]
                    pfu = Prefetch(wub, [(lambda wb, blk=blk: P.dma("pool", wb.t[:].rearrange("p a b -> p (a b)"), w_up16.t[blk],
                                                                    r=[w_up16.b], w=[wb.b])) for blk in range(16)], 2)
                    for blk in range(16):
                        wb = pfu.get(blk)
                        for cc in range(4):
                            fc = blk * 4 + cc
                            pa = ps_next()

                            def mm(e, pa=pa, wb=wb, cc=cc):
                                for kc in range(16):
                                    ins = e.matmul(pa.t[:, 0:n], lhsT=wb.t[:, kc, cc * 128:(cc + 1) * 128],
                                                   rhs=hT.t[:, kc, 0:n], start=(kc == 0), stop=(kc == 15))
                                return ins
                            P.op("pe", mm, r=[wb.b, hT.b], w=[pa.b])
                            r_ = rl[fc % 2]
                            P.op("act", lambda e, pa=pa, r_=r_: e.activation(out=r_.t[:, 0:n], in_=pa.t[:, 0:n], func=AF.Relu),
                                 r=[pa.b], w=[r_.b])
                            P.op("dve", lambda e, r_=r_, fc=fc: e.tensor_tensor(out=aT.t[:, fc, 0:n], in0=r_.t[:, 0:n],
                                                                            in1=r_.t[:, 0:n], op=ALU.mult),
                                 r=[r_.b], w=[aT.b])
                    P.barrier()
                with ExitStack() as st2:
                    mT = sb("mE", [128, 16, 512], F32, st2)
                    wdb = [sb("wdb%d" % i, [128, 16, 512], BF16, st2) for i in range(3)]
                    pacc = PSPool([0, 1, 2, 3])
                    wi = 0
                    pfd = Prefetch(wdb, [(lambda wb, ob=ob, kq=kq: P.dma("pool", wb.t[:].rearrange("p a b -> p (a b)"), w_down16.t[ob, kq],
                                                                         r=[w_down16.b], w=[wb.b])) for ob in range(4) for kq in range(4)], 2)
                    for ob in range(4):
                        pas = [pacc.next() for _ in range(4)]
                        for kq in range(4):
                            wb = pfd.get(wi)
                            wi += 1
                            for oc in range(4):
                                pa = pas[oc]

                                def mm2(e, pa=pa, wb=wb, oc=oc, kq=kq):
                                    for kc in range(16):
                                        ins = e.matmul(pa.t[:, 0:n], lhsT=wb.t[:, kc, oc * 128:(oc + 1) * 128], rhs=aT.t[:, 16 * kq + kc, 0:n],
                                                       start=(kq == 0 and kc == 0), stop=(kq == 3 and kc == 15))
                                    return ins
                                P.op("pe", mm2, r=[wb.b, aT.b], w=[pa.b])
                        for oc in range(4):
                            P.op("act", lambda e, pa=pas[oc], oc=oc, ob=ob: e.activation(out=mT.t[:, 4 * ob + oc, 0:n], in_=pa.t[:, 0:n],
                                                                                        func=AF.Copy), r=[pas[oc].b], w=[mT.b])
                    post_norm_residual(st2, l, s, 5, mT, t0, n, last, "E", xt)
                    P.barrier()

    def modulate_tg(st, src, s, ia, ib, hT, t0, n, xt=None):
        class V:
            pass
        hv = TB(None)
        hv.b = hT.b

        class _T:
            def __getitem__(self, key):
                p, kc, sl = key
                return hT.t[p, kc, sl.start - t0:sl.stop - t0]
        hv.t = _T()
        modulate(st, src, s, ia, ib, hv, [(t0, n)], xt)


    SCALE = 1.0 / math.sqrt(HD)

    def stage_C1(l, s, with_ctx):
        with ExitStack() as st:
            qT = sb("qaT", [128, 8, T], BF16, st)
            kT = sb("kaT", [128, 2, T], BF16, st)
            va = sb("vaS", [128, 18, 256], BF16, st)
            oT = sb("oaT", [128, 8, T], BF16, st)
            esk = sb("esk", [128, 8], F32, st)
            eskB = sb("eskB", [128, 8, 128], F32, st)
            mk = sb("mkA", [128, 2, 4, 128], BF16, st)
            pTs = [sb("pTa%d" % i, [128, 512], BF16, st) for i in range(3)]
            den = sb("denA", [128, 512], F32, st)
            fmv = s_fm.t.rearrange("(c p) t -> p c t", p=128)
            for h in range(8):
                P.dma("sp", qT.t[:, h, :], fmv[:, OFF_QA + h, :], r=[s_fm.b], w=[qT.b])
            for h in range(2):
                P.dma("sp", kT.t[:, h, :], fmv[:, OFF_KA + h, :], r=[s_fm.b], w=[kT.b])
            vav = s_va.t.rearrange("(tt p) c -> p tt c", p=128)
            for j in range(3):
                P.dma("sp", va.t[:, 6 * j:6 * j + 6, :], vav[:, 6 * j:6 * j + 6, :], r=[s_va.b], w=[va.b])
            P.dma("sp", esk.t[:], sink[l:l + 1, :].partition_broadcast(128), w=[esk.b])
            P.op("act", lambda e: e.activation(out=esk.t[:], in_=esk.t[:], func=AF.Exp), r=[esk.b], w=[esk.b])
            P.op("dve", lambda e: e.tensor_copy(out=eskB.t[:], in_=esk.t[:].unsqueeze(2).broadcast_to([128, 8, 128])),
                 r=[esk.b], w=[eskB.b])
            for j in range(2):
                P.op("dve", lambda e, j=j: e.tensor_copy(out=mk.t[:, j], in_=cm.t[:, 1 + j, :].unsqueeze(1).broadcast_to([128, 4, 128])),
                     r=[cm.b], w=[mk.b])
            qblocks = ([(128 * j, None) for j in range(2)] if with_ctx else []) + [(256 + 128 * i, i) for i in range(16)]
            pacc = PSPool([0, 1, 2, 3])
            pss = PSPool([4, 5, 6, 7])
            pi = 0
            for g in range(2):
                for (t0, xi) in qblocks:
                    keys = [(0, None), (128, None)]
                    if xi is not None:
                        if xi > 0:
                            keys.append((256 + 128 * (xi - 1), 0))
                        keys.append((256 + 128 * xi, None))
                        if xi < 15:
                            keys.append((256 + 128 * (xi + 1), 1))
                    ps_o = pacc.next()
                    ps_d = pacc.next()
                    pss_of = {}

                    def emit_scores_a(ii):
                        k0 = keys[ii][0]
                        ps_s = pss.next()
                        pss_of[ii] = ps_s
                        P.op("pe", lambda e: e.matmul(
                            ps_s.t[:, 0:512].rearrange("p (h q) -> p h q", h=4), lhsT=kT.t[:, g, k0:k0 + 128],
                            rhs=qT.t[:, 4 * g:4 * g + 4, t0:t0 + 128], start=True, stop=True), r=[kT.b, qT.b], w=[ps_s.b])

                    for ii in range(min(3, len(keys))):
                        emit_scores_a(ii)
                    for idx, (k0, mki) in enumerate(keys):
                        if idx + 3 < len(keys):
                            emit_scores_a(idx + 3)
                        ps_s = pss_of.pop(idx)
                        pT = pTs[pi % 3]
                        pi += 1
                        P.op("act", lambda e, ps_s=ps_s, pT=pT: e.activation(out=pT.t[:], in_=ps_s.t[:], func=AF.Exp, scale=SCALE),
                             r=[ps_s.b], w=[pT.b])
                        if mki is not None:
                            P.op("pool", lambda e, pT=pT, mki=mki: e.tensor_tensor(
                                out=pT.t[:], in0=pT.t[:], in1=mk.t[:, mki].rearrange("p h q -> p (h q)"), op=ALU.mult),
                                r=[pT.b, mk.b], w=[pT.b])
                        first, lastk = idx == 0, idx == len(keys) - 1
                        P.op("pe", lambda e, pT=pT: e.matmul(ps_d.t[:], lhsT=ones_b.t[:], rhs=pT.t[:], start=first, stop=lastk),
                             r=[pT.b, ones_b.b], w=[ps_d.b])
                        P.op("pe", lambda e, pT=pT, k0=k0: e.matmul(ps_o.t[:], lhsT=va.t[:, k0 // 128, g * 128:(g + 1) * 128],
                                                                    rhs=pT.t[:], start=first, stop=lastk),
                             r=[pT.b, va.b], w=[ps_o.b])
                    P.op("dve", lambda e: e.tensor_tensor(out=den.t[:], in0=ps_d.t[:],
                                                          in1=eskB.t[:, 4 * g:4 * g + 4, :].rearrange("p h q -> p (h q)"), op=ALU.add),
                         r=[ps_d.b, eskB.b], w=[den.b])
                    P.op("dve", lambda e: e.reciprocal(out=den.t[:], in_=den.t[:]), r=[den.b], w=[den.b])
                    P.op("dve", lambda e: e.tensor_tensor(out=oT.t[:, 4 * g:4 * g + 4, t0:t0 + 128],
                                                          in0=ps_o.t[:].rearrange("p (h q) -> p h q", h=4),
                                                          in1=den.t[:].rearrange("p (h q) -> p h q", h=4), op=ALU.mult),
                         r=[ps_o.b, den.b], w=[oT.b])
            ov = s_o.t[0].rearrange("(c p) t -> p c t", p=128)
            for h in range(8):
                P.dma("sp", ov[:, h, :], oT.t[:, h, :], r=[oT.b], w=[s_o.b])
            P.barrier()

    rpbpad = dscratch("rpbpad", [120, 127], F32)
    dbgC = dscratch("dbgC", [64, 7680], F32)

    def na_valid(r, kr):
        rs = min(max(r - 4, 0), 24)
        return rs <= kr <= rs + 7

    def stage_C2(l, s, with_ctx):
        with ExitStack() as st:
            Ctab = sb("Ctab", [64, 8, 15, 64], F32, st)
            zt = sb("zt", [120, 127], F32, st)
            P.op("dve", lambda e: e.memset(zt.t[:], 0.0), w=[zt.b])
            P.dma("sp", rpbpad.t[:, :], zt.t[:], r=[zt.b], w=[rpbpad.b])
            P.dma("sp", rpbpad.t[:, 48:79], rpb[l].rearrange("(r c) -> r c", c=31), w=[rpbpad.b])
            for h in range(8):
                src = bass.AP(tensor=rpbpad.t.tensor, offset=rpbpad.t.offset + h * 15 * 127, ap=[[1, 64], [127, 15], [1, 64]])
                P.dma("sp", Ctab.t[:, h], src, r=[rpbpad.b], w=[Ctab.b])
            P.op("dve", lambda e: e.tensor_tensor(
                out=Ctab.t[:].rearrange("p h a q -> p (h a) q"), in0=Ctab.t[:].rearrange("p h a q -> p (h a) q"),
                in1=cm.t[0:64, 3, 0:64].unsqueeze(1).broadcast_to([64, 120, 64]), op=ALU.add), r=[Ctab.b, cm.b], w=[Ctab.b])
            if cfg.debug:
                for h in range(8):
                    P.dma("sp", dbgC.t[:, h * 960:(h + 1) * 960], Ctab.t[:, h].rearrange("p a q -> p (a q)"), r=[Ctab.b], w=[dbgC.b])
            ctab_ap = Ctab.t[:]
            pstep = ctab_ap.ap[0][0]
            qh = [sb("qnh%d" % i, [128, T], BF16, st) for i in range(2)]
            kh = [sb("knh%d" % i, [128, T], BF16, st) for i in range(2)]
            v64 = [sb("v64_%d" % i, [128, 36, 128], BF16, st) for i in range(2)]
            pTr = [sb("pTr%d" % i, [128, 512], BF16, st) for i in range(3)]
            for t_ in v64 + pTr:
                P.op("pool", lambda e, t_=t_: e.memset(t_.t[:], 0.0), w=[t_.b])
            vcx = [sb("vcx%d" % i, [128, 2, 128], BF16, st) for i in range(2)]
            oh = [sb("onh%d" % i, [128, T], BF16, st) for i in range(2)]
            pTs = [sb("pTn%d" % i, [128, 512], BF16, st) for i in range(3)]
            sbs = [sb("sbn%d" % i, [64, 512], F32, st) for i in range(3)]
            den = sb("denN", [128, 512], F32, st)
            fmv = s_fm.t.rearrange("(c p) t -> p c t", p=128)
            v64v = s_vn.t.rearrange("(c p) d -> p c d", p=64)
            vcv = s_vn.t[0:256, :].rearrange("(c p) d -> p c d", p=128)
            ov = s_o.t[2].rearrange("(c p) t -> p c t", p=128)
            pi = 0
            bi_ = 0
            pacc = PSPool([0, 1, 2, 3])
            pss = PSPool([4, 5, 6, 7])
            for h in range(8):
                q_, k_, v_, vc_, o_ = qh[h % 2], kh[h % 2], v64[h % 2], vcx[h % 2], oh[h % 2]
                P.dma("sp", q_.t[:], fmv[:, OFF_QN + h, :], r=[s_fm.b], w=[q_.b])
                P.dma("sp", k_.t[:], fmv[:, OFF_KN + h, :], r=[s_fm.b], w=[k_.b])
                for j in range(3):
                    P.dma("sp", v_.t[0:64, 12 * j:12 * j + 12, :], v64v[:, 12 * j:12 * j + 12, h * 128:(h + 1) * 128],
                          r=[s_vn.b], w=[v_.b])
                P.dma("sp", vc_.t[:], vcv[:, :, h * 128:(h + 1) * 128], r=[s_vn.b], w=[vc_.b])
                groups = ([("c", 0, 256)] if with_ctx else []) + [("x", 256 + 512 * G, 512) for G in range(4)]
                for (gk, t0, n) in groups:
                    ps_o = pacc.next()
                    ps_d = pacc.next()
                    items = [("ctx", 0)]
                    if gk == "x":
                        G = (t0 - 256) // 512
                        for kr in range(32):
                            rr = [r for r in range(8 * G, 8 * G + 8) if na_valid(r, kr)]
                            if rr:
                                items.append(("row", kr, rr[0], rr[-1]))
                    items.append(("ctx", 1))
                    LOOK = 3
                    pss_of = {}

                    def emit_scores(ii):
                        it = items[ii]
                        ps_s = pss.next()
                        pss_of[ii] = ps_s
                        if it[0] == "ctx":
                            cb = it[1]
                            P.op("pe", lambda e: e.matmul(ps_s.t[:, 0:n], lhsT=k_.t[:, cb * 128:(cb + 1) * 128], rhs=q_.t[:, t0:t0 + n],
                                                          start=True, stop=True), r=[k_.b, q_.b], w=[ps_s.b])
                        else:
                            _, kr, rlo, rhi = it
                            nr = rhi - rlo + 1
                            c0 = (rlo - 8 * G) * 64
                            nc_ = nr * 64
                            kt0 = 256 + 64 * kr
                            P.op("pe", lambda e: e.matmul(ps_s.t[0:64, 0:nc_], lhsT=k_.t[:, kt0:kt0 + 64],
                                                          rhs=q_.t[:, t0 + c0:t0 + c0 + nc_], start=True, stop=True),
                                 r=[k_.b, q_.b], w=[ps_s.b])

                    for ii in range(min(LOOK, len(items))):
                        emit_scores(ii)
                    for ii, it in enumerate(items):
                        if ii + LOOK < len(items):
                            emit_scores(ii + LOOK)
                        first, lastk = ii == 0, ii == len(items) - 1
                        pT = pTs[pi % 3]
                        pi += 1
                        ps_s = pss_of.pop(ii)
                        if it[0] == "ctx":
                            cb = it[1]
                            P.op("act", lambda e: e.activation(out=pT.t[:, 0:n], in_=ps_s.t[:, 0:n], func=AF.Exp, scale=SCALE),
                                 r=[ps_s.b], w=[pT.b])
                            P.op("pe", lambda e: e.matmul(ps_d.t[:, 0:n], lhsT=ones_b.t[:], rhs=pT.t[:, 0:n], start=first, stop=lastk),
                                 r=[pT.b, ones_b.b], w=[ps_d.b])
                            P.op("pe", lambda e: e.matmul(ps_o.t[:, 0:n], lhsT=vc_.t[:, cb, :], rhs=pT.t[:, 0:n], start=first, stop=lastk),
                                 r=[pT.b, vc_.b], w=[ps_o.b])
                        else:
                            _, kr, rlo, rhi = it
                            pT = pTr[pi % 3]
                            nr = rhi - rlo + 1
                            c0 = (rlo - 8 * G) * 64
                            nc_ = nr * 64
                            sbt = sbs[bi_ % 3]
                            bi_ += 1
                            a_start = 7 + kr - rlo
                            bias = bass.AP(tensor=ctab_ap.tensor, offset=ctab_ap.offset + (h * 15 + a_start) * 64 + 63,
                                           ap=[[pstep, 64], [-64, nr], [-1, 64]])
                            P.op("dve", lambda e: e.scalar_tensor_tensor(
                                out=sbt.t[:, 0:nc_].rearrange("p (r q) -> p r q", q=64),
                                in0=ps_s.t[0:64, 0:nc_].rearrange("p (r q) -> p r q", q=64), scalar=SCALE, in1=bias,
                                op0=ALU.mult, op1=ALU.add), r=[ps_s.b, Ctab.b], w=[sbt.b])
                            P.op("act", lambda e: e.activation(out=pT.t[0:64, 0:nc_], in_=sbt.t[:, 0:nc_], func=AF.Exp),
                                 r=[sbt.b], w=[pT.b])
                            P.op("pe", lambda e: e.matmul(ps_d.t[:, c0:c0 + nc_], lhsT=ones_b.t[:, :], rhs=pT.t[:, 0:nc_],
                                                          start=False, stop=False), r=[pT.b, ones_b.b], w=[ps_d.b])
                            P.op("pe", lambda e: e.matmul(ps_o.t[:, c0:c0 + nc_], lhsT=v_.t[:, 4 + kr, :], rhs=pT.t[:, 0:nc_],
                                                          start=False, stop=False), r=[pT.b, v_.b], w=[ps_o.b])
                    P.op("dve", lambda e: e.reciprocal(out=den.t[:, 0:n], in_=ps_d.t[:, 0:n]), r=[ps_d.b], w=[den.b])
                    P.op("dve", lambda e: e.tensor_tensor(out=o_.t[:, t0:t0 + n], in0=ps_o.t[:, 0:n], in1=den.t[:, 0:n], op=ALU.mult),
                         r=[ps_o.b, den.b], w=[o_.b])
                P.dma("sp", ov[:, h, :], o_.t[:], r=[o_.b], w=[s_o.b])
            P.barrier()


    NT_ = T // 128
    TGRP = [(0, 4), (4, 4), (8, 4), (12, 4), (16, 2)]

    def stage_C3(l, s):
        with ExitStack() as st:
            abT = sb("abT", [128, NT_, 32], F32, st)
            gG = sb("gG", [128, NT_, 16], F32, st)
            bG = sb("bG", [128, NT_, 16], F32, st)
            nA = sb("nA", [128, 16], F32, st)
            dtb = sb("dtb", [128, 16], F32, st)
            cwT = sb("cwT", [128, 5, 24], F32, st)
            gg = sb("ggdn", [128, 1], F32, st)
            abv = s_ab.t.rearrange("(tt p) j -> p tt j", p=128)
            for j in range(3):
                P.dma("sp", abT.t[:, 6 * j:6 * j + 6, :], abv[:, 6 * j:6 * j + 6, :], r=[s_ab.b], w=[abT.b])
            P.dma("sp", nA.t[:], a_log[l:l + 1, :].partition_broadcast(128), w=[nA.b])
            P.dma("sp", dtb.t[:], dt_bias[l:l + 1, :].partition_broadcast(128), w=[dtb.b])
            P.dma("sp", gg.t[:], g_gdn[l].rearrange("(p o) -> p o", o=1), w=[gg.b])
            P.op("act", lambda e: e.activation(out=nA.t[:], in_=nA.t[:], func=AF.Exp), r=[nA.b], w=[nA.b])
            P.op("dve", lambda e: e.tensor_scalar(out=nA.t[:], in0=nA.t[:], scalar1=-1.0, scalar2=None, op0=ALU.mult),
                 r=[nA.b], w=[nA.b])
            P.op("dve", lambda e: e.tensor_tensor(out=gG.t[:], in0=abT.t[:, :, 0:16],
                                                  in1=dtb.t[:].unsqueeze(1).broadcast_to([128, NT_, 16]), op=ALU.add),
                 r=[abT.b, dtb.b], w=[gG.b])
            P.op("act", lambda e: e.activation(out=gG.t[:], in_=gG.t[:], func=AF.Exp), r=[gG.b], w=[gG.b])
            P.op("act", lambda e: e.activation(out=gG.t[:], in_=gG.t[:], func=AF.Ln, bias=ones_f.t[:, 0:1], scale=1.0),
                 r=[gG.b, ones_f.b], w=[gG.b])
            P.op("dve", lambda e: e.tensor_tensor(out=gG.t[:], in0=gG.t[:],
                                                  in1=nA.t[:].unsqueeze(1).broadcast_to([128, NT_, 16]), op=ALU.mult),
                 r=[gG.b, nA.b], w=[gG.b])
            P.op("act", lambda e: e.activation(out=bG.t[:], in_=abT.t[:, :, 16:32], func=AF.Sigmoid), r=[abT.b], w=[bG.b])
            with ExitStack() as st2:
                tmpc = sb("cwtmp", [128, 128], F32, st2)
                P.dma("sp", tmpc.t[0:120, :], conv_w[l].rearrange("j (c p) -> (j c) p", p=128), w=[tmpc.b])
                pst = ps_next()
                P.op("pe", lambda e: e.transpose(out=pst.t[:, 0:120], in_=tmpc.t[0:120, :], identity=cm.t[0:120, 0, 0:120]),
                     r=[tmpc.b, cm.b], w=[pst.b])
                P.op("dve", lambda e: e.tensor_copy(out=cwT.t[:].rearrange("p j c -> p (j c)"), in_=pst.t[:, 0:120]),
                     r=[pst.b], w=[cwT.b])
                P.barrier()
            fmv = s_fm.t.rearrange("(c p) t -> p c t", p=128)
            ov = s_o.t[1].rearrange("(c p) t -> p c t", p=128)
            SEGS = [(0, NCTX), (NCTX, T)]
            idb = cm.t[:, 0, :]
            for h in range(8):
                with ExitStack() as sh:
                    qT = sb("gq", [128, T], F32, sh)
                    k_tm = sb("gktm", [128, NT_, 128], F32, sh)
                    v_tm = sb("gvtm", [128, NT_, 128], F32, sh)
                    KK = sb("gKK", [128, NT_, 128], F32, sh)
                    QKT = sb("gQKT", [128, NT_, 128], F32, sh)
                    with ExitStack() as s1:
                        kT = sb("gk", [128, T], F32, s1)
                        raw = [sb("graw%d" % i, [128, T], BF16, s1) for i in range(3)]
                        acc = [sb("gacc%d" % i, [128, T], F32, s1) for i in range(2)]
                        vT = sb("gv", [128, T], F32, s1)
                        sqbs = [sb("gsq%d" % i, [128, 512], F32, s1) for i in range(2)]
                        rnbs = [sb("grn%d" % i, [128, 512], F32, s1) for i in range(2)]
                        for ci_, which in enumerate(("q", "k", "v")):
                            cidx = ci_ * 8 + h
                            rw = raw[ci_]
                            a_ = acc[ci_ % 2]
                            eng = "dve"
                            P.dma("sp", rw.t[:], fmv[:, OFF_QKVB + cidx, :], r=[s_fm.b], w=[rw.b])
                            P.op(eng, lambda e: e.tensor_scalar(out=a_.t[:], in0=rw.t[:], scalar1=cwT.t[:, 2, cidx:cidx + 1], scalar2=None,
                                                                op0=ALU.mult), r=[rw.b, cwT.b], w=[a_.b])
                            for j in (0, 1, 3, 4):
                                d_ = j - 2
                                for (s0, s1_) in SEGS:
                                    lo, hi = max(s0, s0 - d_), min(s1_, s1_ - d_)
                                    P.op(eng, lambda e: e.scalar_tensor_tensor(
                                        out=a_.t[:, lo:hi], in0=rw.t[:, lo + d_:hi + d_], scalar=cwT.t[:, j, cidx:cidx + 1],
                                        in1=a_.t[:, lo:hi], op0=ALU.mult, op1=ALU.add), r=[rw.b, cwT.b, a_.b], w=[a_.b])
                            dst = {"q": qT, "k": kT, "v": vT}[which]
                            if which == "v":
                                P.op("act", lambda e: e.activation(out=dst.t[:], in_=a_.t[:], func=AF.Silu), r=[a_.b], w=[dst.b])
                            else:
                                P.op("act", lambda e: e.activation(out=a_.t[:], in_=a_.t[:], func=AF.Silu), r=[a_.b], w=[a_.b])
                                for tgi, (t0, n) in enumerate(TGS):
                                    sqb, rnb = sqbs[tgi % 2], rnbs[tgi % 2]
                                    P.op("pool", lambda e: e.tensor_tensor(out=sqb.t[:, 0:n], in0=a_.t[:, t0:t0 + n], in1=a_.t[:, t0:t0 + n],
                                                                          op=ALU.mult), r=[a_.b], w=[sqb.b])
                                    pq = ps_next()
                                    P.op("pe", lambda e: e.matmul(pq.t[:, 0:n], lhsT=ones_f.t[:], rhs=sqb.t[:, 0:n], start=True, stop=True),
                                         r=[sqb.b, ones_f.b], w=[pq.b])
                                    P.op("act", lambda e: e.activation(out=rnb.t[:, 0:n], in_=pq.t[:, 0:n], func=AF.Sqrt,
                                                                       bias=epsc.t[:, 0:1], scale=1.0), r=[pq.b, epsc.b], w=[rnb.b])
                                    P.op("dve", lambda e: e.reciprocal(out=rnb.t[:, 0:n], in_=rnb.t[:, 0:n]), r=[rnb.b], w=[rnb.b])
                                    if which == "q":
                                        P.op("dve", lambda e: e.scalar_tensor_tensor(
                                            out=dst.t[:, t0:t0 + n], in0=a_.t[:, t0:t0 + n], scalar=SCALE, in1=rnb.t[:, 0:n],
                                            op0=ALU.mult, op1=ALU.mult), r=[a_.b, rnb.b], w=[dst.b])
                                    else:
                                        P.op("dve", lambda e: e.tensor_tensor(out=dst.t[:, t0:t0 + n], in0=a_.t[:, t0:t0 + n],
                                                                              in1=rnb.t[:, 0:n], op=ALU.mult), r=[a_.b, rnb.b], w=[dst.b])
                        ei = 0
                        for (g0, gn) in TGRP:
                            for (src_, dst_) in ((kT, k_tm), (vT, v_tm)):
                                pt = ps_next()
                                for ti in range(gn):
                                    tt = g0 + ti
                                    P.op("pe", lambda e: e.transpose(out=pt.t[:, ti * 128:(ti + 1) * 128], in_=src_.t[:, tt * 128:(tt + 1) * 128],
                                                                     identity=idb), r=[src_.b, cm.b], w=[pt.b], inc=(ti == gn - 1))
                                ei += 1
                                if ei % 2:
                                    P.op("act", lambda e: e.activation(out=dst_.t[:, g0:g0 + gn, :].rearrange("p a b -> p (a b)"),
                                                                       in_=pt.t[:, 0:gn * 128], func=AF.Copy), r=[pt.b], w=[dst_.b])
                                else:
                                    P.op("dve", lambda e: e.tensor_copy(out=dst_.t[:, g0:g0 + gn, :].rearrange("p a b -> p (a b)"),
                                                                        in_=pt.t[:, 0:gn * 128]), r=[pt.b], w=[dst_.b])
                            for (rhs_, dst_) in ((kT, KK), (qT, QKT)):
                                pt = ps_next()
                                for ti in range(gn):
                                    tt = g0 + ti
                                    P.op("pe", lambda e: e.matmul(pt.t[:, ti * 128:(ti + 1) * 128], lhsT=kT.t[:, tt * 128:(tt + 1) * 128],
                                                                  rhs=rhs_.t[:, tt * 128:(tt + 1) * 128], start=True, stop=True),
                                         r=[kT.b, rhs_.b], w=[pt.b], inc=(ti == gn - 1))
                                ei += 1
                                if ei % 2:
                                    P.op("act", lambda e: e.activation(out=dst_.t[:, g0:g0 + gn, :].rearrange("p a b -> p (a b)"),
                                                                       in_=pt.t[:, 0:gn * 128], func=AF.Copy), r=[pt.b], w=[dst_.b])
                                else:
                                    P.op("dve", lambda e: e.tensor_copy(out=dst_.t[:, g0:g0 + gn, :].rearrange("p a b -> p (a b)"),
                                                                        in_=pt.t[:, 0:gn * 128]), r=[pt.b], w=[dst_.b])
                        P.barrier()
                    dirs = []
                    pre = []
                    for di in range(2):
                        pre.append((sb("gwT%d" % di, [128, T], F32, sh), sb("gqg%d" % di, [128, T], BF16, sh),
                                    sb("gkd%d" % di, [128, NT_, 128], BF16, sh), sb("gat%d" % di, [128, NT_, 128], BF16, sh),
                                    sb("gu%d" % di, [128, NT_, 128], F32, sh), sb("gegl%d" % di, [128, NT_, 2], F32, sh)))
                    s2 = ExitStack()
                    jobs = []
                    slots = []
                    for si in range(2):
                        B = {}
                        for nm in ("GU", "Gb", "EL", "EU", "Lf", "Dg", "kbg", "vb", "X", "N00", "N01", "N10", "N11", "X0", "X1"):
                            B[nm] = sb("g%s_%d" % (nm, si), [128, 512], F32, s2)
                        slots.append(B)
                    pacc = PSPool([0, 1, 2, 3, 4, 5, 6, 7])
                    for di in range(2):
                        iU, iML, iMU, iSL = (4, 6, 7, 8) if di == 0 else (5, 7, 6, 9)
                        gcol = di * 8 + h
                        wT, qgT, kd, attnT, u_, egl = pre[di]
                        dirs.append((wT, qgT, kd, attnT, u_, egl))
                        if True:
                            gc = sb("ggc", [128, NT_], F32, s2)
                            egc = sb("gegc", [128, NT_], F32, s2)
                            ekd = sb("gekd", [128, NT_], F32, s2)
                            bsc = sb("gbsc", [128, NT_], F32, s2)
                            ghd = sb("gghd", [128, NT_], F32, s2)
                            bhd = sb("gbhd", [128, NT_], F32, s2)
                            P.op("dve", lambda e: e.tensor_copy(out=ghd.t[:], in_=gG.t[:, :, gcol]), r=[gG.b], w=[ghd.b])
                            P.op("dve", lambda e: e.tensor_copy(out=bhd.t[:], in_=bG.t[:, :, gcol]), r=[bG.b], w=[bhd.b])
                            pg = ps_next()
                            P.op("pe", lambda e: e.matmul(pg.t[:, 0:NT_], lhsT=cm.t[:, iU, :], rhs=ghd.t[:], start=True, stop=True),
                                 r=[cm.b, ghd.b], w=[pg.b], inc=False)
                            P.op("pe", lambda e: e.matmul(pg.t[:, 32:32 + NT_], lhsT=cm.t[:, 10, :], rhs=ghd.t[:], start=True, stop=True),
                                 r=[cm.b, ghd.b], w=[pg.b], inc=False)
                            P.op("pe", lambda e: e.matmul(pg.t[:, 64:64 + NT_], lhsT=cm.t[:, 11, :], rhs=ghd.t[:], start=True, stop=True),
                                 r=[cm.b, ghd.b], w=[pg.b], inc=False)
                            P.op("pe", lambda e: e.matmul(pg.t[:, 96:96 + NT_], lhsT=cm.t[:, 12, :], rhs=ghd.t[:], start=True, stop=True),
                                 r=[cm.b, ghd.b], w=[pg.b])
                            P.op("dve", lambda e: e.tensor_copy(out=gc.t[:], in_=pg.t[:, 0:NT_]), r=[pg.b], w=[gc.b])
                            P.op("dve", lambda e: e.tensor_tensor(out=ekd.t[:], in0=pg.t[:, 32:32 + NT_], in1=gc.t[:], op=ALU.subtract),
                                 r=[pg.b, gc.b], w=[ekd.b])
                            P.op("dve", lambda e: e.tensor_copy(out=egl.t[:, :, 0], in_=pg.t[:, 64:64 + NT_]), r=[pg.b], w=[egl.b])
                            P.op("dve", lambda e: e.tensor_copy(out=egl.t[:, :, 1], in_=pg.t[:, 96:96 + NT_]), r=[pg.b], w=[egl.b])
                            P.op("act", lambda e: e.activation(out=egc.t[:], in_=gc.t[:], func=AF.Exp), r=[gc.b], w=[egc.b])
                            P.op("act", lambda e: e.activation(out=ekd.t[:], in_=ekd.t[:], func=AF.Exp), r=[ekd.b], w=[ekd.b])
                            P.op("act", lambda e: e.activation(out=egl.t[:], in_=egl.t[:], func=AF.Exp), r=[egl.b], w=[egl.b])
                            P.op("dve", lambda e: e.tensor_tensor(out=bsc.t[:], in0=bhd.t[:], in1=egc.t[:], op=ALU.mult),
                                 r=[bhd.b, egc.b], w=[bsc.b])
                            P.op("pool", lambda e: e.tensor_tensor(out=kd.t[:], in0=k_tm.t[:],
                                                                  in1=ekd.t[:].unsqueeze(2).broadcast_to([128, NT_, 128]), op=ALU.mult),
                                 r=[k_tm.b, ekd.b], w=[kd.b])
                            def group_gen(g0, gn, B, iU=iU, iML=iML, iMU=iMU, iSL=iSL, ghd=ghd, bhd=bhd, bsc=bsc, egc=egc,
                                          wT=wT, qgT=qgT, attnT=attnT, u_=u_):
                                W_ = gn * 128
                                GU, Gb, EL, EU, Lf, Dg, kbg, vb, Xg = (B[k_] for k_ in ("GU", "Gb", "EL", "EU", "Lf", "Dg", "kbg", "vb", "X"))
                                nb = [[B["N00"], B["N01"]], [B["N10"], B["N11"]]]
                                Xb = [B["X0"], B["X1"]]
                                g3 = lambda t_: t_.t[:, 0:W_].rearrange("p (a b) -> p a b", b=128)
                                p3 = lambda p_: p_.t[:, 0:W_].rearrange("p (a b) -> p a b", b=128)
                                gsl = ghd.t[:, g0:g0 + gn].unsqueeze(2).broadcast_to([128, gn, 128])
                                bcg = lambda t_: t_.t[:, g0:g0 + gn].unsqueeze(2).broadcast_to([128, gn, 128])
                                cmb = lambda i_: cm.t[:, i_, :].unsqueeze(1).broadcast_to([128, gn, 128])
                                P.op("dve", lambda e: e.tensor_tensor(out=g3(GU), in0=cmb(iU), in1=gsl, op=ALU.mult), r=[cm.b, ghd.b], w=[GU.b])
                                P.op("pool", lambda e: e.tensor_copy(out=g3(Gb), in_=gsl), r=[ghd.b], w=[Gb.b])
                                P.op("pool", lambda e: e.tensor_tensor(out=g3(kbg), in0=k_tm.t[:, g0:g0 + gn, :], in1=bcg(bsc), op=ALU.mult),
                                     r=[k_tm.b, bsc.b], w=[kbg.b])
                                P.op("pool", lambda e: e.tensor_tensor(out=g3(vb), in0=v_tm.t[:, g0:g0 + gn, :], in1=bcg(bhd), op=ALU.mult),
                                     r=[v_tm.b, bhd.b], w=[vb.b])
                                P.op("pool", lambda e: e.tensor_tensor(out=g3(Dg), in0=cmb(0), in1=bcg(egc), op=ALU.mult),
                                     r=[cm.b, egc.b], w=[Dg.b])
                                pD = pacc.next()
                                P.op("pe", lambda e: e.matmul(pD.t[:, 0:W_], lhsT=cm.t[:, iU, :], rhs=Gb.t[:, 0:W_], start=True, stop=False),
                                     r=[cm.b, Gb.b], w=[pD.b], inc=False)
                                P.op("pe", lambda e: e.matmul(pD.t[:, 0:W_], lhsT=negones.t[:], rhs=GU.t[:, 0:W_], start=False, stop=True),
                                     r=[negones.b, GU.b], w=[pD.b])
                                pq = pacc.next()
                                P.op("pe", lambda e: e.matmul(pq.t[:, 0:W_], lhsT=ones_f.t[:], rhs=Dg.t[:, 0:W_], start=True, stop=True),
                                     r=[ones_f.b, Dg.b], w=[pq.b])
                                yield
                                P.op("dve", lambda e: e.tensor_tensor(out=g3(EL), in0=p3(pD), in1=cmb(iML), op=ALU.add), r=[pD.b, cm.b], w=[EL.b])
                                P.op("dve", lambda e: e.scalar_tensor_tensor(out=g3(EU), in0=p3(pD), scalar=-1.0, in1=cmb(iMU),
                                                                             op0=ALU.mult, op1=ALU.add), r=[pD.b, cm.b], w=[EU.b])
                                P.op("dve", lambda e: e.tensor_tensor(out=qgT.t[:, g0 * 128:g0 * 128 + W_], in0=pq.t[:, 0:W_],
                                                                      in1=qT.t[:, g0 * 128:g0 * 128 + W_], op=ALU.mult),
                                     r=[pq.b, qT.b], w=[qgT.b])
                                P.op("act", lambda e: e.activation(out=EL.t[:, 0:W_], in_=EL.t[:, 0:W_], func=AF.Exp), r=[EL.b], w=[EL.b])
                                P.op("act", lambda e: e.activation(out=EU.t[:, 0:W_], in_=EU.t[:, 0:W_], func=AF.Exp), r=[EU.b], w=[EU.b])
                                P.op("pool", lambda e: e.tensor_tensor(out=g3(Lf), in0=g3(EL), in1=KK.t[:, g0:g0 + gn, :], op=ALU.mult),
                                     r=[EL.b, KK.b], w=[Lf.b])
                                P.op("pool", lambda e: e.tensor_tensor(out=g3(Lf), in0=g3(Lf), in1=cmb(iSL), op=ALU.mult),
                                     r=[Lf.b, cm.b], w=[Lf.b])
                                P.op("pool", lambda e: e.tensor_tensor(out=g3(Lf), in0=g3(Lf), in1=bcg(bhd), op=ALU.mult),
                                     r=[Lf.b, bhd.b], w=[Lf.b])
                                P.op("dve", lambda e: e.tensor_tensor(out=attnT.t[:, g0:g0 + gn, :], in0=g3(EU), in1=QKT.t[:, g0:g0 + gn, :],
                                                                      op=ALU.mult), r=[EU.b, QKT.b], w=[attnT.b])
                                NT0, N0 = nb[1][0], nb[0][0]
                                P.op("dve", lambda e: e.tensor_scalar(out=NT0.t[:, 0:W_], in0=Lf.t[:, 0:W_], scalar1=-1.0, scalar2=None,
                                                                      op0=ALU.mult), r=[Lf.b], w=[NT0.b])
                                pT_ = pacc.next()
                                for ti in range(gn):
                                    P.op("pe", lambda e: e.transpose(out=pT_.t[:, ti * 128:(ti + 1) * 128], in_=Lf.t[:, ti * 128:(ti + 1) * 128],
                                                                     identity=idb), r=[Lf.b, cm.b], w=[pT_.b], inc=(ti == gn - 1))
                                yield
                                P.op("act", lambda e: e.activation(out=N0.t[:, 0:W_], in_=pT_.t[:, 0:W_], func=AF.Copy, scale=-1.0),
                                     r=[pT_.b], w=[N0.b])
                                Xc = Xb[0]
                                P.op("dve", lambda e: e.scalar_tensor_tensor(out=g3(Xc), in0=p3(pT_), scalar=-1.0, in1=cmb(0),
                                                                             op0=ALU.mult, op1=ALU.add), r=[pT_.b, cm.b], w=[Xc.b])
                                Nc, NTc = N0, NT0
                                for k_ in range(5):
                                    par = (k_ + 1) % 2
                                    Nn, NTn = nb[0][par], nb[1][par]
                                    Xn = Xb[(k_ + 1) % 2]
                                    pN = pacc.next() if k_ < 4 else None
                                    pNT = pacc.next()
                                    for ti in range(gn):
                                        sl = slice(ti * 128, (ti + 1) * 128)
                                        P.op("pe", lambda e: e.matmul(pNT.t[:, sl], lhsT=Nc.t[:, sl], rhs=NTc.t[:, sl], start=True, stop=True),
                                             r=[NTc.b, Nc.b], w=[pNT.b], inc=(ti == gn - 1))
                                    if k_ < 4:
                                        for ti in range(gn):
                                            sl = slice(ti * 128, (ti + 1) * 128)
                                            P.op("pe", lambda e: e.matmul(pN.t[:, sl], lhsT=NTc.t[:, sl], rhs=Nc.t[:, sl], start=True, stop=True),
                                                 r=[NTc.b, Nc.b], w=[pN.b], inc=(ti == gn - 1))
                                    yield
                                    P.op("dve", lambda e: e.tensor_copy(out=NTn.t[:, 0:W_], in_=pNT.t[:, 0:W_]), r=[pNT.b], w=[NTn.b])
                                    if k_ < 4:
                                        P.op("act", lambda e: e.activation(out=Nn.t[:, 0:W_], in_=pN.t[:, 0:W_], func=AF.Copy),
                                             r=[pN.b], w=[Nn.b])
                                    pX = pacc.next()
                                    for ti in range(gn):
                                        sl = slice(ti * 128, (ti + 1) * 128)
                                        P.op("pe", lambda e: e.matmul(pX.t[:, sl], lhsT=NTn.t[:, sl], rhs=Xc.t[:, sl], start=True, stop=True),
                                             r=[NTn.b, Xc.b], w=[pX.b], inc=(ti == gn - 1))
                                    yield
                                    dstX = Xn if k_ < 4 else Xg
                                    P.op("dve", lambda e: e.tensor_tensor(out=dstX.t[:, 0:W_], in0=pX.t[:, 0:W_], in1=Xc.t[:, 0:W_], op=ALU.add),
                                         r=[pX.b, Xc.b], w=[dstX.b])
                                    Nc, NTc, Xc = Nn, NTn, Xn
                                pu = pacc.next()
                                pw = pacc.next()
                                for ti in range(gn):
                                    sl = slice(ti * 128, (ti + 1) * 128)
                                    P.op("pe", lambda e: e.matmul(pu.t[:, sl], lhsT=Xg.t[:, sl], rhs=vb.t[:, sl], start=True, stop=True),
                                         r=[Xg.b, vb.b], w=[pu.b], inc=(ti == gn - 1))
                                for ti in range(gn):
                                    sl = slice(ti * 128, (ti + 1) * 128)
                                    P.op("pe", lambda e: e.matmul(pw.t[:, sl], lhsT=kbg.t[:, sl], rhs=Xg.t[:, sl], start=True, stop=True),
                                         r=[Xg.b, kbg.b], w=[pw.b], inc=(ti == gn - 1))
                                yield
                                P.op("act", lambda e: e.activation(out=u_.t[:, g0:g0 + gn, :].rearrange("p a b -> p (a b)"), in_=pu.t[:, 0:W_],
                                                                   func=AF.Copy), r=[pu.b], w=[u_.b])
                                P.op("act", lambda e: e.activation(out=wT.t[:, g0 * 128:g0 * 128 + W_], in_=pw.t[:, 0:W_], func=AF.Copy),
                                     r=[pw.b], w=[wT.b])

                            for (g0_, gn_) in TGRP:
                                jobs.append((group_gen, g0_, gn_))
                    active = []
                    free = [0, 1]
                    ji = 0
                    while ji < len(jobs) or active:
                        while free and ji < len(jobs):
                            si = free.pop(0)
                            gf, g0_, gn_ = jobs[ji]
                            ji += 1
                            active.append([gf(g0_, gn_, slots[si]), si])
                        for a_ in list(active):
                            try:
                                next(a_[0])
                            except StopIteration:
                                active.remove(a_)
                                free.append(a_[1])
                    P.barrier()
                    s2.close()
                    with ExitStack() as s3:
                        oacc = sb("goacc", [128, T], F32, s3)
                        otmp = sb("gotmp", [128, 64], F32, s3)
                        Sst = [sb("gS%d" % i, [128, 128], F32, s3) for i in range(2)]
                        Sbf = [sb("gSb%d" % i, [128, 128], BF16, s3) for i in range(2)]
                        vnw = [[sb("gvn%d_%d" % (i, j), [128, 128], BF16, s3) for j in range(2)] for i in range(2)]
                        for i in range(2):
                            P.op("dve", lambda e: e.memset(Sst[i].t[:], 0.0), w=[Sst[i].b])
                            P.op("dve", lambda e: e.memset(Sbf[i].t[:], 0.0), w=[Sbf[i].b])
                        order_f = list(range(36))
                        order_b = [3, 2, 1, 0] + list(range(35, 3, -1))
                        written = set()
                        for step in range(36):
                            ctxs = []
                            for di in range(2):
                                c = (order_f, order_b)[di][step]
                                tt, half = c // 2, c % 2
                                ctxs.append(dict(c=c, tt=tt, half=half, r0=half * 64, d=dirs[di], S_=Sst[di], Sb_=Sbf[di],
                                                 vn=vnw[di][step % 2], pA=psum[3 * di], pB=psum[3 * di + 1], pC=psum[3 * di + 2],
                                                 tsl=slice(tt * 128, (tt + 1) * 128)))
                            for X in ctxs:
                                wT = X["d"][0]
                                P.op("pe", lambda e: e.matmul(X["pA"].t[:, 0:128], lhsT=wT.t[:, X["tsl"]], rhs=X["S_"].t[:], start=True, stop=True),
                                     r=[wT.b, X["S_"].b], w=[X["pA"].b])
                            for X in ctxs:
                                u_ = X["d"][4]
                                r0, tt, vn = X["r0"], X["tt"], X["vn"]
                                P.op("dve", lambda e: e.tensor_tensor(out=vn.t[r0:r0 + 64, :], in0=u_.t[r0:r0 + 64, tt, :],
                                                                      in1=X["pA"].t[r0:r0 + 64, 0:128], op=ALU.subtract),
                                     r=[u_.b, X["pA"].b], w=[vn.b])
                            for X in ctxs:
                                wT, qgT, kd, attnT, u_, egl = X["d"]
                                r0, tt, vn, pB, pC, Sb_ = X["r0"], X["tt"], X["vn"], X["pB"], X["pC"], X["Sb_"]
                                P.op("pe", lambda e: e.matmul(pC.t[:, 0:128], lhsT=kd.t[r0:r0 + 64, tt, :], rhs=vn.t[r0:r0 + 64, :],
                                                              start=True, stop=True), r=[kd.b, vn.b], w=[pC.b])
                                P.op("pe", lambda e: e.matmul(pB.t[:, 0:128], lhsT=Sb_.t[:], rhs=qgT.t[:, X["tsl"]], start=True, stop=False),
                                     r=[Sb_.b, qgT.b], w=[pB.b], inc=False)
                                P.op("pe", lambda e: e.matmul(pB.t[:, 0:128], lhsT=vn.t[r0:r0 + 64, :], rhs=attnT.t[r0:r0 + 64, tt, :],
                                                              start=False, stop=True), r=[vn.b, attnT.b], w=[pB.b])
                            for X in ctxs:
                                egl = X["d"][5]
                                S_, pC, tt, half = X["S_"], X["pC"], X["tt"], X["half"]
                                P.op("dve", lambda e: e.scalar_tensor_tensor(out=S_.t[:], in0=S_.t[:], scalar=egl.t[:, tt, half:half + 1],
                                                                             in1=pC.t[:, 0:128], op0=ALU.mult, op1=ALU.add),
                                     r=[S_.b, egl.b, pC.b], w=[S_.b])
                            for X in ctxs:
                                S_, Sb_, pB, c, r0 = X["S_"], X["Sb_"], X["pB"], X["c"], X["r0"]
                                P.op("act", lambda e: e.activation(out=Sb_.t[:], in_=S_.t[:], func=AF.Copy), r=[S_.b], w=[Sb_.b])
                                osl = slice(c * 64, c * 64 + 64)
                                if c not in written:
                                    written.add(c)
                                    P.op("act", lambda e: e.activation(out=oacc.t[:, osl], in_=pB.t[:, r0:r0 + 64], func=AF.Copy),
                                         r=[pB.b], w=[oacc.b])
                                else:
                                    P.op("act", lambda e: e.activation(out=otmp.t[:, 0:64], in_=pB.t[:, r0:r0 + 64], func=AF.Copy),
                                         r=[pB.b], w=[otmp.b])
                                    P.op("pool", lambda e: e.tensor_tensor(out=oacc.t[:, osl], in0=oacc.t[:, osl], in1=otmp.t[:, 0:64],
                                                                          op=ALU.add), r=[otmp.b, oacc.b], w=[oacc.b])
                        zs = sb("gzs", [128, T], BF16, s3)
                        ob = sb("gob", [128, T], BF16, s3)
                        sq2 = sb("gsq2", [128, 512], F32, s3)
                        rn2 = sb("grn2", [128, 512], F32, s3)
                        P.dma("sp", zs.t[:], fmv[:, OFF_ZB + h, :], r=[s_fm.b], w=[zs.b])
                        for (t0, n) in TGS:
                            P.op("pool", lambda e: e.tensor_tensor(out=sq2.t[:, 0:n], in0=oacc.t[:, t0:t0 + n], in1=oacc.t[:, t0:t0 + n],
                                                                  op=ALU.mult), r=[oacc.b], w=[sq2.b])
                            pq = ps_next()
                            P.op("pe", lambda e: e.matmul(pq.t[:, 0:n], lhsT=ones_f.t[:], rhs=sq2.t[:, 0:n], start=True, stop=True),
                                 r=[sq2.b, ones_f.b], w=[pq.b])
                            P.op("act", lambda e: e.activation(out=rn2.t[:, 0:n], in_=pq.t[:, 0:n], func=AF.Sqrt, bias=epsc.t[:, 0:1],
                                                               scale=1.0 / 128), r=[pq.b, epsc.b], w=[rn2.b])
                            P.op("dve", lambda e: e.reciprocal(out=rn2.t[:, 0:n], in_=rn2.t[:, 0:n]), r=[rn2.b], w=[rn2.b])
                            P.op("dve", lambda e: e.scalar_tensor_tensor(out=rn2.t[:, 0:n], in0=oacc.t[:, t0:t0 + n], scalar=gg.t[:, 0:1],
                                                                         in1=rn2.t[:, 0:n], op0=ALU.mult, op1=ALU.mult),
                                 r=[oacc.b, gg.b, rn2.b], w=[rn2.b])
                            P.op("pool", lambda e: e.tensor_tensor(out=ob.t[:, t0:t0 + n], in0=rn2.t[:, 0:n], in1=zs.t[:, t0:t0 + n],
                                                                  op=ALU.mult), r=[rn2.b, zs.b], w=[ob.b])
                        P.dma("sp", ov[:, h, :], ob.t[:], r=[ob.b], w=[s_o.b])
                        P.barrier()

    if "M" in cfg.stages:
        stage_M()
    for l in range(L):
        layer_coefs(l)
        last = (l == L - 1) and not cfg.force_ctx
        for s in range(NS):
            tgs = TGS[1:] if last else TGS
            if "B" in cfg.stages:
                stage_AB(l, s)
            if s == 0 and ("D" in cfg.stages or "E" in cfg.stages):
                stage_W(l)
            if "C" in cfg.stages:
                if "A" in cfg.mixers:
                    stage_C1(l, s, not last)
                if "N" in cfg.mixers:
                    stage_C2(l, s, not last)
                if "B" in cfg.mixers:
                    stage_C3(l, s)
            if "D" in cfg.stages:
                stage_D(l, s, tgs)
            if "E" in cfg.stages:
                stage_E(l, s, tgs)
    P.barrier()
    gs.close()
    return nc, P


def rope_tables():
    t = np.arange(NX)
    nf = HD // 4
    inv = (10000.0 ** (-np.arange(nf, dtype=np.float32) / nf)).astype(np.float32)
    ang_r = (t // 64).astype(np.float32)[:, None] * inv
    ang_c = (t % 64).astype(np.float32)[:, None] * inv
    cosT = np.ones((128, T), np.float32)
    sinT = np.zeros((128, T), np.float32)
    for a, ang in enumerate((ang_r, ang_c)):
        c = np.cos(ang).T.astype(np.float32)
        s_ = np.sin(ang).T.astype(np.float32)
        cosT[a * 64:a * 64 + 32, NCTX:] = c
        cosT[a * 64 + 32:a * 64 + 64, NCTX:] = c
        sinT[a * 64:a * 64 + 32, NCTX:] = -s_
        sinT[a * 64 + 32:a * 64 + 64, NCTX:] = s_
    return cosT, sinT


def w_in_cols():
    qa0, ka0, va0 = 0, 1024, 1280
    qb0 = 1536
    zb0 = qb0 + 3072
    ab0 = zb0 + 1024
    qn0 = ab0 + 32
    kn0, vn0 = qn0 + 1024, qn0 + 2048
    g0 = qn0 + 3072
    perm = np.concatenate([np.arange(32, 64), np.arange(0, 32), np.arange(96, 128), np.arange(64, 96)])
    cols = []
    for h in range(8):
        base = qa0 + h * 128
        cols.append(base + np.arange(128))
        cols.append(base + perm)
    for h in range(2):
        base = ka0 + h * 128
        cols.append(base + np.arange(128))
        cols.append(base + perm)
    cols.append(np.arange(qb0, qb0 + 3072))
    cols.append(np.arange(zb0, zb0 + 1024))
    cols.append(np.arange(qn0, qn0 + 1024))
    cols.append(np.arange(kn0, kn0 + 1024))
    cols.append(np.arange(g0, g0 + 6144))
    cols.append(np.arange(va0, va0 + 256))
    cols.append(np.arange(vn0, vn0 + 1024))
    cols.append(np.arange(ab0, ab0 + 32))
    cols = np.concatenate(cols)
    assert cols.shape[0] == W_ALL
    return cols


def const_masks():
    m = np.zeros((128, 13, 128), np.float32)
    m[:, 0, :] = np.eye(128, dtype=np.float32)
    p = np.arange(128)[:, None]
    f = np.arange(128)[None, :]
    m[:, 1, :] = (p >= f)
    m[:, 2, :] = (p <= f)
    kc = np.arange(64)[:, None]
    qc = 63 - np.arange(64)[None, :]
    cs = np.clip(qc - 8, 0, 48)
    ok = (kc >= cs) & (kc < cs + 16)
    m[:64, 3, :64] = np.where(ok, 0.0, -30000.0)
    same = (p // 64) == (f // 64)
    m[:, 4, :] = same & (p <= f)
    m[:, 5, :] = same & (p >= f)
    m[:, 6, :] = np.where(same & (p >= f), 0.0, -30000.0)
    m[:, 7, :] = np.where(same & (p <= f), 0.0, -30000.0)
    m[:, 8, :] = same & (p > f)
    m[:, 9, :] = same & (p < f)
    m[:, 10, :] = same
    m[:, 11, :] = (p < 64) & (f >= 0)
    m[:, 12, :] = (p >= 64) & (f >= 0)
    return m


def _tile_w(w, kcb):
    lead = w.shape[:-2]
    K_, N_ = w.shape[-2:]
    kg = K_ // (128 * kcb)
    a = w.reshape(lead + (kg, kcb, 128, N_ // 512, 512))
    nl = len(lead)
    a = np.transpose(a, tuple(range(nl)) + (nl + 3, nl + 0, nl + 2, nl + 1, nl + 4))
    a = np.ascontiguousarray(a).reshape(lead + (N_ // 512, kg, 128, kcb * 512))
    if kg == 1:
        a = a.reshape(lead + (N_ // 512, 128, kcb * 512))
    return a


def host_inputs(inputs, n_cores=8):
    x = np.asarray(inputs["x"], np.float32)
    ctx = np.asarray(inputs["ctx"], np.float32)
    c = np.asarray(inputs["c"], np.float32)
    cols = w_in_cols()
    shared = {
        "w_mod": np.ascontiguousarray(inputs["w_mod"], np.float32),
        "b_mod": np.ascontiguousarray(inputs["b_mod"], np.float32),
        "gvec": np.ascontiguousarray(np.stack([inputs["g_pre_mix"], inputs["g_post_mix"], inputs["g_pre_mlp"],
                                               inputs["g_post_mlp"]], axis=1), np.float32),
        "w_in": np.ascontiguousarray(np.asarray(inputs["w_in"], np.float32)[:, :, cols]),
        "conv_w": np.ascontiguousarray(inputs["conv_w"], np.float32),
        "a_log": np.ascontiguousarray(np.asarray(inputs["a_log"], np.float32).reshape(-1, 16)),
        "dt_bias": np.ascontiguousarray(np.asarray(inputs["dt_bias"], np.float32).reshape(-1, 16)),
        "g_gdn": np.ascontiguousarray(inputs["g_gdn_out"], np.float32),
        "sink": np.ascontiguousarray(inputs["sink"], np.float32),
        "rpb": np.ascontiguousarray(np.asarray(inputs["rpb"], np.float32).reshape(np.asarray(inputs["rpb"]).shape[0], -1)),
        "w_branch": _tile_w(np.asarray(inputs["w_branch"], np.float32), 8),
        "w_out": _tile_w(np.asarray(inputs["w_out"], np.float32), 8),
        "w_up": _tile_w(np.asarray(inputs["w_up"], np.float32), 16),
        "w_down": _tile_w(np.asarray(inputs["w_down"], np.float32), 16),
    }
    cosT, sinT = rope_tables()
    shared["ropec"] = cosT
    shared["ropes"] = sinT
    shared["cmasks"] = const_masks()
    maps = []
    for core in range(n_cores):
        b0 = core * NSEQ
        xin = np.empty((NSEQ, D, T), np.float32)
        for s in range(NSEQ):
            xin[s, :, :NCTX] = ctx[b0 + s].T
            xin[s, :, NCTX:] = x[b0 + s].T
        c3 = np.stack([c[b0], c[b0 + 1], np.asarray(inputs["c_ctx"], np.float32)], axis=0)
        m = dict(shared)
        m["xin"] = xin
        m["c3"] = np.ascontiguousarray(c3)
        maps.append(m)
    return maps


_CACHE = {}


def kernel(**inputs):
    n_cores = 8
    if "nc" not in _CACHE:
        _CACHE["nc"] = build_program(Cfg())[0]
    nc = _CACHE["nc"]
    maps = host_inputs(inputs, n_cores)
    res = run_bass_kernel_spmd(nc, maps, core_ids=list(range(n_cores)))
    out = np.empty((16, NX, D), np.float32)
    for core in range(n_cores):
        y = res.results[core]["yout"]
        for s in range(NSEQ):
            out[core * NSEQ + s] = y[s].T
    return out
```
